# Optimizing a Trainium2 kernel written in Bass

```python
import jax, jax.numpy as jnp
from jax import lax
import numpy as np

D_MODEL = 2048
BATCH = 1
SEQ = 16384
DEPTH = 2

N_MEM = 256
D_MIX = D_MODEL
GROUP_W = D_MIX // 4
LRU_W = GROUP_W
LRU_BLOCKS = 8
LRU_BLOCK_W = LRU_W // LRU_BLOCKS
CONV_W = 4
LRU_C = 8.0
RWKV_W = GROUP_W
RWKV_HEAD = 64
RWKV_HEADS = RWKV_W // RWKV_HEAD
DECAY_LORA = 32
ICLR_LORA = 32
RWKV_SHIFT_W = 3 * RWKV_W + DECAY_LORA + ICLR_LORA
RWKV_GN_EPS = 64e-5
SWA_HEAD = 64
SWA_HEADS = GROUP_W // SWA_HEAD
SWA_KV_HEADS = 2
SWA_Q_PER_KV = SWA_HEADS // SWA_KV_HEADS
WINDOW = 128
BLOCK = 128
MEM_HEADS = 4
MEM_HEAD = GROUP_W // MEM_HEADS
EPS = 1e-6

IN_SPLITS = (
    LRU_W, LRU_W,
    RWKV_SHIFT_W, RWKV_W,
    GROUP_W, SWA_KV_HEADS * SWA_HEAD, SWA_KV_HEADS * SWA_HEAD, GROUP_W,
    GROUP_W, GROUP_W,
)
IN_WIDTH = sum(IN_SPLITS)

kernel_name = "hybrid_parallel_heads_lru_rwkv7_swa_mem"


def _split(t, sizes):
    offs = [int(o) for o in np.cumsum(sizes)[:-1]]
    return jnp.split(t, offs, axis=-1)


def rmsnorm(x, g, eps=EPS):
    xf = x.astype(jnp.float32)
    y = xf * lax.rsqrt(jnp.mean(xf * xf, axis=-1, keepdims=True) + eps)
    return (y * g.astype(jnp.float32)).astype(x.dtype)


def token_shift(t):
    return jnp.pad(t, ((0, 0), (1, 0), (0, 0)))[:, :-1]


def causal_dwconv(x, w, b):
    y = lax.conv_general_dilated(
        x, w[:, None, :].astype(x.dtype), window_strides=(1,), padding=[(CONV_W - 1, 0)],
        dimension_numbers=("NWC", "WIO", "NWC"), feature_group_count=x.shape[-1])
    return y + b.astype(x.dtype)


def rg_lru(xc, w_a, b_a, w_x, b_x, lam):
    B, S, C = xc.shape
    xf = xc.astype(jnp.float32)
    xb = xf.reshape(B, S, LRU_BLOCKS, LRU_BLOCK_W)
    r = jax.nn.sigmoid(jnp.einsum("bshi,hij->bshj", xb, w_a.astype(jnp.float32)).reshape(B, S, C) + b_a)
    i = jax.nn.sigmoid(jnp.einsum("bshi,hij->bshj", xb, w_x.astype(jnp.float32)).reshape(B, S, C) + b_x)
    log_a = -LRU_C * r * jax.nn.softplus(-lam.astype(jnp.float32))
    a = jnp.exp(log_a)
    mult = jnp.sqrt(jnp.maximum(-jnp.expm1(2.0 * log_a), 1e-12))
    u = mult * (i * xf)

    def combine(c1, c2):
        a1, b1 = c1
        a2, b2 = c2
        return a1 * a2, a2 * b1 + b2

    _, h = lax.associative_scan(combine, (a, u), axis=1)
    return h


def rwkv7_time_mix(p_shift, mu, w0, w_up, a0, a_up, k_k, k_a, r_k, gn_g, gn_b):
    B, S, _ = p_shift.shape
    H, N = RWKV_HEADS, RWKV_HEAD
    pf = p_shift.astype(jnp.float32)
    pm = pf + (token_shift(pf) - pf) * mu
    r, k, v, wd, ad = _split(pm, (RWKV_W, RWKV_W, RWKV_W, DECAY_LORA, ICLR_LORA))
    w = -jax.nn.softplus(-(w0 + jnp.tanh(wd) @ w_up)) - 0.5
    decay = jnp.exp(-jnp.exp(w))
    a = jax.nn.sigmoid(a0 + ad @ a_up)
    kk = (k * k_k).reshape(B, S, H, N)
    kk = kk / jnp.maximum(jnp.linalg.norm(kk, axis=-1, keepdims=True), 1e-12)
    k = k * (1.0 + (a - 1.0) * k_a)
    hd = lambda t: t.reshape(B, S, H, N)
    r_h, w_h, k_h, v_h, a_h = hd(r), hd(decay), hd(k), hd(v), hd(a)
    b_h = kk * a_h

    def step(state, inp):
        r_t, w_t, k_t, v_t, kk_t, b_t = inp
        sa = jnp.einsum("bhij,bhj->bhi", state, -kk_t)
        state = (state * w_t[:, :, None, :] + sa[..., None] * b_t[:, :, None, :]
                 + v_t[..., None] * k_t[:, :, None, :])
        return state, jnp.einsum("bhij,bhj->bhi", state, r_t)

    xs = tuple(jnp.moveaxis(t, 1, 0) for t in (r_h, w_h, k_h, v_h, kk, b_h))
    _, y = lax.scan(step, jnp.zeros((B, H, N, N), jnp.float32), xs)
    y = jnp.moveaxis(y, 0, 1)
    mean = jnp.mean(y, axis=-1, keepdims=True)
    var = jnp.mean(jnp.square(y - mean), axis=-1, keepdims=True)
    y = ((y - mean) * lax.rsqrt(var + RWKV_GN_EPS)).reshape(B, S, RWKV_W) * gn_g + gn_b
    bonus = jnp.sum(r_h * k_h * r_k, axis=-1, keepdims=True) * v_h
    return y + bonus.reshape(B, S, RWKV_W)


def sliding_window_attention(q, k, v, q_g, k_g, sinks):
    B, S, _ = q.shape
    nb = S // BLOCK
    f32 = jnp.float32
    q = rmsnorm(q.reshape(B, S, SWA_HEADS, SWA_HEAD).astype(f32), q_g)
    k = rmsnorm(k.reshape(B, S, SWA_KV_HEADS, SWA_HEAD).astype(f32), k_g)
    v = v.reshape(B, S, SWA_KV_HEADS, SWA_HEAD).astype(f32)
    qb = q.reshape(B, nb, BLOCK, SWA_KV_HEADS, SWA_Q_PER_KV, SWA_HEAD)
    kb = k.reshape(B, nb, BLOCK, SWA_KV_HEADS, SWA_HEAD)
    vb = v.reshape(B, nb, BLOCK, SWA_KV_HEADS, SWA_HEAD)

    def with_prev(t):
        prev = jnp.pad(t, ((0, 0), (1, 0), (0, 0), (0, 0), (0, 0)))[:, :-1]
        return jnp.concatenate([prev, t], axis=2)

    kw, vw = with_prev(kb), with_prev(vb)
    scores = jnp.einsum("bnqkgd,bnskd->bnkgqs", qb, kw) * (SWA_HEAD ** -0.5)
    qi = jnp.arange(BLOCK)[:, None]
    sj = jnp.arange(2 * BLOCK)[None, :] - BLOCK
    rel = qi - sj
    band = (rel >= 0) & (rel < WINDOW)
    key_ok = (jnp.arange(nb)[:, None] * BLOCK + sj) >= 0
    mask = band[None] & key_ok[:, None, :]
    scores = jnp.where(mask[None, :, None, None], scores, -jnp.inf)
    sink = jnp.broadcast_to(
        sinks.astype(f32).reshape(SWA_KV_HEADS, SWA_Q_PER_KV)[None, None, :, :, None, None],
        scores.shape[:-1] + (1,))
    probs = jax.nn.softmax(jnp.concatenate([scores, sink], axis=-1), axis=-1)[..., :-1]
    out = jnp.einsum("bnkgqs,bnskd->bnqkgd", probs, vw)
    return out.reshape(B, S, GROUP_W)


def memory_cross_attention(q, mem, mem_g, w_kv, q_g, k_g):
    B, S, _ = q.shape
    M = mem.shape[1]
    f32 = jnp.float32
    kv = rmsnorm(mem, mem_g) @ w_kv
    mk, mv = jnp.split(kv, 2, axis=-1)
    q = rmsnorm(q.reshape(B, S, MEM_HEADS, MEM_HEAD).astype(f32), q_g)
    mk = rmsnorm(mk.reshape(B, M, MEM_HEADS, MEM_HEAD).astype(f32), k_g)
    mv = mv.reshape(B, M, MEM_HEADS, MEM_HEAD).astype(f32)
    s = jnp.einsum("bshd,bmhd->bhsm", q, mk) * (MEM_HEAD ** -0.5)
    p = jax.nn.softmax(s, axis=-1)
    return jnp.einsum("bhsm,bmhd->bshd", p, mv).reshape(B, S, GROUP_W)


def setup_inputs(seed: int = 0) -> dict:
    key = jax.random.key(seed)
    ks = jax.random.split(key, 32)
    f32 = jnp.float32
    L = DEPTH

    def nrm(k, shape, s):
        return s * jax.random.normal(k, shape, f32)

    a_c = jax.random.uniform(ks[10], (L, LRU_W), f32, 0.9, 0.999)
    p = a_c ** (1.0 / LRU_C)
    return {
        "x": nrm(ks[0], (BATCH, SEQ, D_MODEL), 1.0),
        "mem": nrm(ks[1], (BATCH, N_MEM, D_MODEL), 1.0),
        "norm_g": 1.0 + nrm(ks[2], (L, D_MODEL), 0.02),
        "w_in": nrm(ks[3], (L, D_MODEL, IN_WIDTH), D_MODEL ** -0.5),
        "conv_w": nrm(ks[4], (L, CONV_W, LRU_W), CONV_W ** -0.5),
        "conv_b": nrm(ks[5], (L, LRU_W), 0.01),
        "lru_wa": nrm(ks[6], (L, LRU_BLOCKS, LRU_BLOCK_W, LRU_BLOCK_W), LRU_BLOCK_W ** -0.5),
        "lru_ba": nrm(ks[7], (L, LRU_W), 0.01),
        "lru_wx": nrm(ks[8], (L, LRU_BLOCKS, LRU_BLOCK_W, LRU_BLOCK_W), LRU_BLOCK_W ** -0.5),
        "lru_bx": nrm(ks[9], (L, LRU_W), 0.01),
        "lru_lambda": jnp.log(p) - jnp.log1p(-p),
        "rw_mu": jax.random.uniform(ks[11], (L, RWKV_SHIFT_W), f32, 0.2, 0.8),
        "rw_w0": jax.random.uniform(ks[12], (L, RWKV_W), f32, -6.0, -1.0),
        "rw_w_up": nrm(ks[13], (L, DECAY_LORA, RWKV_W), 0.5 * DECAY_LORA ** -0.5),
        "rw_a0": nrm(ks[14], (L, RWKV_W), 0.5),
        "rw_a_up": nrm(ks[15], (L, ICLR_LORA, RWKV_W), 0.5 * ICLR_LORA ** -0.5),
        "rw_k_k": 0.85 + nrm(ks[16], (L, RWKV_W), 0.05),
        "rw_k_a": 1.0 + nrm(ks[17], (L, RWKV_W), 0.05),
        "rw_r_k": nrm(ks[18], (L, RWKV_HEADS, RWKV_HEAD), 0.1),
        "rw_gn_g": 1.0 + nrm(ks[19], (L, RWKV_W), 0.02),
        "rw_gn_b": nrm(ks[20], (L, RWKV_W), 0.01),
        "swa_q_g": 1.0 + nrm(ks[21], (L, SWA_HEAD), 0.02),
        "swa_k_g": 1.0 + nrm(ks[22], (L, SWA_HEAD), 0.02),
        "swa_sinks": nrm(ks[23], (L, SWA_HEADS), 0.5),
        "mem_norm_g": 1.0 + nrm(ks[24], (L, D_MODEL), 0.02),
        "w_mem_kv": nrm(ks[25], (L, D_MODEL, 2 * GROUP_W), D_MODEL ** -0.5),
        "mem_q_g": 1.0 + nrm(ks[26], (L, MEM_HEAD), 0.02),
        "mem_k_g": 1.0 + nrm(ks[27], (L, MEM_HEAD), 0.02),
        "w_out": nrm(ks[28], (L, D_MIX, D_MODEL), 0.5 * D_MIX ** -0.5),
    }


def reference(x, mem, norm_g, w_in, conv_w, conv_b, lru_wa, lru_ba, lru_wx, lru_bx, lru_lambda,
              rw_mu, rw_w0, rw_w_up, rw_a0, rw_a_up, rw_k_k, rw_k_a, rw_r_k, rw_gn_g, rw_gn_b,
              swa_q_g, swa_k_g, swa_sinks, mem_norm_g, w_mem_kv, mem_q_g, mem_k_g, w_out):
    f32 = jnp.float32
    for l in range(DEPTH):
        h = rmsnorm(x, norm_g[l])
        p = h @ w_in[l]
        (lru_x, lru_g, rw_p, rw_g, sq, sk, sv, sg, mq, mg) = _split(p, IN_SPLITS)
        y_lru = rg_lru(causal_dwconv(lru_x, conv_w[l], conv_b[l]),
                       lru_wa[l], lru_ba[l], lru_wx[l], lru_bx[l], lru_lambda[l])
        y_rw = rwkv7_time_mix(rw_p, rw_mu[l], rw_w0[l], rw_w_up[l], rw_a0[l], rw_a_up[l],
                              rw_k_k[l], rw_k_a[l], rw_r_k[l], rw_gn_g[l], rw_gn_b[l])
        y_swa = sliding_window_attention(sq, sk, sv, swa_q_g[l], swa_k_g[l], swa_sinks[l])
        y_mem = memory_cross_attention(mq, mem, mem_norm_g[l], w_mem_kv[l], mem_q_g[l], mem_k_g[l])
        o = jnp.concatenate([
            y_lru * jax.nn.silu(lru_g.astype(f32)),
            y_rw * jax.nn.silu(rw_g.astype(f32)),
            y_swa * jax.nn.silu(sg.astype(f32)),
            y_mem * jax.nn.silu(mg.astype(f32)),
        ], axis=-1).astype(x.dtype)
        x = x + o @ w_out[l]
    return x
```

```python
from concourse.bass_utils import run_bass_kernel_spmd
from contextlib import ExitStack
import numpy as np
import concourse.bass as bass
import concourse.mybir as mybir

F32 = mybir.dt.float32
BF16 = mybir.dt.bfloat16
ALU = mybir.AluOpType
AF = mybir.ActivationFunctionType
AX = mybir.AxisListType

MAXV = 30000
ENGS = ("pe", "act", "dve", "pool", "sp")


class Ev:
    __slots__ = ("key", "n")

    def __init__(self, key, n=None):
        self.key = key
        self.n = n


class Res:
    __slots__ = ("name", "writers", "readers", "excl")

    def __init__(self, name="", excl=False):
        self.name = name
        self.writers = []
        self.readers = []
        self.excl = excl


class Counter:
    def __init__(self, prog, name):
        self.sem = prog.es.enter_context(prog.nc.semaphore(name))
        self.total = 0
        self.key = ("d", id(self))
        prog.counters[self.key] = self


class Tile:
    def __init__(self, prog, shape, dtype, name=None, psum=False, persistent=False):
        prog.nsb += 1
        nm = f"t{prog.nsb}_{name or ''}"
        st = prog.es if persistent else prog.stage_es
        if psum:
            self.t = st.enter_context(prog.nc.psum_tensor(nm, list(shape), dtype))
        else:
            self.t = st.enter_context(prog.nc.sbuf_tensor(nm, list(shape), dtype))
        self.r = Res(name or "", excl=psum)
        self.prog = prog
        self._c = None

    @property
    def c(self):
        if self._c is None:
            self._c = self.prog.counter()
        return self._c

    def __getitem__(self, idx):
        return self.t[idx]


class Op:
    __slots__ = ("eng", "fn", "waits", "signal", "ev", "ctr")


def _rs(xs):
    return [getattr(x, "r", x) for x in xs]


class Prog:
    def __init__(self, nc):
        self.nc = nc
        self.es = ExitStack()
        self.stage_es = ExitStack()
        self.ops = {e: [] for e in ENGS}
        self.sigcnt = {e: 0 for e in ENGS}
        self.emitted = {e: 0 for e in ENGS}
        self.pending = {e: [] for e in ENGS}
        self.waited = {e: {} for e in ENGS}
        self.counters = {}
        self.free_counters = []
        self.stage_counters = []
        self.esems = {e: [] for e in ENGS}
        self.nsb = 0
        self.nstage = 0
        self.total_ops = 0

    def tile(self, shape, dtype, name=None, psum=False, persistent=False):
        return Tile(self, shape, dtype, name, psum, persistent)

    def counter(self, name=None, persistent=False):
        if self.free_counters and not persistent:
            c = self.free_counters.pop()
        else:
            self.nsb += 1
            c = Counter(self, name or f"ctr{self.nsb}")
        if not persistent:
            self.stage_counters.append(c)
        return c

    def _deps(self, reads, writes, accs):
        waits = []
        for r in reads:
            waits.extend(r.writers)
        for r in writes:
            waits.extend(r.writers)
            waits.extend(r.readers)
        for r in accs:
            waits.extend(r.readers)
        return waits

    def _post(self, ev, reads, writes, accs):
        for r in reads:
            r.readers.append(ev)
            if len(r.readers) > 64:
                r.readers = _compact(r.readers)
        for r in writes:
            r.writers = [ev]
            r.readers = []
        for r in accs:
            r.writers.append(ev)
            if len(r.writers) > 64:
                r.writers = _compact(r.writers)

    def op(self, eng, fn, reads=(), writes=(), accs=(), signal=True):
        reads, writes, accs = _rs(reads), _rs(writes), _rs(accs)
        ex = [r for r in reads if r.excl]
        if ex:
            reads = [r for r in reads if not r.excl]
            writes = list(writes) + ex
        o = Op()
        o.eng = eng
        o.fn = fn
        o.ctr = None
        waits = self._deps(reads, writes, accs)
        if eng == "pe":
            waits = [w for w in waits if w.key != "pe"]
        o.waits = waits
        o.signal = False
        ev = Ev(eng)
        o.ev = ev
        self.ops[eng].append(o)
        self.pending[eng].append(ev)
        if signal:
            self.signal_last(eng)
        self._post(ev, reads, writes, accs)
        return ev

    def signal_last(self, eng):
        o = self.ops[eng][-1]
        if o.signal:
            return
        o.signal = True
        self.sigcnt[eng] += 1
        for p in self.pending[eng]:
            p.n = self.sigcnt[eng]
        self.pending[eng] = []

    def dma(self, eng, ctr, fn, reads=(), writes=(), accs=(), inc=16):
        reads, writes, accs = _rs(reads), _rs(writes), _rs(accs)
        o = Op()
        o.eng = eng
        o.fn = fn
        o.ctr = ctr
        o.waits = self._deps(reads, writes, accs)
        o.signal = (inc == 16)
        ctr.total += inc
        assert ctr.total < 60000
        ev = Ev(ctr.key, ctr.total)
        o.ev = ev
        self.ops[eng].append(o)
        self._post(ev, reads, writes, accs)
        return ev

    def finish(self, eng="sp"):
        o = Op()
        o.eng = eng
        o.fn = None
        o.ctr = None
        o.signal = False
        o.ev = Ev(eng)
        o.waits = [Ev(c.key, c.total) for c in self.counters.values() if c.total]
        self.ops[eng].append(o)

    def end_stage(self):
        nc = self.nc
        self.finish("sp")
        for e in ENGS:
            if self.pending[e]:
                self.signal_last(e)
            need = (self.sigcnt[e] + MAXV - 1) // MAXV + 1
            while len(self.esems[e]) < need:
                self.esems[e].append(self.es.enter_context(nc.semaphore(f"s_{e}{len(self.esems[e])}")))
        prog = self

        def resolve(ev):
            if isinstance(ev.key, tuple):
                return (ev.key, prog.counters[ev.key].sem, ev.n)
            assert ev.n is not None, f"unresolved event on {ev.key}"
            idx = (ev.n - 1) // MAXV
            return ((ev.key, idx), prog.esems[ev.key][idx], (ev.n - 1) % MAXV + 1)

        def run(e):
            def body(eng):
                waited = prog.waited[e]
                cnt = prog.emitted[e]
                for o in prog.ops[e]:
                    need = {}
                    for w in o.waits:
                        k, sem, v = resolve(w)
                        if waited.get(k, 0) >= v:
                            continue
                        if k not in need or need[k][1] < v:
                            need[k] = (sem, v)
                    for k, (sem, v) in need.items():
                        eng.wait_ge(sem, v)
                        waited[k] = v
                    if o.fn is None:
                        continue
                    inst = o.fn(eng)
                    if o.ctr is not None:
                        if o.signal:
                            inst.then_inc(o.ctr.sem, 16)
                        else:
                            inst.then_inc(o.ctr.sem)
                    elif o.signal:
                        cnt += 1
                        idx = (cnt - 1) // MAXV
                        inst.then_inc(prog.esems[e][idx], 1)
                prog.emitted[e] = cnt
            return body

        with nc.Block() as block:
            block.tensor(run("pe"))
            block.scalar(run("act"))
            block.vector(run("dve"))
            block.gpsimd(run("pool"))
            block.sync(run("sp"))
        for e in ENGS:
            assert self.emitted[e] == self.sigcnt[e], (e, self.emitted[e], self.sigcnt[e])
            self.total_ops += len(self.ops[e])
            self.ops[e] = []
        self.stage_es.close()
        self.stage_es = ExitStack()
        self.free_counters.extend(self.stage_counters)
        self.stage_counters = []
        self.nstage += 1

    def close(self):
        self.es.close()


def _compact(evs):
    best = {}
    for ev in evs:
        if ev.n is None:
            best[id(ev)] = ev
            continue
        k = ev.key
        if k not in best or best[k].n < ev.n:
            best[k] = ev
    return list(best.values())


LRU_C = 8.0


def load_col(P, ctr, res, dst_ap, src_ap, eng="sp"):
    P.dma(eng, ctr, lambda e: e.dma_start(out=dst_ap, in_=src_ap), accs=[res])


def lru_stage(P, CP, NTOK, xrows, grows, orows, convw, convb, wa, ba, wx, bx, lam, TP=2048, tag=""):
    nb = CP // 64
    npc = NTOK // TP
    prm = P.tile([CP, 16], F32, f"lruprm{tag}")
    wabd = P.tile([CP, CP], F32, f"wabd{tag}")
    wxbd = P.tile([CP, CP], F32, f"wxbd{tag}")
    if nb > 1:
        P.op("pool", lambda e: e.memset(wabd[:], 0.0), writes=[wabd])
        P.op("pool", lambda e: e.memset(wxbd[:], 0.0), writes=[wxbd])
    for b in range(nb):
        P.dma("sp", wabd.c, lambda e, b=b: e.dma_start(out=wabd[b * 64:(b + 1) * 64, b * 64:(b + 1) * 64], in_=wa[b]),
              accs=[wabd])
        P.dma("sp", wxbd.c, lambda e, b=b: e.dma_start(out=wxbd[b * 64:(b + 1) * 64, b * 64:(b + 1) * 64], in_=wx[b]),
              accs=[wxbd])
    P.dma("sp", prm.c, lambda e: e.dma_start(out=prm[:, 0:4], in_=convw, allow_slow_non_contiguous=True), writes=[prm])
    for i, src in enumerate((convb, ba, bx, lam)):
        P.dma("sp", prm.c, lambda e, i=i, src=src: e.dma_start(out=prm[:, 4 + i:5 + i], in_=src, allow_slow_non_contiguous=True), accs=[prm])
    P.op("act", lambda e: e.activation(out=prm[:, 8:9], in_=prm[:, 7:8], func=AF.Exp, scale=-1.0),
         reads=[prm], accs=[prm])
    P.op("act", lambda e: e.activation(out=prm[:, 9:10], in_=prm[:, 8:9], func=AF.Ln, bias=1.0),
         reads=[prm], accs=[prm])
    P.op("dve", lambda e: e.tensor_scalar(out=prm[:, 10:11], in0=prm[:, 9:10], scalar1=-LRU_C, scalar2=None,
                                         op0=ALU.mult), reads=[prm], accs=[prm])
    P.op("dve", lambda e: e.memset(prm[:, 11:12], 0.0), reads=[prm], accs=[prm])

    xt = [P.tile([CP, TP + 3], F32, f"lxt{tag}{i}") for i in range(2)]
    gt = [P.tile([CP, TP], F32, f"lgt{tag}{i}") for i in range(2)]
    xc = P.tile([CP, TP], F32, f"lxc{tag}")
    rr = P.tile([CP, TP], F32, f"lr{tag}")
    ii = P.tile([CP, TP], F32, f"li{tag}")
    aa = P.tile([CP, TP], F32, f"la{tag}")
    mm = P.tile([CP, TP], F32, f"lm{tag}")
    hh = [P.tile([CP, TP], F32, f"lh{tag}{i}") for i in range(2)]
    ob = [P.tile([CP, TP], BF16, f"lo{tag}{i}") for i in range(2)]
    pg = [P.tile([CP, 512], F32, f"lpg{tag}{i}", psum=True) for i in range(2)]
    npg = 0
    for pi in range(npc):
        s = pi % 2
        t0 = pi * TP
        X, G, H, O = xt[s], gt[s], hh[s], ob[s]
        if pi == 0:
            P.op("pool", lambda e, X=X: e.memset(X[:, 0:3], 0.0), writes=[X])
            P.dma("sp", X.c, lambda e, X=X: e.dma_start(out=X[:, 3:3 + TP], in_=xrows[:, 0:TP]), accs=[X])
        else:
            P.dma("sp", X.c, lambda e, X=X, t0=t0: e.dma_start(out=X[:, :], in_=xrows[:, t0 - 3:t0 + TP]), writes=[X])
        P.dma("sp", G.c, lambda e, G=G, t0=t0: e.dma_start(out=G[:, :], in_=grows[:, t0:t0 + TP]), writes=[G])
        P.op("dve", lambda e, X=X: e.tensor_scalar(out=xc[:], in0=X[:, 3:3 + TP], scalar1=prm[:, 3:4],
                                                  scalar2=prm[:, 4:5], op0=ALU.mult, op1=ALU.add),
             reads=[X, prm], writes=[xc])
        for j in range(3):
            P.op("dve", lambda e, X=X, j=j: e.scalar_tensor_tensor(out=xc[:], in0=X[:, j:j + TP], scalar=prm[:, j:j + 1],
                                                                  in1=xc[:], op0=ALU.mult, op1=ALU.add),
                 reads=[X, prm, xc], writes=[xc])
        for (wbd, bcol, dst) in ((wabd, 5, rr), (wxbd, 6, ii)):
            for sb_ in range(TP // 512):
                pb = pg[npg % 2]
                npg += 1
                P.op("pe", lambda e, wbd=wbd, pb=pb, sb_=sb_: e.matmul(pb[:, :], lhsT=wbd[:, :],
                                                                      rhs=xc[:, sb_ * 512:(sb_ + 1) * 512],
                                                                      start=True, stop=True),
                     reads=[wbd, xc], writes=[pb])
                P.op("act", lambda e, pb=pb, dst=dst, sb_=sb_, bcol=bcol: e.activation(
                    out=dst[:, sb_ * 512:(sb_ + 1) * 512], in_=pb[:, :], func=AF.Sigmoid, bias=prm[:, bcol:bcol + 1]),
                    reads=[pb, prm], writes=[dst] if sb_ == 0 else [], accs=[] if sb_ == 0 else [dst])
        P.op("act", lambda e: e.activation(out=aa[:], in_=rr[:], func=AF.Exp, scale=prm[:, 10:11]),
             reads=[rr, prm], writes=[aa])
        P.op("dve", lambda e: e.tensor_tensor(out=mm[:], in0=aa[:], in1=aa[:], op=ALU.mult), reads=[aa], writes=[mm])
        P.op("dve", lambda e: e.tensor_scalar(out=mm[:], in0=mm[:], scalar1=-1.0, scalar2=1.0, op0=ALU.mult,
                                             op1=ALU.add), reads=[mm], writes=[mm])
        P.op("dve", lambda e: e.tensor_scalar(out=mm[:], in0=mm[:], scalar1=1e-12, scalar2=None, op0=ALU.max),
             reads=[mm], writes=[mm])
        P.op("act", lambda e: e.activation(out=mm[:], in_=mm[:], func=AF.Sqrt), reads=[mm], writes=[mm])
        P.op("dve", lambda e: e.tensor_tensor(out=ii[:], in0=ii[:], in1=xc[:], op=ALU.mult), reads=[ii, xc], writes=[ii])
        P.op("dve", lambda e: e.tensor_tensor(out=ii[:], in0=ii[:], in1=mm[:], op=ALU.mult), reads=[ii, mm], writes=[ii])
        Hp = hh[1 - s]
        init = prm[:, 11:12] if pi == 0 else Hp[:, TP - 1:TP]
        P.op("dve", lambda e, H=H, init=init: e.tensor_tensor_scan(out=H[:], data0=aa[:], data1=ii[:], initial=init,
                                                                  op0=ALU.mult, op1=ALU.add),
             reads=[aa, ii, prm, Hp], writes=[H])
        P.op("act", lambda e, G=G: e.activation(out=G[:], in_=G[:], func=AF.Silu), reads=[G], writes=[G])
        P.op("dve", lambda e, H=H, G=G, O=O: e.tensor_tensor(out=O[:], in0=H[:], in1=G[:], op=ALU.mult),
             reads=[H, G], writes=[O])
        P.dma("sp", O.c, lambda e, O=O, t0=t0: e.dma_start(out=orows[:, t0:t0 + TP], in_=O[:]), reads=[O])


class Consts:
    def __init__(self, P, c_identf, c_identb, c_swamask):
        self.identf = P.tile([128, 128], F32, "identf", persistent=True)
        self.identb = P.tile([128, 128], BF16, "identb", persistent=True)
        self.swamask = P.tile([128, 256], BF16, "swamask", persistent=True)
        self.ones_f = P.tile([128, 128], F32, "ones_f", persistent=True)
        self.ones_b = P.tile([128, 128], BF16, "ones_b", persistent=True)
        P.dma("sp", self.identf.c, lambda e: e.dma_start(out=self.identf[:], in_=c_identf), writes=[self.identf])
        P.dma("sp", self.identb.c, lambda e: e.dma_start(out=self.identb[:], in_=c_identb), writes=[self.identb])
        P.dma("sp", self.swamask.c, lambda e: e.dma_start(out=self.swamask[:], in_=c_swamask), writes=[self.swamask])
        P.op("pool", lambda e: e.memset(self.ones_f[:], 1.0), writes=[self.ones_f])
        P.op("pool", lambda e: e.memset(self.ones_b[:], 1.0), writes=[self.ones_b])


def swa_stage(P, K, NTOK, qrows, krows, vrows, grows, orows, qg, kg, sink, TP=512, tag=""):
    npc = NTOK // TP
    nbk = TP // 128
    prm = P.tile([64, 8], F32, f"swaprm{tag}")
    P.dma("sp", prm.c, lambda e: e.dma_start(out=prm[:, 0:1], in_=qg, allow_slow_non_contiguous=True), writes=[prm])
    P.dma("sp", prm.c, lambda e: e.dma_start(out=prm[:, 1:2], in_=kg, allow_slow_non_contiguous=True), accs=[prm])
    P.dma("sp", prm.c, lambda e: e.dma_start(out=prm[:, 2:3], in_=sink, allow_slow_non_contiguous=True), accs=[prm])
    P.op("dve", lambda e: e.tensor_scalar(out=prm[:, 3:4], in0=prm[:, 0:1], scalar1=0.125, scalar2=None, op0=ALU.mult),
         reads=[prm], accs=[prm])
    P.op("act", lambda e: e.activation(out=prm[:, 4:5], in_=prm[:, 2:3], func=AF.Exp), reads=[prm], accs=[prm])
    W = TP + 128
    kx = [P.tile([64, W], F32, f"skx{tag}{i}") for i in range(2)]
    vx = [P.tile([64, W], F32, f"svx{tag}{i}") for i in range(2)]
    qx = [P.tile([64, TP], F32, f"sqx{tag}{i}") for i in range(2)]
    gx = [P.tile([64, TP], F32, f"sgx{tag}{i}") for i in range(2)]
    sq = P.tile([64, W], F32, f"ssq{tag}")
    rs = P.tile([64, W], F32, f"srs{tag}")
    kn = P.tile([64, W], BF16, f"skn{tag}")
    qn = P.tile([64, TP], BF16, f"sqn{tag}")
    vb = P.tile([128, nbk + 1, 64], BF16, f"svb{tag}")
    E = [P.tile([128, 256], BF16, f"sE{tag}{i}") for i in range(2)]
    dn = P.tile([64, TP], F32, f"sdn{tag}")
    yy = P.tile([64, TP], F32, f"syy{tag}")
    ob = [P.tile([64, TP], BF16, f"sob{tag}{i}") for i in range(2)]
    pn = [P.tile([64, 512], F32, f"spn{tag}{i}", psum=True) for i in range(2)]
    pt = P.tile([128, 512], F32, f"spt{tag}", psum=True)
    psc = [P.tile([128, 256], F32, f"spsc{tag}{i}", psum=True) for i in range(2)]
    pnum = P.tile([64, 512], F32, f"spnum{tag}", psum=True)
    pden = P.tile([64, 512], F32, f"spden{tag}", psum=True)
    npn = 0
    nsc = 0

    def norm(src, width, c0, gcol, dst, dst_c0):
        nonlocal npn
        P.op("dve", lambda e: e.tensor_tensor(out=sq[:, 0:width], in0=src[:, c0:c0 + width], in1=src[:, c0:c0 + width],
                                             op=ALU.mult), reads=[src], writes=[sq])
        o = 0
        first = True
        while o < width:
            w_ = min(512, width - o)
            pb = pn[npn % 2]
            npn += 1
            P.op("pe", lambda e, pb=pb, o=o, w_=w_: e.matmul(pb[:, 0:w_], lhsT=K.ones_f[0:64, 0:64], rhs=sq[:, o:o + w_],
                                                            start=True, stop=True), reads=[K.ones_f, sq], writes=[pb])
            P.op("act", lambda e, pb=pb, o=o, w_=w_: e.activation(out=rs[:, o:o + w_], in_=pb[:, 0:w_], func=AF.Sqrt,
                                                                 scale=1.0 / 64, bias=1e-6),
                 reads=[pb], writes=[rs] if first else [], accs=[] if first else [rs])
            first = False
            o += w_
        P.op("dve", lambda e: e.reciprocal(out=rs[:, 0:width], in_=rs[:, 0:width]), reads=[rs], writes=[rs])
        P.op("dve", lambda e: e.scalar_tensor_tensor(out=dst[:, dst_c0:dst_c0 + width], in0=src[:, c0:c0 + width],
                                                    scalar=prm[:, gcol:gcol + 1], in1=rs[:, 0:width],
                                                    op0=ALU.mult, op1=ALU.mult), reads=[src, prm, rs], writes=[dst])

    for pi in range(npc):
        s = pi % 2
        t0 = pi * TP
        KX, VX, QX, GX, O = kx[s], vx[s], qx[s], gx[s], ob[s]
        lo = 128 if pi == 0 else 0
        P.dma("sp", KX.c, lambda e, KX=KX, t0=t0, lo=lo: e.dma_start(out=KX[:, lo:W], in_=krows[:, t0 - 128 + lo:t0 + TP]),
              writes=[KX])
        P.dma("sp", VX.c, lambda e, VX=VX, t0=t0, lo=lo: e.dma_start(out=VX[:, lo:W], in_=vrows[:, t0 - 128 + lo:t0 + TP]),
              writes=[VX])
        P.dma("sp", QX.c, lambda e, QX=QX, t0=t0: e.dma_start(out=QX[:, :], in_=qrows[:, t0:t0 + TP]), writes=[QX])
        P.dma("sp", GX.c, lambda e, GX=GX, t0=t0: e.dma_start(out=GX[:, :], in_=grows[:, t0:t0 + TP]), writes=[GX])
        norm(KX, W - lo, lo, 1, kn, lo)
        norm(QX, TP, 0, 3, qn, 0)
        b0 = lo // 128
        for b in range(b0, nbk + 1):
            P.op("pe", lambda e, VX=VX, b=b: e.transpose(out=pt[:, b * 64:(b + 1) * 64], in_=VX[:, b * 128:(b + 1) * 128],
                                                        identity=K.identf[0:64, 0:64]),
                 reads=[VX, K.identf], writes=[pt] if b == b0 else [], accs=[] if b == b0 else [pt],
                 signal=(b == nbk))
        P.op("act", lambda e, b0=b0: e.activation(out=vb[:, b0:nbk + 1, :],
                                                 in_=pt[:, b0 * 64:(nbk + 1) * 64].rearrange("p (b d) -> p b d", d=64),
                                                 func=AF.Copy), reads=[pt], writes=[vb])
        for n in range(nbk):
            has_prev = not (pi == 0 and n == 0)
            sc = psc[nsc % 2]
            Eb = E[nsc % 2]
            nsc += 1
            wd = 256 if has_prev else 128
            P.op("pe", lambda e, sc=sc, n=n: e.matmul(sc[:, 0:128], lhsT=kn[:, (n + 1) * 128:(n + 2) * 128],
                                                     rhs=qn[:, n * 128:(n + 1) * 128], start=True, stop=True),
                 reads=[kn, qn], writes=[sc], signal=not has_prev)
            if has_prev:
                P.op("pe", lambda e, sc=sc, n=n: e.matmul(sc[:, 128:256], lhsT=kn[:, n * 128:(n + 1) * 128],
                                                         rhs=qn[:, n * 128:(n + 1) * 128], start=True, stop=True),
                     reads=[kn, qn], accs=[sc])
            P.op("act", lambda e, sc=sc, Eb=Eb, wd=wd: e.activation(out=Eb[:, 0:wd], in_=sc[:, 0:wd], func=AF.Exp),
                 reads=[sc], writes=[Eb])
            P.op("pool", lambda e, Eb=Eb, wd=wd: e.tensor_tensor(out=Eb[:, 0:wd], in0=Eb[:, 0:wd], in1=K.swamask[:, 0:wd],
                                                                op=ALU.mult), reads=[Eb, K.swamask], writes=[Eb])
            cs = slice(n * 128, (n + 1) * 128)
            for (pacc, lhs_cur, lhs_prev) in ((pnum, vb[:, n + 1, :], vb[:, n, :]),
                                              (pden, K.ones_b[:, 0:64], K.ones_b[:, 0:64])):
                P.op("pe", lambda e, pacc=pacc, lhs_cur=lhs_cur, Eb=Eb, cs=cs, has_prev=has_prev: e.matmul(
                    pacc[:, cs], lhsT=lhs_cur, rhs=Eb[:, 0:128], start=True, stop=not has_prev),
                    reads=[vb, K.ones_b, Eb], writes=[pacc] if n == 0 else [], accs=[] if n == 0 else [pacc],
                    signal=False)
                if has_prev:
                    P.op("pe", lambda e, pacc=pacc, lhs_prev=lhs_prev, Eb=Eb, cs=cs: e.matmul(
                        pacc[:, cs], lhsT=lhs_prev, rhs=Eb[:, 128:256], start=False, stop=True),
                        reads=[vb, K.ones_b, Eb], accs=[pacc], signal=False)
            P.signal_last("pe")
        P.op("dve", lambda e: e.tensor_scalar(out=dn[:], in0=pden[:, :], scalar1=prm[:, 4:5], scalar2=None, op0=ALU.add),
             reads=[pden, prm], writes=[dn])
        P.op("dve", lambda e: e.reciprocal(out=dn[:], in_=dn[:]), reads=[dn], writes=[dn])
        P.op("dve", lambda e: e.tensor_tensor(out=yy[:], in0=pnum[:, :], in1=dn[:], op=ALU.mult),
             reads=[pnum, dn], writes=[yy])
        P.op("act", lambda e, GX=GX: e.activation(out=GX[:], in_=GX[:], func=AF.Silu), reads=[GX], writes=[GX])
        P.op("dve", lambda e, GX=GX, O=O: e.tensor_tensor(out=O[:], in0=yy[:], in1=GX[:], op=ALU.mult),
             reads=[yy, GX], writes=[O])
        P.dma("sp", O.c, lambda e, O=O, t0=t0: e.dma_start(out=orows[:, t0:t0 + TP], in_=O[:]), reads=[O])


D_MODEL = 2048
KC = D_MODEL // 128


def norm_transpose(P, K, x_dram, ntile, gsb, hT, pst, eps=1e-6, tag=""):
    xt = [P.tile([128, D_MODEL], F32, f"nxt{tag}{i}") for i in range(2)]
    xs = [P.tile([128, D_MODEL], F32, f"nxs{tag}{i}") for i in range(2)]
    junk = P.tile([128, D_MODEL], BF16, f"njunk{tag}")
    st = P.tile([128, 4 * ntile], F32, f"nst{tag}")
    npst = 0
    for i in range(ntile):
        s = i % 2
        X, XS = xt[s], xs[s]
        P.dma("sp", X.c, lambda e, X=X, i=i: e.dma_start(out=X[:], in_=x_dram[i * 128:(i + 1) * 128, :]), writes=[X])
        c0 = 4 * i
        P.op("act", lambda e, X=X, c0=c0: e.activation(out=junk[:], in_=X[:], func=AF.Square, accum_out=st[:, c0:c0 + 1]),
             reads=[X], writes=[junk], accs=[st])
        P.op("dve", lambda e, c0=c0: e.tensor_scalar(out=st[:, c0 + 1:c0 + 2], in0=st[:, c0:c0 + 1], scalar1=1.0 / D_MODEL,
                                                    scalar2=eps, op0=ALU.mult, op1=ALU.add), reads=[st], accs=[st])
        P.op("act", lambda e, c0=c0: e.activation(out=st[:, c0 + 2:c0 + 3], in_=st[:, c0 + 1:c0 + 2], func=AF.Sqrt),
             reads=[st], accs=[st])
        P.op("dve", lambda e, c0=c0: e.reciprocal(out=st[:, c0 + 3:c0 + 4], in_=st[:, c0 + 2:c0 + 3]),
             reads=[st], accs=[st])
        P.op("act", lambda e, X=X, XS=XS, c0=c0: e.activation(out=XS[:], in_=X[:], func=AF.Copy,
                                                             scale=st[:, c0 + 3:c0 + 4]), reads=[X, st], writes=[XS])
        for kq in range(KC // 4):
            pb = pst[npst % 2]
            npst += 1
            for kk in range(4):
                k = kq * 4 + kk
                P.op("pe", lambda e, XS=XS, pb=pb, k=k, kk=kk: e.transpose(
                    out=pb[:, kk * 128:(kk + 1) * 128], in_=XS[:, k * 128:(k + 1) * 128], identity=K.identf[:]),
                    reads=[XS, K.identf], writes=[pb] if kk == 0 else [], accs=[pb] if kk else [], signal=(kk == 3))
            for kk in range(4):
                k = kq * 4 + kk
                P.op("dve", lambda e, pb=pb, k=k, kk=kk, i=i: e.tensor_scalar(
                    out=hT[:, k, i * 128:(i + 1) * 128], in0=pb[:, kk * 128:(kk + 1) * 128],
                    scalar1=gsb[:, k:k + 1], scalar2=None, op0=ALU.mult), reads=[pb, gsb], accs=[hT])


def load_weight_bf16(P, w_dram, c0, cw, wst, wb):
    wv = w_dram.rearrange("(k p) c -> p k c", p=128)
    P.dma("sp", wst.c, lambda e: e.dma_start(out=wst[:, :, 0:cw], in_=wv[:, :, c0:c0 + cw]), writes=[wst])
    P.op("pool", lambda e: e.tensor_copy(out=wb[:, :, 0:cw], in_=wst[:, :, 0:cw]), reads=[wst], writes=[wb])


def proj_stage(P, K, NT, x_dram, g_dram, w_dram, NCOL, dst64, tag=""):
    ntile = NT // 128
    TG = min(512, NT)
    ntg = NT // TG
    CB = 256
    gsb = P.tile([128, KC], F32, f"pg{tag}")
    P.dma("sp", gsb.c, lambda e: e.dma_start(out=gsb[:], in_=g_dram), writes=[gsb])
    hT = P.tile([128, KC, NT], BF16, f"phT{tag}")
    pp = [P.tile([128, 512], F32, f"ppp{tag}{i}", psum=True) for i in range(4)]
    norm_transpose(P, K, x_dram, ntile, gsb, hT, pp[0:2], tag=tag)
    nblk = (NCOL + CB - 1) // CB
    wst = [P.tile([128, KC, CB], F32, f"pwst{tag}{i}") for i in range(2)]
    wb = [P.tile([128, KC, CB], BF16, f"pwb{tag}{i}") for i in range(2)]
    ost = [P.tile([128, NT], F32, f"post{tag}{i}") for i in range(2)]
    nmm = 0
    nct = 0
    for bi in range(nblk):
        s = bi % 2
        cw = min(CB, NCOL - bi * CB)
        load_weight_bf16(P, w_dram, bi * CB, cw, wst[s], wb[s])
        WB = wb[s]
        for j in range((cw + 127) // 128):
            mw = min(128, cw - j * 128)
            O = ost[nct % 2]
            nct += 1
            for n in range(ntg):
                pb = pp[nmm % 4]
                nmm += 1
                for k in range(KC):
                    P.op("pe", lambda e, WB=WB, k=k, j=j, mw=mw, n=n, pb=pb: e.matmul(
                        pb[0:mw, 0:TG], lhsT=WB[:, k, j * 128:j * 128 + mw], rhs=hT[:, k, n * TG:(n + 1) * TG],
                        start=(k == 0), stop=(k == KC - 1)),
                        reads=[WB, hT], writes=[pb] if k == 0 else [], accs=[pb] if k else [], signal=(k == KC - 1))
                if nmm % 2:
                    P.op("act", lambda e, O=O, mw=mw, n=n, pb=pb: e.activation(
                        out=O[0:mw, n * TG:(n + 1) * TG], in_=pb[0:mw, 0:TG], func=AF.Copy),
                        reads=[pb], writes=[O] if n == 0 else [], accs=[] if n == 0 else [O])
                else:
                    P.op("dve", lambda e, O=O, mw=mw, n=n, pb=pb: e.tensor_copy(
                        out=O[0:mw, n * TG:(n + 1) * TG], in_=pb[0:mw, 0:TG]),
                        reads=[pb], writes=[O] if n == 0 else [], accs=[] if n == 0 else [O])
            c0 = bi * CB + j * 128
            for hh_ in range(mw // 64):
                P.dma("sp", O.c, lambda e, O=O, hh_=hh_, c0=c0: e.dma_start(
                    out=dst64(c0 // 64 + hh_), in_=O[hh_ * 64:(hh_ + 1) * 64, :]), reads=[O])


def out_stage(P, K, NT, x_dram, oT_src, w_dram, out_dram, tag=""):
    TG = min(512, NT)
    ntg = NT // TG
    wst = [P.tile([128, KC, 256], F32, f"owst{tag}{i}") for i in range(2)]
    wo = [P.tile([128, KC, 256], BF16, f"owo{tag}{i}") for i in range(8)]
    for cb in range(8):
        load_weight_bf16(P, w_dram, cb * 256, 256, wst[cb % 2], wo[cb])
    ot = [P.tile([128, KC, TG], BF16, f"oot{tag}{i}") for i in range(2)]
    xt = [P.tile([128, D_MODEL], F32, f"oxt{tag}{i}") for i in range(2)]
    xo = [P.tile([128, D_MODEL], F32, f"oxo{tag}{i}") for i in range(2)]
    pp = [P.tile([128, 512], F32, f"opp{tag}{i}", psum=True) for i in range(4)]
    nmm = 0
    ntl = 0
    for n in range(ntg):
        OT = ot[n % 2]
        for k in range(KC):
            P.dma("sp", OT.c, lambda e, OT=OT, k=k, n=n: e.dma_start(out=OT[:, k, :], in_=oT_src(k)[:, n * TG:(n + 1) * TG]),
                  writes=[OT] if k == 0 else [], accs=[OT] if k else [])
        for tt in range(TG // 128):
            X, XO = xt[ntl % 2], xo[ntl % 2]
            ntl += 1
            r0 = n * TG + tt * 128
            P.dma("sp", X.c, lambda e, X=X, r0=r0: e.dma_start(out=X[:], in_=x_dram[r0:r0 + 128, :]), writes=[X])
            for cb in range(4):
                pb = pp[nmm % 4]
                nmm += 1
                for half in range(2):
                    W_ = wo[cb * 2 + half]
                    for k in range(KC):
                        P.op("pe", lambda e, OT=OT, W_=W_, k=k, tt=tt, pb=pb, half=half: e.matmul(
                            pb[:, half * 256:(half + 1) * 256], lhsT=OT[:, k, tt * 128:(tt + 1) * 128], rhs=W_[:, k, :],
                            start=(k == 0), stop=(k == KC - 1)),
                            reads=[OT, W_], writes=[pb] if (k == 0 and half == 0) else [],
                            accs=[] if (k == 0 and half == 0) else [pb], signal=(k == KC - 1 and half == 1))
                P.op("dve", lambda e, X=X, XO=XO, pb=pb, cb=cb: e.tensor_tensor(
                    out=XO[:, cb * 512:(cb + 1) * 512], in0=pb[:, :], in1=X[:, cb * 512:(cb + 1) * 512], op=ALU.add),
                    reads=[pb, X], writes=[XO] if cb == 0 else [], accs=[] if cb == 0 else [XO])
            P.dma("sp", XO.c, lambda e, XO=XO, r0=r0: e.dma_start(out=out_dram[r0:r0 + 128, :], in_=XO[:]), reads=[XO])


def mem_stage(P, K, NT, qsrc, gsrc, odst, mem_dram, memg_dram, wkv_dram, qg_dram, kg_dram, tag=""):
    TP = min(512, NT)
    npc = NT // TP
    SC = 128.0 ** -0.5
    prm = P.tile([128, 24], F32, f"mprm{tag}")
    gsb = P.tile([128, KC], F32, f"mgsb{tag}")
    P.dma("sp", prm.c, lambda e: e.dma_start(out=prm[:, 0:1], in_=qg_dram, allow_slow_non_contiguous=True), writes=[prm])
    P.dma("sp", prm.c, lambda e: e.dma_start(out=prm[:, 1:2], in_=kg_dram, allow_slow_non_contiguous=True), accs=[prm])
    P.dma("sp", gsb.c, lambda e: e.dma_start(out=gsb[:], in_=memg_dram), writes=[gsb])
    P.op("dve", lambda e: e.tensor_scalar(out=prm[:, 2:3], in0=prm[:, 1:2], scalar1=SC, scalar2=None, op0=ALU.mult),
         reads=[prm], accs=[prm])
    pp = [P.tile([128, 512], F32, f"mpp{tag}{i}", psum=True) for i in range(7)]
    hmT = P.tile([128, KC, 256], BF16, f"mhmT{tag}")
    norm_transpose(P, K, mem_dram, 2, gsb, hmT, pp[0:2], tag="m" + tag)
    wst = [P.tile([128, KC, 256], F32, f"mwst{tag}{i}") for i in range(2)]
    wkv = [P.tile([128, KC, 256], BF16, f"mwkv{tag}{i}") for i in range(4)]
    for cb in range(4):
        load_weight_bf16(P, wkv_dram, cb * 256, 256, wst[cb % 2], wkv[cb])
    mkf = P.tile([128, 2, 512], F32, f"mmkf{tag}")
    mvb = P.tile([128, 2, 512], BF16, f"mmvb{tag}")
    mkT = P.tile([128, 4, 256], BF16, f"mmkT{tag}")
    junk = P.tile([128, 128], F32, f"mjunk{tag}")
    npp = 2
    for mt in range(2):
        for half, dst in ((0, mkf), (1, mvb)):
            pb = pp[npp % 7]
            npp += 1
            for sub in range(2):
                W_ = wkv[half * 2 + sub]
                for k in range(KC):
                    P.op("pe", lambda e, pb=pb, sub=sub, W_=W_, k=k, mt=mt: e.matmul(
                        pb[:, sub * 256:(sub + 1) * 256], lhsT=hmT[:, k, mt * 128:(mt + 1) * 128], rhs=W_[:, k, :],
                        start=(k == 0), stop=(k == KC - 1)),
                        reads=[hmT, W_], writes=[pb] if (k == 0 and sub == 0) else [],
                        accs=[] if (k == 0 and sub == 0) else [pb], signal=(k == KC - 1 and sub == 1))
            P.op("act", lambda e, pb=pb, dst=dst, mt=mt: e.activation(out=dst[:, mt, :], in_=pb[:, :], func=AF.Copy),
                 reads=[pb], accs=[dst])
        for h in range(4):
            c0 = 4 + mt * 8 + h * 2
            P.op("act", lambda e, mt=mt, h=h, c0=c0: e.activation(out=junk[:], in_=mkf[:, mt, h * 128:(h + 1) * 128],
                                                                 func=AF.Square, accum_out=prm[:, c0:c0 + 1]),
                 reads=[mkf], writes=[junk], accs=[prm])
            P.op("act", lambda e, c0=c0: e.activation(out=prm[:, c0 + 1:c0 + 2], in_=prm[:, c0:c0 + 1], func=AF.Sqrt,
                                                     scale=1.0 / 128, bias=1e-6), reads=[prm], accs=[prm])
            P.op("dve", lambda e, c0=c0: e.reciprocal(out=prm[:, c0 + 1:c0 + 2], in_=prm[:, c0 + 1:c0 + 2]),
                 reads=[prm], accs=[prm])
            P.op("dve", lambda e, mt=mt, h=h, c0=c0: e.tensor_scalar(
                out=mkf[:, mt, h * 128:(h + 1) * 128], in0=mkf[:, mt, h * 128:(h + 1) * 128],
                scalar1=prm[:, c0 + 1:c0 + 2], scalar2=None, op0=ALU.mult), reads=[mkf, prm], accs=[mkf])
        pb = pp[npp % 7]
        npp += 1
        for h in range(4):
            P.op("pe", lambda e, pb=pb, mt=mt, h=h: e.transpose(out=pb[:, h * 128:(h + 1) * 128],
                                                               in_=mkf[:, mt, h * 128:(h + 1) * 128], identity=K.identf[:]),
                 reads=[mkf, K.identf], writes=[pb] if h == 0 else [], accs=[pb] if h else [], signal=(h == 3))
        P.op("dve", lambda e, pb=pb, mt=mt: e.tensor_scalar(
            out=mkT[:, :, mt * 128:(mt + 1) * 128], in0=pb[:, :].rearrange("p (h m) -> p h m", m=128),
            scalar1=prm[:, 2:3], scalar2=None, op0=ALU.mult), reads=[pb, prm], accs=[mkT])

    qx = [P.tile([128, TP], F32, f"mqx{tag}{i}") for i in range(2)]
    gx = [P.tile([128, TP], F32, f"mgx{tag}{i}") for i in range(2)]
    sq = P.tile([128, TP], F32, f"msq{tag}")
    rs = P.tile([128, TP], F32, f"mrs{tag}")
    qn = P.tile([128, TP], BF16, f"mqn{tag}")
    E = [P.tile([128, TP], BF16, f"mE{tag}{i}") for i in range(2)]
    dn = P.tile([128, TP], F32, f"mdn{tag}")
    yy = P.tile([128, TP], F32, f"myy{tag}")
    ob = [P.tile([128, TP], BF16, f"mob{tag}{i}") for i in range(2)]
    it = 0
    for pi in range(npc):
        t0 = pi * TP
        for h in range(4):
            QX, GX, O = qx[it % 2], gx[it % 2], ob[it % 2]
            it += 1
            P.dma("sp", QX.c, lambda e, QX=QX, h=h, t0=t0: e.dma_start(out=QX[:], in_=qsrc(h)[:, t0:t0 + TP]), writes=[QX])
            P.dma("sp", GX.c, lambda e, GX=GX, h=h, t0=t0: e.dma_start(out=GX[:], in_=gsrc(h)[:, t0:t0 + TP]), writes=[GX])
            P.op("dve", lambda e, QX=QX: e.tensor_tensor(out=sq[:], in0=QX[:], in1=QX[:], op=ALU.mult), reads=[QX], writes=[sq])
            pn, ps0, ps1, pnum, pden = pp[0], pp[1], pp[2], pp[3], pp[4]
            P.op("pe", lambda e, pn=pn: e.matmul(pn[:, 0:TP], lhsT=K.ones_f[:, :], rhs=sq[:], start=True, stop=True),
                 reads=[K.ones_f, sq], writes=[pn])
            P.op("act", lambda e, pn=pn: e.activation(out=rs[:], in_=pn[:, 0:TP], func=AF.Sqrt, scale=1.0 / 128, bias=1e-6),
                 reads=[pn], writes=[rs])
            P.op("dve", lambda e: e.reciprocal(out=rs[:], in_=rs[:]), reads=[rs], writes=[rs])
            P.op("dve", lambda e, QX=QX: e.scalar_tensor_tensor(out=qn[:], in0=QX[:], scalar=prm[:, 0:1], in1=rs[:],
                                                               op0=ALU.mult, op1=ALU.mult), reads=[QX, prm, rs], writes=[qn])
            for mt, psb in ((0, ps0), (1, ps1)):
                P.op("pe", lambda e, psb=psb, mt=mt, h=h: e.matmul(psb[:, 0:TP], lhsT=mkT[:, h, mt * 128:(mt + 1) * 128],
                                                                  rhs=qn[:], start=True, stop=True),
                     reads=[mkT, qn], writes=[psb])
                P.op("act", lambda e, psb=psb, mt=mt: e.activation(out=E[mt][:], in_=psb[:, 0:TP], func=AF.Exp),
                     reads=[psb], writes=[E[mt]])
            for mt in range(2):
                P.op("pe", lambda e, mt=mt, h=h, pnum=pnum: e.matmul(pnum[:, 0:TP], lhsT=mvb[:, mt, h * 128:(h + 1) * 128],
                                                                    rhs=E[mt][:], start=(mt == 0), stop=(mt == 1)),
                     reads=[mvb, E[mt]], writes=[pnum] if mt == 0 else [], accs=[pnum] if mt else [], signal=(mt == 1))
            for mt in range(2):
                P.op("pe", lambda e, mt=mt, pden=pden: e.matmul(pden[:, 0:TP], lhsT=K.ones_b[:, :], rhs=E[mt][:],
                                                               start=(mt == 0), stop=(mt == 1)),
                     reads=[K.ones_b, E[mt]], writes=[pden] if mt == 0 else [], accs=[pden] if mt else [],
                     signal=(mt == 1))
            P.op("dve", lambda e, pden=pden: e.reciprocal(out=dn[:], in_=pden[:, 0:TP]), reads=[pden], writes=[dn])
            P.op("dve", lambda e, pnum=pnum: e.tensor_tensor(out=yy[:], in0=pnum[:, 0:TP], in1=dn[:], op=ALU.mult),
                 reads=[pnum, dn], writes=[yy])
            P.op("act", lambda e, GX=GX: e.activation(out=GX[:], in_=GX[:], func=AF.Silu), reads=[GX], writes=[GX])
            P.op("pool", lambda e, GX=GX, O=O: e.tensor_tensor(out=O[:], in0=yy[:], in1=GX[:], op=ALU.mult),
                 reads=[yy, GX], writes=[O])
            P.dma("sp", O.c, lambda e, O=O, h=h, t0=t0: e.dma_start(out=odst(h)[:, t0:t0 + TP], in_=O[:]), reads=[O])


class RwConsts:
    def __init__(self, P, c_ui, c_sl, c_reset):
        self.ui = P.tile([128, 256], F32, "rw_ui", persistent=True)
        self.sl = P.tile([128, 128], F32, "rw_sl", persistent=True)
        self.reset = P.tile([64, 1024], F32, "rw_reset", persistent=True)
        P.dma("sp", self.ui.c, lambda e: e.dma_start(out=self.ui[:], in_=c_ui), writes=[self.ui])
        P.dma("sp", self.sl.c, lambda e: e.dma_start(out=self.sl[:], in_=c_sl), writes=[self.sl])
        P.dma("sp", self.reset.c, lambda e: e.dma_start(out=self.reset[:], in_=c_reset), writes=[self.reset])


class Bank:
    def __init__(self, P, name):
        self.t = P.tile([128, 512], F32, name, psum=True)
        self.q = [self.t.r] * 4

    def __getitem__(self, idx):
        return self.t[idx]


def rwkv_stage(P, K, RK, NTOK, rrows, krows, vrows, wdrows, adrows, grows, orows, prm_dram, wup_dram, aup_dram, tag="",
               do_chunk=True, max_it=6):
    TP = 1024
    CH = 128
    npc = NTOK // TP
    ncp = TP // CH
    f = F32
    prm = P.tile([64, 24], f, f"rprm{tag}")
    lup = P.tile([64, 128], f, f"rlup{tag}")
    P.dma("sp", prm.c, lambda e: e.dma_start(out=prm[:, 0:16], in_=prm_dram, allow_slow_non_contiguous=True), writes=[prm])
    P.op("pool", lambda e: e.memset(lup[:], 0.0), writes=[lup])
    P.dma("sp", lup.c, lambda e: e.dma_start(out=lup[0:32, 0:64], in_=wup_dram), accs=[lup])
    P.dma("sp", lup.c, lambda e: e.dma_start(out=lup[32:64, 64:128], in_=aup_dram), accs=[lup])
    P.op("dve", lambda e: e.tensor_scalar(out=prm[:, 16:20], in0=prm[:, 0:4], scalar1=-1.0, scalar2=1.0, op0=ALU.mult,
                                         op1=ALU.add), reads=[prm], accs=[prm])
    P.op("dve", lambda e: e.tensor_scalar(out=prm[:, 20:21], in0=prm[:, 7:8], scalar1=-1.0, scalar2=1.0, op0=ALU.mult,
                                         op1=ALU.add), reads=[prm], accs=[prm])
    xin = {nm: [P.tile([64, TP + 1], f, f"rx{nm}{tag}{i}") for i in range(2)] for nm in ("r", "k", "v", "l")}
    gin = [P.tile([64, TP], f, f"rg{tag}{i}") for i in range(2)]
    T = {nm: P.tile([64, TP], f, f"r_{nm}{tag}") for nm in
         ("R", "Kt", "V", "L", "tmp", "SG", "A", "LW", "cum", "KK", "Kp", "Bv", "e1", "e2", "Bt", "Ktl", "Bh", "Kh",
          "bon", "Y", "t2")}
    AR = P.tile([64, ncp, 256], f, f"r_AR{tag}")
    ob = [P.tile([64, TP], BF16, f"rob{tag}{i}") for i in range(2)]
    Sb = [P.tile([64, 64], f, f"rS{tag}{i}") for i in range(4)]
    G = 2
    ctx = []
    for g in range(G):
        c = dict(bA=Bank(P, f"rbA{tag}{g}"), bB=Bank(P, f"rbB{tag}{g}"), bC=Bank(P, f"rbC{tag}{g}"))
        for nm, shp in (("Gm1", [128, 256]), ("Gm2", [128, 256]), ("P0", [128, 128]), ("P1", [128, 128]),
                        ("PT0", [128, 128]), ("PT1", [128, 128]), ("T", [128, 128]), ("TM", [128, 320]),
                        ("AU", [128, 128]), ("Mt", [64, 64]), ("Qt", [64, 128])):
            c[nm] = P.tile(shp, f, f"rc{nm}{tag}{g}")
        ctx.append(c)
    bY = Bank(P, f"rbY{tag}")
    bS = Bank(P, f"rbS{tag}")
    bN = [ctx[0]["bC"], ctx[1]["bC"]]
    nbn = 0
    ones64 = K.ones_f[0:64, 0:64]
    id64 = K.identf[0:64, 0:64]
    P.op("dve", lambda e: e.memset(Sb[0][:], 0.0), writes=[Sb[0]])
    sidx = 0

    def v3(t):
        return t[:, :].rearrange("p (c t) -> p c t", t=CH)

    def ones_mm(src_ap_fn, nsub, consume):
        nonlocal nbn
        for sb_ in range(nsub):
            bk = bN[nbn % 2]
            nbn += 1
            rd = src_ap_fn(sb_)
            P.op("pe", lambda e, bk=bk, rd=rd: e.matmul(bk[0:64, :], lhsT=ones64, rhs=rd[0], start=True, stop=True),
                 reads=[K.ones_f] + rd[1], writes=bk.q)
            consume(sb_, bk)

    for pi in range(npc):
        s = pi % 2
        t0 = pi * TP
        XR, XK, XV, XL, GX, O = xin["r"][s], xin["k"][s], xin["v"][s], xin["l"][s], gin[s], ob[s]
        for X, rows_list in ((XR, [(rrows, 0, 64)]), (XK, [(krows, 0, 64)]), (XV, [(vrows, 0, 64)]),
                             (XL, [(wdrows, 0, 32), (adrows, 32, 64)])):
            first = True
            if pi == 0:
                P.op("pool", lambda e, X=X: e.memset(X[:, 0:1], 0.0), writes=[X])
                first = False
            for (src, p0, p1) in rows_list:
                if pi == 0:
                    P.dma("sp", X.c, lambda e, X=X, src=src, p0=p0, p1=p1: e.dma_start(out=X[p0:p1, 1:TP + 1], in_=src[:, 0:TP]),
                          accs=[X])
                else:
                    P.dma("sp", X.c, lambda e, X=X, src=src, p0=p0, p1=p1, t0=t0: e.dma_start(
                        out=X[p0:p1, :], in_=src[:, t0 - 1:t0 + TP]), writes=[X] if first else [], accs=[] if first else [X])
                first = False
        P.dma("sp", GX.c, lambda e, GX=GX, t0=t0: e.dma_start(out=GX[:], in_=grows[:, t0:t0 + TP]), writes=[GX])
        for X, dst, col in ((XR, T["R"], 0), (XK, T["Kt"], 1), (XV, T["V"], 2), (XL, T["L"], 3)):
            P.op("pool", lambda e, X=X, col=col: e.tensor_scalar(out=T["tmp"][:], in0=X[:, 0:TP], scalar1=prm[:, col:col + 1],
                                                                scalar2=None, op0=ALU.mult), reads=[X, prm], writes=[T["tmp"]])
            P.op("dve", lambda e, X=X, dst=dst, col=col: e.scalar_tensor_tensor(
                out=dst[:], in0=X[:, 1:TP + 1], scalar=prm[:, 16 + col:17 + col], in1=T["tmp"][:], op0=ALU.mult, op1=ALU.add),
                reads=[X, prm, T["tmp"]], writes=[dst])
        P.op("act", lambda e: e.activation(out=T["L"][0:32, :], in_=T["L"][0:32, :], func=AF.Tanh), reads=[T["L"]], writes=[T["L"]])
        for (lo, hi, bcol, dst) in ((0, 32, 4, T["SG"]), (32, 64, 5, T["A"])):
            for sb_ in range(TP // 512):
                bk = bN[nbn % 2]
                nbn += 1
                P.op("pe", lambda e, bk=bk, lo=lo, hi=hi, sb_=sb_: e.matmul(bk[0:64, :], lhsT=lup[:, 2 * lo:2 * lo + 64],
                                                                           rhs=T["L"][:, sb_ * 512:(sb_ + 1) * 512],
                                                                           start=True, stop=True),
                     reads=[lup, T["L"]], writes=bk.q)
                P.op("act", lambda e, bk=bk, dst=dst, sb_=sb_, bcol=bcol: e.activation(
                    out=dst[:, sb_ * 512:(sb_ + 1) * 512], in_=bk[0:64, :], func=AF.Sigmoid, bias=prm[:, bcol:bcol + 1]),
                    reads=bk.q + [prm], writes=[dst] if sb_ == 0 else [], accs=[] if sb_ == 0 else [dst])
        P.op("dve", lambda e: e.tensor_scalar(out=T["LW"][:], in0=T["SG"][:], scalar1=-0.6065306597126334, scalar2=None,
                                             op0=ALU.mult), reads=[T["SG"]], writes=[T["LW"]])
        P.op("dve", lambda e: e.tensor_tensor_scan(out=T["cum"][:], data0=RK.reset[:, 0:TP], data1=T["LW"][:], initial=0.0,
                                                  op0=ALU.mult, op1=ALU.add), reads=[RK.reset, T["LW"]], writes=[T["cum"]])
        P.op("dve", lambda e: e.tensor_tensor(out=T["tmp"][:], in0=T["cum"][:], in1=T["LW"][:], op=ALU.subtract),
             reads=[T["cum"], T["LW"]], writes=[T["tmp"]])
        P.op("act", lambda e: e.activation(out=T["e1"][:], in_=T["tmp"][:], func=AF.Exp), reads=[T["tmp"]], writes=[T["e1"]])
        P.op("pool", lambda e: e.tensor_scalar(out=T["KK"][:], in0=T["Kt"][:], scalar1=prm[:, 6:7], scalar2=None, op0=ALU.mult),
             reads=[T["Kt"], prm], writes=[T["KK"]])
        P.op("pool", lambda e: e.tensor_tensor(out=T["tmp"][:], in0=T["KK"][:], in1=T["KK"][:], op=ALU.mult),
             reads=[T["KK"]], writes=[T["tmp"]])

        def cons_kk(sb_, bk):
            P.op("act", lambda e, bk=bk, sb_=sb_: e.activation(out=T["t2"][:, sb_ * 512:(sb_ + 1) * 512], in_=bk[0:64, :],
                                                              func=AF.Sqrt), reads=bk.q,
                 writes=[T["t2"]] if sb_ == 0 else [], accs=[] if sb_ == 0 else [T["t2"]])
        ones_mm(lambda sb_: (T["tmp"][:, sb_ * 512:(sb_ + 1) * 512], [T["tmp"]]), TP // 512, cons_kk)
        P.op("dve", lambda e: e.tensor_scalar(out=T["t2"][:], in0=T["t2"][:], scalar1=1e-12, scalar2=None, op0=ALU.max),
             reads=[T["t2"]], writes=[T["t2"]])
        P.op("dve", lambda e: e.reciprocal(out=T["t2"][:], in_=T["t2"][:]), reads=[T["t2"]], writes=[T["t2"]])
        P.op("dve", lambda e: e.tensor_tensor(out=T["KK"][:], in0=T["KK"][:], in1=T["t2"][:], op=ALU.mult),
             reads=[T["KK"], T["t2"]], writes=[T["KK"]])
        P.op("dve", lambda e: e.scalar_tensor_tensor(out=AR[:, :, 0:128], in0=v3(T["KK"]), scalar=-1.0, in1=v3(T["e1"]),
                                                    op0=ALU.mult, op1=ALU.mult), reads=[T["KK"], T["e1"]], writes=[AR])
        P.op("pool", lambda e: e.tensor_scalar(out=T["tmp"][:], in0=T["A"][:], scalar1=prm[:, 7:8], scalar2=prm[:, 20:21],
                                              op0=ALU.mult, op1=ALU.add), reads=[T["A"], prm], writes=[T["tmp"]])
        P.op("pool", lambda e: e.tensor_tensor(out=T["Kp"][:], in0=T["Kt"][:], in1=T["tmp"][:], op=ALU.mult),
             reads=[T["Kt"], T["tmp"]], writes=[T["Kp"]])
        P.op("pool", lambda e: e.tensor_tensor(out=T["Bv"][:], in0=T["KK"][:], in1=T["A"][:], op=ALU.mult),
             reads=[T["KK"], T["A"]], writes=[T["Bv"]])
        P.op("act", lambda e: e.activation(out=T["e1"][:], in_=T["cum"][:], func=AF.Exp), reads=[T["cum"]], writes=[T["e1"]])
        P.op("act", lambda e: e.activation(out=T["e2"][:], in_=T["cum"][:], func=AF.Exp, scale=-1.0), reads=[T["cum"]],
             writes=[T["e2"]])
        P.op("dve", lambda e: e.tensor_tensor(out=AR[:, :, 128:256], in0=v3(T["R"]), in1=v3(T["e1"]), op=ALU.mult),
             reads=[T["R"], T["e1"]], accs=[AR])
        P.op("dve", lambda e: e.tensor_tensor(out=T["Bt"][:], in0=T["Bv"][:], in1=T["e2"][:], op=ALU.mult),
             reads=[T["Bv"], T["e2"]], writes=[T["Bt"]])
        P.op("pool", lambda e: e.tensor_tensor(out=T["Ktl"][:], in0=T["Kp"][:], in1=T["e2"][:], op=ALU.mult),
             reads=[T["Kp"], T["e2"]], writes=[T["Ktl"]])
        for c in range(ncp):
            cs = slice(c * CH, (c + 1) * CH)
            ge = slice(c * CH + CH - 1, c * CH + CH)
            P.op("dve", lambda e, cs=cs, ge=ge: e.tensor_scalar(out=T["Bh"][:, cs], in0=T["Bt"][:, cs], scalar1=T["e1"][:, ge],
                                                               scalar2=None, op0=ALU.mult),
                 reads=[T["Bt"], T["e1"]], writes=[T["Bh"]] if c == 0 else [], accs=[] if c == 0 else [T["Bh"]])
            P.op("pool", lambda e, cs=cs, ge=ge: e.tensor_scalar(out=T["Kh"][:, cs], in0=T["Ktl"][:, cs], scalar1=T["e1"][:, ge],
                                                                scalar2=None, op0=ALU.mult),
                 reads=[T["Ktl"], T["e1"]], writes=[T["Kh"]] if c == 0 else [], accs=[] if c == 0 else [T["Kh"]])
        P.op("dve", lambda e: e.scalar_tensor_tensor(out=T["tmp"][:], in0=T["R"][:], scalar=prm[:, 8:9], in1=T["Kp"][:],
                                                     op0=ALU.mult, op1=ALU.mult), reads=[T["R"], prm, T["Kp"]], writes=[T["tmp"]])

        def cons_bon(sb_, bk):
            P.op("dve", lambda e, bk=bk, sb_=sb_: e.tensor_tensor(out=T["bon"][:, sb_ * 512:(sb_ + 1) * 512], in0=bk[0:64, :],
                                                                 in1=T["V"][:, sb_ * 512:(sb_ + 1) * 512], op=ALU.mult),
                 reads=bk.q + [T["V"]], writes=[T["bon"]] if sb_ == 0 else [], accs=[] if sb_ == 0 else [T["bon"]])
        ones_mm(lambda sb_: (T["tmp"][:, sb_ * 512:(sb_ + 1) * 512], [T["tmp"]]), TP // 512, cons_bon)

        if not do_chunk:
            P.op("dve", lambda e: e.memset(T["Y"][:], 0.0), writes=[T["Y"]])
        for pair in range(ncp // G if do_chunk else 0):
            steps = []
            for g in range(G):
                c = pair * G + g
                C = ctx[g]
                bA, bB = C["bA"], C["bB"]
                cs = slice(c * CH, (c + 1) * CH)
                st = []

                def sA(C=C, bA=bA, bB=bB, c=c, cs=cs):
                    P.op("pe", lambda e: e.matmul(bA[:, 0:256], lhsT=T["Bt"][:, cs], rhs=AR[:, c, :], start=True, stop=True),
                         reads=[T["Bt"], AR], writes=bA.q, signal=False)
                    P.op("pe", lambda e: e.matmul(bA[:, 256:512], lhsT=T["Ktl"][:, cs], rhs=AR[:, c, :], start=True, stop=True),
                         reads=[T["Ktl"], AR], accs=bA.q, signal=False)
                    P.op("pe", lambda e: e.matmul(bB[:, 0:128], lhsT=AR[:, c, 0:128], rhs=T["Bt"][:, cs], start=True, stop=True),
                         reads=[T["Bt"], AR], writes=[bB.q[0]], signal=False)
                    for i4, src in enumerate((AR[:, c, 0:128], T["V"][:, cs], T["Bh"][:, cs], T["Kh"][:, cs])):
                        col = (0, 128, 192, 256)[i4]
                        P.op("pe", lambda e, src=src, i4=i4: e.transpose(out=bB[:, 128 + i4 * 64:192 + i4 * 64], in_=src,
                                                                        identity=id64),
                             reads=[AR, T["V"], T["Bh"], T["Kh"], K.identf], writes=[bB.q[1], bB.q[2]] if i4 == 0 else [],
                             accs=[] if i4 == 0 else [bB.q[1], bB.q[2]], signal=(i4 == 3))
                st.append(sA)

                def sAe(C=C, bA=bA, bB=bB):
                    P.op("dve", lambda e: e.tensor_tensor(out=C["Gm1"][:], in0=bA[:, 0:256], in1=RK.ui[:], op=ALU.mult),
                         reads=[bA.q[0], bA.q[1], RK.ui], writes=[C["Gm1"]])
                    P.op("dve", lambda e: e.tensor_tensor(out=C["Gm2"][:], in0=bA[:, 256:512], in1=RK.ui[:], op=ALU.mult),
                         reads=[bA.q[2], bA.q[3], RK.ui], writes=[C["Gm2"]])
                    P.op("dve", lambda e: e.tensor_tensor(out=C["PT0"][:], in0=bB[:, 0:128], in1=RK.sl[:], op=ALU.mult),
                         reads=[bB.q[0], RK.sl], writes=[C["PT0"]])
                    P.op("pool", lambda e: e.tensor_tensor(out=C["T"][:], in0=C["Gm1"][:, 0:128], in1=K.identf[:], op=ALU.add),
                         reads=[C["Gm1"], K.identf], writes=[C["T"]])
                    P.op("act", lambda e: e.activation(out=C["TM"][:, 0:64], in_=bB[:, 128:192], func=AF.Copy),
                         reads=[bB.q[1]], writes=[C["TM"]])
                    P.op("act", lambda e: e.activation(out=C["TM"][:, 128:320], in_=bB[:, 192:384], func=AF.Copy),
                         reads=[bB.q[1], bB.q[2]], accs=[C["TM"]])
                st.append(sAe)
                for it in range(6):
                    last = (it == 5)

                    def sI1(C=C, bA=bA, it=it, last=last):
                        Pc = C["Gm1"] if it == 0 else C[f"P{it % 2}"]
                        Pc_ap = C["Gm1"][:, 0:128] if it == 0 else C[f"P{it % 2}"][:]
                        PTc = C["PT0"] if it == 0 else C[f"PT{it % 2}"]
                        if not last:
                            P.op("pe", lambda e: e.matmul(bA[:, 0:128], lhsT=PTc[:], rhs=Pc_ap, start=True, stop=True),
                                 reads=[PTc, Pc], writes=[bA.q[0]], signal=False)
                        P.op("pe", lambda e: e.matmul(bA[:, 128:256], lhsT=Pc_ap, rhs=PTc[:], start=True, stop=True),
                             reads=[PTc, Pc], writes=[bA.q[1]])
                    st.append(sI1)

                    def sI2(C=C, bA=bA, it=it, last=last):
                        Pn = C[f"P{(it + 1) % 2}"]
                        PTn = C[f"PT{(it + 1) % 2}"]
                        if it == 0:
                            PTn = C["PT1"]
                        P.op("act", lambda e: e.activation(out=PTn[:], in_=bA[:, 128:256], func=AF.Copy),
                             reads=[bA.q[1]], writes=[PTn])
                        if not last:
                            P.op("act", lambda e: e.activation(out=Pn[:], in_=bA[:, 0:128], func=AF.Copy),
                                 reads=[bA.q[0]], writes=[Pn])
                    st.append(sI2)

                    def sI3(C=C, bC=C["bC"], it=it):
                        PTn = C[f"PT{(it + 1) % 2}"]
                        if it == 0:
                            PTn = C["PT1"]
                        P.op("pe", lambda e: e.matmul(bC[:, 0:128], lhsT=PTn[:], rhs=C["T"][:], start=True, stop=True),
                             reads=[PTn, C["T"]], writes=[bC.q[0]])
                    st.append(sI3)

                    def sI4(C=C, bC=C["bC"]):
                        P.op("dve", lambda e: e.tensor_tensor(out=C["T"][:], in0=bC[:, 0:128], in1=C["T"][:], op=ALU.add),
                             reads=[bC.q[0], C["T"]], writes=[C["T"]])
                    st.append(sI4)

                def sW(C=C, bB=bB):
                    P.op("pe", lambda e: e.matmul(bB[:, 384:448], lhsT=C["Gm2"][:, 0:128], rhs=C["TM"][:, 128:192],
                                                  start=True, stop=True), reads=[C["Gm2"], C["TM"]], writes=[bB.q[3]])
                st.append(sW)

                def sWe(C=C, bB=bB):
                    P.op("act", lambda e: e.activation(out=C["TM"][:, 64:128], in_=bB[:, 384:448], func=AF.Copy),
                         reads=[bB.q[3]], accs=[C["TM"]])
                st.append(sWe)

                def sAU(C=C, bA=bA):
                    P.op("pe", lambda e: e.matmul(bA[:, 384:512], lhsT=C["T"][:], rhs=C["TM"][:, 0:128], start=True, stop=True),
                         reads=[C["T"], C["TM"]], writes=[bA.q[3]])
                st.append(sAU)

                def sAUe(C=C, bA=bA):
                    P.op("act", lambda e: e.activation(out=C["AU"][:], in_=bA[:, 384:512], func=AF.Copy),
                         reads=[bA.q[3]], writes=[C["AU"]])
                st.append(sAUe)

                def sMQ(C=C, bB=bB):
                    P.op("pe", lambda e: e.matmul(bB[0:64, 448:512], lhsT=C["AU"][:, 0:64], rhs=C["TM"][:, 192:256],
                                                  start=True, stop=True), reads=[C["AU"], C["TM"]], writes=[bB.q[3]], signal=False)
                    P.op("pe", lambda e: e.matmul(bB[0:64, 0:128], lhsT=C["AU"][:, 0:64], rhs=C["Gm1"][:, 128:256],
                                                  start=True, stop=True), reads=[C["AU"], C["Gm1"]], writes=[bB.q[0]])
                st.append(sMQ)

                def sMQe(C=C, bB=bB, c=c, cs=cs):
                    ge = slice(c * CH + CH - 1, c * CH + CH)
                    P.op("dve", lambda e: e.scalar_tensor_tensor(out=C["Mt"][:], in0=id64, scalar=T["e1"][:, ge],
                                                                in1=bB[0:64, 448:512], op0=ALU.mult, op1=ALU.add),
                         reads=[K.identf, T["e1"], bB.q[3]], writes=[C["Mt"]])
                    P.op("dve", lambda e: e.tensor_tensor(out=C["Qt"][:], in0=bB[0:64, 0:128], in1=AR[:, c, 128:256], op=ALU.add),
                         reads=[bB.q[0], AR], writes=[C["Qt"]])
                st.append(sMQe)
                steps.append(st)
            for si in range(len(steps[0])):
                for g in range(G):
                    steps[g][si]()
            for g in range(G):
                c = pair * G + g
                C = ctx[g]
                S0 = Sb[sidx % 4]
                S1 = Sb[(sidx + 1) % 4]
                sidx += 1
                ycol = (c % 4) * 128
                yq = bY.q[c % 4]
                P.op("pe", lambda e, S0=S0, C=C, ycol=ycol: e.matmul(bY[0:64, ycol:ycol + 128], lhsT=S0[:], rhs=C["Qt"][:],
                                                                    start=True, stop=False),
                     reads=[S0, C["Qt"]], writes=[yq], signal=False)
                P.op("pe", lambda e, C=C, ycol=ycol: e.matmul(bY[0:64, ycol:ycol + 128], lhsT=C["AU"][:, 64:128],
                                                             rhs=C["Gm1"][:, 128:256], start=False, stop=False),
                     reads=[C["AU"], C["Gm1"]], accs=[yq], signal=False)
                P.op("pe", lambda e, C=C, ycol=ycol: e.matmul(bY[0:64, ycol:ycol + 128], lhsT=C["TM"][:, 128:192],
                                                             rhs=C["Gm2"][:, 128:256], start=False, stop=True),
                     reads=[C["TM"], C["Gm2"]], accs=[yq], signal=False)
                P.op("pe", lambda e, C=C: e.matmul(bS[0:64, 0:64], lhsT=C["TM"][:, 192:256], rhs=C["AU"][:, 64:128],
                                                   start=True, stop=False), reads=[C["TM"], C["AU"]], writes=[bS.q[0]], signal=False)
                P.op("pe", lambda e, C=C: e.matmul(bS[0:64, 0:64], lhsT=C["TM"][:, 256:320], rhs=C["TM"][:, 128:192],
                                                   start=False, stop=False), reads=[C["TM"]], accs=[bS.q[0]], signal=False)
                P.op("pe", lambda e, C=C, S0=S0: e.matmul(bS[0:64, 0:64], lhsT=C["Mt"][:], rhs=S0[:], start=False, stop=True),
                     reads=[C["Mt"], S0], accs=[bS.q[0]])
                P.op("act", lambda e, S1=S1: e.activation(out=S1[:], in_=bS[0:64, 0:64], func=AF.Copy),
                     reads=[bS.q[0]], writes=[S1])
                if c % 4 == 3:
                    y0 = (c - 3) * CH
                    P.op("act", lambda e, y0=y0: e.activation(out=T["Y"][:, y0:y0 + 512], in_=bY[0:64, :], func=AF.Copy),
                         reads=bY.q, writes=[T["Y"]] if c == 3 else [], accs=[] if c == 3 else [T["Y"]])
        def cons_mean(sb_, bk):
            ss = slice(sb_ * 512, (sb_ + 1) * 512)
            P.op("dve", lambda e, bk=bk, ss=ss: e.scalar_tensor_tensor(out=T["tmp"][:, ss], in0=bk[0:64, :], scalar=-1.0 / 64,
                                                                      in1=T["Y"][:, ss], op0=ALU.mult, op1=ALU.add),
                 reads=bk.q + [T["Y"]], writes=[T["tmp"]] if sb_ == 0 else [], accs=[] if sb_ == 0 else [T["tmp"]])
        ones_mm(lambda sb_: (T["Y"][:, sb_ * 512:(sb_ + 1) * 512], [T["Y"]]), TP // 512, cons_mean)
        P.op("pool", lambda e: e.tensor_tensor(out=T["t2"][:], in0=T["tmp"][:], in1=T["tmp"][:], op=ALU.mult),
             reads=[T["tmp"]], writes=[T["t2"]])

        def cons_var(sb_, bk):
            ss = slice(sb_ * 512, (sb_ + 1) * 512)
            P.op("act", lambda e, bk=bk, ss=ss: e.activation(out=T["e2"][:, ss], in_=bk[0:64, :], func=AF.Sqrt, scale=1.0 / 64,
                                                            bias=64e-5), reads=bk.q,
                 writes=[T["e2"]] if sb_ == 0 else [], accs=[] if sb_ == 0 else [T["e2"]])
        ones_mm(lambda sb_: (T["t2"][:, sb_ * 512:(sb_ + 1) * 512], [T["t2"]]), TP // 512, cons_var)
        P.op("dve", lambda e: e.reciprocal(out=T["e2"][:], in_=T["e2"][:]), reads=[T["e2"]], writes=[T["e2"]])
        P.op("dve", lambda e: e.tensor_tensor(out=T["tmp"][:], in0=T["tmp"][:], in1=T["e2"][:], op=ALU.mult),
             reads=[T["tmp"], T["e2"]], writes=[T["tmp"]])
        P.op("dve", lambda e: e.tensor_scalar(out=T["tmp"][:], in0=T["tmp"][:], scalar1=prm[:, 9:10], scalar2=prm[:, 10:11],
                                             op0=ALU.mult, op1=ALU.add), reads=[T["tmp"], prm], writes=[T["tmp"]])
        P.op("pool", lambda e: e.tensor_tensor(out=T["tmp"][:], in0=T["tmp"][:], in1=T["bon"][:], op=ALU.add),
             reads=[T["tmp"], T["bon"]], writes=[T["tmp"]])
        P.op("act", lambda e, GX=GX: e.activation(out=GX[:], in_=GX[:], func=AF.Silu), reads=[GX], writes=[GX])
        P.op("dve", lambda e, GX=GX, O=O: e.tensor_tensor(out=O[:], in0=T["tmp"][:], in1=GX[:], op=ALU.mult),
             reads=[T["tmp"], GX], writes=[O])
        P.dma("sp", O.c, lambda e, O=O, t0=t0: e.dma_start(out=orows[:, t0:t0 + TP], in_=O[:]), reads=[O])


import ml_dtypes

NCORES = 8
SEQ = 16384
NT = SEQ // NCORES
NCH_SEQ = 69


def _consts_np():
    s_ = np.arange(128)[:, None]
    t_ = np.arange(128)[None, :]
    reset = np.ones((64, 1024), np.float32)
    reset[:, ::128] = 0
    return dict(
        c_if=np.eye(128, dtype=np.float32),
        c_ib=np.eye(128).astype(ml_dtypes.bfloat16),
        c_mask=np.concatenate([(s_ <= t_), (s_ > t_)], axis=1).astype(ml_dtypes.bfloat16),
        c_ui=np.concatenate([(t_ > s_), (t_ >= s_)], axis=1).astype(np.float32),
        c_sl=(s_ > t_).astype(np.float32),
        c_reset=reset,
    )


def _din(nc, name, shape, dt=F32):
    return nc.dram_tensor(name, list(shape), dt, kind="ExternalInput").ap()


def _dout(nc, name, shape, dt=F32):
    return nc.dram_tensor(name, list(shape), dt, kind="ExternalOutput").ap()


def _mk_consts(nc, P, rw=False):
    K = Consts(P, _din(nc, "c_if", [128, 128]), _din(nc, "c_ib", [128, 128], BF16), _din(nc, "c_mask", [128, 256], BF16))
    RK = None
    if rw:
        RK = RwConsts(P, _din(nc, "c_ui", [128, 256]), _din(nc, "c_sl", [128, 128]), _din(nc, "c_reset", [64, 1024]))
    return K, RK


def build_tok(with_out, with_proj):
    nc = bass.Bass("TRN2", target_bir_lowering=False)
    P = Prog(nc)
    K, _ = _mk_consts(nc, P)
    x = _din(nc, "x", [NT, 2048])
    xcur = x
    if with_out:
        oT = _din(nc, "oT", [2048, NT], BF16)
        wo = _din(nc, "wo", [2048, 2048])
        xout = _dout(nc, "xout", [NT, 2048])
        out_stage(P, K, NT, x, lambda k: oT[128 * k:128 * k + 128, :], wo, xout)
        P.end_stage()
        xcur = xout
    if with_proj:
        g = _din(nc, "g", [128, 16])
        w = _din(nc, "w", [2048, 5440])
        pT = _dout(nc, "pT", [NCH_SEQ * 64, NT])
        pmem = nc.dram_tensor("pmem", [1024, NT], F32).ap()
        omem = _dout(nc, "omem", [512, NT], BF16)

        def dst64(i):
            if i < NCH_SEQ:
                return pT[64 * i:64 * i + 64, :]
            j = i - NCH_SEQ
            return pmem[64 * j:64 * j + 64, :]
        proj_stage(P, K, NT, xcur, g, w, 5440, dst64)
        P.end_stage()
        mem_stage(P, K, NT, lambda h: pmem[128 * h:128 * h + 128, :], lambda h: pmem[512 + 128 * h:512 + 128 * h + 128, :],
                  lambda h: omem[128 * h:128 * h + 128, :], _din(nc, "mem", [256, 2048]), _din(nc, "memg", [128, 16]),
                  _din(nc, "wkv", [2048, 1024]), _din(nc, "mqg", [128, 1]), _din(nc, "mkg", [128, 1]))
        P.end_stage()
    P.close()
    return nc


def build_head():
    nc = bass.Bass("TRN2", target_bir_lowering=False)
    P = Prog(nc)
    K, RK = _mk_consts(nc, P, rw=True)
    pin = _din(nc, "pin", [11 * 64, SEQ])
    sm = _din(nc, "sm", [64, 32])
    lw = _din(nc, "lw", [2, 64, 64])
    lup = _din(nc, "lup", [64, 64])
    oR = _dout(nc, "oR", [192, SEQ], BF16)

    def rows(i, lo=0, hi=64):
        return pin[64 * i + lo:64 * i + hi, :]
    lru_stage(P, 64, SEQ, rows(0), rows(1), oR[0:64, :], sm[:, 0:4], sm[:, 4:5], lw[0:1], sm[:, 5:6], lw[1:2], sm[:, 6:7],
              sm[:, 7:8])
    P.end_stage()
    rwkv_stage(P, K, RK, SEQ, rows(2), rows(3), rows(4), rows(5, 0, 32), rows(5, 32, 64), rows(6), oR[64:128, :],
               sm[:, 8:24], lup[0:32, :], lup[32:64, :])
    P.end_stage()
    swa_stage(P, K, SEQ, rows(7), rows(8), rows(9), rows(10), oR[128:192, :], sm[:, 24:25], sm[:, 25:26], sm[:, 26:27])
    P.end_stage()
    P.close()
    return nc


def _g16(v):
    return np.ascontiguousarray(np.asarray(v, np.float32).reshape(16, 128).T)


def _head_small(inp, l, h):
    hs = slice(64 * h, 64 * h + 64)
    sm = np.zeros((64, 32), np.float32)
    sm[:, 0:4] = inp["conv_w"][l][:, hs].T
    sm[:, 4] = inp["conv_b"][l][hs]
    sm[:, 5] = inp["lru_ba"][l][hs]
    sm[:, 6] = inp["lru_bx"][l][hs]
    sm[:, 7] = inp["lru_lambda"][l][hs]
    mu = inp["rw_mu"][l]
    sm[:, 8] = mu[0:512][hs]
    sm[:, 9] = mu[512:1024][hs]
    sm[:, 10] = mu[1024:1536][hs]
    sm[:, 11] = mu[1536:1600]
    sm[:, 12] = inp["rw_w0"][l][hs]
    sm[:, 13] = inp["rw_a0"][l][hs]
    sm[:, 14] = inp["rw_k_k"][l][hs]
    sm[:, 15] = inp["rw_k_a"][l][hs]
    sm[:, 16] = inp["rw_r_k"][l][h]
    sm[:, 17] = inp["rw_gn_g"][l][hs]
    sm[:, 18] = inp["rw_gn_b"][l][hs]
    sm[:, 24] = inp["swa_q_g"][l]
    sm[:, 25] = inp["swa_k_g"][l]
    sm[:, 26] = inp["swa_sinks"][l][h]
    lw = np.stack([inp["lru_wa"][l][h], inp["lru_wx"][l][h]]).astype(np.float32)
    lup = np.concatenate([inp["rw_w_up"][l][:, hs], inp["rw_a_up"][l][:, hs]], axis=0).astype(np.float32)
    return sm, lw, np.ascontiguousarray(lup)


TPF = 2048


def build_fused(seq=SEQ, depth=2):
    nc = bass.Bass("TRN2", target_bir_lowering=False)
    P = Prog(nc)
    K, RK = _mk_consts(nc, P, rw=True)
    x = _din(nc, "x", [seq, 2048])
    out = _dout(nc, "out", [seq, 2048])
    mem = _din(nc, "mem", [256, 2048])
    norm_g = _din(nc, "norm_g", [depth, 128, 16])
    w_in = _din(nc, "w_in", [depth, 2048, 5440])
    memg = _din(nc, "memg", [depth, 128, 16])
    wkv = _din(nc, "wkv", [depth, 2048, 1024])
    mqk = _din(nc, "mqk", [depth, 128, 2])
    w_out = _din(nc, "w_out", [depth, 2048, 2048])
    lru_sm = _din(nc, "lru_sm", [depth, 512, 8])
    lru_w = _din(nc, "lru_w", [depth, 2, 8, 64, 64])
    rw_sm = _din(nc, "rw_sm", [depth, 8, 64, 16])
    rw_lup = _din(nc, "rw_lup", [depth, 8, 64, 64])
    swa_sm = _din(nc, "swa_sm", [depth, 8, 64, 4])
    x1 = nc.dram_tensor("x1_scr", [seq, 2048], F32).ap()
    p_lru = nc.dram_tensor("p_lru", [1024, seq], F32).ap()
    p_rw = nc.dram_tensor("p_rw", [2112, seq], F32).ap()
    p_swa = nc.dram_tensor("p_swa", [1280, seq], F32).ap()
    p_mem = nc.dram_tensor("p_mem", [1024, seq], F32).ap()
    oT = nc.dram_tensor("oT_scr", [2048, seq], BF16).ap()
    for l in range(depth):
        xin = x if l == 0 else x1
        xout = out if l == depth - 1 else x1
        for tp in range(seq // TPF):
            ts_ = slice(tp * TPF, (tp + 1) * TPF)

            def dst64(i, ts_=ts_):
                if i < 16:
                    return p_lru[64 * i:64 * i + 64, ts_]
                if i < 49:
                    return p_rw[64 * (i - 16):64 * (i - 16) + 64, ts_]
                if i < 69:
                    return p_swa[64 * (i - 49):64 * (i - 49) + 64, ts_]
                return p_mem[64 * (i - 69):64 * (i - 69) + 64, ts_]
            proj_stage(P, K, TPF, xin[ts_, :], norm_g[l], w_in[l], 5440, dst64)
            P.end_stage()
        mem_stage(P, K, seq, lambda h: p_mem[128 * h:128 * h + 128, :], lambda h: p_mem[512 + 128 * h:512 + 128 * h + 128, :],
                  lambda h: oT[1536 + 128 * h:1536 + 128 * h + 128, :], mem, memg[l], wkv[l], mqk[l][:, 0:1], mqk[l][:, 1:2])
        P.end_stage()
        for ct in range(4):
            cs = slice(128 * ct, 128 * ct + 128)
            lru_stage(P, 128, seq, p_lru[cs, :], p_lru[512 + 128 * ct:512 + 128 * ct + 128, :], oT[cs, :],
                      lru_sm[l][cs, 0:4], lru_sm[l][cs, 4:5], lru_w[l][0][2 * ct:2 * ct + 2], lru_sm[l][cs, 5:6],
                      lru_w[l][1][2 * ct:2 * ct + 2], lru_sm[l][cs, 6:7], lru_sm[l][cs, 7:8])
            P.end_stage()
        for h in range(8):
            hs = slice(64 * h, 64 * h + 64)
            rwkv_stage(P, K, RK, seq, p_rw[hs, :], p_rw[512 + 64 * h:512 + 64 * h + 64, :],
                       p_rw[1024 + 64 * h:1024 + 64 * h + 64, :], p_rw[1536:1568, :], p_rw[1568:1600, :],
                       p_rw[1600 + 64 * h:1600 + 64 * h + 64, :], oT[512 + 64 * h:512 + 64 * h + 64, :],
                       rw_sm[l][h], rw_lup[l][h][0:32, :], rw_lup[l][h][32:64, :])
            P.end_stage()
        for h in range(8):
            kv = h // 4
            swa_stage(P, K, seq, p_swa[64 * h:64 * h + 64, :], p_swa[512 + 64 * kv:512 + 64 * kv + 64, :],
                      p_swa[640 + 64 * kv:640 + 64 * kv + 64, :], p_swa[768 + 64 * h:768 + 64 * h + 64, :],
                      oT[1024 + 64 * h:1024 + 64 * h + 64, :], swa_sm[l][h][:, 0:1], swa_sm[l][h][:, 1:2], swa_sm[l][h][:, 2:3])
            P.end_stage()
        out_stage(P, K, seq, xin, lambda k: oT[128 * k:128 * k + 128, :], w_out[l], xout)
        P.end_stage()
    P.close()
    return nc, P


def _fused_inputs(inp, depth=2):
    f = np.float32
    L = depth
    m = dict(_consts_np())
    m["x"] = np.ascontiguousarray(inp["x"][0], dtype=f)
    m["mem"] = np.ascontiguousarray(inp["mem"][0], dtype=f)
    m["norm_g"] = np.stack([_g16(inp["norm_g"][l]) for l in range(L)])
    m["w_in"] = np.ascontiguousarray(inp["w_in"][:L], dtype=f)
    m["memg"] = np.stack([_g16(inp["mem_norm_g"][l]) for l in range(L)])
    m["wkv"] = np.ascontiguousarray(inp["w_mem_kv"][:L], dtype=f)
    m["mqk"] = np.ascontiguousarray(np.stack([inp["mem_q_g"][:L], inp["mem_k_g"][:L]], axis=-1), dtype=f)
    m["w_out"] = np.ascontiguousarray(inp["w_out"][:L], dtype=f)
    lru_sm = np.zeros((L, 512, 8), f)
    lru_sm[:, :, 0:4] = np.transpose(inp["conv_w"][:L], (0, 2, 1))
    lru_sm[:, :, 4] = inp["conv_b"][:L]
    lru_sm[:, :, 5] = inp["lru_ba"][:L]
    lru_sm[:, :, 6] = inp["lru_bx"][:L]
    lru_sm[:, :, 7] = inp["lru_lambda"][:L]
    m["lru_sm"] = lru_sm
    m["lru_w"] = np.ascontiguousarray(np.stack([inp["lru_wa"][:L], inp["lru_wx"][:L]], axis=1), dtype=f)
    rw_sm = np.zeros((L, 8, 64, 16), f)
    rw_lup = np.zeros((L, 8, 64, 64), f)
    swa_sm = np.zeros((L, 8, 64, 4), f)
    for l in range(L):
        for h in range(8):
            sm, _, lup = _head_small(inp, l, h)
            rw_sm[l, h] = sm[:, 8:24]
            rw_lup[l, h] = lup
            swa_sm[l, h, :, 0:3] = sm[:, 24:27]
    m["rw_sm"] = rw_sm
    m["rw_lup"] = rw_lup
    m["swa_sm"] = swa_sm
    return m


def kernel(**inputs):
    inp = {k: np.asarray(v) for k, v in inputs.items()}
    nc, _ = build_fused()
    m = _fused_inputs(inp)
    cores = list(range(NCORES))
    res = run_bass_kernel_spmd(nc, [m for _ in cores], core_ids=cores).results
    return np.asarray(res[0]["out"], dtype=np.float32).reshape(1, SEQ, 2048)
```

```python
from concourse.bass_utils import run_bass_kernel_spmd
from contextlib import ExitStack
import numpy as np
import concourse.bass as bass
import concourse.mybir as mybir

F32 = mybir.dt.float32
BF16 = mybir.dt.bfloat16
ALU = mybir.AluOpType
AF = mybir.ActivationFunctionType
AX = mybir.AxisListType

MAXV = 30000
ENGS = ("pe", "act", "dve", "pool", "sp")


class Ev:
    __slots__ = ("key", "n")

    def __init__(self, key, n=None):
        self.key = key
        self.n = n


class Res:
    __slots__ = ("name", "writers", "readers", "excl")

    def __init__(self, name="", excl=False):
        self.name = name
        self.writers = []
        self.readers = []
        self.excl = excl


class Counter:
    def __init__(self, prog, name):
        self.sem = prog.es.enter_context(prog.nc.semaphore(name))
        self.total = 0
        self.key = ("d", id(self))
        prog.counters[self.key] = self


class Tile:
    def __init__(self, prog, shape, dtype, name=None, psum=False, persistent=False):
        prog.nsb += 1
        nm = f"t{prog.nsb}_{name or ''}"
        st = prog.es if persistent else prog.stage_es
        if psum:
            self.t = st.enter_context(prog.nc.psum_tensor(nm, list(shape), dtype))
        else:
            self.t = st.enter_context(prog.nc.sbuf_tensor(nm, list(shape), dtype))
        self.r = Res(name or "", excl=psum)
        self.prog = prog
        self._c = None

    @property
    def c(self):
        if self._c is None:
            self._c = self.prog.counter()
        return self._c

    def __getitem__(self, idx):
        return self.t[idx]


class Op:
    __slots__ = ("eng", "fn", "waits", "signal", "ev", "ctr")


def _rs(xs):
    return [getattr(x, "r", x) for x in xs]


class Prog:
    def __init__(self, nc):
        self.nc = nc
        self.es = ExitStack()
        self.stage_es = ExitStack()
        self.ops = {e: [] for e in ENGS}
        self.sigcnt = {e: 0 for e in ENGS}
        self.emitted = {e: 0 for e in ENGS}
        self.pending = {e: [] for e in ENGS}
        self.waited = {e: {} for e in ENGS}
        self.counters = {}
        self.free_counters = []
        self.stage_counters = []
        self.esems = {e: [] for e in ENGS}
        self.nsb = 0
        self.nstage = 0
        self.total_ops = 0

    def tile(self, shape, dtype, name=None, psum=False, persistent=False):
        return Tile(self, shape, dtype, name, psum, persistent)

    def counter(self, name=None, persistent=False):
        if self.free_counters and not persistent:
            c = self.free_counters.pop()
        else:
            self.nsb += 1
            c = Counter(self, name or f"ctr{self.nsb}")
        if not persistent:
            self.stage_counters.append(c)
        return c

    def _deps(self, reads, writes, accs):
        waits = []
        for r in reads:
            waits.extend(r.writers)
        for r in writes:
            waits.extend(r.writers)
            waits.extend(r.readers)
        for r in accs:
            waits.extend(r.readers)
        return waits

    def _post(self, ev, reads, writes, accs):
        for r in reads:
            r.readers.append(ev)
            if len(r.readers) > 64:
                r.readers = _compact(r.readers)
        for r in writes:
            r.writers = [ev]
            r.readers = []
        for r in accs:
            r.writers.append(ev)
            if len(r.writers) > 64:
                r.writers = _compact(r.writers)

    def op(self, eng, fn, reads=(), writes=(), accs=(), signal=True):
        reads, writes, accs = _rs(reads), _rs(writes), _rs(accs)
        ex = [r for r in reads if r.excl]
        if ex:
            reads = [r for r in reads if not r.excl]
            writes = list(writes) + ex
        o = Op()
        o.eng = eng
        o.fn = fn
        o.ctr = None
        waits = self._deps(reads, writes, accs)
        if eng == "pe":
            waits = [w for w in waits if w.key != "pe"]
        o.waits = waits
        o.signal = False
        ev = Ev(eng)
        o.ev = ev
        self.ops[eng].append(o)
        self.pending[eng].append(ev)
        if signal:
            self.signal_last(eng)
        self._post(ev, reads, writes, accs)
        return ev

    def signal_last(self, eng):
        o = self.ops[eng][-1]
        if o.signal:
            return
        o.signal = True
        self.sigcnt[eng] += 1
        for p in self.pending[eng]:
            p.n = self.sigcnt[eng]
        self.pending[eng] = []

    def dma(self, eng, ctr, fn, reads=(), writes=(), accs=(), inc=16):
        reads, writes, accs = _rs(reads), _rs(writes), _rs(accs)
        o = Op()
        o.eng = eng
        o.fn = fn
        o.ctr = ctr
        o.waits = self._deps(reads, writes, accs)
        o.signal = (inc == 16)
        ctr.total += inc
        assert ctr.total < 60000
        ev = Ev(ctr.key, ctr.total)
        o.ev = ev
        self.ops[eng].append(o)
        self._post(ev, reads, writes, accs)
        return ev

    def finish(self, eng="sp"):
        o = Op()
        o.eng = eng
        o.fn = None
        o.ctr = None
        o.signal = False
        o.ev = Ev(eng)
        o.waits = [Ev(c.key, c.total) for c in self.counters.values() if c.total]
        self.ops[eng].append(o)

    def end_stage(self):
        nc = self.nc
        self.finish("sp")
        for e in ENGS:
            if self.pending[e]:
                self.signal_last(e)
            need = (self.sigcnt[e] + MAXV - 1) // MAXV + 1
            while len(self.esems[e]) < need:
                self.esems[e].append(self.es.enter_context(nc.semaphore(f"s_{e}{len(self.esems[e])}")))
        prog = self

        def resolve(ev):
            if isinstance(ev.key, tuple):
                return (ev.key, prog.counters[ev.key].sem, ev.n)
            assert ev.n is not None, f"unresolved event on {ev.key}"
            idx = (ev.n - 1) // MAXV
            return ((ev.key, idx), prog.esems[ev.key][idx], (ev.n - 1) % MAXV + 1)

        def run(e):
            def body(eng):
                waited = prog.waited[e]
                cnt = prog.emitted[e]
                for o in prog.ops[e]:
                    need = {}
                    for w in o.waits:
                        k, sem, v = resolve(w)
                        if waited.get(k, 0) >= v:
                            continue
                        if k not in need or need[k][1] < v:
                            need[k] = (sem, v)
                    for k, (sem, v) in need.items():
                        eng.wait_ge(sem, v)
                        waited[k] = v
                    if o.fn is None:
                        continue
                    inst = o.fn(eng)
                    if o.ctr is not None:
                        if o.signal:
                            inst.then_inc(o.ctr.sem, 16)
                        else:
                            inst.then_inc(o.ctr.sem)
                    elif o.signal:
                        cnt += 1
                        idx = (cnt - 1) // MAXV
                        inst.then_inc(prog.esems[e][idx], 1)
                prog.emitted[e] = cnt
            return body

        with nc.Block() as block:
            block.tensor(run("pe"))
            block.scalar(run("act"))
            block.vector(run("dve"))
            block.gpsimd(run("pool"))
            block.sync(run("sp"))
        for e in ENGS:
            assert self.emitted[e] == self.sigcnt[e], (e, self.emitted[e], self.sigcnt[e])
            self.total_ops += len(self.ops[e])
            self.ops[e] = []
        self.stage_es.close()
        self.stage_es = ExitStack()
        self.free_counters.extend(self.stage_counters)
        self.stage_counters = []
        self.nstage += 1

    def close(self):
        self.es.close()


def _compact(evs):
    best = {}
    for ev in evs:
        if ev.n is None:
            best[id(ev)] = ev
            continue
        k = ev.key
        if k not in best or best[k].n < ev.n:
            best[k] = ev
    return list(best.values())


LRU_C = 8.0


def load_col(P, ctr, res, dst_ap, src_ap, eng="sp"):
    P.dma(eng, ctr, lambda e: e.dma_start(out=dst_ap, in_=src_ap), accs=[res])


def lru_stage(P, CP, NTOK, xrows, grows, orows, convw, convb, wa, ba, wx, bx, lam, TP=2048, tag=""):
    nb = CP // 64
    npc = NTOK // TP
    prm = P.tile([CP, 16], F32, f"lruprm{tag}")
    wabd = P.tile([CP, CP], F32, f"wabd{tag}")
    wxbd = P.tile([CP, CP], F32, f"wxbd{tag}")
    if nb > 1:
        P.op("pool", lambda e: e.memset(wabd[:], 0.0), writes=[wabd])
        P.op("pool", lambda e: e.memset(wxbd[:], 0.0), writes=[wxbd])
    for b in range(nb):
        P.dma("sp", wabd.c, lambda e, b=b: e.dma_start(out=wabd[b * 64:(b + 1) * 64, b * 64:(b + 1) * 64], in_=wa[b]),
              accs=[wabd])
        P.dma("sp", wxbd.c, lambda e, b=b: e.dma_start(out=wxbd[b * 64:(b + 1) * 64, b * 64:(b + 1) * 64], in_=wx[b]),
              accs=[wxbd])
    P.dma("sp", prm.c, lambda e: e.dma_start(out=prm[:, 0:4], in_=convw, allow_slow_non_contiguous=True), writes=[prm])
    for i, src in enumerate((convb, ba, bx, lam)):
        P.dma("sp", prm.c, lambda e, i=i, src=src: e.dma_start(out=prm[:, 4 + i:5 + i], in_=src, allow_slow_non_contiguous=True), accs=[prm])
    P.op("act", lambda e: e.activation(out=prm[:, 8:9], in_=prm[:, 7:8], func=AF.Exp, scale=-1.0),
         reads=[prm], accs=[prm])
    P.op("act", lambda e: e.activation(out=prm[:, 9:10], in_=prm[:, 8:9], func=AF.Ln, bias=1.0),
         reads=[prm], accs=[prm])
    P.op("dve", lambda e: e.tensor_scalar(out=prm[:, 10:11], in0=prm[:, 9:10], scalar1=-LRU_C, scalar2=None,
                                         op0=ALU.mult), reads=[prm], accs=[prm])
    P.op("dve", lambda e: e.memset(prm[:, 11:12], 0.0), reads=[prm], accs=[prm])

    xt = [P.tile([CP, TP + 3], F32, f"lxt{tag}{i}") for i in range(2)]
    gt = [P.tile([CP, TP], F32, f"lgt{tag}{i}") for i in range(2)]
    xc = P.tile([CP, TP], F32, f"lxc{tag}")
    rr = P.tile([CP, TP], F32, f"lr{tag}")
    ii = P.tile([CP, TP], F32, f"li{tag}")
    aa = P.tile([CP, TP], F32, f"la{tag}")
    mm = P.tile([CP, TP], F32, f"lm{tag}")
    hh = [P.tile([CP, TP], F32, f"lh{tag}{i}") for i in range(2)]
    ob = [P.tile([CP, TP], BF16, f"lo{tag}{i}") for i in range(2)]
    pg = [P.tile([CP, 512], F32, f"lpg{tag}{i}", psum=True) for i in range(2)]
    npg = 0
    for pi in range(npc):
        s = pi % 2
        t0 = pi * TP
        X, G, H, O = xt[s], gt[s], hh[s], ob[s]
        if pi == 0:
            P.op("pool", lambda e, X=X: e.memset(X[:, 0:3], 0.0), writes=[X])
            P.dma("sp", X.c, lambda e, X=X: e.dma_start(out=X[:, 3:3 + TP], in_=xrows[:, 0:TP]), accs=[X])
        else:
            P.dma("sp", X.c, lambda e, X=X, t0=t0: e.dma_start(out=X[:, :], in_=xrows[:, t0 - 3:t0 + TP]), writes=[X])
        P.dma("sp", G.c, lambda e, G=G, t0=t0: e.dma_start(out=G[:, :], in_=grows[:, t0:t0 + TP]), writes=[G])
        P.op("dve", lambda e, X=X: e.tensor_scalar(out=xc[:], in0=X[:, 3:3 + TP], scalar1=prm[:, 3:4],
                                                  scalar2=prm[:, 4:5], op0=ALU.mult, op1=ALU.add),
             reads=[X, prm], writes=[xc])
        for j in range(3):
            P.op("dve", lambda e, X=X, j=j: e.scalar_tensor_tensor(out=xc[:], in0=X[:, j:j + TP], scalar=prm[:, j:j + 1],
                                                                  in1=xc[:], op0=ALU.mult, op1=ALU.add),
                 reads=[X, prm, xc], writes=[xc])
        for (wbd, bcol, dst) in ((wabd, 5, rr), (wxbd, 6, ii)):
            for sb_ in range(TP // 512):
                pb = pg[npg % 2]
                npg += 1
                P.op("pe", lambda e, wbd=wbd, pb=pb, sb_=sb_: e.matmul(pb[:, :], lhsT=wbd[:, :],
                                                                      rhs=xc[:, sb_ * 512:(sb_ + 1) * 512],
                                                                      start=True, stop=True),
                     reads=[wbd, xc], writes=[pb])
                P.op("act", lambda e, pb=pb, dst=dst, sb_=sb_, bcol=bcol: e.activation(
                    out=dst[:, sb_ * 512:(sb_ + 1) * 512], in_=pb[:, :], func=AF.Sigmoid, bias=prm[:, bcol:bcol + 1]),
                    reads=[pb, prm], writes=[dst] if sb_ == 0 else [], accs=[] if sb_ == 0 else [dst])
        P.op("act", lambda e: e.activation(out=aa[:], in_=rr[:], func=AF.Exp, scale=prm[:, 10:11]),
             reads=[rr, prm], writes=[aa])
        P.op("dve", lambda e: e.tensor_tensor(out=mm[:], in0=aa[:], in1=aa[:], op=ALU.mult), reads=[aa], writes=[mm])
        P.op("dve", lambda e: e.tensor_scalar(out=mm[:], in0=mm[:], scalar1=-1.0, scalar2=1.0, op0=ALU.mult,
                                             op1=ALU.add), reads=[mm], writes=[mm])
        P.op("dve", lambda e: e.tensor_scalar(out=mm[:], in0=mm[:], scalar1=1e-12, scalar2=None, op0=ALU.max),
             reads=[mm], writes=[mm])
        P.op("act", lambda e: e.activation(out=mm[:], in_=mm[:], func=AF.Sqrt), reads=[mm], writes=[mm])
        P.op("dve", lambda e: e.tensor_tensor(out=ii[:], in0=ii[:], in1=xc[:], op=ALU.mult), reads=[ii, xc], writes=[ii])
        P.op("dve", lambda e: e.tensor_tensor(out=ii[:], in0=ii[:], in1=mm[:], op=ALU.mult), reads=[ii, mm], writes=[ii])
        Hp = hh[1 - s]
        init = prm[:, 11:12] if pi == 0 else Hp[:, TP - 1:TP]
        P.op("dve", lambda e, H=H, init=init: e.tensor_tensor_scan(out=H[:], data0=aa[:], data1=ii[:], initial=init,
                                                                  op0=ALU.mult, op1=ALU.add),
             reads=[aa, ii, prm, Hp], writes=[H])
        P.op("act", lambda e, G=G: e.activation(out=G[:], in_=G[:], func=AF.Silu), reads=[G], writes=[G])
        P.op("dve", lambda e, H=H, G=G, O=O: e.tensor_tensor(out=O[:], in0=H[:], in1=G[:], op=ALU.mult),
             reads=[H, G], writes=[O])
        P.dma("sp", O.c, lambda e, O=O, t0=t0: e.dma_start(out=orows[:, t0:t0 + TP], in_=O[:]), reads=[O])


class Consts:
    def __init__(self, P, c_identf, c_identb, c_swamask):
        self.identf = P.tile([128, 128], F32, "identf", persistent=True)
        self.identb = P.tile([128, 128], BF16, "identb", persistent=True)
        self.swamask = P.tile([128, 256], BF16, "swamask", persistent=True)
        self.ones_f = P.tile([128, 128], F32, "ones_f", persistent=True)
        self.ones_b = P.tile([128, 128], BF16, "ones_b", persistent=True)
        P.dma("sp", self.identf.c, lambda e: e.dma_start(out=self.identf[:], in_=c_identf), writes=[self.identf])
        P.dma("sp", self.identb.c, lambda e: e.dma_start(out=self.identb[:], in_=c_identb), writes=[self.identb])
        P.dma("sp", self.swamask.c, lambda e: e.dma_start(out=self.swamask[:], in_=c_swamask), writes=[self.swamask])
        P.op("pool", lambda e: e.memset(self.ones_f[:], 1.0), writes=[self.ones_f])
        P.op("pool", lambda e: e.memset(self.ones_b[:], 1.0), writes=[self.ones_b])


def swa_stage(P, K, NTOK, qrows, krows, vrows, grows, orows, qg, kg, sink, TP=512, tag=""):
    npc = NTOK // TP
    nbk = TP // 128
    prm = P.tile([64, 8], F32, f"swaprm{tag}")
    P.dma("sp", prm.c, lambda e: e.dma_start(out=prm[:, 0:1], in_=qg, allow_slow_non_contiguous=True), writes=[prm])
    P.dma("sp", prm.c, lambda e: e.dma_start(out=prm[:, 1:2], in_=kg, allow_slow_non_contiguous=True), accs=[prm])
    P.dma("sp", prm.c, lambda e: e.dma_start(out=prm[:, 2:3], in_=sink, allow_slow_non_contiguous=True), accs=[prm])
    P.op("dve", lambda e: e.tensor_scalar(out=prm[:, 3:4], in0=prm[:, 0:1], scalar1=0.125, scalar2=None, op0=ALU.mult),
         reads=[prm], accs=[prm])
    P.op("act", lambda e: e.activation(out=prm[:, 4:5], in_=prm[:, 2:3], func=AF.Exp), reads=[prm], accs=[prm])
    W = TP + 128
    kx = [P.tile([64, W], F32, f"skx{tag}{i}") for i in range(2)]
    vx = [P.tile([64, W], F32, f"svx{tag}{i}") for i in range(2)]
    qx = [P.tile([64, TP], F32, f"sqx{tag}{i}") for i in range(2)]
    gx = [P.tile([64, TP], F32, f"sgx{tag}{i}") for i in range(2)]
    sq = P.tile([64, W], F32, f"ssq{tag}")
    rs = P.tile([64, W], F32, f"srs{tag}")
    kn = P.tile([64, W], BF16, f"skn{tag}")
    qn = P.tile([64, TP], BF16, f"sqn{tag}")
    vb = P.tile([128, nbk + 1, 64], BF16, f"svb{tag}")
    E = [P.tile([128, 256], BF16, f"sE{tag}{i}") for i in range(2)]
    dn = P.tile([64, TP], F32, f"sdn{tag}")
    yy = P.tile([64, TP], F32, f"syy{tag}")
    ob = [P.tile([64, TP], BF16, f"sob{tag}{i}") for i in range(2)]
    pn = [P.tile([64, 512], F32, f"spn{tag}{i}", psum=True) for i in range(2)]
    pt = P.tile([128, 512], F32, f"spt{tag}", psum=True)
    psc = [P.tile([128, 256], F32, f"spsc{tag}{i}", psum=True) for i in range(2)]
    pnum = P.tile([64, 512], F32, f"spnum{tag}", psum=True)
    pden = P.tile([64, 512], F32, f"spden{tag}", psum=True)
    npn = 0
    nsc = 0

    def norm(src, width, c0, gcol, dst, dst_c0):
        nonlocal npn
        P.op("dve", lambda e: e.tensor_tensor(out=sq[:, 0:width], in0=src[:, c0:c0 + width], in1=src[:, c0:c0 + width],
                                             op=ALU.mult), reads=[src], writes=[sq])
        o = 0
        first = True
        while o < width:
            w_ = min(512, width - o)
            pb = pn[npn % 2]
            npn += 1
            P.op("pe", lambda e, pb=pb, o=o, w_=w_: e.matmul(pb[:, 0:w_], lhsT=K.ones_f[0:64, 0:64], rhs=sq[:, o:o + w_],
                                                            start=True, stop=True), reads=[K.ones_f, sq], writes=[pb])
            P.op("act", lambda e, pb=pb, o=o, w_=w_: e.activation(out=rs[:, o:o + w_], in_=pb[:, 0:w_], func=AF.Sqrt,
                                                                 scale=1.0 / 64, bias=1e-6),
                 reads=[pb], writes=[rs] if first else [], accs=[] if first else [rs])
            first = False
            o += w_
        P.op("dve", lambda e: e.reciprocal(out=rs[:, 0:width], in_=rs[:, 0:width]), reads=[rs], writes=[rs])
        P.op("dve", lambda e: e.scalar_tensor_tensor(out=dst[:, dst_c0:dst_c0 + width], in0=src[:, c0:c0 + width],
                                                    scalar=prm[:, gcol:gcol + 1], in1=rs[:, 0:width],
                                                    op0=ALU.mult, op1=ALU.mult), reads=[src, prm, rs], writes=[dst])

    for pi in range(npc):
        s = pi % 2
        t0 = pi * TP
        KX, VX, QX, GX, O = kx[s], vx[s], qx[s], gx[s], ob[s]
        lo = 128 if pi == 0 else 0
        P.dma("sp", KX.c, lambda e, KX=KX, t0=t0, lo=lo: e.dma_start(out=KX[:, lo:W], in_=krows[:, t0 - 128 + lo:t0 + TP]),
              writes=[KX])
        P.dma("sp", VX.c, lambda e, VX=VX, t0=t0, lo=lo: e.dma_start(out=VX[:, lo:W], in_=vrows[:, t0 - 128 + lo:t0 + TP]),
              writes=[VX])
        P.dma("sp", QX.c, lambda e, QX=QX, t0=t0: e.dma_start(out=QX[:, :], in_=qrows[:, t0:t0 + TP]), writes=[QX])
        P.dma("sp", GX.c, lambda e, GX=GX, t0=t0: e.dma_start(out=GX[:, :], in_=grows[:, t0:t0 + TP]), writes=[GX])
        norm(KX, W - lo, lo, 1, kn, lo)
        norm(QX, TP, 0, 3, qn, 0)
        b0 = lo // 128
        for b in range(b0, nbk + 1):
            P.op("pe", lambda e, VX=VX, b=b: e.transpose(out=pt[:, b * 64:(b + 1) * 64], in_=VX[:, b * 128:(b + 1) * 128],
                                                        identity=K.identf[0:64, 0:64]),
                 reads=[VX, K.identf], writes=[pt] if b == b0 else [], accs=[] if b == b0 else [pt],
                 signal=(b == nbk))
        P.op("act", lambda e, b0=b0: e.activation(out=vb[:, b0:nbk + 1, :],
                                                 in_=pt[:, b0 * 64:(nbk + 1) * 64].rearrange("p (b d) -> p b d", d=64),
                                                 func=AF.Copy), reads=[pt], writes=[vb])
        for n in range(nbk):
            has_prev = not (pi == 0 and n == 0)
            sc = psc[nsc % 2]
            Eb = E[nsc % 2]
            nsc += 1
            wd = 256 if has_prev else 128
            P.op("pe", lambda e, sc=sc, n=n: e.matmul(sc[:, 0:128], lhsT=kn[:, (n + 1) * 128:(n + 2) * 128],
                                                     rhs=qn[:, n * 128:(n + 1) * 128], start=True, stop=True),
                 reads=[kn, qn], writes=[sc], signal=not has_prev)
            if has_prev:
                P.op("pe", lambda e, sc=sc, n=n: e.matmul(sc[:, 128:256], lhsT=kn[:, n * 128:(n + 1) * 128],
                                                         rhs=qn[:, n * 128:(n + 1) * 128], start=True, stop=True),
                     reads=[kn, qn], accs=[sc])
            P.op("act", lambda e, sc=sc, Eb=Eb, wd=wd: e.activation(out=Eb[:, 0:wd], in_=sc[:, 0:wd], func=AF.Exp),
                 reads=[sc], writes=[Eb])
            P.op("pool", lambda e, Eb=Eb, wd=wd: e.tensor_tensor(out=Eb[:, 0:wd], in0=Eb[:, 0:wd], in1=K.swamask[:, 0:wd],
                                                                op=ALU.mult), reads=[Eb, K.swamask], writes=[Eb])
            cs = slice(n * 128, (n + 1) * 128)
            for (pacc, lhs_cur, lhs_prev) in ((pnum, vb[:, n + 1, :], vb[:, n, :]),
                                              (pden, K.ones_b[:, 0:64], K.ones_b[:, 0:64])):
                P.op("pe", lambda e, pacc=pacc, lhs_cur=lhs_cur, Eb=Eb, cs=cs, has_prev=has_prev: e.matmul(
                    pacc[:, cs], lhsT=lhs_cur, rhs=Eb[:, 0:128], start=True, stop=not has_prev),
                    reads=[vb, K.ones_b, Eb], writes=[pacc] if n == 0 else [], accs=[] if n == 0 else [pacc],
                    signal=False)
                if has_prev:
                    P.op("pe", lambda e, pacc=pacc, lhs_prev=lhs_prev, Eb=Eb, cs=cs: e.matmul(
                        pacc[:, cs], lhsT=lhs_prev, rhs=Eb[:, 128:256], start=False, stop=True),
                        reads=[vb, K.ones_b, Eb], accs=[pacc], signal=False)
            P.signal_last("pe")
        P.op("dve", lambda e: e.tensor_scalar(out=dn[:], in0=pden[:, :], scalar1=prm[:, 4:5], scalar2=None, op0=ALU.add),
             reads=[pden, prm], writes=[dn])
        P.op("dve", lambda e: e.reciprocal(out=dn[:], in_=dn[:]), reads=[dn], writes=[dn])
        P.op("dve", lambda e: e.tensor_tensor(out=yy[:], in0=pnum[:, :], in1=dn[:], op=ALU.mult),
             reads=[pnum, dn], writes=[yy])
        P.op("act", lambda e, GX=GX: e.activation(out=GX[:], in_=GX[:], func=AF.Silu), reads=[GX], writes=[GX])
        P.op("dve", lambda e, GX=GX, O=O: e.tensor_tensor(out=O[:], in0=yy[:], in1=GX[:], op=ALU.mult),
             reads=[yy, GX], writes=[O])
        P.dma("sp", O.c, lambda e, O=O, t0=t0: e.dma_start(out=orows[:, t0:t0 + TP], in_=O[:]), reads=[O])


D_MODEL = 2048
KC = D_MODEL // 128


def norm_transpose(P, K, x_dram, ntile, gsb, hT, pst, eps=1e-6, tag=""):
    xt = [P.tile([128, D_MODEL], F32, f"nxt{tag}{i}") for i in range(2)]
    xs = [P.tile([128, D_MODEL], F32, f"nxs{tag}{i}") for i in range(2)]
    junk = P.tile([128, D_MODEL], BF16, f"njunk{tag}")
    st = P.tile([128, 4 * ntile], F32, f"nst{tag}")
    npst = 0
    for i in range(ntile):
        s = i % 2
        X, XS = xt[s], xs[s]
        P.dma("sp", X.c, lambda e, X=X, i=i: e.dma_start(out=X[:], in_=x_dram[i * 128:(i + 1) * 128, :]), writes=[X])
        c0 = 4 * i
        P.op("act", lambda e, X=X, c0=c0: e.activation(out=junk[:], in_=X[:], func=AF.Square, accum_out=st[:, c0:c0 + 1]),
             reads=[X], writes=[junk], accs=[st])
        P.op("dve", lambda e, c0=c0: e.tensor_scalar(out=st[:, c0 + 1:c0 + 2], in0=st[:, c0:c0 + 1], scalar1=1.0 / D_MODEL,
                                                    scalar2=eps, op0=ALU.mult, op1=ALU.add), reads=[st], accs=[st])
        P.op("act", lambda e, c0=c0: e.activation(out=st[:, c0 + 2:c0 + 3], in_=st[:, c0 + 1:c0 + 2], func=AF.Sqrt),
             reads=[st], accs=[st])
        P.op("dve", lambda e, c0=c0: e.reciprocal(out=st[:, c0 + 3:c0 + 4], in_=st[:, c0 + 2:c0 + 3]),
             reads=[st], accs=[st])
        P.op("act", lambda e, X=X, XS=XS, c0=c0: e.activation(out=XS[:], in_=X[:], func=AF.Copy,
                                                             scale=st[:, c0 + 3:c0 + 4]), reads=[X, st], writes=[XS])
        for kq in range(KC // 4):
            pb = pst[npst % 2]
            npst += 1
            for kk in range(4):
                k = kq * 4 + kk
                P.op("pe", lambda e, XS=XS, pb=pb, k=k, kk=kk: e.transpose(
                    out=pb[:, kk * 128:(kk + 1) * 128], in_=XS[:, k * 128:(k + 1) * 128], identity=K.identf[:]),
                    reads=[XS, K.identf], writes=[pb] if kk == 0 else [], accs=[pb] if kk else [], signal=(kk == 3))
            for kk in range(4):
                k = kq * 4 + kk
                P.op("dve", lambda e, pb=pb, k=k, kk=kk, i=i: e.tensor_scalar(
                    out=hT[:, k, i * 128:(i + 1) * 128], in0=pb[:, kk * 128:(kk + 1) * 128],
                    scalar1=gsb[:, k:k + 1], scalar2=None, op0=ALU.mult), reads=[pb, gsb], accs=[hT])


def load_weight_bf16(P, w_dram, c0, cw, wst, wb):
    wv = w_dram.rearrange("(k p) c -> p k c", p=128)
    P.dma("sp", wst.c, lambda e: e.dma_start(out=wst[:, :, 0:cw], in_=wv[:, :, c0:c0 + cw]), writes=[wst])
    P.op("pool", lambda e: e.tensor_copy(out=wb[:, :, 0:cw], in_=wst[:, :, 0:cw]), reads=[wst], writes=[wb])


def proj_stage(P, K, NT, x_dram, g_dram, w_dram, NCOL, dst64, tag=""):
    ntile = NT // 128
    TG = min(512, NT)
    ntg = NT // TG
    CB = 256
    gsb = P.tile([128, KC], F32, f"pg{tag}")
    P.dma("sp", gsb.c, lambda e: e.dma_start(out=gsb[:], in_=g_dram), writes=[gsb])
    hT = P.tile([128, KC, NT], BF16, f"phT{tag}")
    pp = [P.tile([128, 512], F32, f"ppp{tag}{i}", psum=True) for i in range(4)]
    norm_transpose(P, K, x_dram, ntile, gsb, hT, pp[0:2], tag=tag)
    nblk = (NCOL + CB - 1) // CB
    wst = [P.tile([128, KC, CB], F32, f"pwst{tag}{i}") for i in range(2)]
    wb = [P.tile([128, KC, CB], BF16, f"pwb{tag}{i}") for i in range(2)]
    ost = [P.tile([128, NT], F32, f"post{tag}{i}") for i in range(2)]
    nmm = 0
    nct = 0
    for bi in range(nblk):
        s = bi % 2
        cw = min(CB, NCOL - bi * CB)
        load_weight_bf16(P, w_dram, bi * CB, cw, wst[s], wb[s])
        WB = wb[s]
        for j in range((cw + 127) // 128):
            mw = min(128, cw - j * 128)
            O = ost[nct % 2]
            nct += 1
            for n in range(ntg):
                pb = pp[nmm % 4]
                nmm += 1
                for k in range(KC):
                    P.op("pe", lambda e, WB=WB, k=k, j=j, mw=mw, n=n, pb=pb: e.matmul(
                        pb[0:mw, 0:TG], lhsT=WB[:, k, j * 128:j * 128 + mw], rhs=hT[:, k, n * TG:(n + 1) * TG],
                        start=(k == 0), stop=(k == KC - 1)),
                        reads=[WB, hT], writes=[pb] if k == 0 else [], accs=[pb] if k else [], signal=(k == KC - 1))
                if nmm % 2:
                    P.op("act", lambda e, O=O, mw=mw, n=n, pb=pb: e.activation(
                        out=O[0:mw, n * TG:(n + 1) * TG], in_=pb[0:mw, 0:TG], func=AF.Copy),
                        reads=[pb], writes=[O] if n == 0 else [], accs=[] if n == 0 else [O])
                else:
                    P.op("dve", lambda e, O=O, mw=mw, n=n, pb=pb: e.tensor_copy(
                        out=O[0:mw, n * TG:(n + 1) * TG], in_=pb[0:mw, 0:TG]),
                        reads=[pb], writes=[O] if n == 0 else [], accs=[] if n == 0 else [O])
            c0 = bi * CB + j * 128
            for hh_ in range(mw // 64):
                P.dma("sp", O.c, lambda e, O=O, hh_=hh_, c0=c0: e.dma_start(
                    out=dst64(c0 // 64 + hh_), in_=O[hh_ * 64:(hh_ + 1) * 64, :]), reads=[O])


def out_stage(P, K, NT, x_dram, oT_src, w_dram, out_dram, tag=""):
    TG = min(512, NT)
    ntg = NT // TG
    wst = [P.tile([128, KC, 256], F32, f"owst{tag}{i}") for i in range(2)]
    wo = [P.tile([128, KC, 512], BF16, f"owo{tag}{i}") for i in range(4)]
    wv = w_dram.rearrange("(k p) c -> p k c", p=128)
    for cb in range(8):
        WS, WO, hf = wst[cb % 2], wo[cb // 2], cb % 2
        P.dma("sp", WS.c, lambda e, WS=WS, cb=cb: e.dma_start(out=WS[:, :, :], in_=wv[:, :, cb * 256:(cb + 1) * 256]), writes=[WS])
        P.op("pool", lambda e, WS=WS, WO=WO, hf=hf: e.tensor_copy(out=WO[:, :, hf * 256:(hf + 1) * 256], in_=WS[:, :, :]),
             reads=[WS], writes=[WO] if hf == 0 else [], accs=[WO] if hf else [])
    ot = [P.tile([128, KC, TG], BF16, f"oot{tag}{i}") for i in range(2)]
    xt = [P.tile([128, D_MODEL], F32, f"oxt{tag}{i}") for i in range(2)]
    xo = [P.tile([128, D_MODEL], F32, f"oxo{tag}{i}") for i in range(2)]
    pp = [P.tile([128, 512], F32, f"opp{tag}{i}", psum=True) for i in range(4)]
    nmm = 0
    ntl = 0
    for n in range(ntg):
        OT = ot[n % 2]
        for k in range(KC):
            P.dma("sp", OT.c, lambda e, OT=OT, k=k, n=n: e.dma_start(out=OT[:, k, :], in_=oT_src(k)[:, n * TG:(n + 1) * TG]),
                  writes=[OT] if k == 0 else [], accs=[OT] if k else [])
        for tt in range(TG // 128):
            X, XO = xt[ntl % 2], xo[ntl % 2]
            ntl += 1
            r0 = n * TG + tt * 128
            P.dma("sp", X.c, lambda e, X=X, r0=r0: e.dma_start(out=X[:], in_=x_dram[r0:r0 + 128, :]), writes=[X])
            for cb in range(4):
                pb = pp[nmm % 4]
                nmm += 1
                W_ = wo[cb]
                for k in range(KC):
                    P.op("pe", lambda e, OT=OT, W_=W_, k=k, tt=tt, pb=pb: e.matmul(
                        pb[:, :], lhsT=OT[:, k, tt * 128:(tt + 1) * 128], rhs=W_[:, k, :],
                        start=(k == 0), stop=(k == KC - 1)),
                        reads=[OT, W_], writes=[pb] if k == 0 else [], accs=[pb] if k else [], signal=(k == KC - 1))
                P.op("dve", lambda e, X=X, XO=XO, pb=pb, cb=cb: e.tensor_tensor(
                    out=XO[:, cb * 512:(cb + 1) * 512], in0=pb[:, :], in1=X[:, cb * 512:(cb + 1) * 512], op=ALU.add),
                    reads=[pb, X], writes=[XO] if cb == 0 else [], accs=[] if cb == 0 else [XO])
            P.dma("sp", XO.c, lambda e, XO=XO, r0=r0: e.dma_start(out=out_dram[r0:r0 + 128, :], in_=XO[:]), reads=[XO])


def mem_stage(P, K, NT, qsrc, gsrc, odst, mem_dram, memg_dram, wkv_dram, qg_dram, kg_dram, tag=""):
    TP = min(512, NT)
    npc = NT // TP
    SC = 128.0 ** -0.5
    prm = P.tile([128, 24], F32, f"mprm{tag}")
    gsb = P.tile([128, KC], F32, f"mgsb{tag}")
    P.dma("sp", prm.c, lambda e: e.dma_start(out=prm[:, 0:1], in_=qg_dram, allow_slow_non_contiguous=True), writes=[prm])
    P.dma("sp", prm.c, lambda e: e.dma_start(out=prm[:, 1:2], in_=kg_dram, allow_slow_non_contiguous=True), accs=[prm])
    P.dma("sp", gsb.c, lambda e: e.dma_start(out=gsb[:], in_=memg_dram), writes=[gsb])
    P.op("dve", lambda e: e.tensor_scalar(out=prm[:, 2:3], in0=prm[:, 1:2], scalar1=SC, scalar2=None, op0=ALU.mult),
         reads=[prm], accs=[prm])
    pp = [P.tile([128, 512], F32, f"mpp{tag}{i}", psum=True) for i in range(7)]
    hmT = P.tile([128, KC, 256], BF16, f"mhmT{tag}")
    norm_transpose(P, K, mem_dram, 2, gsb, hmT, pp[0:2], tag="m" + tag)
    wst = [P.tile([128, KC, 256], F32, f"mwst{tag}{i}") for i in range(2)]
    wkv = [P.tile([128, KC, 256], BF16, f"mwkv{tag}{i}") for i in range(4)]
    for cb in range(4):
        load_weight_bf16(P, wkv_dram, cb * 256, 256, wst[cb % 2], wkv[cb])
    mkf = P.tile([128, 2, 512], F32, f"mmkf{tag}")
    mvb = P.tile([128, 2, 512], BF16, f"mmvb{tag}")
    mkT = P.tile([128, 4, 256], BF16, f"mmkT{tag}")
    junk = P.tile([128, 128], F32, f"mjunk{tag}")
    npp = 2
    for mt in range(2):
        for half, dst in ((0, mkf), (1, mvb)):
            pb = pp[npp % 7]
            npp += 1
            for sub in range(2):
                W_ = wkv[half * 2 + sub]
                for k in range(KC):
                    P.op("pe", lambda e, pb=pb, sub=sub, W_=W_, k=k, mt=mt: e.matmul(
                        pb[:, sub * 256:(sub + 1) * 256], lhsT=hmT[:, k, mt * 128:(mt + 1) * 128], rhs=W_[:, k, :],
                        start=(k == 0), stop=(k == KC - 1)),
                        reads=[hmT, W_], writes=[pb] if (k == 0 and sub == 0) else [],
                        accs=[] if (k == 0 and sub == 0) else [pb], signal=(k == KC - 1 and sub == 1))
            P.op("act", lambda e, pb=pb, dst=dst, mt=mt: e.activation(out=dst[:, mt, :], in_=pb[:, :], func=AF.Copy),
                 reads=[pb], accs=[dst])
        for h in range(4):
            c0 = 4 + mt * 8 + h * 2
            P.op("act", lambda e, mt=mt, h=h, c0=c0: e.activation(out=junk[:], in_=mkf[:, mt, h * 128:(h + 1) * 128],
                                                                 func=AF.Square, accum_out=prm[:, c0:c0 + 1]),
                 reads=[mkf], writes=[junk], accs=[prm])
            P.op("act", lambda e, c0=c0: e.activation(out=prm[:, c0 + 1:c0 + 2], in_=prm[:, c0:c0 + 1], func=AF.Sqrt,
                                                     scale=1.0 / 128, bias=1e-6), reads=[prm], accs=[prm])
            P.op("dve", lambda e, c0=c0: e.reciprocal(out=prm[:, c0 + 1:c0 + 2], in_=prm[:, c0 + 1:c0 + 2]),
                 reads=[prm], accs=[prm])
            P.op("dve", lambda e, mt=mt, h=h, c0=c0: e.tensor_scalar(
                out=mkf[:, mt, h * 128:(h + 1) * 128], in0=mkf[:, mt, h * 128:(h + 1) * 128],
                scalar1=prm[:, c0 + 1:c0 + 2], scalar2=None, op0=ALU.mult), reads=[mkf, prm], accs=[mkf])
        pb = pp[npp % 7]
        npp += 1
        for h in range(4):
            P.op("pe", lambda e, pb=pb, mt=mt, h=h: e.transpose(out=pb[:, h * 128:(h + 1) * 128],
                                                               in_=mkf[:, mt, h * 128:(h + 1) * 128], identity=K.identf[:]),
                 reads=[mkf, K.identf], writes=[pb] if h == 0 else [], accs=[pb] if h else [], signal=(h == 3))
        P.op("dve", lambda e, pb=pb, mt=mt: e.tensor_scalar(
            out=mkT[:, :, mt * 128:(mt + 1) * 128], in0=pb[:, :].rearrange("p (h m) -> p h m", m=128),
            scalar1=prm[:, 2:3], scalar2=None, op0=ALU.mult), reads=[pb, prm], accs=[mkT])

    qx = [P.tile([128, TP], F32, f"mqx{tag}{i}") for i in range(2)]
    gx = [P.tile([128, TP], F32, f"mgx{tag}{i}") for i in range(2)]
    sq = P.tile([128, TP], F32, f"msq{tag}")
    rs = P.tile([128, TP], F32, f"mrs{tag}")
    qn = P.tile([128, TP], BF16, f"mqn{tag}")
    E = [P.tile([128, TP], BF16, f"mE{tag}{i}") for i in range(2)]
    dn = P.tile([128, TP], F32, f"mdn{tag}")
    yy = P.tile([128, TP], F32, f"myy{tag}")
    ob = [P.tile([128, TP], BF16, f"mob{tag}{i}") for i in range(2)]
    it = 0
    for pi in range(npc):
        t0 = pi * TP
        for h in range(4):
            QX, GX, O = qx[it % 2], gx[it % 2], ob[it % 2]
            it += 1
            P.dma("sp", QX.c, lambda e, QX=QX, h=h, t0=t0: e.dma_start(out=QX[:], in_=qsrc(h)[:, t0:t0 + TP]), writes=[QX])
            P.dma("sp", GX.c, lambda e, GX=GX, h=h, t0=t0: e.dma_start(out=GX[:], in_=gsrc(h)[:, t0:t0 + TP]), writes=[GX])
            P.op("dve", lambda e, QX=QX: e.tensor_tensor(out=sq[:], in0=QX[:], in1=QX[:], op=ALU.mult), reads=[QX], writes=[sq])
            pn, ps0, ps1, pnum, pden = pp[0], pp[1], pp[2], pp[3], pp[4]
            P.op("pe", lambda e, pn=pn: e.matmul(pn[:, 0:TP], lhsT=K.ones_f[:, :], rhs=sq[:], start=True, stop=True),
                 reads=[K.ones_f, sq], writes=[pn])
            P.op("act", lambda e, pn=pn: e.activation(out=rs[:], in_=pn[:, 0:TP], func=AF.Sqrt, scale=1.0 / 128, bias=1e-6),
                 reads=[pn], writes=[rs])
            P.op("dve", lambda e: e.reciprocal(out=rs[:], in_=rs[:]), reads=[rs], writes=[rs])
            P.op("dve", lambda e, QX=QX: e.scalar_tensor_tensor(out=qn[:], in0=QX[:], scalar=prm[:, 0:1], in1=rs[:],
                                                               op0=ALU.mult, op1=ALU.mult), reads=[QX, prm, rs], writes=[qn])
            for mt, psb in ((0, ps0), (1, ps1)):
                P.op("pe", lambda e, psb=psb, mt=mt, h=h: e.matmul(psb[:, 0:TP], lhsT=mkT[:, h, mt * 128:(mt + 1) * 128],
                                                                  rhs=qn[:], start=True, stop=True),
                     reads=[mkT, qn], writes=[psb])
                P.op("act", lambda e, psb=psb, mt=mt: e.activation(out=E[mt][:], in_=psb[:, 0:TP], func=AF.Exp),
                     reads=[psb], writes=[E[mt]])
            for mt in range(2):
                P.op("pe", lambda e, mt=mt, h=h, pnum=pnum: e.matmul(pnum[:, 0:TP], lhsT=mvb[:, mt, h * 128:(h + 1) * 128],
                                                                    rhs=E[mt][:], start=(mt == 0), stop=(mt == 1)),
                     reads=[mvb, E[mt]], writes=[pnum] if mt == 0 else [], accs=[pnum] if mt else [], signal=(mt == 1))
            for mt in range(2):
                P.op("pe", lambda e, mt=mt, pden=pden: e.matmul(pden[:, 0:TP], lhsT=K.ones_b[:, :], rhs=E[mt][:],
                                                               start=(mt == 0), stop=(mt == 1)),
                     reads=[K.ones_b, E[mt]], writes=[pden] if mt == 0 else [], accs=[pden] if mt else [],
                     signal=(mt == 1))
            P.op("dve", lambda e, pden=pden: e.reciprocal(out=dn[:], in_=pden[:, 0:TP]), reads=[pden], writes=[dn])
            P.op("dve", lambda e, pnum=pnum: e.tensor_tensor(out=yy[:], in0=pnum[:, 0:TP], in1=dn[:], op=ALU.mult),
                 reads=[pnum, dn], writes=[yy])
            P.op("act", lambda e, GX=GX: e.activation(out=GX[:], in_=GX[:], func=AF.Silu), reads=[GX], writes=[GX])
            P.op("pool", lambda e, GX=GX, O=O: e.tensor_tensor(out=O[:], in0=yy[:], in1=GX[:], op=ALU.mult),
                 reads=[yy, GX], writes=[O])
            P.dma("sp", O.c, lambda e, O=O, h=h, t0=t0: e.dma_start(out=odst(h)[:, t0:t0 + TP], in_=O[:]), reads=[O])


class RwConsts:
    def __init__(self, P, c_ui, c_sl, c_reset):
        self.ui = P.tile([128, 256], F32, "rw_ui", persistent=True)
        self.sl = P.tile([128, 128], F32, "rw_sl", persistent=True)
        self.reset = P.tile([64, 1024], F32, "rw_reset", persistent=True)
        P.dma("sp", self.ui.c, lambda e: e.dma_start(out=self.ui[:], in_=c_ui), writes=[self.ui])
        P.dma("sp", self.sl.c, lambda e: e.dma_start(out=self.sl[:], in_=c_sl), writes=[self.sl])
        P.dma("sp", self.reset.c, lambda e: e.dma_start(out=self.reset[:], in_=c_reset), writes=[self.reset])


class Bank:
    def __init__(self, P, name):
        self.t = P.tile([128, 512], F32, name, psum=True)
        self.q = [self.t.r] * 4

    def __getitem__(self, idx):
        return self.t[idx]


def rwkv_stage(P, K, RK, NTOK, rrows, krows, vrows, wdrows, adrows, grows, orows, prm_dram, wup_dram, aup_dram, tag="",
               do_chunk=True, max_it=6):
    TP = 1024
    CH = 128
    npc = NTOK // TP
    ncp = TP // CH
    f = F32
    prm = P.tile([64, 24], f, f"rprm{tag}")
    lup = P.tile([64, 128], f, f"rlup{tag}")
    P.dma("sp", prm.c, lambda e: e.dma_start(out=prm[:, 0:16], in_=prm_dram, allow_slow_non_contiguous=True), writes=[prm])
    P.op("pool", lambda e: e.memset(lup[:], 0.0), writes=[lup])
    P.dma("sp", lup.c, lambda e: e.dma_start(out=lup[0:32, 0:64], in_=wup_dram), accs=[lup])
    P.dma("sp", lup.c, lambda e: e.dma_start(out=lup[32:64, 64:128], in_=aup_dram), accs=[lup])
    P.op("dve", lambda e: e.tensor_scalar(out=prm[:, 16:20], in0=prm[:, 0:4], scalar1=-1.0, scalar2=1.0, op0=ALU.mult,
                                         op1=ALU.add), reads=[prm], accs=[prm])
    P.op("dve", lambda e: e.tensor_scalar(out=prm[:, 20:21], in0=prm[:, 7:8], scalar1=-1.0, scalar2=1.0, op0=ALU.mult,
                                         op1=ALU.add), reads=[prm], accs=[prm])
    xin = {nm: [P.tile([64, TP + 1], f, f"rx{nm}{tag}{i}") for i in range(2)] for nm in ("r", "k", "v", "l")}
    gin = [P.tile([64, TP], f, f"rg{tag}{i}") for i in range(2)]
    T = {nm: P.tile([64, TP], f, f"r_{nm}{tag}") for nm in
         ("R", "Kt", "V", "L", "tmp", "SG", "A", "LW", "cum", "KK", "Kp", "Bv", "e1", "e2",
          "bon", "Y", "t2")}
    for nm in ("Bt", "Ktl", "Bh", "Kh", "Vb"):
        T[nm] = P.tile([64, TP], BF16, f"r_{nm}{tag}")
    AR = P.tile([64, ncp, 256], BF16, f"r_AR{tag}")
    ARf = P.tile([64, ncp, 128], f, f"r_ARf{tag}")
    ob = [P.tile([64, TP], BF16, f"rob{tag}{i}") for i in range(2)]
    Sb = [P.tile([64, 64], f, f"rS{tag}{i}") for i in range(4)]
    G = 3
    banks = [(Bank(P, f"rbA{tag}{g}"), Bank(P, f"rbB{tag}{g}")) for g in range(G)]
    ctxs = []
    for par in range(2):
        row = []
        for g in range(G):
            c = dict(bA=banks[g][0], bB=banks[g][1], bC=banks[g][1])
            for nm, shp in (("Gm1", [128, 256]), ("Gm2", [128, 256]), ("P0", [128, 128]), ("P1", [128, 128]),
                            ("PT0", [128, 128]), ("PT1", [128, 128]), ("T", [128, 128]), ("TM", [128, 320]),
                            ("AU", [128, 128]), ("Mt", [64, 64]), ("Qt", [64, 128])):
                c[nm] = P.tile(shp, f if nm in ("Mt", "Qt") else BF16, f"rc{nm}{tag}{par}{g}")
            row.append(c)
        ctxs.append(row)
    ngrp = 0
    bY = Bank(P, f"rbY{tag}")
    bS = Bank(P, f"rbS{tag}")
    bN = [banks[0][1], banks[1][1]]
    nbn = 0
    ones64 = K.ones_f[0:64, 0:64]
    id64 = K.identf[0:64, 0:64]
    P.op("dve", lambda e: e.memset(Sb[0][:], 0.0), writes=[Sb[0]])
    sidx = 0

    def v3(t):
        return t[:, :].rearrange("p (c t) -> p c t", t=CH)

    def ones_mm(src_ap_fn, nsub, consume):
        nonlocal nbn
        for sb_ in range(nsub):
            bk = bN[nbn % 2]
            nbn += 1
            rd = src_ap_fn(sb_)
            P.op("pe", lambda e, bk=bk, rd=rd: e.matmul(bk[0:64, :], lhsT=ones64, rhs=rd[0], start=True, stop=True),
                 reads=[K.ones_f] + rd[1], writes=bk.q)
            consume(sb_, bk)

    def serial_phase(grp, ctx):
        nonlocal sidx
        for g, c in enumerate(grp):
            C = ctx[g]
            S0 = Sb[sidx % 4]
            S1 = Sb[(sidx + 1) % 4]
            sidx += 1
            ycol = (c % 4) * 128
            yq = bY.q[c % 4]
            P.op("pe", lambda e, S0=S0, C=C, ycol=ycol: e.matmul(bY[0:64, ycol:ycol + 128], lhsT=S0[:], rhs=C["Qt"][:],
                                                                start=True, stop=False),
                 reads=[S0, C["Qt"]], writes=[yq], signal=False)
            P.op("pe", lambda e, C=C, ycol=ycol: e.matmul(bY[0:64, ycol:ycol + 128], lhsT=C["AU"][:, 64:128],
                                                         rhs=C["Gm1"][:, 128:256], start=False, stop=False),
                 reads=[C["AU"], C["Gm1"]], accs=[yq], signal=False)
            P.op("pe", lambda e, C=C, ycol=ycol: e.matmul(bY[0:64, ycol:ycol + 128], lhsT=C["TM"][:, 128:192],
                                                         rhs=C["Gm2"][:, 128:256], start=False, stop=True),
                 reads=[C["TM"], C["Gm2"]], accs=[yq], signal=False)
            P.op("pe", lambda e, C=C: e.matmul(bS[0:64, 0:64], lhsT=C["TM"][:, 192:256], rhs=C["AU"][:, 64:128],
                                               start=True, stop=False), reads=[C["TM"], C["AU"]], writes=[bS.q[0]], signal=False)
            P.op("pe", lambda e, C=C: e.matmul(bS[0:64, 0:64], lhsT=C["TM"][:, 256:320], rhs=C["TM"][:, 128:192],
                                               start=False, stop=False), reads=[C["TM"]], accs=[bS.q[0]], signal=False)
            P.op("pe", lambda e, C=C, S0=S0: e.matmul(bS[0:64, 0:64], lhsT=C["Mt"][:], rhs=S0[:], start=False, stop=True),
                 reads=[C["Mt"], S0], accs=[bS.q[0]])
            P.op("act", lambda e, S1=S1: e.activation(out=S1[:], in_=bS[0:64, 0:64], func=AF.Copy),
                 reads=[bS.q[0]], writes=[S1])
            if c % 4 == 3:
                y0 = (c - 3) * CH
                P.op("act", lambda e, y0=y0: e.activation(out=T["Y"][:, y0:y0 + 512], in_=bY[0:64, :], func=AF.Copy),
                     reads=bY.q, writes=[T["Y"]] if c == 3 else [], accs=[] if c == 3 else [T["Y"]])

    for pi in range(npc):
        s = pi % 2
        t0 = pi * TP
        XR, XK, XV, XL, GX, O = xin["r"][s], xin["k"][s], xin["v"][s], xin["l"][s], gin[s], ob[s]
        for X, rows_list in ((XR, [(rrows, 0, 64)]), (XK, [(krows, 0, 64)]), (XV, [(vrows, 0, 64)]),
                             (XL, [(wdrows, 0, 32), (adrows, 32, 64)])):
            first = True
            if pi == 0:
                P.op("pool", lambda e, X=X: e.memset(X[:, 0:1], 0.0), writes=[X])
                first = False
            for (src, p0, p1) in rows_list:
                if pi == 0:
                    P.dma("sp", X.c, lambda e, X=X, src=src, p0=p0, p1=p1: e.dma_start(out=X[p0:p1, 1:TP + 1], in_=src[:, 0:TP]),
                          accs=[X])
                else:
                    P.dma("sp", X.c, lambda e, X=X, src=src, p0=p0, p1=p1, t0=t0: e.dma_start(
                        out=X[p0:p1, :], in_=src[:, t0 - 1:t0 + TP]), writes=[X] if first else [], accs=[] if first else [X])
                first = False
        P.dma("sp", GX.c, lambda e, GX=GX, t0=t0: e.dma_start(out=GX[:], in_=grows[:, t0:t0 + TP]), writes=[GX])
        for X, dst, col in ((XR, T["R"], 0), (XK, T["Kt"], 1), (XV, T["V"], 2), (XL, T["L"], 3)):
            P.op("act", lambda e, X=X, col=col: e.activation(out=T["tmp"][:], in_=X[:, 0:TP], func=AF.Copy,
                                                            scale=prm[:, col:col + 1]), reads=[X, prm], writes=[T["tmp"]])
            P.op("dve", lambda e, X=X, dst=dst, col=col: e.scalar_tensor_tensor(
                out=dst[:], in0=X[:, 1:TP + 1], scalar=prm[:, 16 + col:17 + col], in1=T["tmp"][:], op0=ALU.mult, op1=ALU.add),
                reads=[X, prm, T["tmp"]], writes=[dst])
        P.op("act", lambda e: e.activation(out=T["L"][0:32, :], in_=T["L"][0:32, :], func=AF.Tanh), reads=[T["L"]], writes=[T["L"]])
        for (lo, hi, bcol, dst) in ((0, 32, 4, T["SG"]), (32, 64, 5, T["A"])):
            for sb_ in range(TP // 512):
                bk = bN[nbn % 2]
                nbn += 1
                P.op("pe", lambda e, bk=bk, lo=lo, hi=hi, sb_=sb_: e.matmul(bk[0:64, :], lhsT=lup[:, 2 * lo:2 * lo + 64],
                                                                           rhs=T["L"][:, sb_ * 512:(sb_ + 1) * 512],
                                                                           start=True, stop=True),
                     reads=[lup, T["L"]], writes=bk.q)
                P.op("act", lambda e, bk=bk, dst=dst, sb_=sb_, bcol=bcol: e.activation(
                    out=dst[:, sb_ * 512:(sb_ + 1) * 512], in_=bk[0:64, :], func=AF.Sigmoid, bias=prm[:, bcol:bcol + 1]),
                    reads=bk.q + [prm], writes=[dst] if sb_ == 0 else [], accs=[] if sb_ == 0 else [dst])
        P.op("dve", lambda e: e.tensor_scalar(out=T["LW"][:], in0=T["SG"][:], scalar1=-0.6065306597126334, scalar2=None,
                                             op0=ALU.mult), reads=[T["SG"]], writes=[T["LW"]])
        P.op("dve", lambda e: e.tensor_tensor_scan(out=T["cum"][:], data0=RK.reset[:, 0:TP], data1=T["LW"][:], initial=0.0,
                                                  op0=ALU.mult, op1=ALU.add), reads=[RK.reset, T["LW"]], writes=[T["cum"]])
        P.op("dve", lambda e: e.tensor_tensor(out=T["tmp"][:], in0=T["cum"][:], in1=T["LW"][:], op=ALU.subtract),
             reads=[T["cum"], T["LW"]], writes=[T["tmp"]])
        P.op("act", lambda e: e.activation(out=T["e1"][:], in_=T["tmp"][:], func=AF.Exp), reads=[T["tmp"]], writes=[T["e1"]])
        P.op("act", lambda e: e.activation(out=T["KK"][:], in_=T["Kt"][:], func=AF.Copy, scale=prm[:, 6:7]),
             reads=[T["Kt"], prm], writes=[T["KK"]])
        P.op("act", lambda e: e.activation(out=T["tmp"][:], in_=T["KK"][:], func=AF.Square),
             reads=[T["KK"]], writes=[T["tmp"]])

        def cons_kk(sb_, bk):
            P.op("act", lambda e, bk=bk, sb_=sb_: e.activation(out=T["t2"][:, sb_ * 512:(sb_ + 1) * 512], in_=bk[0:64, :],
                                                              func=AF.Ln, bias=1e-24), reads=bk.q,
                 writes=[T["t2"]] if sb_ == 0 else [], accs=[] if sb_ == 0 else [T["t2"]])
        ones_mm(lambda sb_: (T["tmp"][:, sb_ * 512:(sb_ + 1) * 512], [T["tmp"]]), TP // 512, cons_kk)
        P.op("act", lambda e: e.activation(out=T["t2"][:], in_=T["t2"][:], func=AF.Exp, scale=-0.5),
             reads=[T["t2"]], writes=[T["t2"]])
        P.op("dve", lambda e: e.tensor_tensor(out=T["KK"][:], in0=T["KK"][:], in1=T["t2"][:], op=ALU.mult),
             reads=[T["KK"], T["t2"]], writes=[T["KK"]])
        P.op("dve", lambda e: e.scalar_tensor_tensor(out=AR[:, :, 0:128], in0=v3(T["KK"]), scalar=-1.0, in1=v3(T["e1"]),
                                                    op0=ALU.mult, op1=ALU.mult), reads=[T["KK"], T["e1"]], writes=[AR])
        P.op("act", lambda e: e.activation(out=T["tmp"][:], in_=T["A"][:], func=AF.Identity, scale=prm[:, 7:8],
                                           bias=prm[:, 20:21]), reads=[T["A"], prm], writes=[T["tmp"]])
        P.op("dve", lambda e: e.tensor_tensor(out=T["Kp"][:], in0=T["Kt"][:], in1=T["tmp"][:], op=ALU.mult),
             reads=[T["Kt"], T["tmp"]], writes=[T["Kp"]])
        P.op("dve", lambda e: e.tensor_tensor(out=T["Bv"][:], in0=T["KK"][:], in1=T["A"][:], op=ALU.mult),
             reads=[T["KK"], T["A"]], writes=[T["Bv"]])
        P.op("act", lambda e: e.activation(out=T["e1"][:], in_=T["cum"][:], func=AF.Exp), reads=[T["cum"]], writes=[T["e1"]])
        P.op("act", lambda e: e.activation(out=T["e2"][:], in_=T["cum"][:], func=AF.Exp, scale=-1.0), reads=[T["cum"]],
             writes=[T["e2"]])
        P.op("dve", lambda e: e.tensor_tensor(out=ARf[:, :, :], in0=v3(T["R"]), in1=v3(T["e1"]), op=ALU.mult),
             reads=[T["R"], T["e1"]], writes=[ARf])
        P.op("act", lambda e: e.activation(out=AR[:, :, 128:256], in_=ARf[:, :, :], func=AF.Copy), reads=[ARf], accs=[AR])
        P.op("act", lambda e: e.activation(out=T["Vb"][:], in_=T["V"][:], func=AF.Copy), reads=[T["V"]], writes=[T["Vb"]])
        P.op("dve", lambda e: e.tensor_tensor(out=T["Bt"][:], in0=T["Bv"][:], in1=T["e2"][:], op=ALU.mult),
             reads=[T["Bv"], T["e2"]], writes=[T["Bt"]])
        P.op("dve", lambda e: e.tensor_tensor(out=T["Ktl"][:], in0=T["Kp"][:], in1=T["e2"][:], op=ALU.mult),
             reads=[T["Kp"], T["e2"]], writes=[T["Ktl"]])
        for c in range(ncp):
            cs = slice(c * CH, (c + 1) * CH)
            ge = slice(c * CH + CH - 1, c * CH + CH)
            P.op("dve", lambda e, cs=cs, ge=ge: e.tensor_scalar(out=T["Bh"][:, cs], in0=T["Bt"][:, cs], scalar1=T["e1"][:, ge],
                                                               scalar2=None, op0=ALU.mult),
                 reads=[T["Bt"], T["e1"]], writes=[T["Bh"]] if c == 0 else [], accs=[] if c == 0 else [T["Bh"]])
            P.op("act", lambda e, cs=cs, ge=ge: e.activation(out=T["Kh"][:, cs], in_=T["Ktl"][:, cs], func=AF.Copy,
                                                            scale=T["e1"][:, ge]),
                 reads=[T["Ktl"], T["e1"]], writes=[T["Kh"]] if c == 0 else [], accs=[] if c == 0 else [T["Kh"]])
        P.op("dve", lambda e: e.scalar_tensor_tensor(out=T["tmp"][:], in0=T["R"][:], scalar=prm[:, 8:9], in1=T["Kp"][:],
                                                     op0=ALU.mult, op1=ALU.mult), reads=[T["R"], prm, T["Kp"]], writes=[T["tmp"]])

        def cons_bon(sb_, bk):
            P.op("dve", lambda e, bk=bk, sb_=sb_: e.tensor_tensor(out=T["bon"][:, sb_ * 512:(sb_ + 1) * 512], in0=bk[0:64, :],
                                                                 in1=T["V"][:, sb_ * 512:(sb_ + 1) * 512], op=ALU.mult),
                 reads=bk.q + [T["V"]], writes=[T["bon"]] if sb_ == 0 else [], accs=[] if sb_ == 0 else [T["bon"]])
        ones_mm(lambda sb_: (T["tmp"][:, sb_ * 512:(sb_ + 1) * 512], [T["tmp"]]), TP // 512, cons_bon)

        if not do_chunk:
            P.op("dve", lambda e: e.memset(T["Y"][:], 0.0), writes=[T["Y"]])
        groups = [list(range(i, min(i + G, ncp))) for i in range(0, ncp, G)] if do_chunk else []
        pending_serial = None
        for grp in groups:
            steps = []
            ctx = ctxs[ngrp % 2]
            ngrp += 1
            for g, c in enumerate(grp):
                C = ctx[g]
                bA, bB = C["bA"], C["bB"]
                cs = slice(c * CH, (c + 1) * CH)
                st = []

                def sA(C=C, bA=bA, bB=bB, c=c, cs=cs):
                    P.op("pe", lambda e: e.matmul(bA[:, 0:256], lhsT=T["Bt"][:, cs], rhs=AR[:, c, :], start=True, stop=True),
                         reads=[T["Bt"], AR], writes=bA.q, signal=False)
                    P.op("pe", lambda e: e.matmul(bA[:, 256:512], lhsT=T["Ktl"][:, cs], rhs=AR[:, c, :], start=True, stop=True),
                         reads=[T["Ktl"], AR], accs=bA.q, signal=False)
                    P.op("pe", lambda e: e.matmul(bB[:, 0:128], lhsT=AR[:, c, 0:128], rhs=T["Bt"][:, cs], start=True, stop=True),
                         reads=[T["Bt"], AR], writes=[bB.q[0]], signal=False)
                    for i4, src in enumerate((AR[:, c, 0:128], T["Vb"][:, cs], T["Bh"][:, cs], T["Kh"][:, cs])):
                        P.op("pe", lambda e, src=src, i4=i4: e.matmul(bB[:, 128 + i4 * 64:192 + i4 * 64], lhsT=src,
                                                                     rhs=K.identb[0:64, 0:64], start=True, stop=True),
                             reads=[AR, T["Vb"], T["Bh"], T["Kh"], K.identb], writes=[bB.q[1], bB.q[2]] if i4 == 0 else [],
                             accs=[] if i4 == 0 else [bB.q[1], bB.q[2]], signal=(i4 == 3))
                st.append(sA)

                def sAe(C=C, bA=bA, bB=bB):
                    P.op("dve", lambda e: e.tensor_tensor(out=C["Gm1"][:], in0=bA[:, 0:256], in1=RK.ui[:], op=ALU.mult),
                         reads=[bA.q[0], bA.q[1], RK.ui], writes=[C["Gm1"]])
                    P.op("dve", lambda e: e.tensor_tensor(out=C["Gm2"][:], in0=bA[:, 256:512], in1=RK.ui[:], op=ALU.mult),
                         reads=[bA.q[2], bA.q[3], RK.ui], writes=[C["Gm2"]])
                    P.op("dve", lambda e: e.tensor_tensor(out=C["PT0"][:], in0=bB[:, 0:128], in1=RK.sl[:], op=ALU.mult),
                         reads=[bB.q[0], RK.sl], writes=[C["PT0"]])
                    P.op("pool", lambda e: e.tensor_tensor(out=C["T"][:], in0=C["Gm1"][:, 0:128], in1=K.identb[:], op=ALU.add),
                         reads=[C["Gm1"], K.identb], writes=[C["T"]])
                    P.op("act", lambda e: e.activation(out=C["TM"][:, 0:64], in_=bB[:, 128:192], func=AF.Copy),
                         reads=[bB.q[1]], writes=[C["TM"]])
                    P.op("act", lambda e: e.activation(out=C["TM"][:, 128:320], in_=bB[:, 192:384], func=AF.Copy),
                         reads=[bB.q[1], bB.q[2]], accs=[C["TM"]])
                st.append(sAe)
                for it in range(6):
                    last = (it == 5)

                    def sI1(C=C, bA=bA, it=it, last=last):
                        Pc = C["Gm1"] if it == 0 else C[f"P{it % 2}"]
                        Pc_ap = C["Gm1"][:, 0:128] if it == 0 else C[f"P{it % 2}"][:]
                        PTc = C["PT0"] if it == 0 else C[f"PT{it % 2}"]
                        if not last:
                            P.op("pe", lambda e: e.matmul(bA[:, 0:128], lhsT=PTc[:], rhs=Pc_ap, start=True, stop=True),
                                 reads=[PTc, Pc], writes=[bA.q[0]], signal=False)
                        P.op("pe", lambda e: e.matmul(bA[:, 128:256], lhsT=Pc_ap, rhs=PTc[:], start=True, stop=True),
                             reads=[PTc, Pc], writes=[bA.q[1]])
                    st.append(sI1)

                    def sI2(C=C, bA=bA, it=it, last=last):
                        Pn = C[f"P{(it + 1) % 2}"]
                        PTn = C[f"PT{(it + 1) % 2}"]
                        if it == 0:
                            PTn = C["PT1"]
                        P.op("act", lambda e: e.activation(out=PTn[:], in_=bA[:, 128:256], func=AF.Copy),
                             reads=[bA.q[1]], writes=[PTn])
                        if not last:
                            P.op("act", lambda e: e.activation(out=Pn[:], in_=bA[:, 0:128], func=AF.Copy),
                                 reads=[bA.q[0]], writes=[Pn])
                    st.append(sI2)

                    def sI3(C=C, bC=C["bC"], it=it):
                        PTn = C[f"PT{(it + 1) % 2}"]
                        if it == 0:
                            PTn = C["PT1"]
                        P.op("pe", lambda e: e.matmul(bC[:, 0:128], lhsT=PTn[:], rhs=C["T"][:], start=True, stop=True),
                             reads=[PTn, C["T"]], writes=[bC.q[0]])
                    st.append(sI3)

                    def sI4(C=C, bC=C["bC"]):
                        P.op("dve", lambda e: e.tensor_tensor(out=C["T"][:], in0=bC[:, 0:128], in1=C["T"][:], op=ALU.add),
                             reads=[bC.q[0], C["T"]], writes=[C["T"]])
                    st.append(sI4)

                def sW(C=C, bB=bB):
                    P.op("pe", lambda e: e.matmul(bB[:, 384:448], lhsT=C["Gm2"][:, 0:128], rhs=C["TM"][:, 128:192],
                                                  start=True, stop=True), reads=[C["Gm2"], C["TM"]], writes=[bB.q[3]])
                st.append(sW)

                def sWe(C=C, bB=bB):
                    P.op("act", lambda e: e.activation(out=C["TM"][:, 64:128], in_=bB[:, 384:448], func=AF.Copy),
                         reads=[bB.q[3]], accs=[C["TM"]])
                st.append(sWe)

                def sAU(C=C, bA=bA):
                    P.op("pe", lambda e: e.matmul(bA[:, 384:512], lhsT=C["T"][:], rhs=C["TM"][:, 0:128], start=True, stop=True),
                         reads=[C["T"], C["TM"]], writes=[bA.q[3]])
                st.append(sAU)

                def sAUe(C=C, bA=bA):
                    P.op("act", lambda e: e.activation(out=C["AU"][:], in_=bA[:, 384:512], func=AF.Copy),
                         reads=[bA.q[3]], writes=[C["AU"]])
                st.append(sAUe)

                def sMQ(C=C, bB=bB):
                    P.op("pe", lambda e: e.matmul(bB[0:64, 448:512], lhsT=C["AU"][:, 0:64], rhs=C["TM"][:, 192:256],
                                                  start=True, stop=True), reads=[C["AU"], C["TM"]], writes=[bB.q[3]], signal=False)
                    P.op("pe", lambda e: e.matmul(bB[0:64, 0:128], lhsT=C["AU"][:, 0:64], rhs=C["Gm1"][:, 128:256],
                                                  start=True, stop=True), reads=[C["AU"], C["Gm1"]], writes=[bB.q[0]])
                st.append(sMQ)

                def sMQe(C=C, bB=bB, c=c, cs=cs):
                    ge = slice(c * CH + CH - 1, c * CH + CH)
                    P.op("dve", lambda e: e.scalar_tensor_tensor(out=C["Mt"][:], in0=id64, scalar=T["e1"][:, ge],
                                                                in1=bB[0:64, 448:512], op0=ALU.mult, op1=ALU.add),
                         reads=[K.identf, T["e1"], bB.q[3]], writes=[C["Mt"]])
                    P.op("dve", lambda e: e.tensor_tensor(out=C["Qt"][:], in0=bB[0:64, 0:128], in1=ARf[:, c, :], op=ALU.add),
                         reads=[bB.q[0], ARf], writes=[C["Qt"]])
                st.append(sMQe)
                steps.append(st)
            for si in range(len(steps[0])):
                for g in range(len(grp)):
                    steps[g][si]()
            if pending_serial is not None:
                pending_serial()
            pending_serial = (lambda grp=grp, ctx=ctx: serial_phase(grp, ctx))
        if pending_serial is not None:
            pending_serial()
            pending_serial = None
        def cons_mean(sb_, bk):
            ss = slice(sb_ * 512, (sb_ + 1) * 512)
            P.op("dve", lambda e, bk=bk, ss=ss: e.scalar_tensor_tensor(out=T["tmp"][:, ss], in0=bk[0:64, :], scalar=-1.0 / 64,
                                                                      in1=T["Y"][:, ss], op0=ALU.mult, op1=ALU.add),
                 reads=bk.q + [T["Y"]], writes=[T["tmp"]] if sb_ == 0 else [], accs=[] if sb_ == 0 else [T["tmp"]])
        ones_mm(lambda sb_: (T["Y"][:, sb_ * 512:(sb_ + 1) * 512], [T["Y"]]), TP // 512, cons_mean)
        P.op("act", lambda e: e.activation(out=T["t2"][:], in_=T["tmp"][:], func=AF.Square),
             reads=[T["tmp"]], writes=[T["t2"]])

        def cons_var(sb_, bk):
            ss = slice(sb_ * 512, (sb_ + 1) * 512)
            P.op("act", lambda e, bk=bk, ss=ss: e.activation(out=T["e2"][:, ss], in_=bk[0:64, :], func=AF.Ln, scale=1.0 / 64,
                                                            bias=64e-5), reads=bk.q,
                 writes=[T["e2"]] if sb_ == 0 else [], accs=[] if sb_ == 0 else [T["e2"]])
        ones_mm(lambda sb_: (T["t2"][:, sb_ * 512:(sb_ + 1) * 512], [T["t2"]]), TP // 512, cons_var)
        P.op("act", lambda e: e.activation(out=T["e2"][:], in_=T["e2"][:], func=AF.Exp, scale=-0.5), reads=[T["e2"]], writes=[T["e2"]])
        P.op("dve", lambda e: e.tensor_tensor(out=T["tmp"][:], in0=T["tmp"][:], in1=T["e2"][:], op=ALU.mult),
             reads=[T["tmp"], T["e2"]], writes=[T["tmp"]])
        P.op("dve", lambda e: e.tensor_scalar(out=T["tmp"][:], in0=T["tmp"][:], scalar1=prm[:, 9:10], scalar2=prm[:, 10:11],
                                             op0=ALU.mult, op1=ALU.add), reads=[T["tmp"], prm], writes=[T["tmp"]])
        P.op("dve", lambda e: e.tensor_tensor(out=T["tmp"][:], in0=T["tmp"][:], in1=T["bon"][:], op=ALU.add),
             reads=[T["tmp"], T["bon"]], writes=[T["tmp"]])
        P.op("act", lambda e, GX=GX: e.activation(out=GX[:], in_=GX[:], func=AF.Silu), reads=[GX], writes=[GX])
        P.op("dve", lambda e, GX=GX, O=O: e.tensor_tensor(out=O[:], in0=T["tmp"][:], in1=GX[:], op=ALU.mult),
             reads=[T["tmp"], GX], writes=[O])
        P.dma("sp", O.c, lambda e, O=O, t0=t0: e.dma_start(out=orows[:, t0:t0 + TP], in_=O[:]), reads=[O])


import ml_dtypes

NCORES = 8
SEQ = 16384
NT = SEQ // NCORES
NCH_SEQ = 69


def _consts_np():
    s_ = np.arange(128)[:, None]
    t_ = np.arange(128)[None, :]
    reset = np.ones((64, 1024), np.float32)
    reset[:, ::128] = 0
    return dict(
        c_if=np.eye(128, dtype=np.float32),
        c_ib=np.eye(128).astype(ml_dtypes.bfloat16),
        c_mask=np.concatenate([(s_ <= t_), (s_ > t_)], axis=1).astype(ml_dtypes.bfloat16),
        c_ui=np.concatenate([(t_ > s_), (t_ >= s_)], axis=1).astype(np.float32),
        c_sl=(s_ > t_).astype(np.float32),
        c_reset=reset,
    )


def _din(nc, name, shape, dt=F32):
    return nc.dram_tensor(name, list(shape), dt, kind="ExternalInput").ap()


def _dout(nc, name, shape, dt=F32):
    return nc.dram_tensor(name, list(shape), dt, kind="ExternalOutput").ap()


def _mk_consts(nc, P, rw=False):
    K = Consts(P, _din(nc, "c_if", [128, 128]), _din(nc, "c_ib", [128, 128], BF16), _din(nc, "c_mask", [128, 256], BF16))
    RK = None
    if rw:
        RK = RwConsts(P, _din(nc, "c_ui", [128, 256]), _din(nc, "c_sl", [128, 128]), _din(nc, "c_reset", [64, 1024]))
    return K, RK


def build_tok(with_out, with_proj):
    nc = bass.Bass("TRN2", target_bir_lowering=False)
    P = Prog(nc)
    K, _ = _mk_consts(nc, P)
    x = _din(nc, "x", [NT, 2048])
    xcur = x
    if with_out:
        oT = _din(nc, "oT", [2048, NT], BF16)
        wo = _din(nc, "wo", [2048, 2048])
        xout = _dout(nc, "xout", [NT, 2048])
        out_stage(P, K, NT, x, lambda k: oT[128 * k:128 * k + 128, :], wo, xout)
        P.end_stage()
        xcur = xout
    if with_proj:
        g = _din(nc, "g", [128, 16])
        w = _din(nc, "w", [2048, 5440])
        pT = _dout(nc, "pT", [NCH_SEQ * 64, NT])
        pmem = nc.dram_tensor("pmem", [1024, NT], F32).ap()
        omem = _dout(nc, "omem", [512, NT], BF16)

        def dst64(i):
            if i < NCH_SEQ:
                return pT[64 * i:64 * i + 64, :]
            j = i - NCH_SEQ
            return pmem[64 * j:64 * j + 64, :]
        proj_stage(P, K, NT, xcur, g, w, 5440, dst64)
        P.end_stage()
        mem_stage(P, K, NT, lambda h: pmem[128 * h:128 * h + 128, :], lambda h: pmem[512 + 128 * h:512 + 128 * h + 128, :],
                  lambda h: omem[128 * h:128 * h + 128, :], _din(nc, "mem", [256, 2048]), _din(nc, "memg", [128, 16]),
                  _din(nc, "wkv", [2048, 1024]), _din(nc, "mqg", [128, 1]), _din(nc, "mkg", [128, 1]))
        P.end_stage()
    P.close()
    return nc


def build_head():
    nc = bass.Bass("TRN2", target_bir_lowering=False)
    P = Prog(nc)
    K, RK = _mk_consts(nc, P, rw=True)
    pin = _din(nc, "pin", [11 * 64, SEQ])
    sm = _din(nc, "sm", [64, 32])
    lw = _din(nc, "lw", [2, 64, 64])
    lup = _din(nc, "lup", [64, 64])
    oR = _dout(nc, "oR", [192, SEQ], BF16)

    def rows(i, lo=0, hi=64):
        return pin[64 * i + lo:64 * i + hi, :]
    lru_stage(P, 64, SEQ, rows(0), rows(1), oR[0:64, :], sm[:, 0:4], sm[:, 4:5], lw[0:1], sm[:, 5:6], lw[1:2], sm[:, 6:7],
              sm[:, 7:8])
    P.end_stage()
    rwkv_stage(P, K, RK, SEQ, rows(2), rows(3), rows(4), rows(5, 0, 32), rows(5, 32, 64), rows(6), oR[64:128, :],
               sm[:, 8:24], lup[0:32, :], lup[32:64, :])
    P.end_stage()
    swa_stage(P, K, SEQ, rows(7), rows(8), rows(9), rows(10), oR[128:192, :], sm[:, 24:25], sm[:, 25:26], sm[:, 26:27])
    P.end_stage()
    P.close()
    return nc


def _g16(v):
    return np.ascontiguousarray(np.asarray(v, np.float32).reshape(16, 128).T)


def _head_small(inp, l, h):
    hs = slice(64 * h, 64 * h + 64)
    sm = np.zeros((64, 32), np.float32)
    sm[:, 0:4] = inp["conv_w"][l][:, hs].T
    sm[:, 4] = inp["conv_b"][l][hs]
    sm[:, 5] = inp["lru_ba"][l][hs]
    sm[:, 6] = inp["lru_bx"][l][hs]
    sm[:, 7] = inp["lru_lambda"][l][hs]
    mu = inp["rw_mu"][l]
    sm[:, 8] = mu[0:512][hs]
    sm[:, 9] = mu[512:1024][hs]
    sm[:, 10] = mu[1024:1536][hs]
    sm[:, 11] = mu[1536:1600]
    sm[:, 12] = inp["rw_w0"][l][hs]
    sm[:, 13] = inp["rw_a0"][l][hs]
    sm[:, 14] = inp["rw_k_k"][l][hs]
    sm[:, 15] = inp["rw_k_a"][l][hs]
    sm[:, 16] = inp["rw_r_k"][l][h]
    sm[:, 17] = inp["rw_gn_g"][l][hs]
    sm[:, 18] = inp["rw_gn_b"][l][hs]
    sm[:, 24] = inp["swa_q_g"][l]
    sm[:, 25] = inp["swa_k_g"][l]
    sm[:, 26] = inp["swa_sinks"][l][h]
    lw = np.stack([inp["lru_wa"][l][h], inp["lru_wx"][l][h]]).astype(np.float32)
    lup = np.concatenate([inp["rw_w_up"][l][:, hs], inp["rw_a_up"][l][:, hs]], axis=0).astype(np.float32)
    return sm, lw, np.ascontiguousarray(lup)


TPF = 2048


def build_fused(seq=SEQ, depth=2):
    nc = bass.Bass("TRN2", target_bir_lowering=False)
    P = Prog(nc)
    K, RK = _mk_consts(nc, P, rw=True)
    x = _din(nc, "x", [seq, 2048])
    out = _dout(nc, "out", [seq, 2048])
    mem = _din(nc, "mem", [256, 2048])
    norm_g = _din(nc, "norm_g", [depth, 128, 16])
    w_in = _din(nc, "w_in", [depth, 2048, 5440])
    memg = _din(nc, "memg", [depth, 128, 16])
    wkv = _din(nc, "wkv", [depth, 2048, 1024])
    mqk = _din(nc, "mqk", [depth, 128, 2])
    w_out = _din(nc, "w_out", [depth, 2048, 2048])
    lru_sm = _din(nc, "lru_sm", [depth, 512, 8])
    lru_w = _din(nc, "lru_w", [depth, 2, 8, 64, 64])
    rw_sm = _din(nc, "rw_sm", [depth, 8, 64, 16])
    rw_lup = _din(nc, "rw_lup", [depth, 8, 64, 64])
    swa_sm = _din(nc, "swa_sm", [depth, 8, 64, 4])
    x1 = nc.dram_tensor("x1_scr", [seq, 2048], F32).ap()
    p_lru = nc.dram_tensor("p_lru", [1024, seq], F32).ap()
    p_rw = nc.dram_tensor("p_rw", [2112, seq], F32).ap()
    p_swa = nc.dram_tensor("p_swa", [1280, seq], F32).ap()
    p_mem = nc.dram_tensor("p_mem", [1024, seq], F32).ap()
    oT = nc.dram_tensor("oT_scr", [2048, seq], BF16).ap()
    for l in range(depth):
        xin = x if l == 0 else x1
        xout = out if l == depth - 1 else x1
        for tp in range(seq // TPF):
            ts_ = slice(tp * TPF, (tp + 1) * TPF)

            def dst64(i, ts_=ts_):
                if i < 16:
                    return p_lru[64 * i:64 * i + 64, ts_]
                if i < 49:
                    return p_rw[64 * (i - 16):64 * (i - 16) + 64, ts_]
                if i < 69:
                    return p_swa[64 * (i - 49):64 * (i - 49) + 64, ts_]
                return p_mem[64 * (i - 69):64 * (i - 69) + 64, ts_]
            proj_stage(P, K, TPF, xin[ts_, :], norm_g[l], w_in[l], 5440, dst64)
            P.end_stage()
        mem_stage(P, K, seq, lambda h: p_mem[128 * h:128 * h + 128, :], lambda h: p_mem[512 + 128 * h:512 + 128 * h + 128, :],
                  lambda h: oT[1536 + 128 * h:1536 + 128 * h + 128, :], mem, memg[l], wkv[l], mqk[l][:, 0:1], mqk[l][:, 1:2])
        P.end_stage()
        for ct in range(4):
            cs = slice(128 * ct, 128 * ct + 128)
            lru_stage(P, 128, seq, p_lru[cs, :], p_lru[512 + 128 * ct:512 + 128 * ct + 128, :], oT[cs, :],
                      lru_sm[l][cs, 0:4], lru_sm[l][cs, 4:5], lru_w[l][0][2 * ct:2 * ct + 2], lru_sm[l][cs, 5:6],
                      lru_w[l][1][2 * ct:2 * ct + 2], lru_sm[l][cs, 6:7], lru_sm[l][cs, 7:8])
            P.end_stage()
        for h in range(8):
            hs = slice(64 * h, 64 * h + 64)
            rwkv_stage(P, K, RK, seq, p_rw[hs, :], p_rw[512 + 64 * h:512 + 64 * h + 64, :],
                       p_rw[1024 + 64 * h:1024 + 64 * h + 64, :], p_rw[1536:1568, :], p_rw[1568:1600, :],
                       p_rw[1600 + 64 * h:1600 + 64 * h + 64, :], oT[512 + 64 * h:512 + 64 * h + 64, :],
                       rw_sm[l][h], rw_lup[l][h][0:32, :], rw_lup[l][h][32:64, :])
            P.end_stage()
        for h in range(8):
            kv = h // 4
            swa_stage(P, K, seq, p_swa[64 * h:64 * h + 64, :], p_swa[512 + 64 * kv:512 + 64 * kv + 64, :],
                      p_swa[640 + 64 * kv:640 + 64 * kv + 64, :], p_swa[768 + 64 * h:768 + 64 * h + 64, :],
                      oT[1024 + 64 * h:1024 + 64 * h + 64, :], swa_sm[l][h][:, 0:1], swa_sm[l][h][:, 1:2], swa_sm[l][h][:, 2:3])
            P.end_stage()
        out_stage(P, K, seq, xin, lambda k: oT[128 * k:128 * k + 128, :], w_out[l], xout)
        P.end_stage()
    P.close()
    return nc, P


def _fused_inputs(inp, depth=2):
    f = np.float32
    L = depth
    m = dict(_consts_np())
    m["x"] = np.ascontiguousarray(inp["x"][0], dtype=f)
    m["mem"] = np.ascontiguousarray(inp["mem"][0], dtype=f)
    m["norm_g"] = np.stack([_g16(inp["norm_g"][l]) for l in range(L)])
    m["w_in"] = np.ascontiguousarray(inp["w_in"][:L], dtype=f)
    m["memg"] = np.stack([_g16(inp["mem_norm_g"][l]) for l in range(L)])
    m["wkv"] = np.ascontiguousarray(inp["w_mem_kv"][:L], dtype=f)
    m["mqk"] = np.ascontiguousarray(np.stack([inp["mem_q_g"][:L], inp["mem_k_g"][:L]], axis=-1), dtype=f)
    m["w_out"] = np.ascontiguousarray(inp["w_out"][:L], dtype=f)
    lru_sm = np.zeros((L, 512, 8), f)
    lru_sm[:, :, 0:4] = np.transpose(inp["conv_w"][:L], (0, 2, 1))
    lru_sm[:, :, 4] = inp["conv_b"][:L]
    lru_sm[:, :, 5] = inp["lru_ba"][:L]
    lru_sm[:, :, 6] = inp["lru_bx"][:L]
    lru_sm[:, :, 7] = inp["lru_lambda"][:L]
    m["lru_sm"] = lru_sm
    m["lru_w"] = np.ascontiguousarray(np.stack([inp["lru_wa"][:L], inp["lru_wx"][:L]], axis=1), dtype=f)
    rw_sm = np.zeros((L, 8, 64, 16), f)
    rw_lup = np.zeros((L, 8, 64, 64), f)
    swa_sm = np.zeros((L, 8, 64, 4), f)
    for l in range(L):
        for h in range(8):
            sm, _, lup = _head_small(inp, l, h)
            rw_sm[l, h] = sm[:, 8:24]
            rw_lup[l, h] = lup
            swa_sm[l, h, :, 0:3] = sm[:, 24:27]
    m["rw_sm"] = rw_sm
    m["rw_lup"] = rw_lup
    m["swa_sm"] = swa_sm
    return m


def kernel(**inputs):
    inp = {k: np.asarray(v) for k, v in inputs.items()}
    nc, _ = build_fused()
    m = _fused_inputs(inp)
    cores = list(range(NCORES))
    res = run_bass_kernel_spmd(nc, [m for _ in cores], core_ids=cores).results
    return np.asarray(res[0]["out"], dtype=np.float32).reshape(1, SEQ, 2048)
```

```python
from concourse.bass_utils import run_bass_kernel_spmd
from contextlib import ExitStack
import numpy as np
import concourse.bass as bass
import concourse.mybir as mybir

F32 = mybir.dt.float32
BF16 = mybir.dt.bfloat16
ALU = mybir.AluOpType
AF = mybir.ActivationFunctionType
AX = mybir.AxisListType

MAXV = 30000
ENGS = ("pe", "act", "dve", "pool", "sp")


class Ev:
    __slots__ = ("key", "n")

    def __init__(self, key, n=None):
        self.key = key
        self.n = n


class Res:
    __slots__ = ("name", "writers", "readers", "excl")

    def __init__(self, name="", excl=False):
        self.name = name
        self.writers = []
        self.readers = []
        self.excl = excl


class Counter:
    def __init__(self, prog, name):
        self.sem = prog.es.enter_context(prog.nc.semaphore(name))
        self.total = 0
        self.key = ("d", id(self))
        prog.counters[self.key] = self


class Tile:
    def __init__(self, prog, shape, dtype, name=None, psum=False, persistent=False):
        prog.nsb += 1
        nm = f"t{prog.nsb}_{name or ''}"
        st = prog.es if persistent else prog.stage_es
        if psum:
            self.t = st.enter_context(prog.nc.psum_tensor(nm, list(shape), dtype))
        else:
            self.t = st.enter_context(prog.nc.sbuf_tensor(nm, list(shape), dtype))
        self.r = Res(name or "", excl=psum)
        self.prog = prog
        self._c = None

    @property
    def c(self):
        if self._c is None:
            self._c = self.prog.counter()
        return self._c

    def __getitem__(self, idx):
        return self.t[idx]


class Op:
    __slots__ = ("eng", "fn", "waits", "signal", "ev", "ctr")


def _rs(xs):
    return [getattr(x, "r", x) for x in xs]


class Prog:
    def __init__(self, nc):
        self.nc = nc
        self.es = ExitStack()
        self.stage_es = ExitStack()
        self.ops = {e: [] for e in ENGS}
        self.sigcnt = {e: 0 for e in ENGS}
        self.emitted = {e: 0 for e in ENGS}
        self.pending = {e: [] for e in ENGS}
        self.waited = {e: {} for e in ENGS}
        self.counters = {}
        self.free_counters = []
        self.stage_counters = []
        self.esems = {e: [] for e in ENGS}
        self.nsb = 0
        self.nstage = 0
        self.total_ops = 0

    def tile(self, shape, dtype, name=None, psum=False, persistent=False):
        return Tile(self, shape, dtype, name, psum, persistent)

    def counter(self, name=None, persistent=False):
        if self.free_counters and not persistent:
            c = self.free_counters.pop()
        else:
            self.nsb += 1
            c = Counter(self, name or f"ctr{self.nsb}")
        if not persistent:
            self.stage_counters.append(c)
        return c

    def _deps(self, reads, writes, accs):
        waits = []
        for r in reads:
            waits.extend(r.writers)
        for r in writes:
            waits.extend(r.writers)
            waits.extend(r.readers)
        for r in accs:
            waits.extend(r.readers)
        return waits

    def _post(self, ev, reads, writes, accs):
        for r in reads:
            r.readers.append(ev)
            if len(r.readers) > 64:
                r.readers = _compact(r.readers)
        for r in writes:
            r.writers = [ev]
            r.readers = []
        for r in accs:
            r.writers.append(ev)
            if len(r.writers) > 64:
                r.writers = _compact(r.writers)

    def op(self, eng, fn, reads=(), writes=(), accs=(), signal=True):
        reads, writes, accs = _rs(reads), _rs(writes), _rs(accs)
        ex = [r for r in reads if r.excl]
        if ex:
            reads = [r for r in reads if not r.excl]
            writes = list(writes) + ex
        o = Op()
        o.eng = eng
        o.fn = fn
        o.ctr = None
        waits = self._deps(reads, writes, accs)
        if eng == "pe":
            waits = [w for w in waits if w.key != "pe"]
        o.waits = waits
        o.signal = False
        ev = Ev(eng)
        o.ev = ev
        self.ops[eng].append(o)
        self.pending[eng].append(ev)
        if signal:
            self.signal_last(eng)
        self._post(ev, reads, writes, accs)
        return ev

    def signal_last(self, eng):
        o = self.ops[eng][-1]
        if o.signal:
            return
        o.signal = True
        self.sigcnt[eng] += 1
        for p in self.pending[eng]:
            p.n = self.sigcnt[eng]
        self.pending[eng] = []

    def dma(self, eng, ctr, fn, reads=(), writes=(), accs=(), inc=16):
        reads, writes, accs = _rs(reads), _rs(writes), _rs(accs)
        o = Op()
        o.eng = eng
        o.fn = fn
        o.ctr = ctr
        o.waits = self._deps(reads, writes, accs)
        o.signal = (inc == 16)
        ctr.total += inc
        assert ctr.total < 60000
        ev = Ev(ctr.key, ctr.total)
        o.ev = ev
        self.ops[eng].append(o)
        self._post(ev, reads, writes, accs)
        return ev

    def finish(self, eng="sp"):
        o = Op()
        o.eng = eng
        o.fn = None
        o.ctr = None
        o.signal = False
        o.ev = Ev(eng)
        o.waits = [Ev(c.key, c.total) for c in self.counters.values() if c.total]
        self.ops[eng].append(o)

    def end_stage(self):
        nc = self.nc
        self.finish("sp")
        for e in ENGS:
            if self.pending[e]:
                self.signal_last(e)
            need = (self.sigcnt[e] + MAXV - 1) // MAXV + 1
            while len(self.esems[e]) < need:
                self.esems[e].append(self.es.enter_context(nc.semaphore(f"s_{e}{len(self.esems[e])}")))
        prog = self

        def resolve(ev):
            if isinstance(ev.key, tuple):
                return (ev.key, prog.counters[ev.key].sem, ev.n)
            assert ev.n is not None, f"unresolved event on {ev.key}"
            idx = (ev.n - 1) // MAXV
            return ((ev.key, idx), prog.esems[ev.key][idx], (ev.n - 1) % MAXV + 1)

        def run(e):
            def body(eng):
                waited = prog.waited[e]
                cnt = prog.emitted[e]
                for o in prog.ops[e]:
                    need = {}
                    for w in o.waits:
                        k, sem, v = resolve(w)
                        if waited.get(k, 0) >= v:
                            continue
                        if k not in need or need[k][1] < v:
                            need[k] = (sem, v)
                    for k, (sem, v) in need.items():
                        eng.wait_ge(sem, v)
                        waited[k] = v
                    if o.fn is None:
                        continue
                    inst = o.fn(eng)
                    if o.ctr is not None:
                        if o.signal:
                            inst.then_inc(o.ctr.sem, 16)
                        else:
                            inst.then_inc(o.ctr.sem)
                    elif o.signal:
                        cnt += 1
                        idx = (cnt - 1) // MAXV
                        inst.then_inc(prog.esems[e][idx], 1)
                prog.emitted[e] = cnt
            return body

        with nc.Block() as block:
            block.tensor(run("pe"))
            block.scalar(run("act"))
            block.vector(run("dve"))
            block.gpsimd(run("pool"))
            block.sync(run("sp"))
        for e in ENGS:
            assert self.emitted[e] == self.sigcnt[e], (e, self.emitted[e], self.sigcnt[e])
            self.total_ops += len(self.ops[e])
            self.ops[e] = []
        self.stage_es.close()
        self.stage_es = ExitStack()
        self.free_counters.extend(self.stage_counters)
        self.stage_counters = []
        self.nstage += 1

    def close(self):
        self.es.close()


def _compact(evs):
    best = {}
    for ev in evs:
        if ev.n is None:
            best[id(ev)] = ev
            continue
        k = ev.key
        if k not in best or best[k].n < ev.n:
            best[k] = ev
    return list(best.values())


LRU_C = 8.0


def load_col(P, ctr, res, dst_ap, src_ap, eng="sp"):
    P.dma(eng, ctr, lambda e: e.dma_start(out=dst_ap, in_=src_ap), accs=[res])


def lru_stage(P, CP, NTOK, xrows, grows, orows, convw, convb, wa, ba, wx, bx, lam, TP=2048, tag=""):
    nb = CP // 64
    npc = NTOK // TP
    prm = P.tile([CP, 16], F32, f"lruprm{tag}")
    wabd = P.tile([CP, CP], F32, f"wabd{tag}")
    wxbd = P.tile([CP, CP], F32, f"wxbd{tag}")
    if nb > 1:
        P.op("pool", lambda e: e.memset(wabd[:], 0.0), writes=[wabd])
        P.op("pool", lambda e: e.memset(wxbd[:], 0.0), writes=[wxbd])
    for b in range(nb):
        P.dma("sp", wabd.c, lambda e, b=b: e.dma_start(out=wabd[b * 64:(b + 1) * 64, b * 64:(b + 1) * 64], in_=wa[b]),
              accs=[wabd])
        P.dma("sp", wxbd.c, lambda e, b=b: e.dma_start(out=wxbd[b * 64:(b + 1) * 64, b * 64:(b + 1) * 64], in_=wx[b]),
              accs=[wxbd])
    P.dma("sp", prm.c, lambda e: e.dma_start(out=prm[:, 0:4], in_=convw, allow_slow_non_contiguous=True), writes=[prm])
    for i, src in enumerate((convb, ba, bx, lam)):
        P.dma("sp", prm.c, lambda e, i=i, src=src: e.dma_start(out=prm[:, 4 + i:5 + i], in_=src, allow_slow_non_contiguous=True), accs=[prm])
    P.op("act", lambda e: e.activation(out=prm[:, 8:9], in_=prm[:, 7:8], func=AF.Exp, scale=-1.0),
         reads=[prm], accs=[prm])
    P.op("act", lambda e: e.activation(out=prm[:, 9:10], in_=prm[:, 8:9], func=AF.Ln, bias=1.0),
         reads=[prm], accs=[prm])
    P.op("dve", lambda e: e.tensor_scalar(out=prm[:, 10:11], in0=prm[:, 9:10], scalar1=-LRU_C, scalar2=None,
                                         op0=ALU.mult), reads=[prm], accs=[prm])
    P.op("dve", lambda e: e.memset(prm[:, 11:12], 0.0), reads=[prm], accs=[prm])

    xt = [P.tile([CP, TP + 3], F32, f"lxt{tag}{i}") for i in range(2)]
    gt = [P.tile([CP, TP], F32, f"lgt{tag}{i}") for i in range(2)]
    xc = P.tile([CP, TP], F32, f"lxc{tag}")
    rr = P.tile([CP, TP], F32, f"lr{tag}")
    ii = P.tile([CP, TP], F32, f"li{tag}")
    aa = P.tile([CP, TP], F32, f"la{tag}")
    mm = P.tile([CP, TP], F32, f"lm{tag}")
    hh = [P.tile([CP, TP], F32, f"lh{tag}{i}") for i in range(2)]
    ob = [P.tile([CP, TP], BF16, f"lo{tag}{i}") for i in range(2)]
    pg = [P.tile([CP, 512], F32, f"lpg{tag}{i}", psum=True) for i in range(2)]
    npg = 0
    for pi in range(npc):
        s = pi % 2
        t0 = pi * TP
        X, G, H, O = xt[s], gt[s], hh[s], ob[s]
        if pi == 0:
            P.op("pool", lambda e, X=X: e.memset(X[:, 0:3], 0.0), writes=[X])
            P.dma("sp", X.c, lambda e, X=X: e.dma_start(out=X[:, 3:3 + TP], in_=xrows[:, 0:TP]), accs=[X])
        else:
            P.dma("sp", X.c, lambda e, X=X, t0=t0: e.dma_start(out=X[:, :], in_=xrows[:, t0 - 3:t0 + TP]), writes=[X])
        P.dma("sp", G.c, lambda e, G=G, t0=t0: e.dma_start(out=G[:, :], in_=grows[:, t0:t0 + TP]), writes=[G])
        P.op("dve", lambda e, X=X: e.tensor_scalar(out=xc[:], in0=X[:, 3:3 + TP], scalar1=prm[:, 3:4],
                                                  scalar2=prm[:, 4:5], op0=ALU.mult, op1=ALU.add),
             reads=[X, prm], writes=[xc])
        for j in range(3):
            P.op("dve", lambda e, X=X, j=j: e.scalar_tensor_tensor(out=xc[:], in0=X[:, j:j + TP], scalar=prm[:, j:j + 1],
                                                                  in1=xc[:], op0=ALU.mult, op1=ALU.add),
                 reads=[X, prm, xc], writes=[xc])
        for (wbd, bcol, dst) in ((wabd, 5, rr), (wxbd, 6, ii)):
            for sb_ in range(TP // 512):
                pb = pg[npg % 2]
                npg += 1
                P.op("pe", lambda e, wbd=wbd, pb=pb, sb_=sb_: e.matmul(pb[:, :], lhsT=wbd[:, :],
                                                                      rhs=xc[:, sb_ * 512:(sb_ + 1) * 512],
                                                                      start=True, stop=True),
                     reads=[wbd, xc], writes=[pb])
                P.op("act", lambda e, pb=pb, dst=dst, sb_=sb_, bcol=bcol: e.activation(
                    out=dst[:, sb_ * 512:(sb_ + 1) * 512], in_=pb[:, :], func=AF.Sigmoid, bias=prm[:, bcol:bcol + 1]),
                    reads=[pb, prm], writes=[dst] if sb_ == 0 else [], accs=[] if sb_ == 0 else [dst])
        P.op("act", lambda e: e.activation(out=aa[:], in_=rr[:], func=AF.Exp, scale=prm[:, 10:11]),
             reads=[rr, prm], writes=[aa])
        P.op("dve", lambda e: e.tensor_tensor(out=mm[:], in0=aa[:], in1=aa[:], op=ALU.mult), reads=[aa], writes=[mm])
        P.op("dve", lambda e: e.tensor_scalar(out=mm[:], in0=mm[:], scalar1=-1.0, scalar2=1.0, op0=ALU.mult,
                                             op1=ALU.add), reads=[mm], writes=[mm])
        P.op("dve", lambda e: e.tensor_scalar(out=mm[:], in0=mm[:], scalar1=1e-12, scalar2=None, op0=ALU.max),
             reads=[mm], writes=[mm])
        P.op("act", lambda e: e.activation(out=mm[:], in_=mm[:], func=AF.Sqrt), reads=[mm], writes=[mm])
        P.op("dve", lambda e: e.tensor_tensor(out=ii[:], in0=ii[:], in1=xc[:], op=ALU.mult), reads=[ii, xc], writes=[ii])
        P.op("dve", lambda e: e.tensor_tensor(out=ii[:], in0=ii[:], in1=mm[:], op=ALU.mult), reads=[ii, mm], writes=[ii])
        Hp = hh[1 - s]
        init = prm[:, 11:12] if pi == 0 else Hp[:, TP - 1:TP]
        P.op("dve", lambda e, H=H, init=init: e.tensor_tensor_scan(out=H[:], data0=aa[:], data1=ii[:], initial=init,
                                                                  op0=ALU.mult, op1=ALU.add),
             reads=[aa, ii, prm, Hp], writes=[H])
        P.op("act", lambda e, G=G: e.activation(out=G[:], in_=G[:], func=AF.Silu), reads=[G], writes=[G])
        P.op("dve", lambda e, H=H, G=G, O=O: e.tensor_tensor(out=O[:], in0=H[:], in1=G[:], op=ALU.mult),
             reads=[H, G], writes=[O])
        P.dma("sp", O.c, lambda e, O=O, t0=t0: e.dma_start(out=orows[:, t0:t0 + TP], in_=O[:]), reads=[O])


class Consts:
    def __init__(self, P, c_identf, c_identb, c_swamask):
        self.identf = P.tile([128, 128], F32, "identf", persistent=True)
        self.identb = P.tile([128, 128], BF16, "identb", persistent=True)
        self.swamask = P.tile([128, 256], BF16, "swamask", persistent=True)
        self.ones_f = P.tile([128, 128], F32, "ones_f", persistent=True)
        self.ones_b = P.tile([128, 128], BF16, "ones_b", persistent=True)
        P.dma("sp", self.identf.c, lambda e: e.dma_start(out=self.identf[:], in_=c_identf), writes=[self.identf])
        P.dma("sp", self.identb.c, lambda e: e.dma_start(out=self.identb[:], in_=c_identb), writes=[self.identb])
        P.dma("sp", self.swamask.c, lambda e: e.dma_start(out=self.swamask[:], in_=c_swamask), writes=[self.swamask])
        P.op("pool", lambda e: e.memset(self.ones_f[:], 1.0), writes=[self.ones_f])
        P.op("pool", lambda e: e.memset(self.ones_b[:], 1.0), writes=[self.ones_b])


def swa_stage(P, K, NTOK, qrows, krows, vrows, grows, orows, qg, kg, sink, TP=512, tag=""):
    npc = NTOK // TP
    nbk = TP // 128
    prm = P.tile([64, 8], F32, f"swaprm{tag}")
    P.dma("sp", prm.c, lambda e: e.dma_start(out=prm[:, 0:1], in_=qg, allow_slow_non_contiguous=True), writes=[prm])
    P.dma("sp", prm.c, lambda e: e.dma_start(out=prm[:, 1:2], in_=kg, allow_slow_non_contiguous=True), accs=[prm])
    P.dma("sp", prm.c, lambda e: e.dma_start(out=prm[:, 2:3], in_=sink, allow_slow_non_contiguous=True), accs=[prm])
    P.op("dve", lambda e: e.tensor_scalar(out=prm[:, 3:4], in0=prm[:, 0:1], scalar1=0.125, scalar2=None, op0=ALU.mult),
         reads=[prm], accs=[prm])
    P.op("act", lambda e: e.activation(out=prm[:, 4:5], in_=prm[:, 2:3], func=AF.Exp), reads=[prm], accs=[prm])
    W = TP + 128
    kx = [P.tile([64, W], F32, f"skx{tag}{i}") for i in range(2)]
    vx = [P.tile([64, W], F32, f"svx{tag}{i}") for i in range(2)]
    qx = [P.tile([64, TP], F32, f"sqx{tag}{i}") for i in range(2)]
    gx = [P.tile([64, TP], F32, f"sgx{tag}{i}") for i in range(2)]
    sq = P.tile([64, W], F32, f"ssq{tag}")
    rs = P.tile([64, W], F32, f"srs{tag}")
    kn = P.tile([64, W], BF16, f"skn{tag}")
    qn = P.tile([64, TP], BF16, f"sqn{tag}")
    vb = P.tile([128, nbk + 1, 64], BF16, f"svb{tag}")
    E = [P.tile([128, 256], BF16, f"sE{tag}{i}") for i in range(2)]
    dn = P.tile([64, TP], F32, f"sdn{tag}")
    yy = P.tile([64, TP], F32, f"syy{tag}")
    ob = [P.tile([64, TP], BF16, f"sob{tag}{i}") for i in range(2)]
    pn = [P.tile([64, 512], F32, f"spn{tag}{i}", psum=True) for i in range(2)]
    pt = P.tile([128, 512], F32, f"spt{tag}", psum=True)
    psc = [P.tile([128, 256], F32, f"spsc{tag}{i}", psum=True) for i in range(2)]
    pnum = P.tile([64, 512], F32, f"spnum{tag}", psum=True)
    pden = P.tile([64, 512], F32, f"spden{tag}", psum=True)
    npn = 0
    nsc = 0

    def norm(src, width, c0, gcol, dst, dst_c0):
        nonlocal npn
        P.op("dve", lambda e: e.tensor_tensor(out=sq[:, 0:width], in0=src[:, c0:c0 + width], in1=src[:, c0:c0 + width],
                                             op=ALU.mult), reads=[src], writes=[sq])
        o = 0
        first = True
        while o < width:
            w_ = min(512, width - o)
            pb = pn[npn % 2]
            npn += 1
            P.op("pe", lambda e, pb=pb, o=o, w_=w_: e.matmul(pb[:, 0:w_], lhsT=K.ones_f[0:64, 0:64], rhs=sq[:, o:o + w_],
                                                            start=True, stop=True), reads=[K.ones_f, sq], writes=[pb])
            P.op("act", lambda e, pb=pb, o=o, w_=w_: e.activation(out=rs[:, o:o + w_], in_=pb[:, 0:w_], func=AF.Sqrt,
                                                                 scale=1.0 / 64, bias=1e-6),
                 reads=[pb], writes=[rs] if first else [], accs=[] if first else [rs])
            first = False
            o += w_
        P.op("dve", lambda e: e.reciprocal(out=rs[:, 0:width], in_=rs[:, 0:width]), reads=[rs], writes=[rs])
        P.op("dve", lambda e: e.scalar_tensor_tensor(out=dst[:, dst_c0:dst_c0 + width], in0=src[:, c0:c0 + width],
                                                    scalar=prm[:, gcol:gcol + 1], in1=rs[:, 0:width],
                                                    op0=ALU.mult, op1=ALU.mult), reads=[src, prm, rs], writes=[dst])

    for pi in range(npc):
        s = pi % 2
        t0 = pi * TP
        KX, VX, QX, GX, O = kx[s], vx[s], qx[s], gx[s], ob[s]
        lo = 128 if pi == 0 else 0
        P.dma("sp", KX.c, lambda e, KX=KX, t0=t0, lo=lo: e.dma_start(out=KX[:, lo:W], in_=krows[:, t0 - 128 + lo:t0 + TP]),
              writes=[KX])
        P.dma("sp", VX.c, lambda e, VX=VX, t0=t0, lo=lo: e.dma_start(out=VX[:, lo:W], in_=vrows[:, t0 - 128 + lo:t0 + TP]),
              writes=[VX])
        P.dma("sp", QX.c, lambda e, QX=QX, t0=t0: e.dma_start(out=QX[:, :], in_=qrows[:, t0:t0 + TP]), writes=[QX])
        P.dma("sp", GX.c, lambda e, GX=GX, t0=t0: e.dma_start(out=GX[:, :], in_=grows[:, t0:t0 + TP]), writes=[GX])
        norm(KX, W - lo, lo, 1, kn, lo)
        norm(QX, TP, 0, 3, qn, 0)
        b0 = lo // 128
        for b in range(b0, nbk + 1):
            P.op("pe", lambda e, VX=VX, b=b: e.transpose(out=pt[:, b * 64:(b + 1) * 64], in_=VX[:, b * 128:(b + 1) * 128],
                                                        identity=K.identf[0:64, 0:64]),
                 reads=[VX, K.identf], writes=[pt] if b == b0 else [], accs=[] if b == b0 else [pt],
                 signal=(b == nbk))
        P.op("act", lambda e, b0=b0: e.activation(out=vb[:, b0:nbk + 1, :],
                                                 in_=pt[:, b0 * 64:(nbk + 1) * 64].rearrange("p (b d) -> p b d", d=64),
                                                 func=AF.Copy), reads=[pt], writes=[vb])
        for n in range(nbk):
            has_prev = not (pi == 0 and n == 0)
            sc = psc[nsc % 2]
            Eb = E[nsc % 2]
            nsc += 1
            wd = 256 if has_prev else 128
            P.op("pe", lambda e, sc=sc, n=n: e.matmul(sc[:, 0:128], lhsT=kn[:, (n + 1) * 128:(n + 2) * 128],
                                                     rhs=qn[:, n * 128:(n + 1) * 128], start=True, stop=True),
                 reads=[kn, qn], writes=[sc], signal=not has_prev)
            if has_prev:
                P.op("pe", lambda e, sc=sc, n=n: e.matmul(sc[:, 128:256], lhsT=kn[:, n * 128:(n + 1) * 128],
                                                         rhs=qn[:, n * 128:(n + 1) * 128], start=True, stop=True),
                     reads=[kn, qn], accs=[sc])
            P.op("act", lambda e, sc=sc, Eb=Eb, wd=wd: e.activation(out=Eb[:, 0:wd], in_=sc[:, 0:wd], func=AF.Exp),
                 reads=[sc], writes=[Eb])
            P.op("pool", lambda e, Eb=Eb, wd=wd: e.tensor_tensor(out=Eb[:, 0:wd], in0=Eb[:, 0:wd], in1=K.swamask[:, 0:wd],
                                                                op=ALU.mult), reads=[Eb, K.swamask], writes=[Eb])
            cs = slice(n * 128, (n + 1) * 128)
            for (pacc, lhs_cur, lhs_prev) in ((pnum, vb[:, n + 1, :], vb[:, n, :]),
                                              (pden, K.ones_b[:, 0:64], K.ones_b[:, 0:64])):
                P.op("pe", lambda e, pacc=pacc, lhs_cur=lhs_cur, Eb=Eb, cs=cs, has_prev=has_prev: e.matmul(
                    pacc[:, cs], lhsT=lhs_cur, rhs=Eb[:, 0:128], start=True, stop=not has_prev),
                    reads=[vb, K.ones_b, Eb], writes=[pacc] if n == 0 else [], accs=[] if n == 0 else [pacc],
                    signal=False)
                if has_prev:
                    P.op("pe", lambda e, pacc=pacc, lhs_prev=lhs_prev, Eb=Eb, cs=cs: e.matmul(
                        pacc[:, cs], lhsT=lhs_prev, rhs=Eb[:, 128:256], start=False, stop=True),
                        reads=[vb, K.ones_b, Eb], accs=[pacc], signal=False)
            P.signal_last("pe")
        P.op("dve", lambda e: e.tensor_scalar(out=dn[:], in0=pden[:, :], scalar1=prm[:, 4:5], scalar2=None, op0=ALU.add),
             reads=[pden, prm], writes=[dn])
        P.op("dve", lambda e: e.reciprocal(out=dn[:], in_=dn[:]), reads=[dn], writes=[dn])
        P.op("dve", lambda e: e.tensor_tensor(out=yy[:], in0=pnum[:, :], in1=dn[:], op=ALU.mult),
             reads=[pnum, dn], writes=[yy])
        P.op("act", lambda e, GX=GX: e.activation(out=GX[:], in_=GX[:], func=AF.Silu), reads=[GX], writes=[GX])
        P.op("dve", lambda e, GX=GX, O=O: e.tensor_tensor(out=O[:], in0=yy[:], in1=GX[:], op=ALU.mult),
             reads=[yy, GX], writes=[O])
        P.dma("sp", O.c, lambda e, O=O, t0=t0: e.dma_start(out=orows[:, t0:t0 + TP], in_=O[:]), reads=[O])


D_MODEL = 2048
KC = D_MODEL // 128


def norm_transpose(P, K, x_dram, ntile, gsb, hT, pst, eps=1e-6, tag=""):
    xt = [P.tile([128, D_MODEL], F32, f"nxt{tag}{i}") for i in range(2)]
    xs = [P.tile([128, D_MODEL], F32, f"nxs{tag}{i}") for i in range(2)]
    junk = P.tile([128, D_MODEL], BF16, f"njunk{tag}")
    st = P.tile([128, 4 * ntile], F32, f"nst{tag}")
    npst = 0
    for i in range(ntile):
        s = i % 2
        X, XS = xt[s], xs[s]
        P.dma("sp", X.c, lambda e, X=X, i=i: e.dma_start(out=X[:], in_=x_dram[i * 128:(i + 1) * 128, :]), writes=[X])
        c0 = 4 * i
        P.op("act", lambda e, X=X, c0=c0: e.activation(out=junk[:], in_=X[:], func=AF.Square, accum_out=st[:, c0:c0 + 1]),
             reads=[X], writes=[junk], accs=[st])
        P.op("dve", lambda e, c0=c0: e.tensor_scalar(out=st[:, c0 + 1:c0 + 2], in0=st[:, c0:c0 + 1], scalar1=1.0 / D_MODEL,
                                                    scalar2=eps, op0=ALU.mult, op1=ALU.add), reads=[st], accs=[st])
        P.op("act", lambda e, c0=c0: e.activation(out=st[:, c0 + 2:c0 + 3], in_=st[:, c0 + 1:c0 + 2], func=AF.Sqrt),
             reads=[st], accs=[st])
        P.op("dve", lambda e, c0=c0: e.reciprocal(out=st[:, c0 + 3:c0 + 4], in_=st[:, c0 + 2:c0 + 3]),
             reads=[st], accs=[st])
        P.op("act", lambda e, X=X, XS=XS, c0=c0: e.activation(out=XS[:], in_=X[:], func=AF.Copy,
                                                             scale=st[:, c0 + 3:c0 + 4]), reads=[X, st], writes=[XS])
        for kq in range(KC // 4):
            pb = pst[npst % 2]
            npst += 1
            for kk in range(4):
                k = kq * 4 + kk
                P.op("pe", lambda e, XS=XS, pb=pb, k=k, kk=kk: e.transpose(
                    out=pb[:, kk * 128:(kk + 1) * 128], in_=XS[:, k * 128:(k + 1) * 128], identity=K.identf[:]),
                    reads=[XS, K.identf], writes=[pb] if kk == 0 else [], accs=[pb] if kk else [], signal=(kk == 3))
            for kk in range(4):
                k = kq * 4 + kk
                P.op("dve", lambda e, pb=pb, k=k, kk=kk, i=i: e.tensor_scalar(
                    out=hT[:, k, i * 128:(i + 1) * 128], in0=pb[:, kk * 128:(kk + 1) * 128],
                    scalar1=gsb[:, k:k + 1], scalar2=None, op0=ALU.mult), reads=[pb, gsb], accs=[hT])


def load_weight_bf16(P, w_dram, c0, cw, wst, wb):
    wv = w_dram.rearrange("(k p) c -> p k c", p=128)
    P.dma("sp", wst.c, lambda e: e.dma_start(out=wst[:, :, 0:cw], in_=wv[:, :, c0:c0 + cw]), writes=[wst])
    P.op("pool", lambda e: e.tensor_copy(out=wb[:, :, 0:cw], in_=wst[:, :, 0:cw]), reads=[wst], writes=[wb])


def proj_stage(P, K, NT, x_dram, g_dram, w_dram, NCOL, dst64, tag=""):
    ntile = NT // 128
    TG = min(512, NT)
    ntg = NT // TG
    CB = 256
    gsb = P.tile([128, KC], F32, f"pg{tag}")
    P.dma("sp", gsb.c, lambda e: e.dma_start(out=gsb[:], in_=g_dram), writes=[gsb])
    hT = P.tile([128, KC, NT], BF16, f"phT{tag}")
    pp = [P.tile([128, 512], F32, f"ppp{tag}{i}", psum=True) for i in range(8)]
    norm_transpose(P, K, x_dram, ntile, gsb, hT, pp[0:2], tag=tag)
    nblk = (NCOL + CB - 1) // CB
    wst = [P.tile([128, KC, CB], F32, f"pwst{tag}{i}") for i in range(2)]
    wb = [P.tile([128, KC, CB], BF16, f"pwb{tag}{i}") for i in range(2)]
    ost = [P.tile([128, NT], F32, f"post{tag}{i}") for i in range(2)]
    nmm = 0
    nct = 0
    for bi in range(nblk):
        s = bi % 2
        cw = min(CB, NCOL - bi * CB)
        load_weight_bf16(P, w_dram, bi * CB, cw, wst[s], wb[s])
        WB = wb[s]
        for j in range((cw + 127) // 128):
            mw = min(128, cw - j * 128)
            O = ost[nct % 2]
            nct += 1
            pbs = [pp[(nct % 2) * 4 + n] for n in range(ntg)]
            for k in range(KC):
                for n in range(ntg):
                    pb = pbs[n]
                    P.op("pe", lambda e, WB=WB, k=k, j=j, mw=mw, n=n, pb=pb: e.matmul(
                        pb[0:mw, 0:TG], lhsT=WB[:, k, j * 128:j * 128 + mw], rhs=hT[:, k, n * TG:(n + 1) * TG],
                        start=(k == 0), stop=(k == KC - 1)),
                        reads=[WB, hT], writes=[pb] if k == 0 else [], accs=[pb] if k else [],
                        signal=(k == KC - 1 and n == ntg - 1))
            for n in range(ntg):
                pb = pbs[n]
                nmm += 1
                if nmm % 2:
                    P.op("act", lambda e, O=O, mw=mw, n=n, pb=pb: e.activation(
                        out=O[0:mw, n * TG:(n + 1) * TG], in_=pb[0:mw, 0:TG], func=AF.Copy),
                        reads=[pb], writes=[O] if n == 0 else [], accs=[] if n == 0 else [O])
                else:
                    P.op("dve", lambda e, O=O, mw=mw, n=n, pb=pb: e.tensor_copy(
                        out=O[0:mw, n * TG:(n + 1) * TG], in_=pb[0:mw, 0:TG]),
                        reads=[pb], writes=[O] if n == 0 else [], accs=[] if n == 0 else [O])
            c0 = bi * CB + j * 128
            for hh_ in range(mw // 64):
                P.dma("sp", O.c, lambda e, O=O, hh_=hh_, c0=c0: e.dma_start(
                    out=dst64(c0 // 64 + hh_), in_=O[hh_ * 64:(hh_ + 1) * 64, :]), reads=[O])


def out_stage(P, K, NT, x_dram, oT_src, w_dram, out_dram, tag=""):
    TG = min(512, NT)
    ntg = NT // TG
    wst = [P.tile([128, KC, 256], F32, f"owst{tag}{i}") for i in range(2)]
    wo = [P.tile([128, KC, 512], BF16, f"owo{tag}{i}") for i in range(4)]
    wv = w_dram.rearrange("(k p) c -> p k c", p=128)
    for cb in range(8):
        WS, WO, hf = wst[cb % 2], wo[cb // 2], cb % 2
        P.dma("sp", WS.c, lambda e, WS=WS, cb=cb: e.dma_start(out=WS[:, :, :], in_=wv[:, :, cb * 256:(cb + 1) * 256]), writes=[WS])
        P.op("pool", lambda e, WS=WS, WO=WO, hf=hf: e.tensor_copy(out=WO[:, :, hf * 256:(hf + 1) * 256], in_=WS[:, :, :]),
             reads=[WS], writes=[WO] if hf == 0 else [], accs=[WO] if hf else [])
    ot = [P.tile([128, KC, TG], BF16, f"oot{tag}{i}") for i in range(2)]
    xt = [P.tile([128, D_MODEL], F32, f"oxt{tag}{i}") for i in range(2)]
    xo = [P.tile([128, D_MODEL], F32, f"oxo{tag}{i}") for i in range(2)]
    pp = [P.tile([128, 512], F32, f"opp{tag}{i}", psum=True) for i in range(4)]
    nmm = 0
    ntl = 0
    for n in range(ntg):
        OT = ot[n % 2]
        for k in range(KC):
            P.dma("sp", OT.c, lambda e, OT=OT, k=k, n=n: e.dma_start(out=OT[:, k, :], in_=oT_src(k)[:, n * TG:(n + 1) * TG]),
                  writes=[OT] if k == 0 else [], accs=[OT] if k else [])
        for tt in range(TG // 128):
            X, XO = xt[ntl % 2], xo[ntl % 2]
            ntl += 1
            r0 = n * TG + tt * 128
            P.dma("sp", X.c, lambda e, X=X, r0=r0: e.dma_start(out=X[:], in_=x_dram[r0:r0 + 128, :]), writes=[X])
            for cb in range(4):
                pb = pp[nmm % 4]
                nmm += 1
                W_ = wo[cb]
                for k in range(KC):
                    P.op("pe", lambda e, OT=OT, W_=W_, k=k, tt=tt, pb=pb: e.matmul(
                        pb[:, :], lhsT=OT[:, k, tt * 128:(tt + 1) * 128], rhs=W_[:, k, :],
                        start=(k == 0), stop=(k == KC - 1)),
                        reads=[OT, W_], writes=[pb] if k == 0 else [], accs=[pb] if k else [], signal=(k == KC - 1))
                P.op("dve", lambda e, X=X, XO=XO, pb=pb, cb=cb: e.tensor_tensor(
                    out=XO[:, cb * 512:(cb + 1) * 512], in0=pb[:, :], in1=X[:, cb * 512:(cb + 1) * 512], op=ALU.add),
                    reads=[pb, X], writes=[XO] if cb == 0 else [], accs=[] if cb == 0 else [XO])
            P.dma("sp", XO.c, lambda e, XO=XO, r0=r0: e.dma_start(out=out_dram[r0:r0 + 128, :], in_=XO[:]), reads=[XO])


def mem_stage(P, K, NT, qsrc, gsrc, odst, mem_dram, memg_dram, wkv_dram, qg_dram, kg_dram, tag=""):
    TP = min(512, NT)
    npc = NT // TP
    SC = 128.0 ** -0.5
    prm = P.tile([128, 24], F32, f"mprm{tag}")
    gsb = P.tile([128, KC], F32, f"mgsb{tag}")
    P.dma("sp", prm.c, lambda e: e.dma_start(out=prm[:, 0:1], in_=qg_dram, allow_slow_non_contiguous=True), writes=[prm])
    P.dma("sp", prm.c, lambda e: e.dma_start(out=prm[:, 1:2], in_=kg_dram, allow_slow_non_contiguous=True), accs=[prm])
    P.dma("sp", gsb.c, lambda e: e.dma_start(out=gsb[:], in_=memg_dram), writes=[gsb])
    P.op("dve", lambda e: e.tensor_scalar(out=prm[:, 2:3], in0=prm[:, 1:2], scalar1=SC, scalar2=None, op0=ALU.mult),
         reads=[prm], accs=[prm])
    pp = [P.tile([128, 512], F32, f"mpp{tag}{i}", psum=True) for i in range(7)]
    hmT = P.tile([128, KC, 256], BF16, f"mhmT{tag}")
    norm_transpose(P, K, mem_dram, 2, gsb, hmT, pp[0:2], tag="m" + tag)
    wst = [P.tile([128, KC, 256], F32, f"mwst{tag}{i}") for i in range(2)]
    wkv = [P.tile([128, KC, 256], BF16, f"mwkv{tag}{i}") for i in range(4)]
    for cb in range(4):
        load_weight_bf16(P, wkv_dram, cb * 256, 256, wst[cb % 2], wkv[cb])
    mkf = P.tile([128, 2, 512], F32, f"mmkf{tag}")
    mvb = P.tile([128, 2, 512], BF16, f"mmvb{tag}")
    mkT = P.tile([128, 4, 256], BF16, f"mmkT{tag}")
    junk = P.tile([128, 128], F32, f"mjunk{tag}")
    npp = 2
    for mt in range(2):
        for half, dst in ((0, mkf), (1, mvb)):
            pb = pp[npp % 7]
            npp += 1
            for sub in range(2):
                W_ = wkv[half * 2 + sub]
                for k in range(KC):
                    P.op("pe", lambda e, pb=pb, sub=sub, W_=W_, k=k, mt=mt: e.matmul(
                        pb[:, sub * 256:(sub + 1) * 256], lhsT=hmT[:, k, mt * 128:(mt + 1) * 128], rhs=W_[:, k, :],
                        start=(k == 0), stop=(k == KC - 1)),
                        reads=[hmT, W_], writes=[pb] if (k == 0 and sub == 0) else [],
                        accs=[] if (k == 0 and sub == 0) else [pb], signal=(k == KC - 1 and sub == 1))
            P.op("act", lambda e, pb=pb, dst=dst, mt=mt: e.activation(out=dst[:, mt, :], in_=pb[:, :], func=AF.Copy),
                 reads=[pb], accs=[dst])
        for h in range(4):
            c0 = 4 + mt * 8 + h * 2
            P.op("act", lambda e, mt=mt, h=h, c0=c0: e.activation(out=junk[:], in_=mkf[:, mt, h * 128:(h + 1) * 128],
                                                                 func=AF.Square, accum_out=prm[:, c0:c0 + 1]),
                 reads=[mkf], writes=[junk], accs=[prm])
            P.op("act", lambda e, c0=c0: e.activation(out=prm[:, c0 + 1:c0 + 2], in_=prm[:, c0:c0 + 1], func=AF.Sqrt,
                                                     scale=1.0 / 128, bias=1e-6), reads=[prm], accs=[prm])
            P.op("dve", lambda e, c0=c0: e.reciprocal(out=prm[:, c0 + 1:c0 + 2], in_=prm[:, c0 + 1:c0 + 2]),
                 reads=[prm], accs=[prm])
            P.op("dve", lambda e, mt=mt, h=h, c0=c0: e.tensor_scalar(
                out=mkf[:, mt, h * 128:(h + 1) * 128], in0=mkf[:, mt, h * 128:(h + 1) * 128],
                scalar1=prm[:, c0 + 1:c0 + 2], scalar2=None, op0=ALU.mult), reads=[mkf, prm], accs=[mkf])
        pb = pp[npp % 7]
        npp += 1
        for h in range(4):
            P.op("pe", lambda e, pb=pb, mt=mt, h=h: e.transpose(out=pb[:, h * 128:(h + 1) * 128],
                                                               in_=mkf[:, mt, h * 128:(h + 1) * 128], identity=K.identf[:]),
                 reads=[mkf, K.identf], writes=[pb] if h == 0 else [], accs=[pb] if h else [], signal=(h == 3))
        P.op("dve", lambda e, pb=pb, mt=mt: e.tensor_scalar(
            out=mkT[:, :, mt * 128:(mt + 1) * 128], in0=pb[:, :].rearrange("p (h m) -> p h m", m=128),
            scalar1=prm[:, 2:3], scalar2=None, op0=ALU.mult), reads=[pb, prm], accs=[mkT])

    qx = [P.tile([128, TP], F32, f"mqx{tag}{i}") for i in range(2)]
    gx = [P.tile([128, TP], F32, f"mgx{tag}{i}") for i in range(2)]
    sq = P.tile([128, TP], F32, f"msq{tag}")
    rs = P.tile([128, TP], F32, f"mrs{tag}")
    qn = P.tile([128, TP], BF16, f"mqn{tag}")
    E = [P.tile([128, TP], BF16, f"mE{tag}{i}") for i in range(2)]
    dn = P.tile([128, TP], F32, f"mdn{tag}")
    yy = P.tile([128, TP], F32, f"myy{tag}")
    ob = [P.tile([128, TP], BF16, f"mob{tag}{i}") for i in range(2)]
    it = 0
    for pi in range(npc):
        t0 = pi * TP
        for h in range(4):
            QX, GX, O = qx[it % 2], gx[it % 2], ob[it % 2]
            it += 1
            P.dma("sp", QX.c, lambda e, QX=QX, h=h, t0=t0: e.dma_start(out=QX[:], in_=qsrc(h)[:, t0:t0 + TP]), writes=[QX])
            P.dma("sp", GX.c, lambda e, GX=GX, h=h, t0=t0: e.dma_start(out=GX[:], in_=gsrc(h)[:, t0:t0 + TP]), writes=[GX])
            P.op("act", lambda e, QX=QX: e.activation(out=sq[:], in_=QX[:], func=AF.Square), reads=[QX], writes=[sq])
            pn, ps0, ps1, pnum, pden = pp[0], pp[1], pp[2], pp[3], pp[4]
            P.op("pe", lambda e, pn=pn: e.matmul(pn[:, 0:TP], lhsT=K.ones_f[:, :], rhs=sq[:], start=True, stop=True),
                 reads=[K.ones_f, sq], writes=[pn])
            P.op("act", lambda e, pn=pn: e.activation(out=rs[:], in_=pn[:, 0:TP], func=AF.Ln, scale=1.0 / 128, bias=1e-6),
                 reads=[pn], writes=[rs])
            P.op("act", lambda e: e.activation(out=rs[:], in_=rs[:], func=AF.Exp, scale=-0.5), reads=[rs], writes=[rs])
            P.op("dve", lambda e, QX=QX: e.scalar_tensor_tensor(out=qn[:], in0=QX[:], scalar=prm[:, 0:1], in1=rs[:],
                                                               op0=ALU.mult, op1=ALU.mult), reads=[QX, prm, rs], writes=[qn])
            for mt, psb in ((0, ps0), (1, ps1)):
                P.op("pe", lambda e, psb=psb, mt=mt, h=h: e.matmul(psb[:, 0:TP], lhsT=mkT[:, h, mt * 128:(mt + 1) * 128],
                                                                  rhs=qn[:], start=True, stop=True),
                     reads=[mkT, qn], writes=[psb])
                P.op("act", lambda e, psb=psb, mt=mt: e.activation(out=E[mt][:], in_=psb[:, 0:TP], func=AF.Exp),
                     reads=[psb], writes=[E[mt]])
            for mt in range(2):
                P.op("pe", lambda e, mt=mt, h=h, pnum=pnum: e.matmul(pnum[:, 0:TP], lhsT=mvb[:, mt, h * 128:(h + 1) * 128],
                                                                    rhs=E[mt][:], start=(mt == 0), stop=(mt == 1)),
                     reads=[mvb, E[mt]], writes=[pnum] if mt == 0 else [], accs=[pnum] if mt else [], signal=(mt == 1))
            for mt in range(2):
                P.op("pe", lambda e, mt=mt, pden=pden: e.matmul(pden[:, 0:TP], lhsT=K.ones_b[:, :], rhs=E[mt][:],
                                                               start=(mt == 0), stop=(mt == 1)),
                     reads=[K.ones_b, E[mt]], writes=[pden] if mt == 0 else [], accs=[pden] if mt else [],
                     signal=(mt == 1))
            P.op("act", lambda e, pden=pden: e.activation(out=dn[:], in_=pden[:, 0:TP], func=AF.Ln), reads=[pden], writes=[dn])
            P.op("act", lambda e: e.activation(out=dn[:], in_=dn[:], func=AF.Exp, scale=-1.0), reads=[dn], writes=[dn])
            P.op("dve", lambda e, pnum=pnum: e.tensor_tensor(out=yy[:], in0=pnum[:, 0:TP], in1=dn[:], op=ALU.mult),
                 reads=[pnum, dn], writes=[yy])
            P.op("act", lambda e, GX=GX: e.activation(out=GX[:], in_=GX[:], func=AF.Silu), reads=[GX], writes=[GX])
            P.op("dve", lambda e, GX=GX, O=O: e.tensor_tensor(out=O[:], in0=yy[:], in1=GX[:], op=ALU.mult),
                 reads=[yy, GX], writes=[O])
            P.dma("sp", O.c, lambda e, O=O, h=h, t0=t0: e.dma_start(out=odst(h)[:, t0:t0 + TP], in_=O[:]), reads=[O])


class RwConsts:
    def __init__(self, P, c_ui, c_sl, c_reset):
        self.ui = P.tile([128, 256], F32, "rw_ui", persistent=True)
        self.sl = P.tile([128, 128], F32, "rw_sl", persistent=True)
        self.reset = P.tile([64, 1024], F32, "rw_reset", persistent=True)
        P.dma("sp", self.ui.c, lambda e: e.dma_start(out=self.ui[:], in_=c_ui), writes=[self.ui])
        P.dma("sp", self.sl.c, lambda e: e.dma_start(out=self.sl[:], in_=c_sl), writes=[self.sl])
        P.dma("sp", self.reset.c, lambda e: e.dma_start(out=self.reset[:], in_=c_reset), writes=[self.reset])


class Bank:
    def __init__(self, P, name):
        self.t = P.tile([128, 512], F32, name, psum=True)
        self.q = [self.t.r] * 4

    def __getitem__(self, idx):
        return self.t[idx]


def rwkv_stage(P, K, RK, NTOK, rrows, krows, vrows, wdrows, adrows, grows, orows, prm_dram, wup_dram, aup_dram, tag="",
               do_chunk=True, max_it=6):
    TP = 1024
    CH = 128
    npc = NTOK // TP
    ncp = TP // CH
    f = F32
    prm = P.tile([64, 24], f, f"rprm{tag}")
    lup = P.tile([64, 128], f, f"rlup{tag}")
    P.dma("sp", prm.c, lambda e: e.dma_start(out=prm[:, 0:16], in_=prm_dram, allow_slow_non_contiguous=True), writes=[prm])
    P.op("pool", lambda e: e.memset(lup[:], 0.0), writes=[lup])
    P.dma("sp", lup.c, lambda e: e.dma_start(out=lup[0:32, 0:64], in_=wup_dram), accs=[lup])
    P.dma("sp", lup.c, lambda e: e.dma_start(out=lup[32:64, 64:128], in_=aup_dram), accs=[lup])
    P.op("dve", lambda e: e.tensor_scalar(out=prm[:, 16:20], in0=prm[:, 0:4], scalar1=-1.0, scalar2=1.0, op0=ALU.mult,
                                         op1=ALU.add), reads=[prm], accs=[prm])
    P.op("dve", lambda e: e.tensor_scalar(out=prm[:, 20:21], in0=prm[:, 7:8], scalar1=-1.0, scalar2=1.0, op0=ALU.mult,
                                         op1=ALU.add), reads=[prm], accs=[prm])
    xin = {nm: [P.tile([64, TP + 1], f, f"rx{nm}{tag}{i}") for i in range(2)] for nm in ("r", "k", "v", "l")}
    gin = [P.tile([64, TP], f, f"rg{tag}{i}") for i in range(2)]
    T = {nm: P.tile([64, TP], f, f"r_{nm}{tag}") for nm in
         ("R", "Kt", "V", "L", "tmp", "SG", "A", "LW", "cum", "KK", "Kp", "Bv", "e1", "e2",
          "bon", "Y", "t2")}
    for nm in ("Bt", "Ktl", "Bh", "Kh", "Vb"):
        T[nm] = P.tile([64, TP], BF16, f"r_{nm}{tag}")
    AR = P.tile([64, ncp, 256], BF16, f"r_AR{tag}")
    ARf = P.tile([64, ncp, 128], f, f"r_ARf{tag}")
    ob = [P.tile([64, TP], BF16, f"rob{tag}{i}") for i in range(2)]
    Sb = [P.tile([64, 64], f, f"rS{tag}{i}") for i in range(4)]
    G = 3
    banks = [(Bank(P, f"rbA{tag}{g}"), Bank(P, f"rbB{tag}{g}")) for g in range(G)]
    ctxs = []
    for par in range(2):
        row = []
        for g in range(G):
            c = dict(bA=banks[g][0], bB=banks[g][1], bC=banks[g][1])
            for nm, shp in (("Gm1", [128, 256]), ("Gm2", [128, 256]), ("P0", [128, 128]), ("P1", [128, 128]),
                            ("PT0", [128, 128]), ("PT1", [128, 128]), ("T", [128, 128]), ("TM", [128, 320]),
                            ("AU", [128, 128]), ("Mt", [64, 64]), ("Qt", [64, 128])):
                c[nm] = P.tile(shp, f if nm in ("Mt", "Qt") else BF16, f"rc{nm}{tag}{par}{g}")
            row.append(c)
        ctxs.append(row)
    ngrp = 0
    bY = Bank(P, f"rbY{tag}")
    bS = Bank(P, f"rbS{tag}")
    bN = [banks[0][1], banks[1][1]]
    nbn = 0
    ones64 = K.ones_f[0:64, 0:64]
    id64 = K.identf[0:64, 0:64]
    P.op("dve", lambda e: e.memset(Sb[0][:], 0.0), writes=[Sb[0]])
    sidx = 0

    def v3(t):
        return t[:, :].rearrange("p (c t) -> p c t", t=CH)

    def ones_mm(src_ap_fn, nsub, consume):
        nonlocal nbn
        for sb_ in range(nsub):
            bk = bN[nbn % 2]
            nbn += 1
            rd = src_ap_fn(sb_)
            P.op("pe", lambda e, bk=bk, rd=rd: e.matmul(bk[0:64, :], lhsT=ones64, rhs=rd[0], start=True, stop=True),
                 reads=[K.ones_f] + rd[1], writes=bk.q)
            consume(sb_, bk)

    def serial_phase(grp, ctx):
        nonlocal sidx
        for g, c in enumerate(grp):
            C = ctx[g]
            S0 = Sb[sidx % 4]
            S1 = Sb[(sidx + 1) % 4]
            sidx += 1
            ycol = (c % 4) * 128
            yq = bY.q[c % 4]
            P.op("pe", lambda e, S0=S0, C=C, ycol=ycol: e.matmul(bY[0:64, ycol:ycol + 128], lhsT=S0[:], rhs=C["Qt"][:],
                                                                start=True, stop=False),
                 reads=[S0, C["Qt"]], writes=[yq], signal=False)
            P.op("pe", lambda e, C=C, ycol=ycol: e.matmul(bY[0:64, ycol:ycol + 128], lhsT=C["AU"][:, 64:128],
                                                         rhs=C["Gm1"][:, 128:256], start=False, stop=False),
                 reads=[C["AU"], C["Gm1"]], accs=[yq], signal=False)
            P.op("pe", lambda e, C=C, ycol=ycol: e.matmul(bY[0:64, ycol:ycol + 128], lhsT=C["TM"][:, 128:192],
                                                         rhs=C["Gm2"][:, 128:256], start=False, stop=True),
                 reads=[C["TM"], C["Gm2"]], accs=[yq], signal=False)
            P.op("pe", lambda e, C=C: e.matmul(bS[0:64, 0:64], lhsT=C["TM"][:, 192:256], rhs=C["AU"][:, 64:128],
                                               start=True, stop=False), reads=[C["TM"], C["AU"]], writes=[bS.q[0]], signal=False)
            P.op("pe", lambda e, C=C: e.matmul(bS[0:64, 0:64], lhsT=C["TM"][:, 256:320], rhs=C["TM"][:, 128:192],
                                               start=False, stop=False), reads=[C["TM"]], accs=[bS.q[0]], signal=False)
            P.op("pe", lambda e, C=C, S0=S0: e.matmul(bS[0:64, 0:64], lhsT=C["Mt"][:], rhs=S0[:], start=False, stop=True),
                 reads=[C["Mt"], S0], accs=[bS.q[0]])
            P.op("act", lambda e, S1=S1: e.activation(out=S1[:], in_=bS[0:64, 0:64], func=AF.Copy),
                 reads=[bS.q[0]], writes=[S1])
            if c % 4 == 3:
                y0 = (c - 3) * CH
                P.op("act", lambda e, y0=y0: e.activation(out=T["Y"][:, y0:y0 + 512], in_=bY[0:64, :], func=AF.Copy),
                     reads=bY.q, writes=[T["Y"]] if c == 3 else [], accs=[] if c == 3 else [T["Y"]])

    for pi in range(npc):
        s = pi % 2
        t0 = pi * TP
        XR, XK, XV, XL, GX, O = xin["r"][s], xin["k"][s], xin["v"][s], xin["l"][s], gin[s], ob[s]
        for X, rows_list in ((XR, [(rrows, 0, 64)]), (XK, [(krows, 0, 64)]), (XV, [(vrows, 0, 64)]),
                             (XL, [(wdrows, 0, 32), (adrows, 32, 64)])):
            first = True
            if pi == 0:
                P.op("pool", lambda e, X=X: e.memset(X[:, 0:1], 0.0), writes=[X])
                first = False
            for (src, p0, p1) in rows_list:
                if pi == 0:
                    P.dma("sp", X.c, lambda e, X=X, src=src, p0=p0, p1=p1: e.dma_start(out=X[p0:p1, 1:TP + 1], in_=src[:, 0:TP]),
                          accs=[X])
                else:
                    P.dma("sp", X.c, lambda e, X=X, src=src, p0=p0, p1=p1, t0=t0: e.dma_start(
                        out=X[p0:p1, :], in_=src[:, t0 - 1:t0 + TP]), writes=[X] if first else [], accs=[] if first else [X])
                first = False
        P.dma("sp", GX.c, lambda e, GX=GX, t0=t0: e.dma_start(out=GX[:], in_=grows[:, t0:t0 + TP]), writes=[GX])
        for X, dst, col in ((XR, T["R"], 0), (XK, T["Kt"], 1), (XV, T["V"], 2), (XL, T["L"], 3)):
            P.op("act", lambda e, X=X, col=col: e.activation(out=T["tmp"][:], in_=X[:, 0:TP], func=AF.Copy,
                                                            scale=prm[:, col:col + 1]), reads=[X, prm], writes=[T["tmp"]])
            P.op("dve", lambda e, X=X, dst=dst, col=col: e.scalar_tensor_tensor(
                out=dst[:], in0=X[:, 1:TP + 1], scalar=prm[:, 16 + col:17 + col], in1=T["tmp"][:], op0=ALU.mult, op1=ALU.add),
                reads=[X, prm, T["tmp"]], writes=[dst])
        P.op("act", lambda e: e.activation(out=T["L"][0:32, :], in_=T["L"][0:32, :], func=AF.Tanh), reads=[T["L"]], writes=[T["L"]])
        for (lo, hi, bcol, dst) in ((0, 32, 4, T["SG"]), (32, 64, 5, T["A"])):
            for sb_ in range(TP // 512):
                bk = bN[nbn % 2]
                nbn += 1
                P.op("pe", lambda e, bk=bk, lo=lo, hi=hi, sb_=sb_: e.matmul(bk[0:64, :], lhsT=lup[:, 2 * lo:2 * lo + 64],
                                                                           rhs=T["L"][:, sb_ * 512:(sb_ + 1) * 512],
                                                                           start=True, stop=True),
                     reads=[lup, T["L"]], writes=bk.q)
                P.op("act", lambda e, bk=bk, dst=dst, sb_=sb_, bcol=bcol: e.activation(
                    out=dst[:, sb_ * 512:(sb_ + 1) * 512], in_=bk[0:64, :], func=AF.Sigmoid, bias=prm[:, bcol:bcol + 1]),
                    reads=bk.q + [prm], writes=[dst] if sb_ == 0 else [], accs=[] if sb_ == 0 else [dst])
        P.op("dve", lambda e: e.tensor_scalar(out=T["LW"][:], in0=T["SG"][:], scalar1=-0.6065306597126334, scalar2=None,
                                             op0=ALU.mult), reads=[T["SG"]], writes=[T["LW"]])
        P.op("dve", lambda e: e.tensor_tensor_scan(out=T["cum"][:], data0=RK.reset[:, 0:TP], data1=T["LW"][:], initial=0.0,
                                                  op0=ALU.mult, op1=ALU.add), reads=[RK.reset, T["LW"]], writes=[T["cum"]])
        P.op("dve", lambda e: e.tensor_tensor(out=T["tmp"][:], in0=T["cum"][:], in1=T["LW"][:], op=ALU.subtract),
             reads=[T["cum"], T["LW"]], writes=[T["tmp"]])
        P.op("act", lambda e: e.activation(out=T["e1"][:], in_=T["tmp"][:], func=AF.Exp), reads=[T["tmp"]], writes=[T["e1"]])
        P.op("act", lambda e: e.activation(out=T["KK"][:], in_=T["Kt"][:], func=AF.Copy, scale=prm[:, 6:7]),
             reads=[T["Kt"], prm], writes=[T["KK"]])
        P.op("act", lambda e: e.activation(out=T["tmp"][:], in_=T["KK"][:], func=AF.Square),
             reads=[T["KK"]], writes=[T["tmp"]])

        def cons_kk(sb_, bk):
            P.op("act", lambda e, bk=bk, sb_=sb_: e.activation(out=T["t2"][:, sb_ * 512:(sb_ + 1) * 512], in_=bk[0:64, :],
                                                              func=AF.Ln, bias=1e-24), reads=bk.q,
                 writes=[T["t2"]] if sb_ == 0 else [], accs=[] if sb_ == 0 else [T["t2"]])
        ones_mm(lambda sb_: (T["tmp"][:, sb_ * 512:(sb_ + 1) * 512], [T["tmp"]]), TP // 512, cons_kk)
        P.op("act", lambda e: e.activation(out=T["t2"][:], in_=T["t2"][:], func=AF.Exp, scale=-0.5),
             reads=[T["t2"]], writes=[T["t2"]])
        P.op("dve", lambda e: e.tensor_tensor(out=T["KK"][:], in0=T["KK"][:], in1=T["t2"][:], op=ALU.mult),
             reads=[T["KK"], T["t2"]], writes=[T["KK"]])
        P.op("dve", lambda e: e.scalar_tensor_tensor(out=AR[:, :, 0:128], in0=v3(T["KK"]), scalar=-1.0, in1=v3(T["e1"]),
                                                    op0=ALU.mult, op1=ALU.mult), reads=[T["KK"], T["e1"]], writes=[AR])
        P.op("act", lambda e: e.activation(out=T["tmp"][:], in_=T["A"][:], func=AF.Identity, scale=prm[:, 7:8],
                                           bias=prm[:, 20:21]), reads=[T["A"], prm], writes=[T["tmp"]])
        P.op("dve", lambda e: e.tensor_tensor(out=T["Kp"][:], in0=T["Kt"][:], in1=T["tmp"][:], op=ALU.mult),
             reads=[T["Kt"], T["tmp"]], writes=[T["Kp"]])
        P.op("dve", lambda e: e.tensor_tensor(out=T["Bv"][:], in0=T["KK"][:], in1=T["A"][:], op=ALU.mult),
             reads=[T["KK"], T["A"]], writes=[T["Bv"]])
        P.op("act", lambda e: e.activation(out=T["e1"][:], in_=T["cum"][:], func=AF.Exp), reads=[T["cum"]], writes=[T["e1"]])
        P.op("act", lambda e: e.activation(out=T["e2"][:], in_=T["cum"][:], func=AF.Exp, scale=-1.0), reads=[T["cum"]],
             writes=[T["e2"]])
        P.op("dve", lambda e: e.tensor_tensor(out=ARf[:, :, :], in0=v3(T["R"]), in1=v3(T["e1"]), op=ALU.mult),
             reads=[T["R"], T["e1"]], writes=[ARf])
        P.op("act", lambda e: e.activation(out=AR[:, :, 128:256], in_=ARf[:, :, :], func=AF.Copy), reads=[ARf], accs=[AR])
        P.op("act", lambda e: e.activation(out=T["Vb"][:], in_=T["V"][:], func=AF.Copy), reads=[T["V"]], writes=[T["Vb"]])
        P.op("dve", lambda e: e.tensor_tensor(out=T["Bt"][:], in0=T["Bv"][:], in1=T["e2"][:], op=ALU.mult),
             reads=[T["Bv"], T["e2"]], writes=[T["Bt"]])
        P.op("dve", lambda e: e.tensor_tensor(out=T["Ktl"][:], in0=T["Kp"][:], in1=T["e2"][:], op=ALU.mult),
             reads=[T["Kp"], T["e2"]], writes=[T["Ktl"]])
        for c in range(ncp):
            cs = slice(c * CH, (c + 1) * CH)
            ge = slice(c * CH + CH - 1, c * CH + CH)
            P.op("dve", lambda e, cs=cs, ge=ge: e.tensor_scalar(out=T["Bh"][:, cs], in0=T["Bt"][:, cs], scalar1=T["e1"][:, ge],
                                                               scalar2=None, op0=ALU.mult),
                 reads=[T["Bt"], T["e1"]], writes=[T["Bh"]] if c == 0 else [], accs=[] if c == 0 else [T["Bh"]])
            P.op("act", lambda e, cs=cs, ge=ge: e.activation(out=T["Kh"][:, cs], in_=T["Ktl"][:, cs], func=AF.Copy,
                                                            scale=T["e1"][:, ge]),
                 reads=[T["Ktl"], T["e1"]], writes=[T["Kh"]] if c == 0 else [], accs=[] if c == 0 else [T["Kh"]])
        P.op("dve", lambda e: e.scalar_tensor_tensor(out=T["tmp"][:], in0=T["R"][:], scalar=prm[:, 8:9], in1=T["Kp"][:],
                                                     op0=ALU.mult, op1=ALU.mult), reads=[T["R"], prm, T["Kp"]], writes=[T["tmp"]])

        def cons_bon(sb_, bk):
            P.op("dve", lambda e, bk=bk, sb_=sb_: e.tensor_tensor(out=T["bon"][:, sb_ * 512:(sb_ + 1) * 512], in0=bk[0:64, :],
                                                                 in1=T["V"][:, sb_ * 512:(sb_ + 1) * 512], op=ALU.mult),
                 reads=bk.q + [T["V"]], writes=[T["bon"]] if sb_ == 0 else [], accs=[] if sb_ == 0 else [T["bon"]])
        ones_mm(lambda sb_: (T["tmp"][:, sb_ * 512:(sb_ + 1) * 512], [T["tmp"]]), TP // 512, cons_bon)

        if not do_chunk:
            P.op("dve", lambda e: e.memset(T["Y"][:], 0.0), writes=[T["Y"]])
        groups = [list(range(i, min(i + G, ncp))) for i in range(0, ncp, G)] if do_chunk else []
        pending_serial = None
        for grp in groups:
            steps = []
            ctx = ctxs[ngrp % 2]
            ngrp += 1
            for g, c in enumerate(grp):
                C = ctx[g]
                bA, bB = C["bA"], C["bB"]
                cs = slice(c * CH, (c + 1) * CH)
                st = []

                def sA(C=C, bA=bA, bB=bB, c=c, cs=cs):
                    P.op("pe", lambda e: e.matmul(bA[:, 0:256], lhsT=T["Bt"][:, cs], rhs=AR[:, c, :], start=True, stop=True),
                         reads=[T["Bt"], AR], writes=bA.q, signal=False)
                    P.op("pe", lambda e: e.matmul(bA[:, 256:512], lhsT=T["Ktl"][:, cs], rhs=AR[:, c, :], start=True, stop=True),
                         reads=[T["Ktl"], AR], accs=bA.q, signal=False)
                    P.op("pe", lambda e: e.matmul(bB[:, 0:128], lhsT=AR[:, c, 0:128], rhs=T["Bt"][:, cs], start=True, stop=True),
                         reads=[T["Bt"], AR], writes=[bB.q[0]], signal=False)
                    for i4, src in enumerate((AR[:, c, 0:128], T["Vb"][:, cs], T["Bh"][:, cs], T["Kh"][:, cs])):
                        P.op("pe", lambda e, src=src, i4=i4: e.matmul(bB[:, 128 + i4 * 64:192 + i4 * 64], lhsT=src,
                                                                     rhs=K.identb[0:64, 0:64], start=True, stop=True),
                             reads=[AR, T["Vb"], T["Bh"], T["Kh"], K.identb], writes=[bB.q[1], bB.q[2]] if i4 == 0 else [],
                             accs=[] if i4 == 0 else [bB.q[1], bB.q[2]], signal=(i4 == 3))
                st.append(sA)

                def sAe(C=C, bA=bA, bB=bB):
                    P.op("dve", lambda e: e.tensor_tensor(out=C["Gm1"][:], in0=bA[:, 0:256], in1=RK.ui[:], op=ALU.mult),
                         reads=[bA.q[0], bA.q[1], RK.ui], writes=[C["Gm1"]])
                    P.op("dve", lambda e: e.tensor_tensor(out=C["Gm2"][:], in0=bA[:, 256:512], in1=RK.ui[:], op=ALU.mult),
                         reads=[bA.q[2], bA.q[3], RK.ui], writes=[C["Gm2"]])
                    P.op("dve", lambda e: e.tensor_tensor(out=C["PT0"][:], in0=bB[:, 0:128], in1=RK.sl[:], op=ALU.mult),
                         reads=[bB.q[0], RK.sl], writes=[C["PT0"]])
                    P.op("pool", lambda e: e.tensor_tensor(out=C["T"][:], in0=C["Gm1"][:, 0:128], in1=K.identb[:], op=ALU.add),
                         reads=[C["Gm1"], K.identb], writes=[C["T"]])
                    P.op("act", lambda e: e.activation(out=C["TM"][:, 0:64], in_=bB[:, 128:192], func=AF.Copy),
                         reads=[bB.q[1]], writes=[C["TM"]])
                    P.op("act", lambda e: e.activation(out=C["TM"][:, 128:320], in_=bB[:, 192:384], func=AF.Copy),
                         reads=[bB.q[1], bB.q[2]], accs=[C["TM"]])
                st.append(sAe)
                for it in range(6):
                    last = (it == 5)

                    def sI1(C=C, bA=bA, it=it, last=last):
                        Pc = C["Gm1"] if it == 0 else C[f"P{it % 2}"]
                        Pc_ap = C["Gm1"][:, 0:128] if it == 0 else C[f"P{it % 2}"][:]
                        PTc = C["PT0"] if it == 0 else C[f"PT{it % 2}"]
                        if not last:
                            P.op("pe", lambda e: e.matmul(bA[:, 0:128], lhsT=PTc[:], rhs=Pc_ap, start=True, stop=True),
                                 reads=[PTc, Pc], writes=[bA.q[0]], signal=False)
                        P.op("pe", lambda e: e.matmul(bA[:, 128:256], lhsT=Pc_ap, rhs=PTc[:], start=True, stop=True),
                             reads=[PTc, Pc], writes=[bA.q[1]])
                    st.append(sI1)

                    def sI2(C=C, bA=bA, it=it, last=last):
                        Pn = C[f"P{(it + 1) % 2}"]
                        PTn = C[f"PT{(it + 1) % 2}"]
                        if it == 0:
                            PTn = C["PT1"]
                        P.op("act", lambda e: e.activation(out=PTn[:], in_=bA[:, 128:256], func=AF.Copy),
                             reads=[bA.q[1]], writes=[PTn])
                        if not last:
                            P.op("act", lambda e: e.activation(out=Pn[:], in_=bA[:, 0:128], func=AF.Copy),
                                 reads=[bA.q[0]], writes=[Pn])
                    st.append(sI2)

                    def sI3(C=C, bC=C["bC"], it=it):
                        PTn = C[f"PT{(it + 1) % 2}"]
                        if it == 0:
                            PTn = C["PT1"]
                        P.op("pe", lambda e: e.matmul(bC[:, 0:128], lhsT=PTn[:], rhs=C["T"][:], start=True, stop=True),
                             reads=[PTn, C["T"]], writes=[bC.q[0]])
                    st.append(sI3)

                    def sI4(C=C, bC=C["bC"]):
                        P.op("dve", lambda e: e.tensor_tensor(out=C["T"][:], in0=bC[:, 0:128], in1=C["T"][:], op=ALU.add),
                             reads=[bC.q[0], C["T"]], writes=[C["T"]])
                    st.append(sI4)

                def sW(C=C, bB=bB):
                    P.op("pe", lambda e: e.matmul(bB[:, 384:448], lhsT=C["Gm2"][:, 0:128], rhs=C["TM"][:, 128:192],
                                                  start=True, stop=True), reads=[C["Gm2"], C["TM"]], writes=[bB.q[3]])
                st.append(sW)

                def sWe(C=C, bB=bB):
                    P.op("act", lambda e: e.activation(out=C["TM"][:, 64:128], in_=bB[:, 384:448], func=AF.Copy),
                         reads=[bB.q[3]], accs=[C["TM"]])
                st.append(sWe)

                def sAU(C=C, bA=bA):
                    P.op("pe", lambda e: e.matmul(bA[:, 384:512], lhsT=C["T"][:], rhs=C["TM"][:, 0:128], start=True, stop=True),
                         reads=[C["T"], C["TM"]], writes=[bA.q[3]])
                st.append(sAU)

                def sAUe(C=C, bA=bA):
                    P.op("act", lambda e: e.activation(out=C["AU"][:], in_=bA[:, 384:512], func=AF.Copy),
                         reads=[bA.q[3]], writes=[C["AU"]])
                st.append(sAUe)

                def sMQ(C=C, bB=bB):
                    P.op("pe", lambda e: e.matmul(bB[0:64, 448:512], lhsT=C["AU"][:, 0:64], rhs=C["TM"][:, 192:256],
                                                  start=True, stop=True), reads=[C["AU"], C["TM"]], writes=[bB.q[3]], signal=False)
                    P.op("pe", lambda e: e.matmul(bB[0:64, 0:128], lhsT=C["AU"][:, 0:64], rhs=C["Gm1"][:, 128:256],
                                                  start=True, stop=True), reads=[C["AU"], C["Gm1"]], writes=[bB.q[0]])
                st.append(sMQ)

                def sMQe(C=C, bB=bB, c=c, cs=cs):
                    ge = slice(c * CH + CH - 1, c * CH + CH)
                    P.op("dve", lambda e: e.scalar_tensor_tensor(out=C["Mt"][:], in0=id64, scalar=T["e1"][:, ge],
                                                                in1=bB[0:64, 448:512], op0=ALU.mult, op1=ALU.add),
                         reads=[K.identf, T["e1"], bB.q[3]], writes=[C["Mt"]])
                    P.op("dve", lambda e: e.tensor_tensor(out=C["Qt"][:], in0=bB[0:64, 0:128], in1=ARf[:, c, :], op=ALU.add),
                         reads=[bB.q[0], ARf], writes=[C["Qt"]])
                st.append(sMQe)
                steps.append(st)
            for si in range(len(steps[0])):
                for g in range(len(grp)):
                    steps[g][si]()
            if pending_serial is not None:
                pending_serial()
            pending_serial = (lambda grp=grp, ctx=ctx: serial_phase(grp, ctx))
        if pending_serial is not None:
            pending_serial()
            pending_serial = None
        def cons_mean(sb_, bk):
            ss = slice(sb_ * 512, (sb_ + 1) * 512)
            P.op("dve", lambda e, bk=bk, ss=ss: e.scalar_tensor_tensor(out=T["tmp"][:, ss], in0=bk[0:64, :], scalar=-1.0 / 64,
                                                                      in1=T["Y"][:, ss], op0=ALU.mult, op1=ALU.add),
                 reads=bk.q + [T["Y"]], writes=[T["tmp"]] if sb_ == 0 else [], accs=[] if sb_ == 0 else [T["tmp"]])
        ones_mm(lambda sb_: (T["Y"][:, sb_ * 512:(sb_ + 1) * 512], [T["Y"]]), TP // 512, cons_mean)
        P.op("act", lambda e: e.activation(out=T["t2"][:], in_=T["tmp"][:], func=AF.Square),
             reads=[T["tmp"]], writes=[T["t2"]])

        def cons_var(sb_, bk):
            ss = slice(sb_ * 512, (sb_ + 1) * 512)
            P.op("act", lambda e, bk=bk, ss=ss: e.activation(out=T["e2"][:, ss], in_=bk[0:64, :], func=AF.Ln, scale=1.0 / 64,
                                                            bias=64e-5), reads=bk.q,
                 writes=[T["e2"]] if sb_ == 0 else [], accs=[] if sb_ == 0 else [T["e2"]])
        ones_mm(lambda sb_: (T["t2"][:, sb_ * 512:(sb_ + 1) * 512], [T["t2"]]), TP // 512, cons_var)
        P.op("act", lambda e: e.activation(out=T["e2"][:], in_=T["e2"][:], func=AF.Exp, scale=-0.5), reads=[T["e2"]], writes=[T["e2"]])
        P.op("dve", lambda e: e.tensor_tensor(out=T["tmp"][:], in0=T["tmp"][:], in1=T["e2"][:], op=ALU.mult),
             reads=[T["tmp"], T["e2"]], writes=[T["tmp"]])
        P.op("dve", lambda e: e.tensor_scalar(out=T["tmp"][:], in0=T["tmp"][:], scalar1=prm[:, 9:10], scalar2=prm[:, 10:11],
                                             op0=ALU.mult, op1=ALU.add), reads=[T["tmp"], prm], writes=[T["tmp"]])
        P.op("dve", lambda e: e.tensor_tensor(out=T["tmp"][:], in0=T["tmp"][:], in1=T["bon"][:], op=ALU.add),
             reads=[T["tmp"], T["bon"]], writes=[T["tmp"]])
        P.op("act", lambda e, GX=GX: e.activation(out=GX[:], in_=GX[:], func=AF.Silu), reads=[GX], writes=[GX])
        P.op("dve", lambda e, GX=GX, O=O: e.tensor_tensor(out=O[:], in0=T["tmp"][:], in1=GX[:], op=ALU.mult),
             reads=[T["tmp"], GX], writes=[O])
        P.dma("sp", O.c, lambda e, O=O, t0=t0: e.dma_start(out=orows[:, t0:t0 + TP], in_=O[:]), reads=[O])


def swa_stage4(P, K, NTOK, qrows, krows, vrows, grows, orows, qg, kg, sinks, TP=512, tag=""):
    npc = NTOK // TP
    nbk = TP // 128
    NH = 4
    prm = P.tile([64, 16], F32, f"s4prm{tag}")
    P.dma("sp", prm.c, lambda e: e.dma_start(out=prm[:, 0:1], in_=qg, allow_slow_non_contiguous=True), writes=[prm])
    P.dma("sp", prm.c, lambda e: e.dma_start(out=prm[:, 1:2], in_=kg, allow_slow_non_contiguous=True), accs=[prm])
    for h in range(NH):
        P.dma("sp", prm.c, lambda e, h=h: e.dma_start(out=prm[:, 4 + h:5 + h], in_=sinks[h], allow_slow_non_contiguous=True),
              accs=[prm])
    P.op("dve", lambda e: e.tensor_scalar(out=prm[:, 3:4], in0=prm[:, 0:1], scalar1=0.125, scalar2=None, op0=ALU.mult),
         reads=[prm], accs=[prm])
    P.op("act", lambda e: e.activation(out=prm[:, 8:12], in_=prm[:, 4:8], func=AF.Exp), reads=[prm], accs=[prm])
    sinkrow = P.tile([64, NH, 128], F32, f"s4sink{tag}")
    for h in range(NH):
        P.op("act", lambda e, h=h: e.activation(out=sinkrow[:, h, :], in_=K.ones_f[0:64, 0:128], func=AF.Identity, scale=0.0,
                                               bias=prm[:, 8 + h:9 + h]), reads=[K.ones_f, prm],
             writes=[sinkrow] if h == 0 else [], accs=[] if h == 0 else [sinkrow])
    mask4 = P.tile([128, 2, NH, 128], BF16, f"s4mask{tag}")
    for j in range(2):
        for h in range(NH):
            P.op("pool", lambda e, j=j, h=h: e.tensor_copy(out=mask4[:, j, h, :], in_=K.swamask[:, j * 128:(j + 1) * 128]),
                 reads=[K.swamask], writes=[mask4] if (j == 0 and h == 0) else [], accs=[] if (j == 0 and h == 0) else [mask4])
    W = TP + 128
    kx = [P.tile([64, W], F32, f"s4kx{tag}{i}") for i in range(2)]
    vx = [P.tile([64, W], F32, f"s4vx{tag}{i}") for i in range(2)]
    qx = [P.tile([64, NH, TP], F32, f"s4qx{tag}{i}") for i in range(2)]
    gx = [P.tile([64, NH, TP], F32, f"s4gx{tag}{i}") for i in range(2)]
    sq = P.tile([64, NH * TP], F32, f"s4sq{tag}")
    rs = P.tile([64, NH * TP], F32, f"s4rs{tag}")
    kn = P.tile([64, W], BF16, f"s4kn{tag}")
    qn = P.tile([64, NH, TP], BF16, f"s4qn{tag}")
    vb = P.tile([128, nbk + 1, 64], BF16, f"s4vb{tag}")
    E = [[P.tile([128, NH, 128], BF16, f"s4E{tag}{i}{j}") for j in range(2)] for i in range(2)]
    dn = P.tile([64, NH, 128], F32, f"s4dn{tag}")
    yy = P.tile([64, NH, TP], F32, f"s4yy{tag}")
    ob = [P.tile([64, NH, TP], BF16, f"s4ob{tag}{i}") for i in range(2)]
    pn = [P.tile([64, 512], F32, f"s4pn{tag}{i}", psum=True) for i in range(2)]
    psc = [[P.tile([128, 512], F32, f"s4psc{tag}{i}{j}", psum=True) for j in range(2)] for i in range(2)]
    pnum = P.tile([64, 512], F32, f"s4pnum{tag}", psum=True)
    pden = P.tile([64, 512], F32, f"s4pden{tag}", psum=True)
    npn = 0
    nsc = 0

    def norm(src_ap, src_t, width, gcol, dst_ap, dst_t, first_dst=True):
        nonlocal npn
        P.op("act", lambda e: e.activation(out=sq[:, 0:width], in_=src_ap, func=AF.Square), reads=[src_t], writes=[sq])
        o = 0
        first = True
        while o < width:
            w_ = min(512, width - o)
            pb = pn[npn % 2]
            npn += 1
            P.op("pe", lambda e, pb=pb, o=o, w_=w_: e.matmul(pb[:, 0:w_], lhsT=K.ones_f[0:64, 0:64], rhs=sq[:, o:o + w_],
                                                            start=True, stop=True), reads=[K.ones_f, sq], writes=[pb])
            P.op("act", lambda e, pb=pb, o=o, w_=w_: e.activation(out=rs[:, o:o + w_], in_=pb[:, 0:w_], func=AF.Ln,
                                                                 scale=1.0 / 64, bias=1e-6),
                 reads=[pb], writes=[rs] if first else [], accs=[] if first else [rs])
            first = False
            o += w_
        P.op("act", lambda e: e.activation(out=rs[:, 0:width], in_=rs[:, 0:width], func=AF.Exp, scale=-0.5),
             reads=[rs], writes=[rs])
        P.op("dve", lambda e: e.scalar_tensor_tensor(out=dst_ap, in0=src_ap, scalar=prm[:, gcol:gcol + 1], in1=rs[:, 0:width],
                                                    op0=ALU.mult, op1=ALU.mult), reads=[src_t, prm, rs],
             writes=[dst_t] if first_dst else [], accs=[] if first_dst else [dst_t])

    for pi in range(npc):
        s = pi % 2
        t0 = pi * TP
        KX, VX, QX, GX, O = kx[s], vx[s], qx[s], gx[s], ob[s]
        lo = 128 if pi == 0 else 0
        P.dma("sp", KX.c, lambda e, KX=KX, t0=t0, lo=lo: e.dma_start(out=KX[:, lo:W], in_=krows[:, t0 - 128 + lo:t0 + TP]),
              writes=[KX])
        P.dma("sp", VX.c, lambda e, VX=VX, t0=t0, lo=lo: e.dma_start(out=VX[:, lo:W], in_=vrows[:, t0 - 128 + lo:t0 + TP]),
              writes=[VX])
        for h in range(NH):
            P.dma("sp", QX.c, lambda e, QX=QX, t0=t0, h=h: e.dma_start(out=QX[:, h, :], in_=qrows[h][:, t0:t0 + TP]),
                  writes=[QX] if h == 0 else [], accs=[] if h == 0 else [QX])
            P.dma("sp", GX.c, lambda e, GX=GX, t0=t0, h=h: e.dma_start(out=GX[:, h, :], in_=grows[h][:, t0:t0 + TP]),
                  writes=[GX] if h == 0 else [], accs=[] if h == 0 else [GX])
        norm(KX[:, lo:W], KX, W - lo, 1, kn[:, lo:W], kn)
        norm(QX[:, :, :].rearrange("p h t -> p (h t)"), QX, NH * TP, 3, qn[:, :, :].rearrange("p h t -> p (h t)"), qn)
        b0 = lo // 128
        pt = pn[npn % 2]
        npn += 1
        for b in range(b0, nbk + 1):
            P.op("pe", lambda e, VX=VX, b=b, pt=pt: e.transpose(out=psc[0][0][:, b * 64:(b + 1) * 64], in_=VX[:, b * 128:(b + 1) * 128],
                                                               identity=K.identf[0:64, 0:64]),
                 reads=[VX, K.identf], writes=[psc[0][0]] if b == b0 else [], accs=[] if b == b0 else [psc[0][0]],
                 signal=(b == nbk))
        P.op("act", lambda e, b0=b0: e.activation(out=vb[:, b0:nbk + 1, :],
                                                 in_=psc[0][0][:, b0 * 64:(nbk + 1) * 64].rearrange("p (b d) -> p b d", d=64),
                                                 func=AF.Copy), reads=[psc[0][0]], writes=[vb])
        for n in range(nbk):
            has_prev = not (pi == 0 and n == 0)
            par = nsc % 2
            nsc += 1
            qs = qn[:, :, n * 128:(n + 1) * 128]
            srcs = [(0, kn[:, (n + 1) * 128:(n + 2) * 128])]
            if has_prev:
                srcs.append((1, kn[:, n * 128:(n + 1) * 128]))
            for j, kap in srcs:
                sc = psc[par][j]
                Eb = E[par][j]
                P.op("pe", lambda e, sc=sc, kap=kap, qs=qs: e.matmul(sc[:, :], lhsT=kap, rhs=qs, start=True, stop=True),
                     reads=[kn, qn], writes=[sc])
                P.op("act", lambda e, sc=sc, Eb=Eb: e.activation(out=Eb[:, :, :],
                                                                in_=sc[:, :].rearrange("p (h q) -> p h q", q=128), func=AF.Exp),
                     reads=[sc], writes=[Eb])
                P.op("dve" if j == 0 else "pool", lambda e, Eb=Eb, j=j: e.tensor_tensor(out=Eb[:, :, :], in0=Eb[:, :, :],
                                                                                       in1=mask4[:, j, :, :], op=ALU.mult),
                     reads=[Eb, mask4], writes=[Eb])
            for (pacc, lhs_cur, lhs_prev) in ((pnum, vb[:, n + 1, :], vb[:, n, :]),
                                              (pden, K.ones_b[:, 0:64], K.ones_b[:, 0:64])):
                P.op("pe", lambda e, pacc=pacc, lhs_cur=lhs_cur, par=par, has_prev=has_prev: e.matmul(
                    pacc[:, :], lhsT=lhs_cur, rhs=E[par][0][:, :, :], start=True, stop=not has_prev),
                    reads=[vb, K.ones_b, E[par][0]], writes=[pacc], signal=False)
                if has_prev:
                    P.op("pe", lambda e, pacc=pacc, lhs_prev=lhs_prev, par=par: e.matmul(
                        pacc[:, :], lhsT=lhs_prev, rhs=E[par][1][:, :, :], start=False, stop=True),
                        reads=[vb, K.ones_b, E[par][1]], accs=[pacc], signal=False)
            P.signal_last("pe")
            P.op("dve", lambda e: e.tensor_tensor(out=dn[:, :, :], in0=pden[:, :].rearrange("p (h q) -> p h q", q=128),
                                                 in1=sinkrow[:, :, :], op=ALU.add), reads=[pden, sinkrow], writes=[dn])
            P.op("act", lambda e: e.activation(out=dn[:, :, :], in_=dn[:, :, :], func=AF.Ln), reads=[dn], writes=[dn])
            P.op("act", lambda e: e.activation(out=dn[:, :, :], in_=dn[:, :, :], func=AF.Exp, scale=-1.0), reads=[dn], writes=[dn])
            P.op("dve", lambda e, n=n: e.tensor_tensor(out=yy[:, :, n * 128:(n + 1) * 128],
                                                      in0=pnum[:, :].rearrange("p (h q) -> p h q", q=128), in1=dn[:, :, :],
                                                      op=ALU.mult), reads=[pnum, dn],
                 writes=[yy] if n == 0 else [], accs=[] if n == 0 else [yy])
        P.op("act", lambda e, GX=GX: e.activation(out=GX[:, :, :], in_=GX[:, :, :], func=AF.Silu), reads=[GX], writes=[GX])
        P.op("dve", lambda e, GX=GX, O=O: e.tensor_tensor(out=O[:, :, :], in0=yy[:, :, :], in1=GX[:, :, :], op=ALU.mult),
             reads=[yy, GX], writes=[O])
        for h in range(NH):
            P.dma("sp", O.c, lambda e, O=O, t0=t0, h=h: e.dma_start(out=orows[h][:, t0:t0 + TP], in_=O[:, h, :]), reads=[O])


import ml_dtypes

NCORES = 8
SEQ = 16384
NT = SEQ // NCORES
NCH_SEQ = 69


def _consts_np():
    s_ = np.arange(128)[:, None]
    t_ = np.arange(128)[None, :]
    reset = np.ones((64, 1024), np.float32)
    reset[:, ::128] = 0
    return dict(
        c_if=np.eye(128, dtype=np.float32),
        c_ib=np.eye(128).astype(ml_dtypes.bfloat16),
        c_mask=np.concatenate([(s_ <= t_), (s_ > t_)], axis=1).astype(ml_dtypes.bfloat16),
        c_ui=np.concatenate([(t_ > s_), (t_ >= s_)], axis=1).astype(np.float32),
        c_sl=(s_ > t_).astype(np.float32),
        c_reset=reset,
    )


def _din(nc, name, shape, dt=F32):
    return nc.dram_tensor(name, list(shape), dt, kind="ExternalInput").ap()


def _dout(nc, name, shape, dt=F32):
    return nc.dram_tensor(name, list(shape), dt, kind="ExternalOutput").ap()


def _mk_consts(nc, P, rw=False):
    K = Consts(P, _din(nc, "c_if", [128, 128]), _din(nc, "c_ib", [128, 128], BF16), _din(nc, "c_mask", [128, 256], BF16))
    RK = None
    if rw:
        RK = RwConsts(P, _din(nc, "c_ui", [128, 256]), _din(nc, "c_sl", [128, 128]), _din(nc, "c_reset", [64, 1024]))
    return K, RK


def build_tok(with_out, with_proj):
    nc = bass.Bass("TRN2", target_bir_lowering=False)
    P = Prog(nc)
    K, _ = _mk_consts(nc, P)
    x = _din(nc, "x", [NT, 2048])
    xcur = x
    if with_out:
        oT = _din(nc, "oT", [2048, NT], BF16)
        wo = _din(nc, "wo", [2048, 2048])
        xout = _dout(nc, "xout", [NT, 2048])
        out_stage(P, K, NT, x, lambda k: oT[128 * k:128 * k + 128, :], wo, xout)
        P.end_stage()
        xcur = xout
    if with_proj:
        g = _din(nc, "g", [128, 16])
        w = _din(nc, "w", [2048, 5440])
        pT = _dout(nc, "pT", [NCH_SEQ * 64, NT])
        pmem = nc.dram_tensor("pmem", [1024, NT], F32).ap()
        omem = _dout(nc, "omem", [512, NT], BF16)

        def dst64(i):
            if i < NCH_SEQ:
                return pT[64 * i:64 * i + 64, :]
            j = i - NCH_SEQ
            return pmem[64 * j:64 * j + 64, :]
        proj_stage(P, K, NT, xcur, g, w, 5440, dst64)
        P.end_stage()
        mem_stage(P, K, NT, lambda h: pmem[128 * h:128 * h + 128, :], lambda h: pmem[512 + 128 * h:512 + 128 * h + 128, :],
                  lambda h: omem[128 * h:128 * h + 128, :], _din(nc, "mem", [256, 2048]), _din(nc, "memg", [128, 16]),
                  _din(nc, "wkv", [2048, 1024]), _din(nc, "mqg", [128, 1]), _din(nc, "mkg", [128, 1]))
        P.end_stage()
    P.close()
    return nc


def build_head():
    nc = bass.Bass("TRN2", target_bir_lowering=False)
    P = Prog(nc)
    K, RK = _mk_consts(nc, P, rw=True)
    pin = _din(nc, "pin", [11 * 64, SEQ])
    sm = _din(nc, "sm", [64, 32])
    lw = _din(nc, "lw", [2, 64, 64])
    lup = _din(nc, "lup", [64, 64])
    oR = _dout(nc, "oR", [192, SEQ], BF16)

    def rows(i, lo=0, hi=64):
        return pin[64 * i + lo:64 * i + hi, :]
    lru_stage(P, 64, SEQ, rows(0), rows(1), oR[0:64, :], sm[:, 0:4], sm[:, 4:5], lw[0:1], sm[:, 5:6], lw[1:2], sm[:, 6:7],
              sm[:, 7:8])
    P.end_stage()
    rwkv_stage(P, K, RK, SEQ, rows(2), rows(3), rows(4), rows(5, 0, 32), rows(5, 32, 64), rows(6), oR[64:128, :],
               sm[:, 8:24], lup[0:32, :], lup[32:64, :])
    P.end_stage()
    swa_stage(P, K, SEQ, rows(7), rows(8), rows(9), rows(10), oR[128:192, :], sm[:, 24:25], sm[:, 25:26], sm[:, 26:27])
    P.end_stage()
    P.close()
    return nc


def _g16(v):
    return np.ascontiguousarray(np.asarray(v, np.float32).reshape(16, 128).T)


def _head_small(inp, l, h):
    hs = slice(64 * h, 64 * h + 64)
    sm = np.zeros((64, 32), np.float32)
    sm[:, 0:4] = inp["conv_w"][l][:, hs].T
    sm[:, 4] = inp["conv_b"][l][hs]
    sm[:, 5] = inp["lru_ba"][l][hs]
    sm[:, 6] = inp["lru_bx"][l][hs]
    sm[:, 7] = inp["lru_lambda"][l][hs]
    mu = inp["rw_mu"][l]
    sm[:, 8] = mu[0:512][hs]
    sm[:, 9] = mu[512:1024][hs]
    sm[:, 10] = mu[1024:1536][hs]
    sm[:, 11] = mu[1536:1600]
    sm[:, 12] = inp["rw_w0"][l][hs]
    sm[:, 13] = inp["rw_a0"][l][hs]
    sm[:, 14] = inp["rw_k_k"][l][hs]
    sm[:, 15] = inp["rw_k_a"][l][hs]
    sm[:, 16] = inp["rw_r_k"][l][h]
    sm[:, 17] = inp["rw_gn_g"][l][hs]
    sm[:, 18] = inp["rw_gn_b"][l][hs]
    sm[:, 24] = inp["swa_q_g"][l]
    sm[:, 25] = inp["swa_k_g"][l]
    sm[:, 26] = inp["swa_sinks"][l][h]
    lw = np.stack([inp["lru_wa"][l][h], inp["lru_wx"][l][h]]).astype(np.float32)
    lup = np.concatenate([inp["rw_w_up"][l][:, hs], inp["rw_a_up"][l][:, hs]], axis=0).astype(np.float32)
    return sm, lw, np.ascontiguousarray(lup)


TPF = 2048


def build_fused(seq=SEQ, depth=2):
    nc = bass.Bass("TRN2", target_bir_lowering=False)
    P = Prog(nc)
    K, RK = _mk_consts(nc, P, rw=True)
    x = _din(nc, "x", [seq, 2048])
    out = _dout(nc, "out", [seq, 2048])
    mem = _din(nc, "mem", [256, 2048])
    norm_g = _din(nc, "norm_g", [depth, 128, 16])
    w_in = _din(nc, "w_in", [depth, 2048, 5440])
    memg = _din(nc, "memg", [depth, 128, 16])
    wkv = _din(nc, "wkv", [depth, 2048, 1024])
    mqk = _din(nc, "mqk", [depth, 128, 2])
    w_out = _din(nc, "w_out", [depth, 2048, 2048])
    lru_sm = _din(nc, "lru_sm", [depth, 512, 8])
    lru_w = _din(nc, "lru_w", [depth, 2, 8, 64, 64])
    rw_sm = _din(nc, "rw_sm", [depth, 8, 64, 16])
    rw_lup = _din(nc, "rw_lup", [depth, 8, 64, 64])
    swa_sm = _din(nc, "swa_sm", [depth, 8, 64, 4])
    x1 = nc.dram_tensor("x1_scr", [seq, 2048], F32).ap()
    p_lru = nc.dram_tensor("p_lru", [1024, seq], F32).ap()
    p_rw = nc.dram_tensor("p_rw", [2112, seq], F32).ap()
    p_swa = nc.dram_tensor("p_swa", [1280, seq], F32).ap()
    p_mem = nc.dram_tensor("p_mem", [1024, seq], F32).ap()
    oT = nc.dram_tensor("oT_scr", [2048, seq], BF16).ap()
    for l in range(depth):
        xin = x if l == 0 else x1
        xout = out if l == depth - 1 else x1
        for tp in range(seq // TPF):
            ts_ = slice(tp * TPF, (tp + 1) * TPF)

            def dst64(i, ts_=ts_):
                if i < 16:
                    return p_lru[64 * i:64 * i + 64, ts_]
                if i < 49:
                    return p_rw[64 * (i - 16):64 * (i - 16) + 64, ts_]
                if i < 69:
                    return p_swa[64 * (i - 49):64 * (i - 49) + 64, ts_]
                return p_mem[64 * (i - 69):64 * (i - 69) + 64, ts_]
            proj_stage(P, K, TPF, xin[ts_, :], norm_g[l], w_in[l], 5440, dst64)
            P.end_stage()
        mem_stage(P, K, seq, lambda h: p_mem[128 * h:128 * h + 128, :], lambda h: p_mem[512 + 128 * h:512 + 128 * h + 128, :],
                  lambda h: oT[1536 + 128 * h:1536 + 128 * h + 128, :], mem, memg[l], wkv[l], mqk[l][:, 0:1], mqk[l][:, 1:2])
        P.end_stage()
        for ct in range(4):
            cs = slice(128 * ct, 128 * ct + 128)
            lru_stage(P, 128, seq, p_lru[cs, :], p_lru[512 + 128 * ct:512 + 128 * ct + 128, :], oT[cs, :],
                      lru_sm[l][cs, 0:4], lru_sm[l][cs, 4:5], lru_w[l][0][2 * ct:2 * ct + 2], lru_sm[l][cs, 5:6],
                      lru_w[l][1][2 * ct:2 * ct + 2], lru_sm[l][cs, 6:7], lru_sm[l][cs, 7:8])
            P.end_stage()
        for h in range(8):
            hs = slice(64 * h, 64 * h + 64)
            rwkv_stage(P, K, RK, seq, p_rw[hs, :], p_rw[512 + 64 * h:512 + 64 * h + 64, :],
                       p_rw[1024 + 64 * h:1024 + 64 * h + 64, :], p_rw[1536:1568, :], p_rw[1568:1600, :],
                       p_rw[1600 + 64 * h:1600 + 64 * h + 64, :], oT[512 + 64 * h:512 + 64 * h + 64, :],
                       rw_sm[l][h], rw_lup[l][h][0:32, :], rw_lup[l][h][32:64, :])
            P.end_stage()
        for kv in range(2):
            hs_ = [4 * kv + i for i in range(4)]
            swa_stage4(P, K, seq, [p_swa[64 * h:64 * h + 64, :] for h in hs_], p_swa[512 + 64 * kv:512 + 64 * kv + 64, :],
                       p_swa[640 + 64 * kv:640 + 64 * kv + 64, :], [p_swa[768 + 64 * h:768 + 64 * h + 64, :] for h in hs_],
                       [oT[1024 + 64 * h:1024 + 64 * h + 64, :] for h in hs_], swa_sm[l][hs_[0]][:, 0:1],
                       swa_sm[l][hs_[0]][:, 1:2], [swa_sm[l][h][:, 2:3] for h in hs_])
            P.end_stage()
        out_stage(P, K, seq, xin, lambda k: oT[128 * k:128 * k + 128, :], w_out[l], xout)
        P.end_stage()
    P.close()
    return nc, P


def _fused_inputs(inp, depth=2):
    f = np.float32
    L = depth
    m = dict(_consts_np())
    m["x"] = np.ascontiguousarray(inp["x"][0], dtype=f)
    m["mem"] = np.ascontiguousarray(inp["mem"][0], dtype=f)
    m["norm_g"] = np.stack([_g16(inp["norm_g"][l]) for l in range(L)])
    m["w_in"] = np.ascontiguousarray(inp["w_in"][:L], dtype=f)
    m["memg"] = np.stack([_g16(inp["mem_norm_g"][l]) for l in range(L)])
    m["wkv"] = np.ascontiguousarray(inp["w_mem_kv"][:L], dtype=f)
    m["mqk"] = np.ascontiguousarray(np.stack([inp["mem_q_g"][:L], inp["mem_k_g"][:L]], axis=-1), dtype=f)
    m["w_out"] = np.ascontiguousarray(inp["w_out"][:L], dtype=f)
    lru_sm = np.zeros((L, 512, 8), f)
    lru_sm[:, :, 0:4] = np.transpose(inp["conv_w"][:L], (0, 2, 1))
    lru_sm[:, :, 4] = inp["conv_b"][:L]
    lru_sm[:, :, 5] = inp["lru_ba"][:L]
    lru_sm[:, :, 6] = inp["lru_bx"][:L]
    lru_sm[:, :, 7] = inp["lru_lambda"][:L]
    m["lru_sm"] = lru_sm
    m["lru_w"] = np.ascontiguousarray(np.stack([inp["lru_wa"][:L], inp["lru_wx"][:L]], axis=1), dtype=f)
    rw_sm = np.zeros((L, 8, 64, 16), f)
    rw_lup = np.zeros((L, 8, 64, 64), f)
    swa_sm = np.zeros((L, 8, 64, 4), f)
    for l in range(L):
        for h in range(8):
            sm, _, lup = _head_small(inp, l, h)
            rw_sm[l, h] = sm[:, 8:24]
            rw_lup[l, h] = lup
            swa_sm[l, h, :, 0:3] = sm[:, 24:27]
    m["rw_sm"] = rw_sm
    m["rw_lup"] = rw_lup
    m["swa_sm"] = swa_sm
    return m


def kernel(**inputs):
    inp = {k: np.asarray(v) for k, v in inputs.items()}
    nc, _ = build_fused()
    m = _fused_inputs(inp)
    cores = list(range(NCORES))
    res = run_bass_kernel_spmd(nc, [m for _ in cores], core_ids=cores).results
    return np.asarray(res[0]["out"], dtype=np.float32).reshape(1, SEQ, 2048)
```

```python
from concourse.bass_utils import run_bass_kernel_spmd
from contextlib import ExitStack
import numpy as np
import concourse.bass as bass
import concourse.mybir as mybir

F32 = mybir.dt.float32
BF16 = mybir.dt.bfloat16
ALU = mybir.AluOpType
AF = mybir.ActivationFunctionType
AX = mybir.AxisListType

MAXV = 30000
ENGS = ("pe", "act", "dve", "pool", "sp")


class Ev:
    __slots__ = ("key", "n")

    def __init__(self, key, n=None):
        self.key = key
        self.n = n


class Res:
    __slots__ = ("name", "writers", "readers", "excl")

    def __init__(self, name="", excl=False):
        self.name = name
        self.writers = []
        self.readers = []
        self.excl = excl


class Counter:
    def __init__(self, prog, name):
        self.sem = prog.es.enter_context(prog.nc.semaphore(name))
        self.total = 0
        self.key = ("d", id(self))
        prog.counters[self.key] = self


class Tile:
    def __init__(self, prog, shape, dtype, name=None, psum=False, persistent=False):
        prog.nsb += 1
        nm = f"t{prog.nsb}_{name or ''}"
        st = prog.es if persistent else prog.stage_es
        if psum:
            self.t = st.enter_context(prog.nc.psum_tensor(nm, list(shape), dtype))
        else:
            self.t = st.enter_context(prog.nc.sbuf_tensor(nm, list(shape), dtype))
        self.r = Res(name or "", excl=psum)
        self.prog = prog
        self._c = None

    @property
    def c(self):
        if self._c is None:
            self._c = self.prog.counter()
        return self._c

    def __getitem__(self, idx):
        return self.t[idx]


class Op:
    __slots__ = ("eng", "fn", "waits", "signal", "ev", "ctr")


def _rs(xs):
    return [getattr(x, "r", x) for x in xs]


class Prog:
    def __init__(self, nc):
        self.nc = nc
        self.es = ExitStack()
        self.stage_es = ExitStack()
        self.ops = {e: [] for e in ENGS}
        self.sigcnt = {e: 0 for e in ENGS}
        self.emitted = {e: 0 for e in ENGS}
        self.pending = {e: [] for e in ENGS}
        self.waited = {e: {} for e in ENGS}
        self.counters = {}
        self.free_counters = []
        self.stage_counters = []
        self.esems = {e: [] for e in ENGS}
        self.nsb = 0
        self.nstage = 0
        self.total_ops = 0

    def tile(self, shape, dtype, name=None, psum=False, persistent=False):
        return Tile(self, shape, dtype, name, psum, persistent)

    def counter(self, name=None, persistent=False):
        if self.free_counters and not persistent:
            c = self.free_counters.pop()
        else:
            self.nsb += 1
            c = Counter(self, name or f"ctr{self.nsb}")
        if not persistent:
            self.stage_counters.append(c)
        return c

    def _deps(self, reads, writes, accs):
        waits = []
        for r in reads:
            waits.extend(r.writers)
        for r in writes:
            waits.extend(r.writers)
            waits.extend(r.readers)
        for r in accs:
            waits.extend(r.readers)
        return waits

    def _post(self, ev, reads, writes, accs):
        for r in reads:
            r.readers.append(ev)
            if len(r.readers) > 64:
                r.readers = _compact(r.readers)
        for r in writes:
            r.writers = [ev]
            r.readers = []
        for r in accs:
            r.writers.append(ev)
            if len(r.writers) > 64:
                r.writers = _compact(r.writers)

    def op(self, eng, fn, reads=(), writes=(), accs=(), signal=True):
        reads, writes, accs = _rs(reads), _rs(writes), _rs(accs)
        ex = [r for r in reads if r.excl]
        if ex:
            reads = [r for r in reads if not r.excl]
            writes = list(writes) + ex
        o = Op()
        o.eng = eng
        o.fn = fn
        o.ctr = None
        waits = self._deps(reads, writes, accs)
        if eng == "pe":
            waits = [w for w in waits if w.key != "pe"]
        o.waits = waits
        o.signal = False
        ev = Ev(eng)
        o.ev = ev
        self.ops[eng].append(o)
        self.pending[eng].append(ev)
        if signal:
            self.signal_last(eng)
        self._post(ev, reads, writes, accs)
        return ev

    def signal_last(self, eng):
        o = self.ops[eng][-1]
        if o.signal:
            return
        o.signal = True
        self.sigcnt[eng] += 1
        for p in self.pending[eng]:
            p.n = self.sigcnt[eng]
        self.pending[eng] = []

    def dma(self, eng, ctr, fn, reads=(), writes=(), accs=(), inc=16):
        reads, writes, accs = _rs(reads), _rs(writes), _rs(accs)
        o = Op()
        o.eng = eng
        o.fn = fn
        o.ctr = ctr
        o.waits = self._deps(reads, writes, accs)
        o.signal = (inc == 16)
        ctr.total += inc
        assert ctr.total < 60000
        ev = Ev(ctr.key, ctr.total)
        o.ev = ev
        self.ops[eng].append(o)
        self._post(ev, reads, writes, accs)
        return ev

    def finish(self, eng="sp"):
        o = Op()
        o.eng = eng
        o.fn = None
        o.ctr = None
        o.signal = False
        o.ev = Ev(eng)
        o.waits = [Ev(c.key, c.total) for c in self.counters.values() if c.total]
        self.ops[eng].append(o)

    def end_stage(self):
        nc = self.nc
        self.finish("sp")
        for e in ENGS:
            if self.pending[e]:
                self.signal_last(e)
            need = (self.sigcnt[e] + MAXV - 1) // MAXV + 1
            while len(self.esems[e]) < need:
                self.esems[e].append(self.es.enter_context(nc.semaphore(f"s_{e}{len(self.esems[e])}")))
        prog = self

        def resolve(ev):
            if isinstance(ev.key, tuple):
                return (ev.key, prog.counters[ev.key].sem, ev.n)
            assert ev.n is not None, f"unresolved event on {ev.key}"
            idx = (ev.n - 1) // MAXV
            return ((ev.key, idx), prog.esems[ev.key][idx], (ev.n - 1) % MAXV + 1)

        def run(e):
            def body(eng):
                waited = prog.waited[e]
                cnt = prog.emitted[e]
                for o in prog.ops[e]:
                    need = {}
                    for w in o.waits:
                        k, sem, v = resolve(w)
                        if waited.get(k, 0) >= v:
                            continue
                        if k not in need or need[k][1] < v:
                            need[k] = (sem, v)
                    for k, (sem, v) in need.items():
                        eng.wait_ge(sem, v)
                        waited[k] = v
                    if o.fn is None:
                        continue
                    inst = o.fn(eng)
                    if o.ctr is not None:
                        if o.signal:
                            inst.then_inc(o.ctr.sem, 16)
                        else:
                            inst.then_inc(o.ctr.sem)
                    elif o.signal:
                        cnt += 1
                        idx = (cnt - 1) // MAXV
                        inst.then_inc(prog.esems[e][idx], 1)
                prog.emitted[e] = cnt
            return body

        with nc.Block() as block:
            block.tensor(run("pe"))
            block.scalar(run("act"))
            block.vector(run("dve"))
            block.gpsimd(run("pool"))
            block.sync(run("sp"))
        for e in ENGS:
            assert self.emitted[e] == self.sigcnt[e], (e, self.emitted[e], self.sigcnt[e])
            self.total_ops += len(self.ops[e])
            self.ops[e] = []
        self.stage_es.close()
        self.stage_es = ExitStack()
        self.free_counters.extend(self.stage_counters)
        self.stage_counters = []
        self.nstage += 1

    def close(self):
        self.es.close()


def _compact(evs):
    best = {}
    for ev in evs:
        if ev.n is None:
            best[id(ev)] = ev
            continue
        k = ev.key
        if k not in best or best[k].n < ev.n:
            best[k] = ev
    return list(best.values())


LRU_C = 8.0


def load_col(P, ctr, res, dst_ap, src_ap, eng="sp"):
    P.dma(eng, ctr, lambda e: e.dma_start(out=dst_ap, in_=src_ap), accs=[res])


def lru_stage(P, CP, NTOK, xrows, grows, orows, convw, convb, wa, ba, wx, bx, lam, TP=2048, tag=""):
    nb = CP // 64
    npc = NTOK // TP
    prm = P.tile([CP, 16], F32, f"lruprm{tag}")
    wabd = P.tile([CP, CP], F32, f"wabd{tag}")
    wxbd = P.tile([CP, CP], F32, f"wxbd{tag}")
    if nb > 1:
        P.op("pool", lambda e: e.memset(wabd[:], 0.0), writes=[wabd])
        P.op("pool", lambda e: e.memset(wxbd[:], 0.0), writes=[wxbd])
    for b in range(nb):
        P.dma("sp", wabd.c, lambda e, b=b: e.dma_start(out=wabd[b * 64:(b + 1) * 64, b * 64:(b + 1) * 64], in_=wa[b]),
              accs=[wabd])
        P.dma("sp", wxbd.c, lambda e, b=b: e.dma_start(out=wxbd[b * 64:(b + 1) * 64, b * 64:(b + 1) * 64], in_=wx[b]),
              accs=[wxbd])
    P.dma("sp", prm.c, lambda e: e.dma_start(out=prm[:, 0:4], in_=convw, allow_slow_non_contiguous=True), writes=[prm])
    for i, src in enumerate((convb, ba, bx, lam)):
        P.dma("sp", prm.c, lambda e, i=i, src=src: e.dma_start(out=prm[:, 4 + i:5 + i], in_=src, allow_slow_non_contiguous=True), accs=[prm])
    P.op("act", lambda e: e.activation(out=prm[:, 8:9], in_=prm[:, 7:8], func=AF.Exp, scale=-1.0),
         reads=[prm], accs=[prm])
    P.op("act", lambda e: e.activation(out=prm[:, 9:10], in_=prm[:, 8:9], func=AF.Ln, bias=1.0),
         reads=[prm], accs=[prm])
    P.op("dve", lambda e: e.tensor_scalar(out=prm[:, 10:11], in0=prm[:, 9:10], scalar1=-LRU_C, scalar2=None,
                                         op0=ALU.mult), reads=[prm], accs=[prm])
    P.op("dve", lambda e: e.memset(prm[:, 11:12], 0.0), reads=[prm], accs=[prm])

    xt = [P.tile([CP, TP + 3], F32, f"lxt{tag}{i}") for i in range(2)]
    gt = [P.tile([CP, TP], F32, f"lgt{tag}{i}") for i in range(2)]
    xc = P.tile([CP, TP], F32, f"lxc{tag}")
    rr = P.tile([CP, TP], F32, f"lr{tag}")
    ii = P.tile([CP, TP], F32, f"li{tag}")
    aa = P.tile([CP, TP], F32, f"la{tag}")
    mm = P.tile([CP, TP], F32, f"lm{tag}")
    hh = [P.tile([CP, TP], F32, f"lh{tag}{i}") for i in range(2)]
    ob = [P.tile([CP, TP], BF16, f"lo{tag}{i}") for i in range(2)]
    pg = [P.tile([CP, 512], F32, f"lpg{tag}{i}", psum=True) for i in range(2)]
    npg = 0
    for pi in range(npc):
        s = pi % 2
        t0 = pi * TP
        X, G, H, O = xt[s], gt[s], hh[s], ob[s]
        if pi == 0:
            P.op("pool", lambda e, X=X: e.memset(X[:, 0:3], 0.0), writes=[X])
            P.dma("sp", X.c, lambda e, X=X: e.dma_start(out=X[:, 3:3 + TP], in_=xrows[:, 0:TP]), accs=[X])
        else:
            P.dma("sp", X.c, lambda e, X=X, t0=t0: e.dma_start(out=X[:, :], in_=xrows[:, t0 - 3:t0 + TP]), writes=[X])
        P.dma("sp", G.c, lambda e, G=G, t0=t0: e.dma_start(out=G[:, :], in_=grows[:, t0:t0 + TP]), writes=[G])
        P.op("dve", lambda e, X=X: e.tensor_scalar(out=xc[:], in0=X[:, 3:3 + TP], scalar1=prm[:, 3:4],
                                                  scalar2=prm[:, 4:5], op0=ALU.mult, op1=ALU.add),
             reads=[X, prm], writes=[xc])
        for j in range(3):
            P.op("dve", lambda e, X=X, j=j: e.scalar_tensor_tensor(out=xc[:], in0=X[:, j:j + TP], scalar=prm[:, j:j + 1],
                                                                  in1=xc[:], op0=ALU.mult, op1=ALU.add),
                 reads=[X, prm, xc], writes=[xc])
        for (wbd, bcol, dst) in ((wabd, 5, rr), (wxbd, 6, ii)):
            for sb_ in range(TP // 512):
                pb = pg[npg % 2]
                npg += 1
                P.op("pe", lambda e, wbd=wbd, pb=pb, sb_=sb_: e.matmul(pb[:, :], lhsT=wbd[:, :],
                                                                      rhs=xc[:, sb_ * 512:(sb_ + 1) * 512],
                                                                      start=True, stop=True),
                     reads=[wbd, xc], writes=[pb])
                P.op("act", lambda e, pb=pb, dst=dst, sb_=sb_, bcol=bcol: e.activation(
                    out=dst[:, sb_ * 512:(sb_ + 1) * 512], in_=pb[:, :], func=AF.Sigmoid, bias=prm[:, bcol:bcol + 1]),
                    reads=[pb, prm], writes=[dst] if sb_ == 0 else [], accs=[] if sb_ == 0 else [dst])
        P.op("act", lambda e: e.activation(out=aa[:], in_=rr[:], func=AF.Exp, scale=prm[:, 10:11]),
             reads=[rr, prm], writes=[aa])
        P.op("dve", lambda e: e.tensor_tensor(out=mm[:], in0=aa[:], in1=aa[:], op=ALU.mult), reads=[aa], writes=[mm])
        P.op("dve", lambda e: e.tensor_scalar(out=mm[:], in0=mm[:], scalar1=-1.0, scalar2=1.0, op0=ALU.mult,
                                             op1=ALU.add), reads=[mm], writes=[mm])
        P.op("dve", lambda e: e.tensor_scalar(out=mm[:], in0=mm[:], scalar1=1e-12, scalar2=None, op0=ALU.max),
             reads=[mm], writes=[mm])
        P.op("act", lambda e: e.activation(out=mm[:], in_=mm[:], func=AF.Sqrt), reads=[mm], writes=[mm])
        P.op("dve", lambda e: e.tensor_tensor(out=ii[:], in0=ii[:], in1=xc[:], op=ALU.mult), reads=[ii, xc], writes=[ii])
        P.op("dve", lambda e: e.tensor_tensor(out=ii[:], in0=ii[:], in1=mm[:], op=ALU.mult), reads=[ii, mm], writes=[ii])
        Hp = hh[1 - s]
        init = prm[:, 11:12] if pi == 0 else Hp[:, TP - 1:TP]
        P.op("dve", lambda e, H=H, init=init: e.tensor_tensor_scan(out=H[:], data0=aa[:], data1=ii[:], initial=init,
                                                                  op0=ALU.mult, op1=ALU.add),
             reads=[aa, ii, prm, Hp], writes=[H])
        P.op("act", lambda e, G=G: e.activation(out=G[:], in_=G[:], func=AF.Silu), reads=[G], writes=[G])
        P.op("dve", lambda e, H=H, G=G, O=O: e.tensor_tensor(out=O[:], in0=H[:], in1=G[:], op=ALU.mult),
             reads=[H, G], writes=[O])
        P.dma("sp", O.c, lambda e, O=O, t0=t0: e.dma_start(out=orows[:, t0:t0 + TP], in_=O[:]), reads=[O])


class Consts:
    def __init__(self, P, c_identf, c_identb, c_swamask):
        self.identf = P.tile([128, 128], F32, "identf", persistent=True)
        self.identb = P.tile([128, 128], BF16, "identb", persistent=True)
        self.swamask = P.tile([128, 256], BF16, "swamask", persistent=True)
        self.ones_f = P.tile([128, 128], F32, "ones_f", persistent=True)
        self.ones_b = P.tile([128, 128], BF16, "ones_b", persistent=True)
        P.dma("sp", self.identf.c, lambda e: e.dma_start(out=self.identf[:], in_=c_identf), writes=[self.identf])
        P.dma("sp", self.identb.c, lambda e: e.dma_start(out=self.identb[:], in_=c_identb), writes=[self.identb])
        P.dma("sp", self.swamask.c, lambda e: e.dma_start(out=self.swamask[:], in_=c_swamask), writes=[self.swamask])
        P.op("pool", lambda e: e.memset(self.ones_f[:], 1.0), writes=[self.ones_f])
        P.op("pool", lambda e: e.memset(self.ones_b[:], 1.0), writes=[self.ones_b])


def swa_stage(P, K, NTOK, qrows, krows, vrows, grows, orows, qg, kg, sink, TP=512, tag=""):
    npc = NTOK // TP
    nbk = TP // 128
    prm = P.tile([64, 8], F32, f"swaprm{tag}")
    P.dma("sp", prm.c, lambda e: e.dma_start(out=prm[:, 0:1], in_=qg, allow_slow_non_contiguous=True), writes=[prm])
    P.dma("sp", prm.c, lambda e: e.dma_start(out=prm[:, 1:2], in_=kg, allow_slow_non_contiguous=True), accs=[prm])
    P.dma("sp", prm.c, lambda e: e.dma_start(out=prm[:, 2:3], in_=sink, allow_slow_non_contiguous=True), accs=[prm])
    P.op("dve", lambda e: e.tensor_scalar(out=prm[:, 3:4], in0=prm[:, 0:1], scalar1=0.125, scalar2=None, op0=ALU.mult),
         reads=[prm], accs=[prm])
    P.op("act", lambda e: e.activation(out=prm[:, 4:5], in_=prm[:, 2:3], func=AF.Exp), reads=[prm], accs=[prm])
    W = TP + 128
    kx = [P.tile([64, W], F32, f"skx{tag}{i}") for i in range(2)]
    vx = [P.tile([64, W], F32, f"svx{tag}{i}") for i in range(2)]
    qx = [P.tile([64, TP], F32, f"sqx{tag}{i}") for i in range(2)]
    gx = [P.tile([64, TP], F32, f"sgx{tag}{i}") for i in range(2)]
    sq = P.tile([64, W], F32, f"ssq{tag}")
    rs = P.tile([64, W], F32, f"srs{tag}")
    kn = P.tile([64, W], BF16, f"skn{tag}")
    qn = P.tile([64, TP], BF16, f"sqn{tag}")
    vb = P.tile([128, nbk + 1, 64], BF16, f"svb{tag}")
    E = [P.tile([128, 256], BF16, f"sE{tag}{i}") for i in range(2)]
    dn = P.tile([64, TP], F32, f"sdn{tag}")
    yy = P.tile([64, TP], F32, f"syy{tag}")
    ob = [P.tile([64, TP], BF16, f"sob{tag}{i}") for i in range(2)]
    pn = [P.tile([64, 512], F32, f"spn{tag}{i}", psum=True) for i in range(2)]
    pt = P.tile([128, 512], F32, f"spt{tag}", psum=True)
    psc = [P.tile([128, 256], F32, f"spsc{tag}{i}", psum=True) for i in range(2)]
    pnum = P.tile([64, 512], F32, f"spnum{tag}", psum=True)
    pden = P.tile([64, 512], F32, f"spden{tag}", psum=True)
    npn = 0
    nsc = 0

    def norm(src, width, c0, gcol, dst, dst_c0):
        nonlocal npn
        P.op("dve", lambda e: e.tensor_tensor(out=sq[:, 0:width], in0=src[:, c0:c0 + width], in1=src[:, c0:c0 + width],
                                             op=ALU.mult), reads=[src], writes=[sq])
        o = 0
        first = True
        while o < width:
            w_ = min(512, width - o)
            pb = pn[npn % 2]
            npn += 1
            P.op("pe", lambda e, pb=pb, o=o, w_=w_: e.matmul(pb[:, 0:w_], lhsT=K.ones_f[0:64, 0:64], rhs=sq[:, o:o + w_],
                                                            start=True, stop=True), reads=[K.ones_f, sq], writes=[pb])
            P.op("act", lambda e, pb=pb, o=o, w_=w_: e.activation(out=rs[:, o:o + w_], in_=pb[:, 0:w_], func=AF.Sqrt,
                                                                 scale=1.0 / 64, bias=1e-6),
                 reads=[pb], writes=[rs] if first else [], accs=[] if first else [rs])
            first = False
            o += w_
        P.op("dve", lambda e: e.reciprocal(out=rs[:, 0:width], in_=rs[:, 0:width]), reads=[rs], writes=[rs])
        P.op("dve", lambda e: e.scalar_tensor_tensor(out=dst[:, dst_c0:dst_c0 + width], in0=src[:, c0:c0 + width],
                                                    scalar=prm[:, gcol:gcol + 1], in1=rs[:, 0:width],
                                                    op0=ALU.mult, op1=ALU.mult), reads=[src, prm, rs], writes=[dst])

    for pi in range(npc):
        s = pi % 2
        t0 = pi * TP
        KX, VX, QX, GX, O = kx[s], vx[s], qx[s], gx[s], ob[s]
        lo = 128 if pi == 0 else 0
        P.dma("sp", KX.c, lambda e, KX=KX, t0=t0, lo=lo: e.dma_start(out=KX[:, lo:W], in_=krows[:, t0 - 128 + lo:t0 + TP]),
              writes=[KX])
        P.dma("sp", VX.c, lambda e, VX=VX, t0=t0, lo=lo: e.dma_start(out=VX[:, lo:W], in_=vrows[:, t0 - 128 + lo:t0 + TP]),
              writes=[VX])
        P.dma("sp", QX.c, lambda e, QX=QX, t0=t0: e.dma_start(out=QX[:, :], in_=qrows[:, t0:t0 + TP]), writes=[QX])
        P.dma("sp", GX.c, lambda e, GX=GX, t0=t0: e.dma_start(out=GX[:, :], in_=grows[:, t0:t0 + TP]), writes=[GX])
        norm(KX, W - lo, lo, 1, kn, lo)
        norm(QX, TP, 0, 3, qn, 0)
        b0 = lo // 128
        for b in range(b0, nbk + 1):
            P.op("pe", lambda e, VX=VX, b=b: e.transpose(out=pt[:, b * 64:(b + 1) * 64], in_=VX[:, b * 128:(b + 1) * 128],
                                                        identity=K.identf[0:64, 0:64]),
                 reads=[VX, K.identf], writes=[pt] if b == b0 else [], accs=[] if b == b0 else [pt],
                 signal=(b == nbk))
        P.op("act", lambda e, b0=b0: e.activation(out=vb[:, b0:nbk + 1, :],
                                                 in_=pt[:, b0 * 64:(nbk + 1) * 64].rearrange("p (b d) -> p b d", d=64),
                                                 func=AF.Copy), reads=[pt], writes=[vb])
        for n in range(nbk):
            has_prev = not (pi == 0 and n == 0)
            sc = psc[nsc % 2]
            Eb = E[nsc % 2]
            nsc += 1
            wd = 256 if has_prev else 128
            P.op("pe", lambda e, sc=sc, n=n: e.matmul(sc[:, 0:128], lhsT=kn[:, (n + 1) * 128:(n + 2) * 128],
                                                     rhs=qn[:, n * 128:(n + 1) * 128], start=True, stop=True),
                 reads=[kn, qn], writes=[sc], signal=not has_prev)
            if has_prev:
                P.op("pe", lambda e, sc=sc, n=n: e.matmul(sc[:, 128:256], lhsT=kn[:, n * 128:(n + 1) * 128],
                                                         rhs=qn[:, n * 128:(n + 1) * 128], start=True, stop=True),
                     reads=[kn, qn], accs=[sc])
            P.op("act", lambda e, sc=sc, Eb=Eb, wd=wd: e.activation(out=Eb[:, 0:wd], in_=sc[:, 0:wd], func=AF.Exp),
                 reads=[sc], writes=[Eb])
            P.op("pool", lambda e, Eb=Eb, wd=wd: e.tensor_tensor(out=Eb[:, 0:wd], in0=Eb[:, 0:wd], in1=K.swamask[:, 0:wd],
                                                                op=ALU.mult), reads=[Eb, K.swamask], writes=[Eb])
            cs = slice(n * 128, (n + 1) * 128)
            for (pacc, lhs_cur, lhs_prev) in ((pnum, vb[:, n + 1, :], vb[:, n, :]),
                                              (pden, K.ones_b[:, 0:64], K.ones_b[:, 0:64])):
                P.op("pe", lambda e, pacc=pacc, lhs_cur=lhs_cur, Eb=Eb, cs=cs, has_prev=has_prev: e.matmul(
                    pacc[:, cs], lhsT=lhs_cur, rhs=Eb[:, 0:128], start=True, stop=not has_prev),
                    reads=[vb, K.ones_b, Eb], writes=[pacc] if n == 0 else [], accs=[] if n == 0 else [pacc],
                    signal=False)
                if has_prev:
                    P.op("pe", lambda e, pacc=pacc, lhs_prev=lhs_prev, Eb=Eb, cs=cs: e.matmul(
                        pacc[:, cs], lhsT=lhs_prev, rhs=Eb[:, 128:256], start=False, stop=True),
                        reads=[vb, K.ones_b, Eb], accs=[pacc], signal=False)
            P.signal_last("pe")
        P.op("dve", lambda e: e.tensor_scalar(out=dn[:], in0=pden[:, :], scalar1=prm[:, 4:5], scalar2=None, op0=ALU.add),
             reads=[pden, prm], writes=[dn])
        P.op("dve", lambda e: e.reciprocal(out=dn[:], in_=dn[:]), reads=[dn], writes=[dn])
        P.op("dve", lambda e: e.tensor_tensor(out=yy[:], in0=pnum[:, :], in1=dn[:], op=ALU.mult),
             reads=[pnum, dn], writes=[yy])
        P.op("act", lambda e, GX=GX: e.activation(out=GX[:], in_=GX[:], func=AF.Silu), reads=[GX], writes=[GX])
        P.op("dve", lambda e, GX=GX, O=O: e.tensor_tensor(out=O[:], in0=yy[:], in1=GX[:], op=ALU.mult),
             reads=[yy, GX], writes=[O])
        P.dma("sp", O.c, lambda e, O=O, t0=t0: e.dma_start(out=orows[:, t0:t0 + TP], in_=O[:]), reads=[O])


D_MODEL = 2048
KC = D_MODEL // 128


def norm_transpose(P, K, x_dram, ntile, gsb, hT, pst, eps=1e-6, tag=""):
    xt = [P.tile([128, D_MODEL], F32, f"nxt{tag}{i}") for i in range(2)]
    xs = [P.tile([128, D_MODEL], F32, f"nxs{tag}{i}") for i in range(2)]
    junk = P.tile([128, D_MODEL], BF16, f"njunk{tag}")
    st = P.tile([128, 4 * ntile], F32, f"nst{tag}")
    npst = 0
    for i in range(ntile):
        s = i % 2
        X, XS = xt[s], xs[s]
        P.dma("sp", X.c, lambda e, X=X, i=i: e.dma_start(out=X[:], in_=x_dram[i * 128:(i + 1) * 128, :]), writes=[X])
        c0 = 4 * i
        P.op("act", lambda e, X=X, c0=c0: e.activation(out=junk[:], in_=X[:], func=AF.Square, accum_out=st[:, c0:c0 + 1]),
             reads=[X], writes=[junk], accs=[st])
        P.op("dve", lambda e, c0=c0: e.tensor_scalar(out=st[:, c0 + 1:c0 + 2], in0=st[:, c0:c0 + 1], scalar1=1.0 / D_MODEL,
                                                    scalar2=eps, op0=ALU.mult, op1=ALU.add), reads=[st], accs=[st])
        P.op("act", lambda e, c0=c0: e.activation(out=st[:, c0 + 2:c0 + 3], in_=st[:, c0 + 1:c0 + 2], func=AF.Sqrt),
             reads=[st], accs=[st])
        P.op("dve", lambda e, c0=c0: e.reciprocal(out=st[:, c0 + 3:c0 + 4], in_=st[:, c0 + 2:c0 + 3]),
             reads=[st], accs=[st])
        P.op("act", lambda e, X=X, XS=XS, c0=c0: e.activation(out=XS[:], in_=X[:], func=AF.Copy,
                                                             scale=st[:, c0 + 3:c0 + 4]), reads=[X, st], writes=[XS])
        for kq in range(KC // 4):
            pb = pst[npst % 2]
            npst += 1
            for kk in range(4):
                k = kq * 4 + kk
                P.op("pe", lambda e, XS=XS, pb=pb, k=k, kk=kk: e.transpose(
                    out=pb[:, kk * 128:(kk + 1) * 128], in_=XS[:, k * 128:(k + 1) * 128], identity=K.identf[:]),
                    reads=[XS, K.identf], writes=[pb] if kk == 0 else [], accs=[pb] if kk else [], signal=(kk == 3))
            for kk in range(4):
                k = kq * 4 + kk
                P.op("dve", lambda e, pb=pb, k=k, kk=kk, i=i: e.tensor_scalar(
                    out=hT[:, k, i * 128:(i + 1) * 128], in0=pb[:, kk * 128:(kk + 1) * 128],
                    scalar1=gsb[:, k:k + 1], scalar2=None, op0=ALU.mult), reads=[pb, gsb], accs=[hT])


def load_weight_bf16(P, w_dram, c0, cw, wst, wb):
    wv = w_dram.rearrange("(k p) c -> p k c", p=128)
    P.dma("sp", wst.c, lambda e: e.dma_start(out=wst[:, :, 0:cw], in_=wv[:, :, c0:c0 + cw]), writes=[wst])
    P.op("pool", lambda e: e.tensor_copy(out=wb[:, :, 0:cw], in_=wst[:, :, 0:cw]), reads=[wst], writes=[wb])


def proj_stage(P, K, NT, x_dram, g_dram, w_dram, NCOL, dst64, tag=""):
    ntile = NT // 128
    TG = min(512, NT)
    ntg = NT // TG
    CB = 256
    gsb = P.tile([128, KC], F32, f"pg{tag}")
    P.dma("sp", gsb.c, lambda e: e.dma_start(out=gsb[:], in_=g_dram), writes=[gsb])
    hT = P.tile([128, KC, NT], BF16, f"phT{tag}")
    pp = [P.tile([128, 512], F32, f"ppp{tag}{i}", psum=True) for i in range(8)]
    norm_transpose(P, K, x_dram, ntile, gsb, hT, pp[0:2], tag=tag)
    nblk = (NCOL + CB - 1) // CB
    wst = [P.tile([128, KC, CB], F32, f"pwst{tag}{i}") for i in range(2)]
    wb = [P.tile([128, KC, CB], BF16, f"pwb{tag}{i}") for i in range(2)]
    ost = [P.tile([128, NT], F32, f"post{tag}{i}") for i in range(2)]
    nmm = 0
    nct = 0
    for bi in range(nblk):
        s = bi % 2
        cw = min(CB, NCOL - bi * CB)
        load_weight_bf16(P, w_dram, bi * CB, cw, wst[s], wb[s])
        WB = wb[s]
        for j in range((cw + 127) // 128):
            mw = min(128, cw - j * 128)
            O = ost[nct % 2]
            nct += 1
            pbs = [pp[(nct % 2) * 4 + n] for n in range(ntg)]
            for k in range(KC):
                for n in range(ntg):
                    pb = pbs[n]
                    P.op("pe", lambda e, WB=WB, k=k, j=j, mw=mw, n=n, pb=pb: e.matmul(
                        pb[0:mw, 0:TG], lhsT=WB[:, k, j * 128:j * 128 + mw], rhs=hT[:, k, n * TG:(n + 1) * TG],
                        start=(k == 0), stop=(k == KC - 1)),
                        reads=[WB, hT], writes=[pb] if k == 0 else [], accs=[pb] if k else [],
                        signal=(k == KC - 1 and n == ntg - 1))
            for n in range(ntg):
                pb = pbs[n]
                nmm += 1
                if nmm % 2:
                    P.op("act", lambda e, O=O, mw=mw, n=n, pb=pb: e.activation(
                        out=O[0:mw, n * TG:(n + 1) * TG], in_=pb[0:mw, 0:TG], func=AF.Copy),
                        reads=[pb], writes=[O] if n == 0 else [], accs=[] if n == 0 else [O])
                else:
                    P.op("dve", lambda e, O=O, mw=mw, n=n, pb=pb: e.tensor_copy(
                        out=O[0:mw, n * TG:(n + 1) * TG], in_=pb[0:mw, 0:TG]),
                        reads=[pb], writes=[O] if n == 0 else [], accs=[] if n == 0 else [O])
            c0 = bi * CB + j * 128
            for hh_ in range(mw // 64):
                P.dma("sp", O.c, lambda e, O=O, hh_=hh_, c0=c0: e.dma_start(
                    out=dst64(c0 // 64 + hh_), in_=O[hh_ * 64:(hh_ + 1) * 64, :]), reads=[O])


def out_stage(P, K, NT, x_dram, oT_src, w_dram, out_dram, tag=""):
    TG = min(512, NT)
    ntg = NT // TG
    wst = [P.tile([128, KC, 256], F32, f"owst{tag}{i}") for i in range(2)]
    wo = [P.tile([128, KC, 512], BF16, f"owo{tag}{i}") for i in range(4)]
    wv = w_dram.rearrange("(k p) c -> p k c", p=128)
    for cb in range(8):
        WS, WO, hf = wst[cb % 2], wo[cb // 2], cb % 2
        P.dma("sp", WS.c, lambda e, WS=WS, cb=cb: e.dma_start(out=WS[:, :, :], in_=wv[:, :, cb * 256:(cb + 1) * 256]), writes=[WS])
        P.op("pool", lambda e, WS=WS, WO=WO, hf=hf: e.tensor_copy(out=WO[:, :, hf * 256:(hf + 1) * 256], in_=WS[:, :, :]),
             reads=[WS], writes=[WO] if hf == 0 else [], accs=[WO] if hf else [])
    ot = [P.tile([128, KC, TG], BF16, f"oot{tag}{i}") for i in range(2)]
    xt = [P.tile([128, D_MODEL], F32, f"oxt{tag}{i}") for i in range(2)]
    xo = [P.tile([128, D_MODEL], F32, f"oxo{tag}{i}") for i in range(2)]
    pp = [P.tile([128, 512], F32, f"opp{tag}{i}", psum=True) for i in range(4)]
    nmm = 0
    ntl = 0
    for n in range(ntg):
        OT = ot[n % 2]
        for k in range(KC):
            P.dma("sp", OT.c, lambda e, OT=OT, k=k, n=n: e.dma_start(out=OT[:, k, :], in_=oT_src(k)[:, n * TG:(n + 1) * TG]),
                  writes=[OT] if k == 0 else [], accs=[OT] if k else [])
        for tt in range(TG // 128):
            X, XO = xt[ntl % 2], xo[ntl % 2]
            ntl += 1
            r0 = n * TG + tt * 128
            P.dma("sp", X.c, lambda e, X=X, r0=r0: e.dma_start(out=X[:], in_=x_dram[r0:r0 + 128, :]), writes=[X])
            for cb in range(4):
                pb = pp[nmm % 4]
                nmm += 1
                W_ = wo[cb]
                for k in range(KC):
                    P.op("pe", lambda e, OT=OT, W_=W_, k=k, tt=tt, pb=pb: e.matmul(
                        pb[:, :], lhsT=OT[:, k, tt * 128:(tt + 1) * 128], rhs=W_[:, k, :],
                        start=(k == 0), stop=(k == KC - 1)),
                        reads=[OT, W_], writes=[pb] if k == 0 else [], accs=[pb] if k else [], signal=(k == KC - 1))
                P.op("dve", lambda e, X=X, XO=XO, pb=pb, cb=cb: e.tensor_tensor(
                    out=XO[:, cb * 512:(cb + 1) * 512], in0=pb[:, :], in1=X[:, cb * 512:(cb + 1) * 512], op=ALU.add),
                    reads=[pb, X], writes=[XO] if cb == 0 else [], accs=[] if cb == 0 else [XO])
            P.dma("sp", XO.c, lambda e, XO=XO, r0=r0: e.dma_start(out=out_dram[r0:r0 + 128, :], in_=XO[:]), reads=[XO])


def mem_stage(P, K, NT, qsrc, gsrc, odst, mem_dram, memg_dram, wkv_dram, qg_dram, kg_dram, tag=""):
    TP = min(512, NT)
    npc = NT // TP
    SC = 128.0 ** -0.5
    prm = P.tile([128, 24], F32, f"mprm{tag}")
    gsb = P.tile([128, KC], F32, f"mgsb{tag}")
    P.dma("sp", prm.c, lambda e: e.dma_start(out=prm[:, 0:1], in_=qg_dram, allow_slow_non_contiguous=True), writes=[prm])
    P.dma("sp", prm.c, lambda e: e.dma_start(out=prm[:, 1:2], in_=kg_dram, allow_slow_non_contiguous=True), accs=[prm])
    P.dma("sp", gsb.c, lambda e: e.dma_start(out=gsb[:], in_=memg_dram), writes=[gsb])
    P.op("dve", lambda e: e.tensor_scalar(out=prm[:, 2:3], in0=prm[:, 1:2], scalar1=SC, scalar2=None, op0=ALU.mult),
         reads=[prm], accs=[prm])
    pp = [P.tile([128, 512], F32, f"mpp{tag}{i}", psum=True) for i in range(7)]
    hmT = P.tile([128, KC, 256], BF16, f"mhmT{tag}")
    norm_transpose(P, K, mem_dram, 2, gsb, hmT, pp[0:2], tag="m" + tag)
    wst = [P.tile([128, KC, 256], F32, f"mwst{tag}{i}") for i in range(2)]
    wkv = [P.tile([128, KC, 256], BF16, f"mwkv{tag}{i}") for i in range(4)]
    for cb in range(4):
        load_weight_bf16(P, wkv_dram, cb * 256, 256, wst[cb % 2], wkv[cb])
    mkf = P.tile([128, 2, 512], F32, f"mmkf{tag}")
    mvb = P.tile([128, 2, 512], BF16, f"mmvb{tag}")
    mkT = P.tile([128, 4, 256], BF16, f"mmkT{tag}")
    junk = P.tile([128, 128], F32, f"mjunk{tag}")
    npp = 2
    for mt in range(2):
        for half, dst in ((0, mkf), (1, mvb)):
            pb = pp[npp % 7]
            npp += 1
            for sub in range(2):
                W_ = wkv[half * 2 + sub]
                for k in range(KC):
                    P.op("pe", lambda e, pb=pb, sub=sub, W_=W_, k=k, mt=mt: e.matmul(
                        pb[:, sub * 256:(sub + 1) * 256], lhsT=hmT[:, k, mt * 128:(mt + 1) * 128], rhs=W_[:, k, :],
                        start=(k == 0), stop=(k == KC - 1)),
                        reads=[hmT, W_], writes=[pb] if (k == 0 and sub == 0) else [],
                        accs=[] if (k == 0 and sub == 0) else [pb], signal=(k == KC - 1 and sub == 1))
            P.op("act", lambda e, pb=pb, dst=dst, mt=mt: e.activation(out=dst[:, mt, :], in_=pb[:, :], func=AF.Copy),
                 reads=[pb], accs=[dst])
        for h in range(4):
            c0 = 4 + mt * 8 + h * 2
            P.op("act", lambda e, mt=mt, h=h, c0=c0: e.activation(out=junk[:], in_=mkf[:, mt, h * 128:(h + 1) * 128],
                                                                 func=AF.Square, accum_out=prm[:, c0:c0 + 1]),
                 reads=[mkf], writes=[junk], accs=[prm])
            P.op("act", lambda e, c0=c0: e.activation(out=prm[:, c0 + 1:c0 + 2], in_=prm[:, c0:c0 + 1], func=AF.Sqrt,
                                                     scale=1.0 / 128, bias=1e-6), reads=[prm], accs=[prm])
            P.op("dve", lambda e, c0=c0: e.reciprocal(out=prm[:, c0 + 1:c0 + 2], in_=prm[:, c0 + 1:c0 + 2]),
                 reads=[prm], accs=[prm])
            P.op("dve", lambda e, mt=mt, h=h, c0=c0: e.tensor_scalar(
                out=mkf[:, mt, h * 128:(h + 1) * 128], in0=mkf[:, mt, h * 128:(h + 1) * 128],
                scalar1=prm[:, c0 + 1:c0 + 2], scalar2=None, op0=ALU.mult), reads=[mkf, prm], accs=[mkf])
        pb = pp[npp % 7]
        npp += 1
        for h in range(4):
            P.op("pe", lambda e, pb=pb, mt=mt, h=h: e.transpose(out=pb[:, h * 128:(h + 1) * 128],
                                                               in_=mkf[:, mt, h * 128:(h + 1) * 128], identity=K.identf[:]),
                 reads=[mkf, K.identf], writes=[pb] if h == 0 else [], accs=[pb] if h else [], signal=(h == 3))
        P.op("dve", lambda e, pb=pb, mt=mt: e.tensor_scalar(
            out=mkT[:, :, mt * 128:(mt + 1) * 128], in0=pb[:, :].rearrange("p (h m) -> p h m", m=128),
            scalar1=prm[:, 2:3], scalar2=None, op0=ALU.mult), reads=[pb, prm], accs=[mkT])

    qx = [P.tile([128, TP], F32, f"mqx{tag}{i}") for i in range(2)]
    gx = [P.tile([128, TP], F32, f"mgx{tag}{i}") for i in range(2)]
    sq = P.tile([128, TP], F32, f"msq{tag}")
    rs = P.tile([128, TP], F32, f"mrs{tag}")
    qn = P.tile([128, TP], BF16, f"mqn{tag}")
    E = [P.tile([128, TP], BF16, f"mE{tag}{i}") for i in range(2)]
    dn = P.tile([128, TP], F32, f"mdn{tag}")
    yy = P.tile([128, TP], F32, f"myy{tag}")
    ob = [P.tile([128, TP], BF16, f"mob{tag}{i}") for i in range(2)]
    it = 0
    for pi in range(npc):
        t0 = pi * TP
        for h in range(4):
            QX, GX, O = qx[it % 2], gx[it % 2], ob[it % 2]
            it += 1
            P.dma("sp", QX.c, lambda e, QX=QX, h=h, t0=t0: e.dma_start(out=QX[:], in_=qsrc(h)[:, t0:t0 + TP]), writes=[QX])
            P.dma("sp", GX.c, lambda e, GX=GX, h=h, t0=t0: e.dma_start(out=GX[:], in_=gsrc(h)[:, t0:t0 + TP]), writes=[GX])
            P.op("act", lambda e, QX=QX: e.activation(out=sq[:], in_=QX[:], func=AF.Square), reads=[QX], writes=[sq])
            pn, ps0, ps1, pnum, pden = pp[0], pp[1], pp[2], pp[3], pp[4]
            P.op("pe", lambda e, pn=pn: e.matmul(pn[:, 0:TP], lhsT=K.ones_f[:, :], rhs=sq[:], start=True, stop=True),
                 reads=[K.ones_f, sq], writes=[pn])
            P.op("act", lambda e, pn=pn: e.activation(out=rs[:], in_=pn[:, 0:TP], func=AF.Ln, scale=1.0 / 128, bias=1e-6),
                 reads=[pn], writes=[rs])
            P.op("act", lambda e: e.activation(out=rs[:], in_=rs[:], func=AF.Exp, scale=-0.5), reads=[rs], writes=[rs])
            P.op("dve", lambda e, QX=QX: e.scalar_tensor_tensor(out=qn[:], in0=QX[:], scalar=prm[:, 0:1], in1=rs[:],
                                                               op0=ALU.mult, op1=ALU.mult), reads=[QX, prm, rs], writes=[qn])
            for mt, psb in ((0, ps0), (1, ps1)):
                P.op("pe", lambda e, psb=psb, mt=mt, h=h: e.matmul(psb[:, 0:TP], lhsT=mkT[:, h, mt * 128:(mt + 1) * 128],
                                                                  rhs=qn[:], start=True, stop=True),
                     reads=[mkT, qn], writes=[psb])
                P.op("act", lambda e, psb=psb, mt=mt: e.activation(out=E[mt][:], in_=psb[:, 0:TP], func=AF.Exp),
                     reads=[psb], writes=[E[mt]])
            for mt in range(2):
                P.op("pe", lambda e, mt=mt, h=h, pnum=pnum: e.matmul(pnum[:, 0:TP], lhsT=mvb[:, mt, h * 128:(h + 1) * 128],
                                                                    rhs=E[mt][:], start=(mt == 0), stop=(mt == 1)),
                     reads=[mvb, E[mt]], writes=[pnum] if mt == 0 else [], accs=[pnum] if mt else [], signal=(mt == 1))
            for mt in range(2):
                P.op("pe", lambda e, mt=mt, pden=pden: e.matmul(pden[:, 0:TP], lhsT=K.ones_b[:, :], rhs=E[mt][:],
                                                               start=(mt == 0), stop=(mt == 1)),
                     reads=[K.ones_b, E[mt]], writes=[pden] if mt == 0 else [], accs=[pden] if mt else [],
                     signal=(mt == 1))
            P.op("act", lambda e, pden=pden: e.activation(out=dn[:], in_=pden[:, 0:TP], func=AF.Ln), reads=[pden], writes=[dn])
            P.op("act", lambda e: e.activation(out=dn[:], in_=dn[:], func=AF.Exp, scale=-1.0), reads=[dn], writes=[dn])
            P.op("dve", lambda e, pnum=pnum: e.tensor_tensor(out=yy[:], in0=pnum[:, 0:TP], in1=dn[:], op=ALU.mult),
                 reads=[pnum, dn], writes=[yy])
            P.op("act", lambda e, GX=GX: e.activation(out=GX[:], in_=GX[:], func=AF.Silu), reads=[GX], writes=[GX])
            P.op("dve", lambda e, GX=GX, O=O: e.tensor_tensor(out=O[:], in0=yy[:], in1=GX[:], op=ALU.mult),
                 reads=[yy, GX], writes=[O])
            P.dma("sp", O.c, lambda e, O=O, h=h, t0=t0: e.dma_start(out=odst(h)[:, t0:t0 + TP], in_=O[:]), reads=[O])


class RwConsts:
    def __init__(self, P, c_ui, c_sl, c_reset):
        self.ui = P.tile([128, 256], F32, "rw_ui", persistent=True)
        self.sl = P.tile([128, 128], F32, "rw_sl", persistent=True)
        self.reset = P.tile([64, 1024], F32, "rw_reset", persistent=True)
        P.dma("sp", self.ui.c, lambda e: e.dma_start(out=self.ui[:], in_=c_ui), writes=[self.ui])
        P.dma("sp", self.sl.c, lambda e: e.dma_start(out=self.sl[:], in_=c_sl), writes=[self.sl])
        P.dma("sp", self.reset.c, lambda e: e.dma_start(out=self.reset[:], in_=c_reset), writes=[self.reset])


class Bank:
    def __init__(self, P, name):
        self.t = P.tile([128, 512], F32, name, psum=True)
        self.q = [self.t.r] * 4

    def __getitem__(self, idx):
        return self.t[idx]


def rwkv_stage(P, K, RK, NTOK, rrows, krows, vrows, wdrows, adrows, grows, orows, prm_dram, wup_dram, aup_dram, tag="",
               do_chunk=True, max_it=6):
    TP = 1024
    CH = 128
    npc = NTOK // TP
    ncp = TP // CH
    f = F32
    prm = P.tile([64, 24], f, f"rprm{tag}")
    lup = P.tile([64, 128], f, f"rlup{tag}")
    P.dma("sp", prm.c, lambda e: e.dma_start(out=prm[:, 0:16], in_=prm_dram, allow_slow_non_contiguous=True), writes=[prm])
    P.op("pool", lambda e: e.memset(lup[:], 0.0), writes=[lup])
    P.dma("sp", lup.c, lambda e: e.dma_start(out=lup[0:32, 0:64], in_=wup_dram), accs=[lup])
    P.dma("sp", lup.c, lambda e: e.dma_start(out=lup[32:64, 64:128], in_=aup_dram), accs=[lup])
    P.op("dve", lambda e: e.tensor_scalar(out=prm[:, 16:20], in0=prm[:, 0:4], scalar1=-1.0, scalar2=1.0, op0=ALU.mult,
                                         op1=ALU.add), reads=[prm], accs=[prm])
    P.op("dve", lambda e: e.tensor_scalar(out=prm[:, 20:21], in0=prm[:, 7:8], scalar1=-1.0, scalar2=1.0, op0=ALU.mult,
                                         op1=ALU.add), reads=[prm], accs=[prm])
    xin = {nm: [P.tile([64, TP + 1], f, f"rx{nm}{tag}{i}") for i in range(2)] for nm in ("r", "k", "v", "l")}
    gin = [P.tile([64, TP], f, f"rg{tag}{i}") for i in range(2)]
    T = {nm: P.tile([64, TP], f, f"r_{nm}{tag}") for nm in
         ("R", "Kt", "V", "L", "tmp", "SG", "A", "LW", "cum", "KK", "Kp", "Bv", "e1", "e2",
          "bon", "Y", "t2")}
    for nm in ("Bt", "Ktl", "Bh", "Kh", "Vb"):
        T[nm] = P.tile([64, TP], BF16, f"r_{nm}{tag}")
    AR = P.tile([64, ncp, 256], BF16, f"r_AR{tag}")
    ARf = P.tile([64, ncp, 128], f, f"r_ARf{tag}")
    ob = [P.tile([64, TP], BF16, f"rob{tag}{i}") for i in range(2)]
    Sb = [P.tile([64, 64], f, f"rS{tag}{i}") for i in range(4)]
    G = 3
    banks = [(Bank(P, f"rbA{tag}{g}"), Bank(P, f"rbB{tag}{g}")) for g in range(G)]
    ctxs = []
    for par in range(2):
        row = []
        for g in range(G):
            c = dict(bA=banks[g][0], bB=banks[g][1], bC=banks[g][1])
            for nm, shp in (("Gm1", [128, 256]), ("Gm2", [128, 256]), ("P0", [128, 128]), ("P1", [128, 128]),
                            ("PT0", [128, 128]), ("PT1", [128, 128]), ("T", [128, 128]), ("TM", [128, 320]),
                            ("AU", [128, 128]), ("Mt", [64, 64]), ("Qt", [64, 128])):
                c[nm] = P.tile(shp, f if nm in ("Mt", "Qt") else BF16, f"rc{nm}{tag}{par}{g}")
            row.append(c)
        ctxs.append(row)
    ngrp = 0
    bY = Bank(P, f"rbY{tag}")
    bS = Bank(P, f"rbS{tag}")
    bN = [banks[0][1], banks[1][1]]
    nbn = 0
    ones64 = K.ones_f[0:64, 0:64]
    id64 = K.identf[0:64, 0:64]
    P.op("dve", lambda e: e.memset(Sb[0][:], 0.0), writes=[Sb[0]])
    sidx = 0

    def v3(t):
        return t[:, :].rearrange("p (c t) -> p c t", t=CH)

    def ones_mm(src_ap_fn, nsub, consume):
        nonlocal nbn
        for sb_ in range(nsub):
            bk = bN[nbn % 2]
            nbn += 1
            rd = src_ap_fn(sb_)
            P.op("pe", lambda e, bk=bk, rd=rd: e.matmul(bk[0:64, :], lhsT=ones64, rhs=rd[0], start=True, stop=True),
                 reads=[K.ones_f] + rd[1], writes=bk.q)
            consume(sb_, bk)

    def serial_phase(grp, ctx):
        nonlocal sidx
        for g, c in enumerate(grp):
            C = ctx[g]
            S0 = Sb[sidx % 4]
            S1 = Sb[(sidx + 1) % 4]
            sidx += 1
            ycol = (c % 4) * 128
            yq = bY.q[c % 4]
            P.op("pe", lambda e, S0=S0, C=C, ycol=ycol: e.matmul(bY[0:64, ycol:ycol + 128], lhsT=S0[:], rhs=C["Qt"][:],
                                                                start=True, stop=False),
                 reads=[S0, C["Qt"]], writes=[yq], signal=False)
            P.op("pe", lambda e, C=C, ycol=ycol: e.matmul(bY[0:64, ycol:ycol + 128], lhsT=C["AU"][:, 64:128],
                                                         rhs=C["Gm1"][:, 128:256], start=False, stop=False),
                 reads=[C["AU"], C["Gm1"]], accs=[yq], signal=False)
            P.op("pe", lambda e, C=C, ycol=ycol: e.matmul(bY[0:64, ycol:ycol + 128], lhsT=C["TM"][:, 128:192],
                                                         rhs=C["Gm2"][:, 128:256], start=False, stop=True),
                 reads=[C["TM"], C["Gm2"]], accs=[yq], signal=False)
            P.op("pe", lambda e, C=C: e.matmul(bS[0:64, 0:64], lhsT=C["TM"][:, 192:256], rhs=C["AU"][:, 64:128],
                                               start=True, stop=False), reads=[C["TM"], C["AU"]], writes=[bS.q[0]], signal=False)
            P.op("pe", lambda e, C=C: e.matmul(bS[0:64, 0:64], lhsT=C["TM"][:, 256:320], rhs=C["TM"][:, 128:192],
                                               start=False, stop=False), reads=[C["TM"]], accs=[bS.q[0]], signal=False)
            P.op("pe", lambda e, C=C, S0=S0: e.matmul(bS[0:64, 0:64], lhsT=C["Mt"][:], rhs=S0[:], start=False, stop=True),
                 reads=[C["Mt"], S0], accs=[bS.q[0]])
            P.op("act", lambda e, S1=S1: e.activation(out=S1[:], in_=bS[0:64, 0:64], func=AF.Copy),
                 reads=[bS.q[0]], writes=[S1])
            if c % 4 == 3:
                y0 = (c - 3) * CH
                P.op("act", lambda e, y0=y0: e.activation(out=T["Y"][:, y0:y0 + 512], in_=bY[0:64, :], func=AF.Copy),
                     reads=bY.q, writes=[T["Y"]] if c == 3 else [], accs=[] if c == 3 else [T["Y"]])

    for pi in range(npc):
        s = pi % 2
        t0 = pi * TP
        XR, XK, XV, XL, GX, O = xin["r"][s], xin["k"][s], xin["v"][s], xin["l"][s], gin[s], ob[s]
        for X, rows_list in ((XR, [(rrows, 0, 64)]), (XK, [(krows, 0, 64)]), (XV, [(vrows, 0, 64)]),
                             (XL, [(wdrows, 0, 32), (adrows, 32, 64)])):
            first = True
            if pi == 0:
                P.op("pool", lambda e, X=X: e.memset(X[:, 0:1], 0.0), writes=[X])
                first = False
            for (src, p0, p1) in rows_list:
                if pi == 0:
                    P.dma("sp", X.c, lambda e, X=X, src=src, p0=p0, p1=p1: e.dma_start(out=X[p0:p1, 1:TP + 1], in_=src[:, 0:TP]),
                          accs=[X])
                else:
                    P.dma("sp", X.c, lambda e, X=X, src=src, p0=p0, p1=p1, t0=t0: e.dma_start(
                        out=X[p0:p1, :], in_=src[:, t0 - 1:t0 + TP]), writes=[X] if first else [], accs=[] if first else [X])
                first = False
        P.dma("sp", GX.c, lambda e, GX=GX, t0=t0: e.dma_start(out=GX[:], in_=grows[:, t0:t0 + TP]), writes=[GX])
        for X, dst, col in ((XR, T["R"], 0), (XK, T["Kt"], 1), (XV, T["V"], 2), (XL, T["L"], 3)):
            P.op("act", lambda e, X=X, col=col: e.activation(out=T["tmp"][:], in_=X[:, 0:TP], func=AF.Copy,
                                                            scale=prm[:, col:col + 1]), reads=[X, prm], writes=[T["tmp"]])
            P.op("dve", lambda e, X=X, dst=dst, col=col: e.scalar_tensor_tensor(
                out=dst[:], in0=X[:, 1:TP + 1], scalar=prm[:, 16 + col:17 + col], in1=T["tmp"][:], op0=ALU.mult, op1=ALU.add),
                reads=[X, prm, T["tmp"]], writes=[dst])
        P.op("act", lambda e: e.activation(out=T["L"][0:32, :], in_=T["L"][0:32, :], func=AF.Tanh), reads=[T["L"]], writes=[T["L"]])
        for (lo, hi, bcol, dst) in ((0, 32, 4, T["SG"]), (32, 64, 5, T["A"])):
            for sb_ in range(TP // 512):
                bk = bN[nbn % 2]
                nbn += 1
                P.op("pe", lambda e, bk=bk, lo=lo, hi=hi, sb_=sb_: e.matmul(bk[0:64, :], lhsT=lup[:, 2 * lo:2 * lo + 64],
                                                                           rhs=T["L"][:, sb_ * 512:(sb_ + 1) * 512],
                                                                           start=True, stop=True),
                     reads=[lup, T["L"]], writes=bk.q)
                P.op("act", lambda e, bk=bk, dst=dst, sb_=sb_, bcol=bcol: e.activation(
                    out=dst[:, sb_ * 512:(sb_ + 1) * 512], in_=bk[0:64, :], func=AF.Sigmoid, bias=prm[:, bcol:bcol + 1]),
                    reads=bk.q + [prm], writes=[dst] if sb_ == 0 else [], accs=[] if sb_ == 0 else [dst])
        P.op("dve", lambda e: e.tensor_scalar(out=T["LW"][:], in0=T["SG"][:], scalar1=-0.6065306597126334, scalar2=None,
                                             op0=ALU.mult), reads=[T["SG"]], writes=[T["LW"]])
        P.op("dve", lambda e: e.tensor_tensor_scan(out=T["cum"][:], data0=RK.reset[:, 0:TP], data1=T["LW"][:], initial=0.0,
                                                  op0=ALU.mult, op1=ALU.add), reads=[RK.reset, T["LW"]], writes=[T["cum"]])
        P.op("dve", lambda e: e.tensor_tensor(out=T["tmp"][:], in0=T["cum"][:], in1=T["LW"][:], op=ALU.subtract),
             reads=[T["cum"], T["LW"]], writes=[T["tmp"]])
        P.op("act", lambda e: e.activation(out=T["e1"][:], in_=T["tmp"][:], func=AF.Exp), reads=[T["tmp"]], writes=[T["e1"]])
        P.op("act", lambda e: e.activation(out=T["KK"][:], in_=T["Kt"][:], func=AF.Copy, scale=prm[:, 6:7]),
             reads=[T["Kt"], prm], writes=[T["KK"]])
        P.op("act", lambda e: e.activation(out=T["tmp"][:], in_=T["KK"][:], func=AF.Square),
             reads=[T["KK"]], writes=[T["tmp"]])

        def cons_kk(sb_, bk):
            P.op("act", lambda e, bk=bk, sb_=sb_: e.activation(out=T["t2"][:, sb_ * 512:(sb_ + 1) * 512], in_=bk[0:64, :],
                                                              func=AF.Ln, bias=1e-24), reads=bk.q,
                 writes=[T["t2"]] if sb_ == 0 else [], accs=[] if sb_ == 0 else [T["t2"]])
        ones_mm(lambda sb_: (T["tmp"][:, sb_ * 512:(sb_ + 1) * 512], [T["tmp"]]), TP // 512, cons_kk)
        P.op("act", lambda e: e.activation(out=T["t2"][:], in_=T["t2"][:], func=AF.Exp, scale=-0.5),
             reads=[T["t2"]], writes=[T["t2"]])
        P.op("dve", lambda e: e.tensor_tensor(out=T["KK"][:], in0=T["KK"][:], in1=T["t2"][:], op=ALU.mult),
             reads=[T["KK"], T["t2"]], writes=[T["KK"]])
        P.op("dve", lambda e: e.scalar_tensor_tensor(out=AR[:, :, 0:128], in0=v3(T["KK"]), scalar=-1.0, in1=v3(T["e1"]),
                                                    op0=ALU.mult, op1=ALU.mult), reads=[T["KK"], T["e1"]], writes=[AR])
        P.op("act", lambda e: e.activation(out=T["tmp"][:], in_=T["A"][:], func=AF.Identity, scale=prm[:, 7:8],
                                           bias=prm[:, 20:21]), reads=[T["A"], prm], writes=[T["tmp"]])
        P.op("dve", lambda e: e.tensor_tensor(out=T["Kp"][:], in0=T["Kt"][:], in1=T["tmp"][:], op=ALU.mult),
             reads=[T["Kt"], T["tmp"]], writes=[T["Kp"]])
        P.op("dve", lambda e: e.tensor_tensor(out=T["Bv"][:], in0=T["KK"][:], in1=T["A"][:], op=ALU.mult),
             reads=[T["KK"], T["A"]], writes=[T["Bv"]])
        P.op("act", lambda e: e.activation(out=T["e1"][:], in_=T["cum"][:], func=AF.Exp), reads=[T["cum"]], writes=[T["e1"]])
        P.op("act", lambda e: e.activation(out=T["e2"][:], in_=T["cum"][:], func=AF.Exp, scale=-1.0), reads=[T["cum"]],
             writes=[T["e2"]])
        P.op("dve", lambda e: e.tensor_tensor(out=ARf[:, :, :], in0=v3(T["R"]), in1=v3(T["e1"]), op=ALU.mult),
             reads=[T["R"], T["e1"]], writes=[ARf])
        P.op("act", lambda e: e.activation(out=AR[:, :, 128:256], in_=ARf[:, :, :], func=AF.Copy), reads=[ARf], accs=[AR])
        P.op("act", lambda e: e.activation(out=T["Vb"][:], in_=T["V"][:], func=AF.Copy), reads=[T["V"]], writes=[T["Vb"]])
        P.op("dve", lambda e: e.tensor_tensor(out=T["Bt"][:], in0=T["Bv"][:], in1=T["e2"][:], op=ALU.mult),
             reads=[T["Bv"], T["e2"]], writes=[T["Bt"]])
        P.op("dve", lambda e: e.tensor_tensor(out=T["Ktl"][:], in0=T["Kp"][:], in1=T["e2"][:], op=ALU.mult),
             reads=[T["Kp"], T["e2"]], writes=[T["Ktl"]])
        for c in range(ncp):
            cs = slice(c * CH, (c + 1) * CH)
            ge = slice(c * CH + CH - 1, c * CH + CH)
            P.op("dve", lambda e, cs=cs, ge=ge: e.tensor_scalar(out=T["Bh"][:, cs], in0=T["Bt"][:, cs], scalar1=T["e1"][:, ge],
                                                               scalar2=None, op0=ALU.mult),
                 reads=[T["Bt"], T["e1"]], writes=[T["Bh"]] if c == 0 else [], accs=[] if c == 0 else [T["Bh"]])
            P.op("act", lambda e, cs=cs, ge=ge: e.activation(out=T["Kh"][:, cs], in_=T["Ktl"][:, cs], func=AF.Copy,
                                                            scale=T["e1"][:, ge]),
                 reads=[T["Ktl"], T["e1"]], writes=[T["Kh"]] if c == 0 else [], accs=[] if c == 0 else [T["Kh"]])
        P.op("dve", lambda e: e.scalar_tensor_tensor(out=T["tmp"][:], in0=T["R"][:], scalar=prm[:, 8:9], in1=T["Kp"][:],
                                                     op0=ALU.mult, op1=ALU.mult), reads=[T["R"], prm, T["Kp"]], writes=[T["tmp"]])

        def cons_bon(sb_, bk):
            P.op("dve", lambda e, bk=bk, sb_=sb_: e.tensor_tensor(out=T["bon"][:, sb_ * 512:(sb_ + 1) * 512], in0=bk[0:64, :],
                                                                 in1=T["V"][:, sb_ * 512:(sb_ + 1) * 512], op=ALU.mult),
                 reads=bk.q + [T["V"]], writes=[T["bon"]] if sb_ == 0 else [], accs=[] if sb_ == 0 else [T["bon"]])
        ones_mm(lambda sb_: (T["tmp"][:, sb_ * 512:(sb_ + 1) * 512], [T["tmp"]]), TP // 512, cons_bon)

        if not do_chunk:
            P.op("dve", lambda e: e.memset(T["Y"][:], 0.0), writes=[T["Y"]])
        groups = [list(range(i, min(i + G, ncp))) for i in range(0, ncp, G)] if do_chunk else []
        pending_serial = None
        for grp in groups:
            steps = []
            ctx = ctxs[ngrp % 2]
            ngrp += 1
            for g, c in enumerate(grp):
                C = ctx[g]
                bA, bB = C["bA"], C["bB"]
                cs = slice(c * CH, (c + 1) * CH)
                st = []

                def sA(C=C, bA=bA, bB=bB, c=c, cs=cs):
                    P.op("pe", lambda e: e.matmul(bA[:, 0:256], lhsT=T["Bt"][:, cs], rhs=AR[:, c, :], start=True, stop=True),
                         reads=[T["Bt"], AR], writes=bA.q, signal=False)
                    P.op("pe", lambda e: e.matmul(bA[:, 256:512], lhsT=T["Ktl"][:, cs], rhs=AR[:, c, :], start=True, stop=True),
                         reads=[T["Ktl"], AR], accs=bA.q, signal=False)
                    P.op("pe", lambda e: e.matmul(bB[:, 0:128], lhsT=AR[:, c, 0:128], rhs=T["Bt"][:, cs], start=True, stop=True),
                         reads=[T["Bt"], AR], writes=[bB.q[0]], signal=False)
                    for i4, src in enumerate((AR[:, c, 0:128], T["Vb"][:, cs], T["Bh"][:, cs], T["Kh"][:, cs])):
                        P.op("pe", lambda e, src=src, i4=i4: e.matmul(bB[:, 128 + i4 * 64:192 + i4 * 64], lhsT=src,
                                                                     rhs=K.identb[0:64, 0:64], start=True, stop=True),
                             reads=[AR, T["Vb"], T["Bh"], T["Kh"], K.identb], writes=[bB.q[1], bB.q[2]] if i4 == 0 else [],
                             accs=[] if i4 == 0 else [bB.q[1], bB.q[2]], signal=(i4 == 3))
                st.append(sA)

                def sAe(C=C, bA=bA, bB=bB):
                    P.op("dve", lambda e: e.tensor_tensor(out=C["Gm1"][:], in0=bA[:, 0:256], in1=RK.ui[:], op=ALU.mult),
                         reads=[bA.q[0], bA.q[1], RK.ui], writes=[C["Gm1"]])
                    P.op("dve", lambda e: e.tensor_tensor(out=C["Gm2"][:], in0=bA[:, 256:512], in1=RK.ui[:], op=ALU.mult),
                         reads=[bA.q[2], bA.q[3], RK.ui], writes=[C["Gm2"]])
                    P.op("dve", lambda e: e.tensor_tensor(out=C["PT0"][:], in0=bB[:, 0:128], in1=RK.sl[:], op=ALU.mult),
                         reads=[bB.q[0], RK.sl], writes=[C["PT0"]])
                    P.op("pool", lambda e: e.tensor_tensor(out=C["T"][:], in0=C["Gm1"][:, 0:128], in1=K.identb[:], op=ALU.add),
                         reads=[C["Gm1"], K.identb], writes=[C["T"]])
                    P.op("act", lambda e: e.activation(out=C["TM"][:, 0:64], in_=bB[:, 128:192], func=AF.Copy),
                         reads=[bB.q[1]], writes=[C["TM"]])
                    P.op("act", lambda e: e.activation(out=C["TM"][:, 128:320], in_=bB[:, 192:384], func=AF.Copy),
                         reads=[bB.q[1], bB.q[2]], accs=[C["TM"]])
                st.append(sAe)
                for it in range(6):
                    last = (it == 5)

                    def sI1(C=C, bA=bA, it=it, last=last):
                        Pc = C["Gm1"] if it == 0 else C[f"P{it % 2}"]
                        Pc_ap = C["Gm1"][:, 0:128] if it == 0 else C[f"P{it % 2}"][:]
                        PTc = C["PT0"] if it == 0 else C[f"PT{it % 2}"]
                        if not last:
                            P.op("pe", lambda e: e.matmul(bA[:, 0:128], lhsT=PTc[:], rhs=Pc_ap, start=True, stop=True),
                                 reads=[PTc, Pc], writes=[bA.q[0]], signal=False)
                        P.op("pe", lambda e: e.matmul(bA[:, 128:256], lhsT=Pc_ap, rhs=PTc[:], start=True, stop=True),
                             reads=[PTc, Pc], writes=[bA.q[1]])
                    st.append(sI1)

                    def sI2(C=C, bA=bA, it=it, last=last):
                        Pn = C[f"P{(it + 1) % 2}"]
                        PTn = C[f"PT{(it + 1) % 2}"]
                        if it == 0:
                            PTn = C["PT1"]
                        P.op("act", lambda e: e.activation(out=PTn[:], in_=bA[:, 128:256], func=AF.Copy),
                             reads=[bA.q[1]], writes=[PTn])
                        if not last:
                            P.op("act", lambda e: e.activation(out=Pn[:], in_=bA[:, 0:128], func=AF.Copy),
                                 reads=[bA.q[0]], writes=[Pn])
                    st.append(sI2)

                    def sI3(C=C, bC=C["bC"], it=it):
                        PTn = C[f"PT{(it + 1) % 2}"]
                        if it == 0:
                            PTn = C["PT1"]
                        P.op("pe", lambda e: e.matmul(bC[:, 0:128], lhsT=PTn[:], rhs=C["T"][:], start=True, stop=True),
                             reads=[PTn, C["T"]], writes=[bC.q[0]])
                    st.append(sI3)

                    def sI4(C=C, bC=C["bC"]):
                        P.op("dve", lambda e: e.tensor_tensor(out=C["T"][:], in0=bC[:, 0:128], in1=C["T"][:], op=ALU.add),
                             reads=[bC.q[0], C["T"]], writes=[C["T"]])
                    st.append(sI4)

                def sW(C=C, bB=bB):
                    P.op("pe", lambda e: e.matmul(bB[:, 384:448], lhsT=C["Gm2"][:, 0:128], rhs=C["TM"][:, 128:192],
                                                  start=True, stop=True), reads=[C["Gm2"], C["TM"]], writes=[bB.q[3]])
                st.append(sW)

                def sWe(C=C, bB=bB):
                    P.op("act", lambda e: e.activation(out=C["TM"][:, 64:128], in_=bB[:, 384:448], func=AF.Copy),
                         reads=[bB.q[3]], accs=[C["TM"]])
                st.append(sWe)

                def sAU(C=C, bA=bA):
                    P.op("pe", lambda e: e.matmul(bA[:, 384:512], lhsT=C["T"][:], rhs=C["TM"][:, 0:128], start=True, stop=True),
                         reads=[C["T"], C["TM"]], writes=[bA.q[3]])
                st.append(sAU)

                def sAUe(C=C, bA=bA):
                    P.op("act", lambda e: e.activation(out=C["AU"][:], in_=bA[:, 384:512], func=AF.Copy),
                         reads=[bA.q[3]], writes=[C["AU"]])
                st.append(sAUe)

                def sMQ(C=C, bB=bB):
                    P.op("pe", lambda e: e.matmul(bB[0:64, 448:512], lhsT=C["AU"][:, 0:64], rhs=C["TM"][:, 192:256],
                                                  start=True, stop=True), reads=[C["AU"], C["TM"]], writes=[bB.q[3]], signal=False)
                    P.op("pe", lambda e: e.matmul(bB[0:64, 0:128], lhsT=C["AU"][:, 0:64], rhs=C["Gm1"][:, 128:256],
                                                  start=True, stop=True), reads=[C["AU"], C["Gm1"]], writes=[bB.q[0]])
                st.append(sMQ)

                def sMQe(C=C, bB=bB, c=c, cs=cs):
                    ge = slice(c * CH + CH - 1, c * CH + CH)
                    P.op("dve", lambda e: e.scalar_tensor_tensor(out=C["Mt"][:], in0=id64, scalar=T["e1"][:, ge],
                                                                in1=bB[0:64, 448:512], op0=ALU.mult, op1=ALU.add),
                         reads=[K.identf, T["e1"], bB.q[3]], writes=[C["Mt"]])
                    P.op("dve", lambda e: e.tensor_tensor(out=C["Qt"][:], in0=bB[0:64, 0:128], in1=ARf[:, c, :], op=ALU.add),
                         reads=[bB.q[0], ARf], writes=[C["Qt"]])
                st.append(sMQe)
                steps.append(st)
            for si in range(len(steps[0])):
                for g in range(len(grp)):
                    steps[g][si]()
            if pending_serial is not None:
                pending_serial()
            pending_serial = (lambda grp=grp, ctx=ctx: serial_phase(grp, ctx))
        if pending_serial is not None:
            pending_serial()
            pending_serial = None
        def cons_mean(sb_, bk):
            ss = slice(sb_ * 512, (sb_ + 1) * 512)
            P.op("dve", lambda e, bk=bk, ss=ss: e.scalar_tensor_tensor(out=T["tmp"][:, ss], in0=bk[0:64, :], scalar=-1.0 / 64,
                                                                      in1=T["Y"][:, ss], op0=ALU.mult, op1=ALU.add),
                 reads=bk.q + [T["Y"]], writes=[T["tmp"]] if sb_ == 0 else [], accs=[] if sb_ == 0 else [T["tmp"]])
        ones_mm(lambda sb_: (T["Y"][:, sb_ * 512:(sb_ + 1) * 512], [T["Y"]]), TP // 512, cons_mean)
        P.op("act", lambda e: e.activation(out=T["t2"][:], in_=T["tmp"][:], func=AF.Square),
             reads=[T["tmp"]], writes=[T["t2"]])

        def cons_var(sb_, bk):
            ss = slice(sb_ * 512, (sb_ + 1) * 512)
            P.op("act", lambda e, bk=bk, ss=ss: e.activation(out=T["e2"][:, ss], in_=bk[0:64, :], func=AF.Ln, scale=1.0 / 64,
                                                            bias=64e-5), reads=bk.q,
                 writes=[T["e2"]] if sb_ == 0 else [], accs=[] if sb_ == 0 else [T["e2"]])
        ones_mm(lambda sb_: (T["t2"][:, sb_ * 512:(sb_ + 1) * 512], [T["t2"]]), TP // 512, cons_var)
        P.op("act", lambda e: e.activation(out=T["e2"][:], in_=T["e2"][:], func=AF.Exp, scale=-0.5), reads=[T["e2"]], writes=[T["e2"]])
        P.op("dve", lambda e: e.tensor_tensor(out=T["tmp"][:], in0=T["tmp"][:], in1=T["e2"][:], op=ALU.mult),
             reads=[T["tmp"], T["e2"]], writes=[T["tmp"]])
        P.op("dve", lambda e: e.tensor_scalar(out=T["tmp"][:], in0=T["tmp"][:], scalar1=prm[:, 9:10], scalar2=prm[:, 10:11],
                                             op0=ALU.mult, op1=ALU.add), reads=[T["tmp"], prm], writes=[T["tmp"]])
        P.op("dve", lambda e: e.tensor_tensor(out=T["tmp"][:], in0=T["tmp"][:], in1=T["bon"][:], op=ALU.add),
             reads=[T["tmp"], T["bon"]], writes=[T["tmp"]])
        P.op("act", lambda e, GX=GX: e.activation(out=GX[:], in_=GX[:], func=AF.Silu), reads=[GX], writes=[GX])
        P.op("dve", lambda e, GX=GX, O=O: e.tensor_tensor(out=O[:], in0=T["tmp"][:], in1=GX[:], op=ALU.mult),
             reads=[T["tmp"], GX], writes=[O])
        P.dma("sp", O.c, lambda e, O=O, t0=t0: e.dma_start(out=orows[:, t0:t0 + TP], in_=O[:]), reads=[O])


def swa_stage4(P, K, NTOK, qrows, krows, vrows, grows, orows, qg, kg, sinks, TP=512, tag=""):
    npc = NTOK // TP
    nbk = TP // 128
    NH = 4
    prm = P.tile([64, 16], F32, f"s4prm{tag}")
    P.dma("sp", prm.c, lambda e: e.dma_start(out=prm[:, 0:1], in_=qg, allow_slow_non_contiguous=True), writes=[prm])
    P.dma("sp", prm.c, lambda e: e.dma_start(out=prm[:, 1:2], in_=kg, allow_slow_non_contiguous=True), accs=[prm])
    for h in range(NH):
        P.dma("sp", prm.c, lambda e, h=h: e.dma_start(out=prm[:, 4 + h:5 + h], in_=sinks[h], allow_slow_non_contiguous=True),
              accs=[prm])
    P.op("dve", lambda e: e.tensor_scalar(out=prm[:, 3:4], in0=prm[:, 0:1], scalar1=0.125, scalar2=None, op0=ALU.mult),
         reads=[prm], accs=[prm])
    P.op("act", lambda e: e.activation(out=prm[:, 8:12], in_=prm[:, 4:8], func=AF.Exp), reads=[prm], accs=[prm])
    sinkrow = P.tile([64, NH, 128], F32, f"s4sink{tag}")
    for h in range(NH):
        P.op("act", lambda e, h=h: e.activation(out=sinkrow[:, h, :], in_=K.ones_f[0:64, 0:128], func=AF.Identity, scale=0.0,
                                               bias=prm[:, 8 + h:9 + h]), reads=[K.ones_f, prm],
             writes=[sinkrow] if h == 0 else [], accs=[] if h == 0 else [sinkrow])
    mask4 = P.tile([128, 2, NH, 128], BF16, f"s4mask{tag}")
    for j in range(2):
        for h in range(NH):
            P.op("pool", lambda e, j=j, h=h: e.tensor_copy(out=mask4[:, j, h, :], in_=K.swamask[:, j * 128:(j + 1) * 128]),
                 reads=[K.swamask], writes=[mask4] if (j == 0 and h == 0) else [], accs=[] if (j == 0 and h == 0) else [mask4])
    W = TP + 128
    kx = [P.tile([64, W], F32, f"s4kx{tag}{i}") for i in range(2)]
    vx = [P.tile([64, W], F32, f"s4vx{tag}{i}") for i in range(2)]
    qx = [P.tile([64, NH, TP], F32, f"s4qx{tag}{i}") for i in range(2)]
    gx = [P.tile([64, NH, TP], F32, f"s4gx{tag}{i}") for i in range(2)]
    sq = P.tile([64, NH * TP], F32, f"s4sq{tag}")
    rs = P.tile([64, NH * TP], F32, f"s4rs{tag}")
    kn = P.tile([64, W], BF16, f"s4kn{tag}")
    qn = P.tile([64, NH, TP], BF16, f"s4qn{tag}")
    vb = P.tile([128, nbk + 1, 64], BF16, f"s4vb{tag}")
    E = [[P.tile([128, NH, 128], BF16, f"s4E{tag}{i}{j}") for j in range(2)] for i in range(2)]
    dn = P.tile([64, NH, 128], F32, f"s4dn{tag}")
    yy = P.tile([64, NH, TP], F32, f"s4yy{tag}")
    ob = [P.tile([64, NH, TP], BF16, f"s4ob{tag}{i}") for i in range(2)]
    pn = [P.tile([64, 512], F32, f"s4pn{tag}{i}", psum=True) for i in range(2)]
    psc = [[P.tile([128, 512], F32, f"s4psc{tag}{i}{j}", psum=True) for j in range(2)] for i in range(2)]
    pnum = P.tile([64, 512], F32, f"s4pnum{tag}", psum=True)
    pden = P.tile([64, 512], F32, f"s4pden{tag}", psum=True)
    npn = 0
    nsc = 0

    def norm(src_ap, src_t, width, gcol, dst_ap, dst_t, first_dst=True):
        nonlocal npn
        P.op("act", lambda e: e.activation(out=sq[:, 0:width], in_=src_ap, func=AF.Square), reads=[src_t], writes=[sq])
        o = 0
        first = True
        while o < width:
            w_ = min(512, width - o)
            pb = pn[npn % 2]
            npn += 1
            P.op("pe", lambda e, pb=pb, o=o, w_=w_: e.matmul(pb[:, 0:w_], lhsT=K.ones_f[0:64, 0:64], rhs=sq[:, o:o + w_],
                                                            start=True, stop=True), reads=[K.ones_f, sq], writes=[pb])
            P.op("act", lambda e, pb=pb, o=o, w_=w_: e.activation(out=rs[:, o:o + w_], in_=pb[:, 0:w_], func=AF.Ln,
                                                                 scale=1.0 / 64, bias=1e-6),
                 reads=[pb], writes=[rs] if first else [], accs=[] if first else [rs])
            first = False
            o += w_
        P.op("act", lambda e: e.activation(out=rs[:, 0:width], in_=rs[:, 0:width], func=AF.Exp, scale=-0.5),
             reads=[rs], writes=[rs])
        P.op("dve", lambda e: e.scalar_tensor_tensor(out=dst_ap, in0=src_ap, scalar=prm[:, gcol:gcol + 1], in1=rs[:, 0:width],
                                                    op0=ALU.mult, op1=ALU.mult), reads=[src_t, prm, rs],
             writes=[dst_t] if first_dst else [], accs=[] if first_dst else [dst_t])

    for pi in range(npc):
        s = pi % 2
        t0 = pi * TP
        KX, VX, QX, GX, O = kx[s], vx[s], qx[s], gx[s], ob[s]
        lo = 128 if pi == 0 else 0
        P.dma("sp", KX.c, lambda e, KX=KX, t0=t0, lo=lo: e.dma_start(out=KX[:, lo:W], in_=krows[:, t0 - 128 + lo:t0 + TP]),
              writes=[KX])
        P.dma("sp", VX.c, lambda e, VX=VX, t0=t0, lo=lo: e.dma_start(out=VX[:, lo:W], in_=vrows[:, t0 - 128 + lo:t0 + TP]),
              writes=[VX])
        for h in range(NH):
            P.dma("sp", QX.c, lambda e, QX=QX, t0=t0, h=h: e.dma_start(out=QX[:, h, :], in_=qrows[h][:, t0:t0 + TP]),
                  writes=[QX] if h == 0 else [], accs=[] if h == 0 else [QX])
            P.dma("sp", GX.c, lambda e, GX=GX, t0=t0, h=h: e.dma_start(out=GX[:, h, :], in_=grows[h][:, t0:t0 + TP]),
                  writes=[GX] if h == 0 else [], accs=[] if h == 0 else [GX])
        norm(KX[:, lo:W], KX, W - lo, 1, kn[:, lo:W], kn)
        norm(QX[:, :, :].rearrange("p h t -> p (h t)"), QX, NH * TP, 3, qn[:, :, :].rearrange("p h t -> p (h t)"), qn)
        b0 = lo // 128
        pt = pn[npn % 2]
        npn += 1
        for b in range(b0, nbk + 1):
            P.op("pe", lambda e, VX=VX, b=b, pt=pt: e.transpose(out=psc[0][0][:, b * 64:(b + 1) * 64], in_=VX[:, b * 128:(b + 1) * 128],
                                                               identity=K.identf[0:64, 0:64]),
                 reads=[VX, K.identf], writes=[psc[0][0]] if b == b0 else [], accs=[] if b == b0 else [psc[0][0]],
                 signal=(b == nbk))
        P.op("act", lambda e, b0=b0: e.activation(out=vb[:, b0:nbk + 1, :],
                                                 in_=psc[0][0][:, b0 * 64:(nbk + 1) * 64].rearrange("p (b d) -> p b d", d=64),
                                                 func=AF.Copy), reads=[psc[0][0]], writes=[vb])
        for n in range(nbk):
            has_prev = not (pi == 0 and n == 0)
            par = nsc % 2
            nsc += 1
            qs = qn[:, :, n * 128:(n + 1) * 128]
            srcs = [(0, kn[:, (n + 1) * 128:(n + 2) * 128])]
            if has_prev:
                srcs.append((1, kn[:, n * 128:(n + 1) * 128]))
            for j, kap in srcs:
                sc = psc[par][j]
                Eb = E[par][j]
                P.op("pe", lambda e, sc=sc, kap=kap, qs=qs: e.matmul(sc[:, :], lhsT=kap, rhs=qs, start=True, stop=True),
                     reads=[kn, qn], writes=[sc])
                P.op("act", lambda e, sc=sc, Eb=Eb: e.activation(out=Eb[:, :, :],
                                                                in_=sc[:, :].rearrange("p (h q) -> p h q", q=128), func=AF.Exp),
                     reads=[sc], writes=[Eb])
                P.op("dve" if j == 0 else "pool", lambda e, Eb=Eb, j=j: e.tensor_tensor(out=Eb[:, :, :], in0=Eb[:, :, :],
                                                                                       in1=mask4[:, j, :, :], op=ALU.mult),
                     reads=[Eb, mask4], writes=[Eb])
            for (pacc, lhs_cur, lhs_prev) in ((pnum, vb[:, n + 1, :], vb[:, n, :]),
                                              (pden, K.ones_b[:, 0:64], K.ones_b[:, 0:64])):
                P.op("pe", lambda e, pacc=pacc, lhs_cur=lhs_cur, par=par, has_prev=has_prev: e.matmul(
                    pacc[:, :], lhsT=lhs_cur, rhs=E[par][0][:, :, :], start=True, stop=not has_prev),
                    reads=[vb, K.ones_b, E[par][0]], writes=[pacc], signal=False)
                if has_prev:
                    P.op("pe", lambda e, pacc=pacc, lhs_prev=lhs_prev, par=par: e.matmul(
                        pacc[:, :], lhsT=lhs_prev, rhs=E[par][1][:, :, :], start=False, stop=True),
                        reads=[vb, K.ones_b, E[par][1]], accs=[pacc], signal=False)
            P.signal_last("pe")
            P.op("dve", lambda e: e.tensor_tensor(out=dn[:, :, :], in0=pden[:, :].rearrange("p (h q) -> p h q", q=128),
                                                 in1=sinkrow[:, :, :], op=ALU.add), reads=[pden, sinkrow], writes=[dn])
            P.op("act", lambda e: e.activation(out=dn[:, :, :], in_=dn[:, :, :], func=AF.Ln), reads=[dn], writes=[dn])
            P.op("act", lambda e: e.activation(out=dn[:, :, :], in_=dn[:, :, :], func=AF.Exp, scale=-1.0), reads=[dn], writes=[dn])
            P.op("dve", lambda e, n=n: e.tensor_tensor(out=yy[:, :, n * 128:(n + 1) * 128],
                                                      in0=pnum[:, :].rearrange("p (h q) -> p h q", q=128), in1=dn[:, :, :],
                                                      op=ALU.mult), reads=[pnum, dn],
                 writes=[yy] if n == 0 else [], accs=[] if n == 0 else [yy])
        P.op("act", lambda e, GX=GX: e.activation(out=GX[:, :, :], in_=GX[:, :, :], func=AF.Silu), reads=[GX], writes=[GX])
        P.op("dve", lambda e, GX=GX, O=O: e.tensor_tensor(out=O[:, :, :], in0=yy[:, :, :], in1=GX[:, :, :], op=ALU.mult),
             reads=[yy, GX], writes=[O])
        for h in range(NH):
            P.dma("sp", O.c, lambda e, O=O, t0=t0, h=h: e.dma_start(out=orows[h][:, t0:t0 + TP], in_=O[:, h, :]), reads=[O])


import ml_dtypes

NCORES = 8
SEQ = 16384
NT = SEQ // NCORES
NCH_SEQ = 69


def _consts_np():
    s_ = np.arange(128)[:, None]
    t_ = np.arange(128)[None, :]
    reset = np.ones((64, 1024), np.float32)
    reset[:, ::128] = 0
    return dict(
        c_if=np.eye(128, dtype=np.float32),
        c_ib=np.eye(128).astype(ml_dtypes.bfloat16),
        c_mask=np.concatenate([(s_ <= t_), (s_ > t_)], axis=1).astype(ml_dtypes.bfloat16),
        c_ui=np.concatenate([(t_ > s_), (t_ >= s_)], axis=1).astype(np.float32),
        c_sl=(s_ > t_).astype(np.float32),
        c_reset=reset,
    )


def _din(nc, name, shape, dt=F32):
    return nc.dram_tensor(name, list(shape), dt, kind="ExternalInput").ap()


def _dout(nc, name, shape, dt=F32):
    return nc.dram_tensor(name, list(shape), dt, kind="ExternalOutput").ap()


def _mk_consts(nc, P, rw=False):
    K = Consts(P, _din(nc, "c_if", [128, 128]), _din(nc, "c_ib", [128, 128], BF16), _din(nc, "c_mask", [128, 256], BF16))
    RK = None
    if rw:
        RK = RwConsts(P, _din(nc, "c_ui", [128, 256]), _din(nc, "c_sl", [128, 128]), _din(nc, "c_reset", [64, 1024]))
    return K, RK


def build_tok(with_out, with_proj):
    nc = bass.Bass("TRN2", target_bir_lowering=False)
    P = Prog(nc)
    K, _ = _mk_consts(nc, P)
    x = _din(nc, "x", [NT, 2048])
    xcur = x
    if with_out:
        oT = _din(nc, "oT", [2048, NT], BF16)
        wo = _din(nc, "wo", [2048, 2048])
        xout = _dout(nc, "xout", [NT, 2048])
        out_stage(P, K, NT, x, lambda k: oT[128 * k:128 * k + 128, :], wo, xout)
        P.end_stage()
        xcur = xout
    if with_proj:
        g = _din(nc, "g", [128, 16])
        w = _din(nc, "w", [2048, 5440])
        pT = _dout(nc, "pT", [NCH_SEQ * 64, NT])
        pmem = nc.dram_tensor("pmem", [1024, NT], F32).ap()
        omem = _dout(nc, "omem", [512, NT], BF16)

        def dst64(i):
            if i < NCH_SEQ:
                return pT[64 * i:64 * i + 64, :]
            j = i - NCH_SEQ
            return pmem[64 * j:64 * j + 64, :]
        proj_stage(P, K, NT, xcur, g, w, 5440, dst64)
        P.end_stage()
        mem_stage(P, K, NT, lambda h: pmem[128 * h:128 * h + 128, :], lambda h: pmem[512 + 128 * h:512 + 128 * h + 128, :],
                  lambda h: omem[128 * h:128 * h + 128, :], _din(nc, "mem", [256, 2048]), _din(nc, "memg", [128, 16]),
                  _din(nc, "wkv", [2048, 1024]), _din(nc, "mqg", [128, 1]), _din(nc, "mkg", [128, 1]))
        P.end_stage()
    P.close()
    return nc


def build_head():
    nc = bass.Bass("TRN2", target_bir_lowering=False)
    P = Prog(nc)
    K, RK = _mk_consts(nc, P, rw=True)
    pin = _din(nc, "pin", [11 * 64, SEQ])
    sm = _din(nc, "sm", [64, 32])
    lw = _din(nc, "lw", [2, 64, 64])
    lup = _din(nc, "lup", [64, 64])
    oR = _dout(nc, "oR", [192, SEQ], BF16)

    def rows(i, lo=0, hi=64):
        return pin[64 * i + lo:64 * i + hi, :]
    lru_stage(P, 64, SEQ, rows(0), rows(1), oR[0:64, :], sm[:, 0:4], sm[:, 4:5], lw[0:1], sm[:, 5:6], lw[1:2], sm[:, 6:7],
              sm[:, 7:8])
    P.end_stage()
    rwkv_stage(P, K, RK, SEQ, rows(2), rows(3), rows(4), rows(5, 0, 32), rows(5, 32, 64), rows(6), oR[64:128, :],
               sm[:, 8:24], lup[0:32, :], lup[32:64, :])
    P.end_stage()
    swa_stage(P, K, SEQ, rows(7), rows(8), rows(9), rows(10), oR[128:192, :], sm[:, 24:25], sm[:, 25:26], sm[:, 26:27])
    P.end_stage()
    P.close()
    return nc


def _g16(v):
    return np.ascontiguousarray(np.asarray(v, np.float32).reshape(16, 128).T)


def _head_small(inp, l, h):
    hs = slice(64 * h, 64 * h + 64)
    sm = np.zeros((64, 32), np.float32)
    sm[:, 0:4] = inp["conv_w"][l][:, hs].T
    sm[:, 4] = inp["conv_b"][l][hs]
    sm[:, 5] = inp["lru_ba"][l][hs]
    sm[:, 6] = inp["lru_bx"][l][hs]
    sm[:, 7] = inp["lru_lambda"][l][hs]
    mu = inp["rw_mu"][l]
    sm[:, 8] = mu[0:512][hs]
    sm[:, 9] = mu[512:1024][hs]
    sm[:, 10] = mu[1024:1536][hs]
    sm[:, 11] = mu[1536:1600]
    sm[:, 12] = inp["rw_w0"][l][hs]
    sm[:, 13] = inp["rw_a0"][l][hs]
    sm[:, 14] = inp["rw_k_k"][l][hs]
    sm[:, 15] = inp["rw_k_a"][l][hs]
    sm[:, 16] = inp["rw_r_k"][l][h]
    sm[:, 17] = inp["rw_gn_g"][l][hs]
    sm[:, 18] = inp["rw_gn_b"][l][hs]
    sm[:, 24] = inp["swa_q_g"][l]
    sm[:, 25] = inp["swa_k_g"][l]
    sm[:, 26] = inp["swa_sinks"][l][h]
    lw = np.stack([inp["lru_wa"][l][h], inp["lru_wx"][l][h]]).astype(np.float32)
    lup = np.concatenate([inp["rw_w_up"][l][:, hs], inp["rw_a_up"][l][:, hs]], axis=0).astype(np.float32)
    return sm, lw, np.ascontiguousarray(lup)


TPF = 2048


def build_fused(seq=SEQ, depth=2):
    nc = bass.Bass("TRN2", target_bir_lowering=False)
    P = Prog(nc)
    K, RK = _mk_consts(nc, P, rw=True)
    x = _din(nc, "x", [seq, 2048])
    out = _dout(nc, "out", [seq, 2048])
    mem = _din(nc, "mem", [256, 2048])
    norm_g = _din(nc, "norm_g", [depth, 128, 16])
    w_in = _din(nc, "w_in", [depth, 2048, 5440])
    memg = _din(nc, "memg", [depth, 128, 16])
    wkv = _din(nc, "wkv", [depth, 2048, 1024])
    mqk = _din(nc, "mqk", [depth, 128, 2])
    w_out = _din(nc, "w_out", [depth, 2048, 2048])
    lru_sm = _din(nc, "lru_sm", [depth, 512, 8])
    lru_w = _din(nc, "lru_w", [depth, 2, 8, 64, 64])
    rw_sm = _din(nc, "rw_sm", [depth, 8, 64, 16])
    rw_lup = _din(nc, "rw_lup", [depth, 8, 64, 64])
    swa_sm = _din(nc, "swa_sm", [depth, 8, 64, 4])
    x1 = nc.dram_tensor("x1_scr", [seq, 2048], F32).ap()
    p_lru = nc.dram_tensor("p_lru", [1024, seq], F32).ap()
    p_rw = nc.dram_tensor("p_rw", [2112, seq], F32).ap()
    p_swa = nc.dram_tensor("p_swa", [1280, seq], F32).ap()
    p_mem = nc.dram_tensor("p_mem", [1024, seq], F32).ap()
    oT = nc.dram_tensor("oT_scr", [2048, seq], BF16).ap()
    for l in range(depth):
        xin = x if l == 0 else x1
        xout = out if l == depth - 1 else x1
        for tp in range(seq // TPF):
            ts_ = slice(tp * TPF, (tp + 1) * TPF)

            def dst64(i, ts_=ts_):
                if i < 16:
                    return p_lru[64 * i:64 * i + 64, ts_]
                if i < 49:
                    return p_rw[64 * (i - 16):64 * (i - 16) + 64, ts_]
                if i < 69:
                    return p_swa[64 * (i - 49):64 * (i - 49) + 64, ts_]
                return p_mem[64 * (i - 69):64 * (i - 69) + 64, ts_]
            proj_stage(P, K, TPF, xin[ts_, :], norm_g[l], w_in[l], 5440, dst64)
            P.end_stage()
        mem_stage(P, K, seq, lambda h: p_mem[128 * h:128 * h + 128, :], lambda h: p_mem[512 + 128 * h:512 + 128 * h + 128, :],
                  lambda h: oT[1536 + 128 * h:1536 + 128 * h + 128, :], mem, memg[l], wkv[l], mqk[l][:, 0:1], mqk[l][:, 1:2])
        P.end_stage()
        for ct in range(4):
            cs = slice(128 * ct, 128 * ct + 128)
            lru_stage(P, 128, seq, p_lru[cs, :], p_lru[512 + 128 * ct:512 + 128 * ct + 128, :], oT[cs, :],
                      lru_sm[l][cs, 0:4], lru_sm[l][cs, 4:5], lru_w[l][0][2 * ct:2 * ct + 2], lru_sm[l][cs, 5:6],
                      lru_w[l][1][2 * ct:2 * ct + 2], lru_sm[l][cs, 6:7], lru_sm[l][cs, 7:8])
            P.end_stage()
        for h in range(8):
            hs = slice(64 * h, 64 * h + 64)
            rwkv_stage(P, K, RK, seq, p_rw[hs, :], p_rw[512 + 64 * h:512 + 64 * h + 64, :],
                       p_rw[1024 + 64 * h:1024 + 64 * h + 64, :], p_rw[1536:1568, :], p_rw[1568:1600, :],
                       p_rw[1600 + 64 * h:1600 + 64 * h + 64, :], oT[512 + 64 * h:512 + 64 * h + 64, :],
                       rw_sm[l][h], rw_lup[l][h][0:32, :], rw_lup[l][h][32:64, :])
            P.end_stage()
        for kv in range(2):
            hs_ = [4 * kv + i for i in range(4)]
            swa_stage4(P, K, seq, [p_swa[64 * h:64 * h + 64, :] for h in hs_], p_swa[512 + 64 * kv:512 + 64 * kv + 64, :],
                       p_swa[640 + 64 * kv:640 + 64 * kv + 64, :], [p_swa[768 + 64 * h:768 + 64 * h + 64, :] for h in hs_],
                       [oT[1024 + 64 * h:1024 + 64 * h + 64, :] for h in hs_], swa_sm[l][hs_[0]][:, 0:1],
                       swa_sm[l][hs_[0]][:, 1:2], [swa_sm[l][h][:, 2:3] for h in hs_])
            P.end_stage()
        out_stage(P, K, seq, xin, lambda k: oT[128 * k:128 * k + 128, :], w_out[l], xout)
        P.end_stage()
    P.close()
    return nc, P


def _fused_inputs(inp, depth=2):
    f = np.float32
    L = depth
    m = dict(_consts_np())
    m["x"] = np.ascontiguousarray(inp["x"][0], dtype=f)
    m["mem"] = np.ascontiguousarray(inp["mem"][0], dtype=f)
    m["norm_g"] = np.stack([_g16(inp["norm_g"][l]) for l in range(L)])
    m["w_in"] = np.ascontiguousarray(inp["w_in"][:L], dtype=f)
    m["memg"] = np.stack([_g16(inp["mem_norm_g"][l]) for l in range(L)])
    m["wkv"] = np.ascontiguousarray(inp["w_mem_kv"][:L], dtype=f)
    m["mqk"] = np.ascontiguousarray(np.stack([inp["mem_q_g"][:L], inp["mem_k_g"][:L]], axis=-1), dtype=f)
    m["w_out"] = np.ascontiguousarray(inp["w_out"][:L], dtype=f)
    lru_sm = np.zeros((L, 512, 8), f)
    lru_sm[:, :, 0:4] = np.transpose(inp["conv_w"][:L], (0, 2, 1))
    lru_sm[:, :, 4] = inp["conv_b"][:L]
    lru_sm[:, :, 5] = inp["lru_ba"][:L]
    lru_sm[:, :, 6] = inp["lru_bx"][:L]
    lru_sm[:, :, 7] = inp["lru_lambda"][:L]
    m["lru_sm"] = lru_sm
    m["lru_w"] = np.ascontiguousarray(np.stack([inp["lru_wa"][:L], inp["lru_wx"][:L]], axis=1), dtype=f)
    rw_sm = np.zeros((L, 8, 64, 16), f)
    rw_lup = np.zeros((L, 8, 64, 64), f)
    swa_sm = np.zeros((L, 8, 64, 4), f)
    for l in range(L):
        for h in range(8):
            sm, _, lup = _head_small(inp, l, h)
            rw_sm[l, h] = sm[:, 8:24]
            rw_lup[l, h] = lup
            swa_sm[l, h, :, 0:3] = sm[:, 24:27]
    m["rw_sm"] = rw_sm
    m["rw_lup"] = rw_lup
    m["swa_sm"] = swa_sm
    return m


def kernel(**inputs):
    inp = {k: np.asarray(v) for k, v in inputs.items()}
    nc, _ = build_fused()
    m = _fused_inputs(inp)
    cores = list(range(NCORES))
    mz = {k_: np.zeros_like(v_) for k_, v_ in m.items()}
    res = run_bass_kernel_spmd(nc, [m] + [mz for _ in cores[1:]], core_ids=cores).results
    return np.asarray(res[0]["out"], dtype=np.float32).reshape(1, SEQ, 2048)
```

```python
from concourse.bass_utils import run_bass_kernel_spmd
from contextlib import ExitStack
import numpy as np
import concourse.bass as bass
import concourse.mybir as mybir

F32 = mybir.dt.float32
BF16 = mybir.dt.bfloat16
ALU = mybir.AluOpType
AF = mybir.ActivationFunctionType
AX = mybir.AxisListType

MAXV = 30000
ENGS = ("pe", "act", "dve", "pool", "sp")


class Ev:
    __slots__ = ("key", "n")

    def __init__(self, key, n=None):
        self.key = key
        self.n = n


class Res:
    __slots__ = ("name", "writers", "readers", "excl")

    def __init__(self, name="", excl=False):
        self.name = name
        self.writers = []
        self.readers = []
        self.excl = excl


class Counter:
    def __init__(self, prog, name):
        self.sem = prog.es.enter_context(prog.nc.semaphore(name))
        self.total = 0
        self.key = ("d", id(self))
        prog.counters[self.key] = self


class Tile:
    def __init__(self, prog, shape, dtype, name=None, psum=False, persistent=False):
        prog.nsb += 1
        nm = f"t{prog.nsb}_{name or ''}"
        st = prog.es if persistent else prog.stage_es
        if psum:
            self.t = st.enter_context(prog.nc.psum_tensor(nm, list(shape), dtype))
        else:
            self.t = st.enter_context(prog.nc.sbuf_tensor(nm, list(shape), dtype))
        self.r = Res(name or "", excl=psum)
        self.prog = prog
        self._c = None

    @property
    def c(self):
        if self._c is None:
            self._c = self.prog.counter()
        return self._c

    def __getitem__(self, idx):
        return self.t[idx]


class Op:
    __slots__ = ("eng", "fn", "waits", "signal", "ev", "ctr")


def _rs(xs):
    return [getattr(x, "r", x) for x in xs]


class Prog:
    def __init__(self, nc):
        self.nc = nc
        self.es = ExitStack()
        self.stage_es = ExitStack()
        self.ops = {e: [] for e in ENGS}
        self.sigcnt = {e: 0 for e in ENGS}
        self.emitted = {e: 0 for e in ENGS}
        self.pending = {e: [] for e in ENGS}
        self.waited = {e: {} for e in ENGS}
        self.counters = {}
        self.free_counters = []
        self.stage_counters = []
        self.esems = {e: [] for e in ENGS}
        self.nsb = 0
        self.nstage = 0
        self.total_ops = 0

    def tile(self, shape, dtype, name=None, psum=False, persistent=False):
        return Tile(self, shape, dtype, name, psum, persistent)

    def counter(self, name=None, persistent=False):
        if self.free_counters and not persistent:
            c = self.free_counters.pop()
        else:
            self.nsb += 1
            c = Counter(self, name or f"ctr{self.nsb}")
        if not persistent:
            self.stage_counters.append(c)
        return c

    def _deps(self, reads, writes, accs):
        waits = []
        for r in reads:
            waits.extend(r.writers)
        for r in writes:
            waits.extend(r.writers)
            waits.extend(r.readers)
        for r in accs:
            waits.extend(r.readers)
        return waits

    def _post(self, ev, reads, writes, accs):
        for r in reads:
            r.readers.append(ev)
            if len(r.readers) > 64:
                r.readers = _compact(r.readers)
        for r in writes:
            r.writers = [ev]
            r.readers = []
        for r in accs:
            r.writers.append(ev)
            if len(r.writers) > 64:
                r.writers = _compact(r.writers)

    def op(self, eng, fn, reads=(), writes=(), accs=(), signal=True):
        reads, writes, accs = _rs(reads), _rs(writes), _rs(accs)
        ex = [r for r in reads if r.excl]
        if ex:
            reads = [r for r in reads if not r.excl]
            writes = list(writes) + ex
        o = Op()
        o.eng = eng
        o.fn = fn
        o.ctr = None
        waits = self._deps(reads, writes, accs)
        if eng == "pe":
            waits = [w for w in waits if w.key != "pe"]
        o.waits = waits
        o.signal = False
        ev = Ev(eng)
        o.ev = ev
        self.ops[eng].append(o)
        self.pending[eng].append(ev)
        if signal:
            self.signal_last(eng)
        self._post(ev, reads, writes, accs)
        return ev

    def signal_last(self, eng):
        o = self.ops[eng][-1]
        if o.signal:
            return
        o.signal = True
        self.sigcnt[eng] += 1
        for p in self.pending[eng]:
            p.n = self.sigcnt[eng]
        self.pending[eng] = []

    def dma(self, eng, ctr, fn, reads=(), writes=(), accs=(), inc=16):
        reads, writes, accs = _rs(reads), _rs(writes), _rs(accs)
        o = Op()
        o.eng = eng
        o.fn = fn
        o.ctr = ctr
        o.waits = self._deps(reads, writes, accs)
        o.signal = (inc == 16)
        ctr.total += inc
        assert ctr.total < 60000
        ev = Ev(ctr.key, ctr.total)
        o.ev = ev
        self.ops[eng].append(o)
        self._post(ev, reads, writes, accs)
        return ev

    def finish(self, eng="sp"):
        o = Op()
        o.eng = eng
        o.fn = None
        o.ctr = None
        o.signal = False
        o.ev = Ev(eng)
        o.waits = [Ev(c.key, c.total) for c in self.counters.values() if c.total]
        self.ops[eng].append(o)

    def end_stage(self):
        nc = self.nc
        self.finish("sp")
        for e in ENGS:
            if self.pending[e]:
                self.signal_last(e)
            need = (self.sigcnt[e] + MAXV - 1) // MAXV + 1
            while len(self.esems[e]) < need:
                self.esems[e].append(self.es.enter_context(nc.semaphore(f"s_{e}{len(self.esems[e])}")))
        prog = self

        def resolve(ev):
            if isinstance(ev.key, tuple):
                return (ev.key, prog.counters[ev.key].sem, ev.n)
            assert ev.n is not None, f"unresolved event on {ev.key}"
            idx = (ev.n - 1) // MAXV
            return ((ev.key, idx), prog.esems[ev.key][idx], (ev.n - 1) % MAXV + 1)

        def run(e):
            def body(eng):
                waited = prog.waited[e]
                cnt = prog.emitted[e]
                for o in prog.ops[e]:
                    need = {}
                    for w in o.waits:
                        k, sem, v = resolve(w)
                        if waited.get(k, 0) >= v:
                            continue
                        if k not in need or need[k][1] < v:
                            need[k] = (sem, v)
                    for k, (sem, v) in need.items():
                        eng.wait_ge(sem, v)
                        waited[k] = v
                    if o.fn is None:
                        continue
                    inst = o.fn(eng)
                    if o.ctr is not None:
                        if o.signal:
                            inst.then_inc(o.ctr.sem, 16)
                        else:
                            inst.then_inc(o.ctr.sem)
                    elif o.signal:
                        cnt += 1
                        idx = (cnt - 1) // MAXV
                        inst.then_inc(prog.esems[e][idx], 1)
                prog.emitted[e] = cnt
            return body

        with nc.Block() as block:
            block.tensor(run("pe"))
            block.scalar(run("act"))
            block.vector(run("dve"))
            block.gpsimd(run("pool"))
            block.sync(run("sp"))
        for e in ENGS:
            assert self.emitted[e] == self.sigcnt[e], (e, self.emitted[e], self.sigcnt[e])
            self.total_ops += len(self.ops[e])
            self.ops[e] = []
        self.stage_es.close()
        self.stage_es = ExitStack()
        self.free_counters.extend(self.stage_counters)
        self.stage_counters = []
        self.nstage += 1

    def close(self):
        self.es.close()


def _compact(evs):
    best = {}
    for ev in evs:
        if ev.n is None:
            best[id(ev)] = ev
            continue
        k = ev.key
        if k not in best or best[k].n < ev.n:
            best[k] = ev
    return list(best.values())


LRU_C = 8.0


def load_col(P, ctr, res, dst_ap, src_ap, eng="sp"):
    P.dma(eng, ctr, lambda e: e.dma_start(out=dst_ap, in_=src_ap), accs=[res])


def lru_stage(P, CP, NTOK, xrows, grows, orows, convw, convb, wa, ba, wx, bx, lam, TP=2048, tag=""):
    nb = CP // 64
    npc = NTOK // TP
    prm = P.tile([CP, 16], F32, f"lruprm{tag}")
    wabd = P.tile([CP, CP], F32, f"wabd{tag}")
    wxbd = P.tile([CP, CP], F32, f"wxbd{tag}")
    if nb > 1:
        P.op("pool", lambda e: e.memset(wabd[:], 0.0), writes=[wabd])
        P.op("pool", lambda e: e.memset(wxbd[:], 0.0), writes=[wxbd])
    for b in range(nb):
        P.dma("sp", wabd.c, lambda e, b=b: e.dma_start(out=wabd[b * 64:(b + 1) * 64, b * 64:(b + 1) * 64], in_=wa[b]),
              accs=[wabd])
        P.dma("sp", wxbd.c, lambda e, b=b: e.dma_start(out=wxbd[b * 64:(b + 1) * 64, b * 64:(b + 1) * 64], in_=wx[b]),
              accs=[wxbd])
    P.dma("sp", prm.c, lambda e: e.dma_start(out=prm[:, 0:4], in_=convw, allow_slow_non_contiguous=True), writes=[prm])
    for i, src in enumerate((convb, ba, bx, lam)):
        P.dma("sp", prm.c, lambda e, i=i, src=src: e.dma_start(out=prm[:, 4 + i:5 + i], in_=src, allow_slow_non_contiguous=True), accs=[prm])
    P.op("act", lambda e: e.activation(out=prm[:, 8:9], in_=prm[:, 7:8], func=AF.Exp, scale=-1.0),
         reads=[prm], accs=[prm])
    P.op("act", lambda e: e.activation(out=prm[:, 9:10], in_=prm[:, 8:9], func=AF.Ln, bias=1.0),
         reads=[prm], accs=[prm])
    P.op("dve", lambda e: e.tensor_scalar(out=prm[:, 10:11], in0=prm[:, 9:10], scalar1=-LRU_C, scalar2=None,
                                         op0=ALU.mult), reads=[prm], accs=[prm])
    P.op("dve", lambda e: e.memset(prm[:, 11:12], 0.0), reads=[prm], accs=[prm])

    xt = [P.tile([CP, TP + 3], F32, f"lxt{tag}{i}") for i in range(2)]
    gt = [P.tile([CP, TP], F32, f"lgt{tag}{i}") for i in range(2)]
    xc = P.tile([CP, TP], F32, f"lxc{tag}")
    rr = P.tile([CP, TP], F32, f"lr{tag}")
    ii = P.tile([CP, TP], F32, f"li{tag}")
    aa = P.tile([CP, TP], F32, f"la{tag}")
    mm = P.tile([CP, TP], F32, f"lm{tag}")
    hh = [P.tile([CP, TP], F32, f"lh{tag}{i}") for i in range(2)]
    ob = [P.tile([CP, TP], BF16, f"lo{tag}{i}") for i in range(2)]
    pg = [P.tile([CP, 512], F32, f"lpg{tag}{i}", psum=True) for i in range(2)]
    npg = 0
    for pi in range(npc):
        s = pi % 2
        t0 = pi * TP
        X, G, H, O = xt[s], gt[s], hh[s], ob[s]
        if pi == 0:
            P.op("pool", lambda e, X=X: e.memset(X[:, 0:3], 0.0), writes=[X])
            P.dma("sp", X.c, lambda e, X=X: e.dma_start(out=X[:, 3:3 + TP], in_=xrows[:, 0:TP]), accs=[X])
        else:
            P.dma("sp", X.c, lambda e, X=X, t0=t0: e.dma_start(out=X[:, :], in_=xrows[:, t0 - 3:t0 + TP]), writes=[X])
        P.dma("sp", G.c, lambda e, G=G, t0=t0: e.dma_start(out=G[:, :], in_=grows[:, t0:t0 + TP]), writes=[G])
        P.op("dve", lambda e, X=X: e.tensor_scalar(out=xc[:], in0=X[:, 3:3 + TP], scalar1=prm[:, 3:4],
                                                  scalar2=prm[:, 4:5], op0=ALU.mult, op1=ALU.add),
             reads=[X, prm], writes=[xc])
        for j in range(3):
            P.op("dve", lambda e, X=X, j=j: e.scalar_tensor_tensor(out=xc[:], in0=X[:, j:j + TP], scalar=prm[:, j:j + 1],
                                                                  in1=xc[:], op0=ALU.mult, op1=ALU.add),
                 reads=[X, prm, xc], writes=[xc])
        for (wbd, bcol, dst) in ((wabd, 5, rr), (wxbd, 6, ii)):
            for sb_ in range(TP // 512):
                pb = pg[npg % 2]
                npg += 1
                P.op("pe", lambda e, wbd=wbd, pb=pb, sb_=sb_: e.matmul(pb[:, :], lhsT=wbd[:, :],
                                                                      rhs=xc[:, sb_ * 512:(sb_ + 1) * 512],
                                                                      start=True, stop=True),
                     reads=[wbd, xc], writes=[pb])
                P.op("act", lambda e, pb=pb, dst=dst, sb_=sb_, bcol=bcol: e.activation(
                    out=dst[:, sb_ * 512:(sb_ + 1) * 512], in_=pb[:, :], func=AF.Sigmoid, bias=prm[:, bcol:bcol + 1]),
                    reads=[pb, prm], writes=[dst] if sb_ == 0 else [], accs=[] if sb_ == 0 else [dst])
        P.op("act", lambda e: e.activation(out=aa[:], in_=rr[:], func=AF.Exp, scale=prm[:, 10:11]),
             reads=[rr, prm], writes=[aa])
        P.op("dve", lambda e: e.tensor_tensor(out=mm[:], in0=aa[:], in1=aa[:], op=ALU.mult), reads=[aa], writes=[mm])
        P.op("dve", lambda e: e.tensor_scalar(out=mm[:], in0=mm[:], scalar1=-1.0, scalar2=1.0, op0=ALU.mult,
                                             op1=ALU.add), reads=[mm], writes=[mm])
        P.op("dve", lambda e: e.tensor_scalar(out=mm[:], in0=mm[:], scalar1=1e-12, scalar2=None, op0=ALU.max),
             reads=[mm], writes=[mm])
        P.op("act", lambda e: e.activation(out=mm[:], in_=mm[:], func=AF.Sqrt), reads=[mm], writes=[mm])
        P.op("dve", lambda e: e.tensor_tensor(out=ii[:], in0=ii[:], in1=xc[:], op=ALU.mult), reads=[ii, xc], writes=[ii])
        P.op("dve", lambda e: e.tensor_tensor(out=ii[:], in0=ii[:], in1=mm[:], op=ALU.mult), reads=[ii, mm], writes=[ii])
        Hp = hh[1 - s]
        init = prm[:, 11:12] if pi == 0 else Hp[:, TP - 1:TP]
        P.op("dve", lambda e, H=H, init=init: e.tensor_tensor_scan(out=H[:], data0=aa[:], data1=ii[:], initial=init,
                                                                  op0=ALU.mult, op1=ALU.add),
             reads=[aa, ii, prm, Hp], writes=[H])
        P.op("act", lambda e, G=G: e.activation(out=G[:], in_=G[:], func=AF.Silu), reads=[G], writes=[G])
        P.op("dve", lambda e, H=H, G=G, O=O: e.tensor_tensor(out=O[:], in0=H[:], in1=G[:], op=ALU.mult),
             reads=[H, G], writes=[O])
        P.dma("sp", O.c, lambda e, O=O, t0=t0: e.dma_start(out=orows[:, t0:t0 + TP], in_=O[:]), reads=[O])


class Consts:
    def __init__(self, P, c_identf, c_identb, c_swamask):
        self.identf = P.tile([128, 128], F32, "identf", persistent=True)
        self.identb = P.tile([128, 128], BF16, "identb", persistent=True)
        self.swamask = P.tile([128, 256], BF16, "swamask", persistent=True)
        self.ones_f = P.tile([128, 128], F32, "ones_f", persistent=True)
        self.ones_b = P.tile([128, 128], BF16, "ones_b", persistent=True)
        P.dma("sp", self.identf.c, lambda e: e.dma_start(out=self.identf[:], in_=c_identf), writes=[self.identf])
        P.dma("sp", self.identb.c, lambda e: e.dma_start(out=self.identb[:], in_=c_identb), writes=[self.identb])
        P.dma("sp", self.swamask.c, lambda e: e.dma_start(out=self.swamask[:], in_=c_swamask), writes=[self.swamask])
        P.op("pool", lambda e: e.memset(self.ones_f[:], 1.0), writes=[self.ones_f])
        P.op("pool", lambda e: e.memset(self.ones_b[:], 1.0), writes=[self.ones_b])


def swa_stage(P, K, NTOK, qrows, krows, vrows, grows, orows, qg, kg, sink, TP=512, tag=""):
    npc = NTOK // TP
    nbk = TP // 128
    prm = P.tile([64, 8], F32, f"swaprm{tag}")
    P.dma("sp", prm.c, lambda e: e.dma_start(out=prm[:, 0:1], in_=qg, allow_slow_non_contiguous=True), writes=[prm])
    P.dma("sp", prm.c, lambda e: e.dma_start(out=prm[:, 1:2], in_=kg, allow_slow_non_contiguous=True), accs=[prm])
    P.dma("sp", prm.c, lambda e: e.dma_start(out=prm[:, 2:3], in_=sink, allow_slow_non_contiguous=True), accs=[prm])
    P.op("dve", lambda e: e.tensor_scalar(out=prm[:, 3:4], in0=prm[:, 0:1], scalar1=0.125, scalar2=None, op0=ALU.mult),
         reads=[prm], accs=[prm])
    P.op("act", lambda e: e.activation(out=prm[:, 4:5], in_=prm[:, 2:3], func=AF.Exp), reads=[prm], accs=[prm])
    W = TP + 128
    kx = [P.tile([64, W], F32, f"skx{tag}{i}") for i in range(2)]
    vx = [P.tile([64, W], F32, f"svx{tag}{i}") for i in range(2)]
    qx = [P.tile([64, TP], F32, f"sqx{tag}{i}") for i in range(2)]
    gx = [P.tile([64, TP], F32, f"sgx{tag}{i}") for i in range(2)]
    sq = P.tile([64, W], F32, f"ssq{tag}")
    rs = P.tile([64, W], F32, f"srs{tag}")
    kn = P.tile([64, W], BF16, f"skn{tag}")
    qn = P.tile([64, TP], BF16, f"sqn{tag}")
    vb = P.tile([128, nbk + 1, 64], BF16, f"svb{tag}")
    E = [P.tile([128, 256], BF16, f"sE{tag}{i}") for i in range(2)]
    dn = P.tile([64, TP], F32, f"sdn{tag}")
    yy = P.tile([64, TP], F32, f"syy{tag}")
    ob = [P.tile([64, TP], BF16, f"sob{tag}{i}") for i in range(2)]
    pn = [P.tile([64, 512], F32, f"spn{tag}{i}", psum=True) for i in range(2)]
    pt = P.tile([128, 512], F32, f"spt{tag}", psum=True)
    psc = [P.tile([128, 256], F32, f"spsc{tag}{i}", psum=True) for i in range(2)]
    pnum = P.tile([64, 512], F32, f"spnum{tag}", psum=True)
    pden = P.tile([64, 512], F32, f"spden{tag}", psum=True)
    npn = 0
    nsc = 0

    def norm(src, width, c0, gcol, dst, dst_c0):
        nonlocal npn
        P.op("dve", lambda e: e.tensor_tensor(out=sq[:, 0:width], in0=src[:, c0:c0 + width], in1=src[:, c0:c0 + width],
                                             op=ALU.mult), reads=[src], writes=[sq])
        o = 0
        first = True
        while o < width:
            w_ = min(512, width - o)
            pb = pn[npn % 2]
            npn += 1
            P.op("pe", lambda e, pb=pb, o=o, w_=w_: e.matmul(pb[:, 0:w_], lhsT=K.ones_f[0:64, 0:64], rhs=sq[:, o:o + w_],
                                                            start=True, stop=True), reads=[K.ones_f, sq], writes=[pb])
            P.op("act", lambda e, pb=pb, o=o, w_=w_: e.activation(out=rs[:, o:o + w_], in_=pb[:, 0:w_], func=AF.Sqrt,
                                                                 scale=1.0 / 64, bias=1e-6),
                 reads=[pb], writes=[rs] if first else [], accs=[] if first else [rs])
            first = False
            o += w_
        P.op("dve", lambda e: e.reciprocal(out=rs[:, 0:width], in_=rs[:, 0:width]), reads=[rs], writes=[rs])
        P.op("dve", lambda e: e.scalar_tensor_tensor(out=dst[:, dst_c0:dst_c0 + width], in0=src[:, c0:c0 + width],
                                                    scalar=prm[:, gcol:gcol + 1], in1=rs[:, 0:width],
                                                    op0=ALU.mult, op1=ALU.mult), reads=[src, prm, rs], writes=[dst])

    for pi in range(npc):
        s = pi % 2
        t0 = pi * TP
        KX, VX, QX, GX, O = kx[s], vx[s], qx[s], gx[s], ob[s]
        lo = 128 if pi == 0 else 0
        P.dma("sp", KX.c, lambda e, KX=KX, t0=t0, lo=lo: e.dma_start(out=KX[:, lo:W], in_=krows[:, t0 - 128 + lo:t0 + TP]),
              writes=[KX])
        P.dma("sp", VX.c, lambda e, VX=VX, t0=t0, lo=lo: e.dma_start(out=VX[:, lo:W], in_=vrows[:, t0 - 128 + lo:t0 + TP]),
              writes=[VX])
        P.dma("sp", QX.c, lambda e, QX=QX, t0=t0: e.dma_start(out=QX[:, :], in_=qrows[:, t0:t0 + TP]), writes=[QX])
        P.dma("sp", GX.c, lambda e, GX=GX, t0=t0: e.dma_start(out=GX[:, :], in_=grows[:, t0:t0 + TP]), writes=[GX])
        norm(KX, W - lo, lo, 1, kn, lo)
        norm(QX, TP, 0, 3, qn, 0)
        b0 = lo // 128
        for b in range(b0, nbk + 1):
            P.op("pe", lambda e, VX=VX, b=b: e.transpose(out=pt[:, b * 64:(b + 1) * 64], in_=VX[:, b * 128:(b + 1) * 128],
                                                        identity=K.identf[0:64, 0:64]),
                 reads=[VX, K.identf], writes=[pt] if b == b0 else [], accs=[] if b == b0 else [pt],
                 signal=(b == nbk))
        P.op("act", lambda e, b0=b0: e.activation(out=vb[:, b0:nbk + 1, :],
                                                 in_=pt[:, b0 * 64:(nbk + 1) * 64].rearrange("p (b d) -> p b d", d=64),
                                                 func=AF.Copy), reads=[pt], writes=[vb])
        for n in range(nbk):
            has_prev = not (pi == 0 and n == 0)
            sc = psc[nsc % 2]
            Eb = E[nsc % 2]
            nsc += 1
            wd = 256 if has_prev else 128
            P.op("pe", lambda e, sc=sc, n=n: e.matmul(sc[:, 0:128], lhsT=kn[:, (n + 1) * 128:(n + 2) * 128],
                                                     rhs=qn[:, n * 128:(n + 1) * 128], start=True, stop=True),
                 reads=[kn, qn], writes=[sc], signal=not has_prev)
            if has_prev:
                P.op("pe", lambda e, sc=sc, n=n: e.matmul(sc[:, 128:256], lhsT=kn[:, n * 128:(n + 1) * 128],
                                                         rhs=qn[:, n * 128:(n + 1) * 128], start=True, stop=True),
                     reads=[kn, qn], accs=[sc])
            P.op("act", lambda e, sc=sc, Eb=Eb, wd=wd: e.activation(out=Eb[:, 0:wd], in_=sc[:, 0:wd], func=AF.Exp),
                 reads=[sc], writes=[Eb])
            P.op("pool", lambda e, Eb=Eb, wd=wd: e.tensor_tensor(out=Eb[:, 0:wd], in0=Eb[:, 0:wd], in1=K.swamask[:, 0:wd],
                                                                op=ALU.mult), reads=[Eb, K.swamask], writes=[Eb])
            cs = slice(n * 128, (n + 1) * 128)
            for (pacc, lhs_cur, lhs_prev) in ((pnum, vb[:, n + 1, :], vb[:, n, :]),
                                              (pden, K.ones_b[:, 0:64], K.ones_b[:, 0:64])):
                P.op("pe", lambda e, pacc=pacc, lhs_cur=lhs_cur, Eb=Eb, cs=cs, has_prev=has_prev: e.matmul(
                    pacc[:, cs], lhsT=lhs_cur, rhs=Eb[:, 0:128], start=True, stop=not has_prev),
                    reads=[vb, K.ones_b, Eb], writes=[pacc] if n == 0 else [], accs=[] if n == 0 else [pacc],
                    signal=False)
                if has_prev:
                    P.op("pe", lambda e, pacc=pacc, lhs_prev=lhs_prev, Eb=Eb, cs=cs: e.matmul(
                        pacc[:, cs], lhsT=lhs_prev, rhs=Eb[:, 128:256], start=False, stop=True),
                        reads=[vb, K.ones_b, Eb], accs=[pacc], signal=False)
            P.signal_last("pe")
        P.op("dve", lambda e: e.tensor_scalar(out=dn[:], in0=pden[:, :], scalar1=prm[:, 4:5], scalar2=None, op0=ALU.add),
             reads=[pden, prm], writes=[dn])
        P.op("dve", lambda e: e.reciprocal(out=dn[:], in_=dn[:]), reads=[dn], writes=[dn])
        P.op("dve", lambda e: e.tensor_tensor(out=yy[:], in0=pnum[:, :], in1=dn[:], op=ALU.mult),
             reads=[pnum, dn], writes=[yy])
        P.op("act", lambda e, GX=GX: e.activation(out=GX[:], in_=GX[:], func=AF.Silu), reads=[GX], writes=[GX])
        P.op("dve", lambda e, GX=GX, O=O: e.tensor_tensor(out=O[:], in0=yy[:], in1=GX[:], op=ALU.mult),
             reads=[yy, GX], writes=[O])
        P.dma("sp", O.c, lambda e, O=O, t0=t0: e.dma_start(out=orows[:, t0:t0 + TP], in_=O[:]), reads=[O])


D_MODEL = 2048
KC = D_MODEL // 128


def norm_transpose(P, K, x_dram, ntile, gsb, hT, pst, eps=1e-6, tag=""):
    xt = [P.tile([128, D_MODEL], F32, f"nxt{tag}{i}") for i in range(2)]
    xs = [P.tile([128, D_MODEL], F32, f"nxs{tag}{i}") for i in range(2)]
    junk = P.tile([128, D_MODEL], BF16, f"njunk{tag}")
    st = P.tile([128, 4 * ntile], F32, f"nst{tag}")
    npst = 0
    for i in range(ntile):
        s = i % 2
        X, XS = xt[s], xs[s]
        P.dma("sp", X.c, lambda e, X=X, i=i: e.dma_start(out=X[:], in_=x_dram[i * 128:(i + 1) * 128, :]), writes=[X])
        c0 = 4 * i
        P.op("act", lambda e, X=X, c0=c0: e.activation(out=junk[:], in_=X[:], func=AF.Square, accum_out=st[:, c0:c0 + 1]),
             reads=[X], writes=[junk], accs=[st])
        P.op("dve", lambda e, c0=c0: e.tensor_scalar(out=st[:, c0 + 1:c0 + 2], in0=st[:, c0:c0 + 1], scalar1=1.0 / D_MODEL,
                                                    scalar2=eps, op0=ALU.mult, op1=ALU.add), reads=[st], accs=[st])
        P.op("act", lambda e, c0=c0: e.activation(out=st[:, c0 + 2:c0 + 3], in_=st[:, c0 + 1:c0 + 2], func=AF.Sqrt),
             reads=[st], accs=[st])
        P.op("dve", lambda e, c0=c0: e.reciprocal(out=st[:, c0 + 3:c0 + 4], in_=st[:, c0 + 2:c0 + 3]),
             reads=[st], accs=[st])
        P.op("act", lambda e, X=X, XS=XS, c0=c0: e.activation(out=XS[:], in_=X[:], func=AF.Copy,
                                                             scale=st[:, c0 + 3:c0 + 4]), reads=[X, st], writes=[XS])
        for kq in range(KC // 4):
            pb = pst[npst % 2]
            npst += 1
            for kk in range(4):
                k = kq * 4 + kk
                P.op("pe", lambda e, XS=XS, pb=pb, k=k, kk=kk: e.transpose(
                    out=pb[:, kk * 128:(kk + 1) * 128], in_=XS[:, k * 128:(k + 1) * 128], identity=K.identf[:]),
                    reads=[XS, K.identf], writes=[pb] if kk == 0 else [], accs=[pb] if kk else [], signal=(kk == 3))
            for kk in range(4):
                k = kq * 4 + kk
                P.op("dve", lambda e, pb=pb, k=k, kk=kk, i=i: e.tensor_scalar(
                    out=hT[:, k, i * 128:(i + 1) * 128], in0=pb[:, kk * 128:(kk + 1) * 128],
                    scalar1=gsb[:, k:k + 1], scalar2=None, op0=ALU.mult), reads=[pb, gsb], accs=[hT])


def load_weight_bf16(P, w_dram, c0, cw, wst, wb):
    wv = w_dram.rearrange("(k p) c -> p k c", p=128)
    P.dma("sp", wst.c, lambda e: e.dma_start(out=wst[:, :, 0:cw], in_=wv[:, :, c0:c0 + cw]), writes=[wst])
    P.op("pool", lambda e: e.tensor_copy(out=wb[:, :, 0:cw], in_=wst[:, :, 0:cw]), reads=[wst], writes=[wb])


def proj_stage(P, K, NT, x_dram, g_dram, w_dram, NCOL, dst64, tag=""):
    ntile = NT // 128
    TG = min(512, NT)
    ntg = NT // TG
    CB = 256
    gsb = P.tile([128, KC], F32, f"pg{tag}")
    P.dma("sp", gsb.c, lambda e: e.dma_start(out=gsb[:], in_=g_dram), writes=[gsb])
    hT = P.tile([128, KC, NT], BF16, f"phT{tag}")
    pp = [P.tile([128, 512], F32, f"ppp{tag}{i}", psum=True) for i in range(8)]
    norm_transpose(P, K, x_dram, ntile, gsb, hT, pp[0:2], tag=tag)
    nblk = (NCOL + CB - 1) // CB
    wst = [P.tile([128, KC, CB], F32, f"pwst{tag}{i}") for i in range(2)]
    wb = [P.tile([128, KC, CB], BF16, f"pwb{tag}{i}") for i in range(2)]
    ost = [P.tile([128, NT], F32, f"post{tag}{i}") for i in range(2)]
    nmm = 0
    nct = 0
    for bi in range(nblk):
        s = bi % 2
        cw = min(CB, NCOL - bi * CB)
        load_weight_bf16(P, w_dram, bi * CB, cw, wst[s], wb[s])
        WB = wb[s]
        for j in range((cw + 127) // 128):
            mw = min(128, cw - j * 128)
            O = ost[nct % 2]
            nct += 1
            pbs = [pp[(nct % 2) * 4 + n] for n in range(ntg)]
            for k in range(KC):
                for n in range(ntg):
                    pb = pbs[n]
                    P.op("pe", lambda e, WB=WB, k=k, j=j, mw=mw, n=n, pb=pb: e.matmul(
                        pb[0:mw, 0:TG], lhsT=WB[:, k, j * 128:j * 128 + mw], rhs=hT[:, k, n * TG:(n + 1) * TG],
                        start=(k == 0), stop=(k == KC - 1)),
                        reads=[WB, hT], writes=[pb] if k == 0 else [], accs=[pb] if k else [],
                        signal=(k == KC - 1 and n == ntg - 1))
            for n in range(ntg):
                pb = pbs[n]
                nmm += 1
                if nmm % 2:
                    P.op("act", lambda e, O=O, mw=mw, n=n, pb=pb: e.activation(
                        out=O[0:mw, n * TG:(n + 1) * TG], in_=pb[0:mw, 0:TG], func=AF.Copy),
                        reads=[pb], writes=[O] if n == 0 else [], accs=[] if n == 0 else [O])
                else:
                    P.op("dve", lambda e, O=O, mw=mw, n=n, pb=pb: e.tensor_copy(
                        out=O[0:mw, n * TG:(n + 1) * TG], in_=pb[0:mw, 0:TG]),
                        reads=[pb], writes=[O] if n == 0 else [], accs=[] if n == 0 else [O])
            c0 = bi * CB + j * 128
            for hh_ in range(mw // 64):
                P.dma("sp", O.c, lambda e, O=O, hh_=hh_, c0=c0: e.dma_start(
                    out=dst64(c0 // 64 + hh_), in_=O[hh_ * 64:(hh_ + 1) * 64, :]), reads=[O])


def out_stage(P, K, NT, x_dram, oT_src, w_dram, out_dram, tag=""):
    TG = min(512, NT)
    ntg = NT // TG
    wst = [P.tile([128, KC, 256], F32, f"owst{tag}{i}") for i in range(2)]
    wo = [P.tile([128, KC, 512], BF16, f"owo{tag}{i}") for i in range(4)]
    wv = w_dram.rearrange("(k p) c -> p k c", p=128)
    for cb in range(8):
        WS, WO, hf = wst[cb % 2], wo[cb // 2], cb % 2
        P.dma("sp", WS.c, lambda e, WS=WS, cb=cb: e.dma_start(out=WS[:, :, :], in_=wv[:, :, cb * 256:(cb + 1) * 256]), writes=[WS])
        P.op("pool", lambda e, WS=WS, WO=WO, hf=hf: e.tensor_copy(out=WO[:, :, hf * 256:(hf + 1) * 256], in_=WS[:, :, :]),
             reads=[WS], writes=[WO] if hf == 0 else [], accs=[WO] if hf else [])
    ot = [P.tile([128, KC, TG], BF16, f"oot{tag}{i}") for i in range(2)]
    xt = [P.tile([128, D_MODEL], F32, f"oxt{tag}{i}") for i in range(2)]
    xo = [P.tile([128, D_MODEL], F32, f"oxo{tag}{i}") for i in range(2)]
    pp = [P.tile([128, 512], F32, f"opp{tag}{i}", psum=True) for i in range(4)]
    nmm = 0
    ntl = 0
    for n in range(ntg):
        OT = ot[n % 2]
        for k in range(KC):
            P.dma("sp", OT.c, lambda e, OT=OT, k=k, n=n: e.dma_start(out=OT[:, k, :], in_=oT_src(k)[:, n * TG:(n + 1) * TG]),
                  writes=[OT] if k == 0 else [], accs=[OT] if k else [])
        for tt in range(TG // 128):
            X, XO = xt[ntl % 2], xo[ntl % 2]
            ntl += 1
            r0 = n * TG + tt * 128
            P.dma("sp", X.c, lambda e, X=X, r0=r0: e.dma_start(out=X[:], in_=x_dram[r0:r0 + 128, :]), writes=[X])
            for cb in range(4):
                pb = pp[nmm % 4]
                nmm += 1
                W_ = wo[cb]
                for k in range(KC):
                    P.op("pe", lambda e, OT=OT, W_=W_, k=k, tt=tt, pb=pb: e.matmul(
                        pb[:, :], lhsT=OT[:, k, tt * 128:(tt + 1) * 128], rhs=W_[:, k, :],
                        start=(k == 0), stop=(k == KC - 1)),
                        reads=[OT, W_], writes=[pb] if k == 0 else [], accs=[pb] if k else [], signal=(k == KC - 1))
                P.op("dve", lambda e, X=X, XO=XO, pb=pb, cb=cb: e.tensor_tensor(
                    out=XO[:, cb * 512:(cb + 1) * 512], in0=pb[:, :], in1=X[:, cb * 512:(cb + 1) * 512], op=ALU.add),
                    reads=[pb, X], writes=[XO] if cb == 0 else [], accs=[] if cb == 0 else [XO])
            P.dma("sp", XO.c, lambda e, XO=XO, r0=r0: e.dma_start(out=out_dram[r0:r0 + 128, :], in_=XO[:]), reads=[XO])


def mem_stage(P, K, NT, qsrc, gsrc, odst, mem_dram, memg_dram, wkv_dram, qg_dram, kg_dram, tag=""):
    TP = min(512, NT)
    npc = NT // TP
    SC = 128.0 ** -0.5
    prm = P.tile([128, 24], F32, f"mprm{tag}")
    gsb = P.tile([128, KC], F32, f"mgsb{tag}")
    P.dma("sp", prm.c, lambda e: e.dma_start(out=prm[:, 0:1], in_=qg_dram, allow_slow_non_contiguous=True), writes=[prm])
    P.dma("sp", prm.c, lambda e: e.dma_start(out=prm[:, 1:2], in_=kg_dram, allow_slow_non_contiguous=True), accs=[prm])
    P.dma("sp", gsb.c, lambda e: e.dma_start(out=gsb[:], in_=memg_dram), writes=[gsb])
    P.op("dve", lambda e: e.tensor_scalar(out=prm[:, 2:3], in0=prm[:, 1:2], scalar1=SC, scalar2=None, op0=ALU.mult),
         reads=[prm], accs=[prm])
    pp = [P.tile([128, 512], F32, f"mpp{tag}{i}", psum=True) for i in range(7)]
    hmT = P.tile([128, KC, 256], BF16, f"mhmT{tag}")
    norm_transpose(P, K, mem_dram, 2, gsb, hmT, pp[0:2], tag="m" + tag)
    wst = [P.tile([128, KC, 256], F32, f"mwst{tag}{i}") for i in range(2)]
    wkv = [P.tile([128, KC, 256], BF16, f"mwkv{tag}{i}") for i in range(4)]
    for cb in range(4):
        load_weight_bf16(P, wkv_dram, cb * 256, 256, wst[cb % 2], wkv[cb])
    mkf = P.tile([128, 2, 512], F32, f"mmkf{tag}")
    mvb = P.tile([128, 2, 512], BF16, f"mmvb{tag}")
    mkT = P.tile([128, 4, 256], BF16, f"mmkT{tag}")
    junk = P.tile([128, 128], F32, f"mjunk{tag}")
    npp = 2
    for mt in range(2):
        for half, dst in ((0, mkf), (1, mvb)):
            pb = pp[npp % 7]
            npp += 1
            for sub in range(2):
                W_ = wkv[half * 2 + sub]
                for k in range(KC):
                    P.op("pe", lambda e, pb=pb, sub=sub, W_=W_, k=k, mt=mt: e.matmul(
                        pb[:, sub * 256:(sub + 1) * 256], lhsT=hmT[:, k, mt * 128:(mt + 1) * 128], rhs=W_[:, k, :],
                        start=(k == 0), stop=(k == KC - 1)),
                        reads=[hmT, W_], writes=[pb] if (k == 0 and sub == 0) else [],
                        accs=[] if (k == 0 and sub == 0) else [pb], signal=(k == KC - 1 and sub == 1))
            P.op("act", lambda e, pb=pb, dst=dst, mt=mt: e.activation(out=dst[:, mt, :], in_=pb[:, :], func=AF.Copy),
                 reads=[pb], accs=[dst])
        for h in range(4):
            c0 = 4 + mt * 8 + h * 2
            P.op("act", lambda e, mt=mt, h=h, c0=c0: e.activation(out=junk[:], in_=mkf[:, mt, h * 128:(h + 1) * 128],
                                                                 func=AF.Square, accum_out=prm[:, c0:c0 + 1]),
                 reads=[mkf], writes=[junk], accs=[prm])
            P.op("act", lambda e, c0=c0: e.activation(out=prm[:, c0 + 1:c0 + 2], in_=prm[:, c0:c0 + 1], func=AF.Sqrt,
                                                     scale=1.0 / 128, bias=1e-6), reads=[prm], accs=[prm])
            P.op("dve", lambda e, c0=c0: e.reciprocal(out=prm[:, c0 + 1:c0 + 2], in_=prm[:, c0 + 1:c0 + 2]),
                 reads=[prm], accs=[prm])
            P.op("dve", lambda e, mt=mt, h=h, c0=c0: e.tensor_scalar(
                out=mkf[:, mt, h * 128:(h + 1) * 128], in0=mkf[:, mt, h * 128:(h + 1) * 128],
                scalar1=prm[:, c0 + 1:c0 + 2], scalar2=None, op0=ALU.mult), reads=[mkf, prm], accs=[mkf])
        pb = pp[npp % 7]
        npp += 1
        for h in range(4):
            P.op("pe", lambda e, pb=pb, mt=mt, h=h: e.transpose(out=pb[:, h * 128:(h + 1) * 128],
                                                               in_=mkf[:, mt, h * 128:(h + 1) * 128], identity=K.identf[:]),
                 reads=[mkf, K.identf], writes=[pb] if h == 0 else [], accs=[pb] if h else [], signal=(h == 3))
        P.op("dve", lambda e, pb=pb, mt=mt: e.tensor_scalar(
            out=mkT[:, :, mt * 128:(mt + 1) * 128], in0=pb[:, :].rearrange("p (h m) -> p h m", m=128),
            scalar1=prm[:, 2:3], scalar2=None, op0=ALU.mult), reads=[pb, prm], accs=[mkT])

    qx = [P.tile([128, TP], F32, f"mqx{tag}{i}") for i in range(2)]
    gx = [P.tile([128, TP], F32, f"mgx{tag}{i}") for i in range(2)]
    sq = P.tile([128, TP], F32, f"msq{tag}")
    rs = P.tile([128, TP], F32, f"mrs{tag}")
    qn = P.tile([128, TP], BF16, f"mqn{tag}")
    E = [P.tile([128, TP], BF16, f"mE{tag}{i}") for i in range(2)]
    dn = P.tile([128, TP], F32, f"mdn{tag}")
    yy = P.tile([128, TP], F32, f"myy{tag}")
    ob = [P.tile([128, TP], BF16, f"mob{tag}{i}") for i in range(2)]
    it = 0
    for pi in range(npc):
        t0 = pi * TP
        for h in range(4):
            QX, GX, O = qx[it % 2], gx[it % 2], ob[it % 2]
            it += 1
            P.dma("sp", QX.c, lambda e, QX=QX, h=h, t0=t0: e.dma_start(out=QX[:], in_=qsrc(h)[:, t0:t0 + TP]), writes=[QX])
            P.dma("sp", GX.c, lambda e, GX=GX, h=h, t0=t0: e.dma_start(out=GX[:], in_=gsrc(h)[:, t0:t0 + TP]), writes=[GX])
            P.op("act", lambda e, QX=QX: e.activation(out=sq[:], in_=QX[:], func=AF.Square), reads=[QX], writes=[sq])
            pn, ps0, ps1, pnum, pden = pp[0], pp[1], pp[2], pp[3], pp[4]
            P.op("pe", lambda e, pn=pn: e.matmul(pn[:, 0:TP], lhsT=K.ones_f[:, :], rhs=sq[:], start=True, stop=True),
                 reads=[K.ones_f, sq], writes=[pn])
            P.op("act", lambda e, pn=pn: e.activation(out=rs[:], in_=pn[:, 0:TP], func=AF.Ln, scale=1.0 / 128, bias=1e-6),
                 reads=[pn], writes=[rs])
            P.op("act", lambda e: e.activation(out=rs[:], in_=rs[:], func=AF.Exp, scale=-0.5), reads=[rs], writes=[rs])
            P.op("dve", lambda e, QX=QX: e.scalar_tensor_tensor(out=qn[:], in0=QX[:], scalar=prm[:, 0:1], in1=rs[:],
                                                               op0=ALU.mult, op1=ALU.mult), reads=[QX, prm, rs], writes=[qn])
            for mt, psb in ((0, ps0), (1, ps1)):
                P.op("pe", lambda e, psb=psb, mt=mt, h=h: e.matmul(psb[:, 0:TP], lhsT=mkT[:, h, mt * 128:(mt + 1) * 128],
                                                                  rhs=qn[:], start=True, stop=True),
                     reads=[mkT, qn], writes=[psb])
                P.op("act", lambda e, psb=psb, mt=mt: e.activation(out=E[mt][:], in_=psb[:, 0:TP], func=AF.Exp),
                     reads=[psb], writes=[E[mt]])
            for mt in range(2):
                P.op("pe", lambda e, mt=mt, h=h, pnum=pnum: e.matmul(pnum[:, 0:TP], lhsT=mvb[:, mt, h * 128:(h + 1) * 128],
                                                                    rhs=E[mt][:], start=(mt == 0), stop=(mt == 1)),
                     reads=[mvb, E[mt]], writes=[pnum] if mt == 0 else [], accs=[pnum] if mt else [], signal=(mt == 1))
            for mt in range(2):
                P.op("pe", lambda e, mt=mt, pden=pden: e.matmul(pden[:, 0:TP], lhsT=K.ones_b[:, :], rhs=E[mt][:],
                                                               start=(mt == 0), stop=(mt == 1)),
                     reads=[K.ones_b, E[mt]], writes=[pden] if mt == 0 else [], accs=[pden] if mt else [],
                     signal=(mt == 1))
            P.op("act", lambda e, pden=pden: e.activation(out=dn[:], in_=pden[:, 0:TP], func=AF.Ln), reads=[pden], writes=[dn])
            P.op("act", lambda e: e.activation(out=dn[:], in_=dn[:], func=AF.Exp, scale=-1.0), reads=[dn], writes=[dn])
            P.op("dve", lambda e, pnum=pnum: e.tensor_tensor(out=yy[:], in0=pnum[:, 0:TP], in1=dn[:], op=ALU.mult),
                 reads=[pnum, dn], writes=[yy])
            P.op("act", lambda e, GX=GX: e.activation(out=GX[:], in_=GX[:], func=AF.Silu), reads=[GX], writes=[GX])
            P.op("dve", lambda e, GX=GX, O=O: e.tensor_tensor(out=O[:], in0=yy[:], in1=GX[:], op=ALU.mult),
                 reads=[yy, GX], writes=[O])
            P.dma("sp", O.c, lambda e, O=O, h=h, t0=t0: e.dma_start(out=odst(h)[:, t0:t0 + TP], in_=O[:]), reads=[O])


class RwConsts:
    def __init__(self, P, c_ui, c_sl, c_reset):
        self.ui = P.tile([128, 256], F32, "rw_ui", persistent=True)
        self.sl = P.tile([128, 128], F32, "rw_sl", persistent=True)
        self.reset = P.tile([64, 1024], F32, "rw_reset", persistent=True)
        P.dma("sp", self.ui.c, lambda e: e.dma_start(out=self.ui[:], in_=c_ui), writes=[self.ui])
        P.dma("sp", self.sl.c, lambda e: e.dma_start(out=self.sl[:], in_=c_sl), writes=[self.sl])
        P.dma("sp", self.reset.c, lambda e: e.dma_start(out=self.reset[:], in_=c_reset), writes=[self.reset])


class Bank:
    def __init__(self, P, name):
        self.t = P.tile([128, 512], F32, name, psum=True)
        self.q = [self.t.r] * 4

    def __getitem__(self, idx):
        return self.t[idx]


def rwkv_stage(P, K, RK, NTOK, rrows, krows, vrows, wdrows, adrows, grows, orows, prm_dram, wup_dram, aup_dram, tag="",
               do_chunk=True, max_it=6):
    TP = 1024
    CH = 128
    npc = NTOK // TP
    ncp = TP // CH
    f = F32
    prm = P.tile([64, 24], f, f"rprm{tag}")
    lup = P.tile([64, 128], f, f"rlup{tag}")
    P.dma("sp", prm.c, lambda e: e.dma_start(out=prm[:, 0:16], in_=prm_dram, allow_slow_non_contiguous=True), writes=[prm])
    P.op("pool", lambda e: e.memset(lup[:], 0.0), writes=[lup])
    P.dma("sp", lup.c, lambda e: e.dma_start(out=lup[0:32, 0:64], in_=wup_dram), accs=[lup])
    P.dma("sp", lup.c, lambda e: e.dma_start(out=lup[32:64, 64:128], in_=aup_dram), accs=[lup])
    P.op("dve", lambda e: e.tensor_scalar(out=prm[:, 16:20], in0=prm[:, 0:4], scalar1=-1.0, scalar2=1.0, op0=ALU.mult,
                                         op1=ALU.add), reads=[prm], accs=[prm])
    P.op("dve", lambda e: e.tensor_scalar(out=prm[:, 20:21], in0=prm[:, 7:8], scalar1=-1.0, scalar2=1.0, op0=ALU.mult,
                                         op1=ALU.add), reads=[prm], accs=[prm])
    xin = {nm: [P.tile([64, TP + 1], f, f"rx{nm}{tag}{i}") for i in range(2)] for nm in ("r", "k", "v", "l")}
    gin = [P.tile([64, TP], f, f"rg{tag}{i}") for i in range(2)]
    T = {nm: P.tile([64, TP], f, f"r_{nm}{tag}") for nm in
         ("R", "Kt", "V", "L", "tmp", "SG", "A", "LW", "cum", "KK", "Kp", "Bv", "e1", "e2",
          "bon", "Y", "t2")}
    for nm in ("Bt", "Ktl", "Bh", "Kh", "Vb"):
        T[nm] = P.tile([64, TP], BF16, f"r_{nm}{tag}")
    AR = P.tile([64, ncp, 256], BF16, f"r_AR{tag}")
    ARf = P.tile([64, ncp, 128], f, f"r_ARf{tag}")
    ob = [P.tile([64, TP], BF16, f"rob{tag}{i}") for i in range(2)]
    Sb = [P.tile([64, 64], f, f"rS{tag}{i}") for i in range(4)]
    G = 3
    banks = [(Bank(P, f"rbA{tag}{g}"), Bank(P, f"rbB{tag}{g}")) for g in range(G)]
    ctxs = []
    for par in range(2):
        row = []
        for g in range(G):
            c = dict(bA=banks[g][0], bB=banks[g][1], bC=banks[g][1])
            for nm, shp in (("Gm1", [128, 256]), ("Gm2", [128, 256]), ("P0", [128, 128]), ("P1", [128, 128]),
                            ("PT0", [128, 128]), ("PT1", [128, 128]), ("T", [128, 128]), ("TM", [128, 320]),
                            ("AU", [128, 128]), ("Mt", [64, 64]), ("Qt", [64, 128])):
                c[nm] = P.tile(shp, f if nm in ("Mt", "Qt") else BF16, f"rc{nm}{tag}{par}{g}")
            row.append(c)
        ctxs.append(row)
    ngrp = 0
    bY = Bank(P, f"rbY{tag}")
    bS = Bank(P, f"rbS{tag}")
    bN = [banks[0][1], banks[1][1]]
    nbn = 0
    ones64 = K.ones_f[0:64, 0:64]
    id64 = K.identf[0:64, 0:64]
    P.op("dve", lambda e: e.memset(Sb[0][:], 0.0), writes=[Sb[0]])
    sidx = 0

    def v3(t):
        return t[:, :].rearrange("p (c t) -> p c t", t=CH)

    def ones_mm(src_ap_fn, nsub, consume):
        nonlocal nbn
        for sb_ in range(nsub):
            bk = bN[nbn % 2]
            nbn += 1
            rd = src_ap_fn(sb_)
            P.op("pe", lambda e, bk=bk, rd=rd: e.matmul(bk[0:64, :], lhsT=ones64, rhs=rd[0], start=True, stop=True),
                 reads=[K.ones_f] + rd[1], writes=bk.q)
            consume(sb_, bk)

    def serial_phase(grp, ctx):
        nonlocal sidx
        for g, c in enumerate(grp):
            C = ctx[g]
            S0 = Sb[sidx % 4]
            S1 = Sb[(sidx + 1) % 4]
            sidx += 1
            ycol = (c % 4) * 128
            yq = bY.q[c % 4]
            P.op("pe", lambda e, S0=S0, C=C, ycol=ycol: e.matmul(bY[0:64, ycol:ycol + 128], lhsT=S0[:], rhs=C["Qt"][:],
                                                                start=True, stop=False),
                 reads=[S0, C["Qt"]], writes=[yq], signal=False)
            P.op("pe", lambda e, C=C, ycol=ycol: e.matmul(bY[0:64, ycol:ycol + 128], lhsT=C["AU"][:, 64:128],
                                                         rhs=C["Gm1"][:, 128:256], start=False, stop=False),
                 reads=[C["AU"], C["Gm1"]], accs=[yq], signal=False)
            P.op("pe", lambda e, C=C, ycol=ycol: e.matmul(bY[0:64, ycol:ycol + 128], lhsT=C["TM"][:, 128:192],
                                                         rhs=C["Gm2"][:, 128:256], start=False, stop=True),
                 reads=[C["TM"], C["Gm2"]], accs=[yq], signal=False)
            P.op("pe", lambda e, C=C: e.matmul(bS[0:64, 0:64], lhsT=C["TM"][:, 192:256], rhs=C["AU"][:, 64:128],
                                               start=True, stop=False), reads=[C["TM"], C["AU"]], writes=[bS.q[0]], signal=False)
            P.op("pe", lambda e, C=C: e.matmul(bS[0:64, 0:64], lhsT=C["TM"][:, 256:320], rhs=C["TM"][:, 128:192],
                                               start=False, stop=False), reads=[C["TM"]], accs=[bS.q[0]], signal=False)
            P.op("pe", lambda e, C=C, S0=S0: e.matmul(bS[0:64, 0:64], lhsT=C["Mt"][:], rhs=S0[:], start=False, stop=True),
                 reads=[C["Mt"], S0], accs=[bS.q[0]])
            P.op("act", lambda e, S1=S1: e.activation(out=S1[:], in_=bS[0:64, 0:64], func=AF.Copy),
                 reads=[bS.q[0]], writes=[S1])
            if c % 4 == 3:
                y0 = (c - 3) * CH
                P.op("act", lambda e, y0=y0: e.activation(out=T["Y"][:, y0:y0 + 512], in_=bY[0:64, :], func=AF.Copy),
                     reads=bY.q, writes=[T["Y"]] if c == 3 else [], accs=[] if c == 3 else [T["Y"]])

    for pi in range(npc):
        s = pi % 2
        t0 = pi * TP
        XR, XK, XV, XL, GX, O = xin["r"][s], xin["k"][s], xin["v"][s], xin["l"][s], gin[s], ob[s]
        for X, rows_list in ((XR, [(rrows, 0, 64)]), (XK, [(krows, 0, 64)]), (XV, [(vrows, 0, 64)]),
                             (XL, [(wdrows, 0, 32), (adrows, 32, 64)])):
            first = True
            if pi == 0:
                P.op("pool", lambda e, X=X: e.memset(X[:, 0:1], 0.0), writes=[X])
                first = False
            for (src, p0, p1) in rows_list:
                if pi == 0:
                    P.dma("sp", X.c, lambda e, X=X, src=src, p0=p0, p1=p1: e.dma_start(out=X[p0:p1, 1:TP + 1], in_=src[:, 0:TP]),
                          accs=[X])
                else:
                    P.dma("sp", X.c, lambda e, X=X, src=src, p0=p0, p1=p1, t0=t0: e.dma_start(
                        out=X[p0:p1, :], in_=src[:, t0 - 1:t0 + TP]), writes=[X] if first else [], accs=[] if first else [X])
                first = False
        P.dma("sp", GX.c, lambda e, GX=GX, t0=t0: e.dma_start(out=GX[:], in_=grows[:, t0:t0 + TP]), writes=[GX])
        for X, dst, col in ((XR, T["R"], 0), (XK, T["Kt"], 1), (XV, T["V"], 2), (XL, T["L"], 3)):
            P.op("act", lambda e, X=X, col=col: e.activation(out=T["tmp"][:], in_=X[:, 0:TP], func=AF.Copy,
                                                            scale=prm[:, col:col + 1]), reads=[X, prm], writes=[T["tmp"]])
            P.op("dve", lambda e, X=X, dst=dst, col=col: e.scalar_tensor_tensor(
                out=dst[:], in0=X[:, 1:TP + 1], scalar=prm[:, 16 + col:17 + col], in1=T["tmp"][:], op0=ALU.mult, op1=ALU.add),
                reads=[X, prm, T["tmp"]], writes=[dst])
        P.op("act", lambda e: e.activation(out=T["L"][0:32, :], in_=T["L"][0:32, :], func=AF.Tanh), reads=[T["L"]], writes=[T["L"]])
        for (lo, hi, bcol, dst) in ((0, 32, 4, T["SG"]), (32, 64, 5, T["A"])):
            for sb_ in range(TP // 512):
                bk = bN[nbn % 2]
                nbn += 1
                P.op("pe", lambda e, bk=bk, lo=lo, hi=hi, sb_=sb_: e.matmul(bk[0:64, :], lhsT=lup[:, 2 * lo:2 * lo + 64],
                                                                           rhs=T["L"][:, sb_ * 512:(sb_ + 1) * 512],
                                                                           start=True, stop=True),
                     reads=[lup, T["L"]], writes=bk.q)
                P.op("act", lambda e, bk=bk, dst=dst, sb_=sb_, bcol=bcol: e.activation(
                    out=dst[:, sb_ * 512:(sb_ + 1) * 512], in_=bk[0:64, :], func=AF.Sigmoid, bias=prm[:, bcol:bcol + 1]),
                    reads=bk.q + [prm], writes=[dst] if sb_ == 0 else [], accs=[] if sb_ == 0 else [dst])
        P.op("dve", lambda e: e.tensor_scalar(out=T["LW"][:], in0=T["SG"][:], scalar1=-0.6065306597126334, scalar2=None,
                                             op0=ALU.mult), reads=[T["SG"]], writes=[T["LW"]])
        P.op("dve", lambda e: e.tensor_tensor_scan(out=T["cum"][:], data0=RK.reset[:, 0:TP], data1=T["LW"][:], initial=0.0,
                                                  op0=ALU.mult, op1=ALU.add), reads=[RK.reset, T["LW"]], writes=[T["cum"]])
        P.op("dve", lambda e: e.tensor_tensor(out=T["tmp"][:], in0=T["cum"][:], in1=T["LW"][:], op=ALU.subtract),
             reads=[T["cum"], T["LW"]], writes=[T["tmp"]])
        P.op("act", lambda e: e.activation(out=T["e1"][:], in_=T["tmp"][:], func=AF.Exp), reads=[T["tmp"]], writes=[T["e1"]])
        P.op("act", lambda e: e.activation(out=T["KK"][:], in_=T["Kt"][:], func=AF.Copy, scale=prm[:, 6:7]),
             reads=[T["Kt"], prm], writes=[T["KK"]])
        P.op("act", lambda e: e.activation(out=T["tmp"][:], in_=T["KK"][:], func=AF.Square),
             reads=[T["KK"]], writes=[T["tmp"]])

        def cons_kk(sb_, bk):
            P.op("act", lambda e, bk=bk, sb_=sb_: e.activation(out=T["t2"][:, sb_ * 512:(sb_ + 1) * 512], in_=bk[0:64, :],
                                                              func=AF.Ln, bias=1e-24), reads=bk.q,
                 writes=[T["t2"]] if sb_ == 0 else [], accs=[] if sb_ == 0 else [T["t2"]])
        ones_mm(lambda sb_: (T["tmp"][:, sb_ * 512:(sb_ + 1) * 512], [T["tmp"]]), TP // 512, cons_kk)
        P.op("act", lambda e: e.activation(out=T["t2"][:], in_=T["t2"][:], func=AF.Exp, scale=-0.5),
             reads=[T["t2"]], writes=[T["t2"]])
        P.op("dve", lambda e: e.tensor_tensor(out=T["KK"][:], in0=T["KK"][:], in1=T["t2"][:], op=ALU.mult),
             reads=[T["KK"], T["t2"]], writes=[T["KK"]])
        P.op("dve", lambda e: e.scalar_tensor_tensor(out=AR[:, :, 0:128], in0=v3(T["KK"]), scalar=-1.0, in1=v3(T["e1"]),
                                                    op0=ALU.mult, op1=ALU.mult), reads=[T["KK"], T["e1"]], writes=[AR])
        P.op("act", lambda e: e.activation(out=T["tmp"][:], in_=T["A"][:], func=AF.Identity, scale=prm[:, 7:8],
                                           bias=prm[:, 20:21]), reads=[T["A"], prm], writes=[T["tmp"]])
        P.op("dve", lambda e: e.tensor_tensor(out=T["Kp"][:], in0=T["Kt"][:], in1=T["tmp"][:], op=ALU.mult),
             reads=[T["Kt"], T["tmp"]], writes=[T["Kp"]])
        P.op("dve", lambda e: e.tensor_tensor(out=T["Bv"][:], in0=T["KK"][:], in1=T["A"][:], op=ALU.mult),
             reads=[T["KK"], T["A"]], writes=[T["Bv"]])
        P.op("act", lambda e: e.activation(out=T["e1"][:], in_=T["cum"][:], func=AF.Exp), reads=[T["cum"]], writes=[T["e1"]])
        P.op("act", lambda e: e.activation(out=T["e2"][:], in_=T["cum"][:], func=AF.Exp, scale=-1.0), reads=[T["cum"]],
             writes=[T["e2"]])
        P.op("dve", lambda e: e.tensor_tensor(out=ARf[:, :, :], in0=v3(T["R"]), in1=v3(T["e1"]), op=ALU.mult),
             reads=[T["R"], T["e1"]], writes=[ARf])
        P.op("act", lambda e: e.activation(out=AR[:, :, 128:256], in_=ARf[:, :, :], func=AF.Copy), reads=[ARf], accs=[AR])
        P.op("act", lambda e: e.activation(out=T["Vb"][:], in_=T["V"][:], func=AF.Copy), reads=[T["V"]], writes=[T["Vb"]])
        P.op("dve", lambda e: e.tensor_tensor(out=T["Bt"][:], in0=T["Bv"][:], in1=T["e2"][:], op=ALU.mult),
             reads=[T["Bv"], T["e2"]], writes=[T["Bt"]])
        P.op("dve", lambda e: e.tensor_tensor(out=T["Ktl"][:], in0=T["Kp"][:], in1=T["e2"][:], op=ALU.mult),
             reads=[T["Kp"], T["e2"]], writes=[T["Ktl"]])
        for c in range(ncp):
            cs = slice(c * CH, (c + 1) * CH)
            ge = slice(c * CH + CH - 1, c * CH + CH)
            P.op("dve", lambda e, cs=cs, ge=ge: e.tensor_scalar(out=T["Bh"][:, cs], in0=T["Bt"][:, cs], scalar1=T["e1"][:, ge],
                                                               scalar2=None, op0=ALU.mult),
                 reads=[T["Bt"], T["e1"]], writes=[T["Bh"]] if c == 0 else [], accs=[] if c == 0 else [T["Bh"]])
            P.op("act", lambda e, cs=cs, ge=ge: e.activation(out=T["Kh"][:, cs], in_=T["Ktl"][:, cs], func=AF.Copy,
                                                            scale=T["e1"][:, ge]),
                 reads=[T["Ktl"], T["e1"]], writes=[T["Kh"]] if c == 0 else [], accs=[] if c == 0 else [T["Kh"]])
        P.op("dve", lambda e: e.scalar_tensor_tensor(out=T["tmp"][:], in0=T["R"][:], scalar=prm[:, 8:9], in1=T["Kp"][:],
                                                     op0=ALU.mult, op1=ALU.mult), reads=[T["R"], prm, T["Kp"]], writes=[T["tmp"]])

        def cons_bon(sb_, bk):
            P.op("dve", lambda e, bk=bk, sb_=sb_: e.tensor_tensor(out=T["bon"][:, sb_ * 512:(sb_ + 1) * 512], in0=bk[0:64, :],
                                                                 in1=T["V"][:, sb_ * 512:(sb_ + 1) * 512], op=ALU.mult),
                 reads=bk.q + [T["V"]], writes=[T["bon"]] if sb_ == 0 else [], accs=[] if sb_ == 0 else [T["bon"]])
        ones_mm(lambda sb_: (T["tmp"][:, sb_ * 512:(sb_ + 1) * 512], [T["tmp"]]), TP // 512, cons_bon)

        if not do_chunk:
            P.op("dve", lambda e: e.memset(T["Y"][:], 0.0), writes=[T["Y"]])
        groups = [list(range(i, min(i + G, ncp))) for i in range(0, ncp, G)] if do_chunk else []
        pending_serial = None
        for grp in groups:
            steps = []
            ctx = ctxs[ngrp % 2]
            ngrp += 1
            for g, c in enumerate(grp):
                C = ctx[g]
                bA, bB = C["bA"], C["bB"]
                cs = slice(c * CH, (c + 1) * CH)
                st = []

                def sA(C=C, bA=bA, bB=bB, c=c, cs=cs):
                    P.op("pe", lambda e: e.matmul(bA[:, 0:256], lhsT=T["Bt"][:, cs], rhs=AR[:, c, :], start=True, stop=True),
                         reads=[T["Bt"], AR], writes=bA.q, signal=False)
                    P.op("pe", lambda e: e.matmul(bA[:, 256:512], lhsT=T["Ktl"][:, cs], rhs=AR[:, c, :], start=True, stop=True),
                         reads=[T["Ktl"], AR], accs=bA.q, signal=False)
                    P.op("pe", lambda e: e.matmul(bB[:, 0:128], lhsT=AR[:, c, 0:128], rhs=T["Bt"][:, cs], start=True, stop=True),
                         reads=[T["Bt"], AR], writes=[bB.q[0]], signal=False)
                    for i4, src in enumerate((AR[:, c, 0:128], T["Vb"][:, cs], T["Bh"][:, cs], T["Kh"][:, cs])):
                        P.op("pe", lambda e, src=src, i4=i4: e.matmul(bB[:, 128 + i4 * 64:192 + i4 * 64], lhsT=src,
                                                                     rhs=K.identb[0:64, 0:64], start=True, stop=True),
                             reads=[AR, T["Vb"], T["Bh"], T["Kh"], K.identb], writes=[bB.q[1], bB.q[2]] if i4 == 0 else [],
                             accs=[] if i4 == 0 else [bB.q[1], bB.q[2]], signal=(i4 == 3))
                st.append(sA)

                def sAe(C=C, bA=bA, bB=bB):
                    P.op("dve", lambda e: e.tensor_tensor(out=C["Gm1"][:], in0=bA[:, 0:256], in1=RK.ui[:], op=ALU.mult),
                         reads=[bA.q[0], bA.q[1], RK.ui], writes=[C["Gm1"]])
                    P.op("dve", lambda e: e.tensor_tensor(out=C["Gm2"][:], in0=bA[:, 256:512], in1=RK.ui[:], op=ALU.mult),
                         reads=[bA.q[2], bA.q[3], RK.ui], writes=[C["Gm2"]])
                    P.op("dve", lambda e: e.tensor_tensor(out=C["PT0"][:], in0=bB[:, 0:128], in1=RK.sl[:], op=ALU.mult),
                         reads=[bB.q[0], RK.sl], writes=[C["PT0"]])
                    P.op("pool", lambda e: e.tensor_tensor(out=C["T"][:], in0=C["Gm1"][:, 0:128], in1=K.identb[:], op=ALU.add),
                         reads=[C["Gm1"], K.identb], writes=[C["T"]])
                    P.op("act", lambda e: e.activation(out=C["TM"][:, 0:64], in_=bB[:, 128:192], func=AF.Copy),
                         reads=[bB.q[1]], writes=[C["TM"]])
                    P.op("act", lambda e: e.activation(out=C["TM"][:, 128:320], in_=bB[:, 192:384], func=AF.Copy),
                         reads=[bB.q[1], bB.q[2]], accs=[C["TM"]])
                st.append(sAe)
                for it in range(6):
                    last = (it == 5)

                    def sI1(C=C, bA=bA, it=it, last=last):
                        Pc = C["Gm1"] if it == 0 else C[f"P{it % 2}"]
                        Pc_ap = C["Gm1"][:, 0:128] if it == 0 else C[f"P{it % 2}"][:]
                        PTc = C["PT0"] if it == 0 else C[f"PT{it % 2}"]
                        if not last:
                            P.op("pe", lambda e: e.matmul(bA[:, 0:128], lhsT=PTc[:], rhs=Pc_ap, start=True, stop=True),
                                 reads=[PTc, Pc], writes=[bA.q[0]], signal=False)
                        P.op("pe", lambda e: e.matmul(bA[:, 128:256], lhsT=Pc_ap, rhs=PTc[:], start=True, stop=True),
                             reads=[PTc, Pc], writes=[bA.q[1]])
                    st.append(sI1)

                    def sI2(C=C, bA=bA, it=it, last=last):
                        Pn = C[f"P{(it + 1) % 2}"]
                        PTn = C[f"PT{(it + 1) % 2}"]
                        if it == 0:
                            PTn = C["PT1"]
                        P.op("act", lambda e: e.activation(out=PTn[:], in_=bA[:, 128:256], func=AF.Copy),
                             reads=[bA.q[1]], writes=[PTn])
                        if not last:
                            P.op("act", lambda e: e.activation(out=Pn[:], in_=bA[:, 0:128], func=AF.Copy),
                                 reads=[bA.q[0]], writes=[Pn])
                    st.append(sI2)

                    def sI3(C=C, bC=C["bC"], it=it):
                        PTn = C[f"PT{(it + 1) % 2}"]
                        if it == 0:
                            PTn = C["PT1"]
                        P.op("pe", lambda e: e.matmul(bC[:, 0:128], lhsT=PTn[:], rhs=C["T"][:], start=True, stop=True),
                             reads=[PTn, C["T"]], writes=[bC.q[0]])
                    st.append(sI3)

                    def sI4(C=C, bC=C["bC"]):
                        P.op("dve", lambda e: e.tensor_tensor(out=C["T"][:], in0=bC[:, 0:128], in1=C["T"][:], op=ALU.add),
                             reads=[bC.q[0], C["T"]], writes=[C["T"]])
                    st.append(sI4)

                def sW(C=C, bB=bB):
                    P.op("pe", lambda e: e.matmul(bB[:, 384:448], lhsT=C["Gm2"][:, 0:128], rhs=C["TM"][:, 128:192],
                                                  start=True, stop=True), reads=[C["Gm2"], C["TM"]], writes=[bB.q[3]])
                st.append(sW)

                def sWe(C=C, bB=bB):
                    P.op("act", lambda e: e.activation(out=C["TM"][:, 64:128], in_=bB[:, 384:448], func=AF.Copy),
                         reads=[bB.q[3]], accs=[C["TM"]])
                st.append(sWe)

                def sAU(C=C, bA=bA):
                    P.op("pe", lambda e: e.matmul(bA[:, 384:512], lhsT=C["T"][:], rhs=C["TM"][:, 0:128], start=True, stop=True),
                         reads=[C["T"], C["TM"]], writes=[bA.q[3]])
                st.append(sAU)

                def sAUe(C=C, bA=bA):
                    P.op("act", lambda e: e.activation(out=C["AU"][:], in_=bA[:, 384:512], func=AF.Copy),
                         reads=[bA.q[3]], writes=[C["AU"]])
                st.append(sAUe)

                def sMQ(C=C, bB=bB):
                    P.op("pe", lambda e: e.matmul(bB[0:64, 448:512], lhsT=C["AU"][:, 0:64], rhs=C["TM"][:, 192:256],
                                                  start=True, stop=True), reads=[C["AU"], C["TM"]], writes=[bB.q[3]], signal=False)
                    P.op("pe", lambda e: e.matmul(bB[0:64, 0:128], lhsT=C["AU"][:, 0:64], rhs=C["Gm1"][:, 128:256],
                                                  start=True, stop=True), reads=[C["AU"], C["Gm1"]], writes=[bB.q[0]])
                st.append(sMQ)

                def sMQe(C=C, bB=bB, c=c, cs=cs):
                    ge = slice(c * CH + CH - 1, c * CH + CH)
                    P.op("dve", lambda e: e.scalar_tensor_tensor(out=C["Mt"][:], in0=id64, scalar=T["e1"][:, ge],
                                                                in1=bB[0:64, 448:512], op0=ALU.mult, op1=ALU.add),
                         reads=[K.identf, T["e1"], bB.q[3]], writes=[C["Mt"]])
                    P.op("dve", lambda e: e.tensor_tensor(out=C["Qt"][:], in0=bB[0:64, 0:128], in1=ARf[:, c, :], op=ALU.add),
                         reads=[bB.q[0], ARf], writes=[C["Qt"]])
                st.append(sMQe)
                steps.append(st)
            for si in range(len(steps[0])):
                for g in range(len(grp)):
                    steps[g][si]()
            if pending_serial is not None:
                pending_serial()
            pending_serial = (lambda grp=grp, ctx=ctx: serial_phase(grp, ctx))
        if pending_serial is not None:
            pending_serial()
            pending_serial = None
        def cons_mean(sb_, bk):
            ss = slice(sb_ * 512, (sb_ + 1) * 512)
            P.op("dve", lambda e, bk=bk, ss=ss: e.scalar_tensor_tensor(out=T["tmp"][:, ss], in0=bk[0:64, :], scalar=-1.0 / 64,
                                                                      in1=T["Y"][:, ss], op0=ALU.mult, op1=ALU.add),
                 reads=bk.q + [T["Y"]], writes=[T["tmp"]] if sb_ == 0 else [], accs=[] if sb_ == 0 else [T["tmp"]])
        ones_mm(lambda sb_: (T["Y"][:, sb_ * 512:(sb_ + 1) * 512], [T["Y"]]), TP // 512, cons_mean)
        P.op("act", lambda e: e.activation(out=T["t2"][:], in_=T["tmp"][:], func=AF.Square),
             reads=[T["tmp"]], writes=[T["t2"]])

        def cons_var(sb_, bk):
            ss = slice(sb_ * 512, (sb_ + 1) * 512)
            P.op("act", lambda e, bk=bk, ss=ss: e.activation(out=T["e2"][:, ss], in_=bk[0:64, :], func=AF.Ln, scale=1.0 / 64,
                                                            bias=64e-5), reads=bk.q,
                 writes=[T["e2"]] if sb_ == 0 else [], accs=[] if sb_ == 0 else [T["e2"]])
        ones_mm(lambda sb_: (T["t2"][:, sb_ * 512:(sb_ + 1) * 512], [T["t2"]]), TP // 512, cons_var)
        P.op("act", lambda e: e.activation(out=T["e2"][:], in_=T["e2"][:], func=AF.Exp, scale=-0.5), reads=[T["e2"]], writes=[T["e2"]])
        P.op("dve", lambda e: e.tensor_tensor(out=T["tmp"][:], in0=T["tmp"][:], in1=T["e2"][:], op=ALU.mult),
             reads=[T["tmp"], T["e2"]], writes=[T["tmp"]])
        P.op("dve", lambda e: e.tensor_scalar(out=T["tmp"][:], in0=T["tmp"][:], scalar1=prm[:, 9:10], scalar2=prm[:, 10:11],
                                             op0=ALU.mult, op1=ALU.add), reads=[T["tmp"], prm], writes=[T["tmp"]])
        P.op("dve", lambda e: e.tensor_tensor(out=T["tmp"][:], in0=T["tmp"][:], in1=T["bon"][:], op=ALU.add),
             reads=[T["tmp"], T["bon"]], writes=[T["tmp"]])
        P.op("act", lambda e, GX=GX: e.activation(out=GX[:], in_=GX[:], func=AF.Silu), reads=[GX], writes=[GX])
        P.op("dve", lambda e, GX=GX, O=O: e.tensor_tensor(out=O[:], in0=T["tmp"][:], in1=GX[:], op=ALU.mult),
             reads=[T["tmp"], GX], writes=[O])
        P.dma("sp", O.c, lambda e, O=O, t0=t0: e.dma_start(out=orows[:, t0:t0 + TP], in_=O[:]), reads=[O])


def swa_stage4(P, K, NTOK, qrows, krows, vrows, grows, orows, qg, kg, sinks, TP=512, tag=""):
    npc = NTOK // TP
    nbk = TP // 128
    NH = 4
    prm = P.tile([64, 16], F32, f"s4prm{tag}")
    P.dma("sp", prm.c, lambda e: e.dma_start(out=prm[:, 0:1], in_=qg, allow_slow_non_contiguous=True), writes=[prm])
    P.dma("sp", prm.c, lambda e: e.dma_start(out=prm[:, 1:2], in_=kg, allow_slow_non_contiguous=True), accs=[prm])
    for h in range(NH):
        P.dma("sp", prm.c, lambda e, h=h: e.dma_start(out=prm[:, 4 + h:5 + h], in_=sinks[h], allow_slow_non_contiguous=True),
              accs=[prm])
    P.op("dve", lambda e: e.tensor_scalar(out=prm[:, 3:4], in0=prm[:, 0:1], scalar1=0.125, scalar2=None, op0=ALU.mult),
         reads=[prm], accs=[prm])
    P.op("act", lambda e: e.activation(out=prm[:, 8:12], in_=prm[:, 4:8], func=AF.Exp), reads=[prm], accs=[prm])
    sinkrow = P.tile([64, NH, 128], F32, f"s4sink{tag}")
    for h in range(NH):
        P.op("act", lambda e, h=h: e.activation(out=sinkrow[:, h, :], in_=K.ones_f[0:64, 0:128], func=AF.Identity, scale=0.0,
                                               bias=prm[:, 8 + h:9 + h]), reads=[K.ones_f, prm],
             writes=[sinkrow] if h == 0 else [], accs=[] if h == 0 else [sinkrow])
    mask4 = P.tile([128, 2, NH, 128], BF16, f"s4mask{tag}")
    for j in range(2):
        for h in range(NH):
            P.op("pool", lambda e, j=j, h=h: e.tensor_copy(out=mask4[:, j, h, :], in_=K.swamask[:, j * 128:(j + 1) * 128]),
                 reads=[K.swamask], writes=[mask4] if (j == 0 and h == 0) else [], accs=[] if (j == 0 and h == 0) else [mask4])
    W = TP + 128
    kx = [P.tile([64, W], F32, f"s4kx{tag}{i}") for i in range(2)]
    vx = [P.tile([64, W], F32, f"s4vx{tag}{i}") for i in range(2)]
    qx = [P.tile([64, NH, TP], F32, f"s4qx{tag}{i}") for i in range(2)]
    gx = [P.tile([64, NH, TP], F32, f"s4gx{tag}{i}") for i in range(2)]
    sq = P.tile([64, NH * TP], F32, f"s4sq{tag}")
    rs = P.tile([64, NH * TP], F32, f"s4rs{tag}")
    kn = P.tile([64, W], BF16, f"s4kn{tag}")
    qn = P.tile([64, NH, TP], BF16, f"s4qn{tag}")
    vb = P.tile([128, nbk + 1, 64], BF16, f"s4vb{tag}")
    E = [[P.tile([128, NH, 128], BF16, f"s4E{tag}{i}{j}") for j in range(2)] for i in range(2)]
    dn = P.tile([64, NH, 128], F32, f"s4dn{tag}")
    yy = P.tile([64, NH, TP], F32, f"s4yy{tag}")
    ob = [P.tile([64, NH, TP], BF16, f"s4ob{tag}{i}") for i in range(2)]
    pn = [P.tile([64, 512], F32, f"s4pn{tag}{i}", psum=True) for i in range(2)]
    psc = [[P.tile([128, 512], F32, f"s4psc{tag}{i}{j}", psum=True) for j in range(2)] for i in range(2)]
    pnum = P.tile([64, 512], F32, f"s4pnum{tag}", psum=True)
    pden = P.tile([64, 512], F32, f"s4pden{tag}", psum=True)
    npn = 0
    nsc = 0

    def norm(src_ap, src_t, width, gcol, dst_ap, dst_t, first_dst=True):
        nonlocal npn
        P.op("act", lambda e: e.activation(out=sq[:, 0:width], in_=src_ap, func=AF.Square), reads=[src_t], writes=[sq])
        o = 0
        first = True
        while o < width:
            w_ = min(512, width - o)
            pb = pn[npn % 2]
            npn += 1
            P.op("pe", lambda e, pb=pb, o=o, w_=w_: e.matmul(pb[:, 0:w_], lhsT=K.ones_f[0:64, 0:64], rhs=sq[:, o:o + w_],
                                                            start=True, stop=True), reads=[K.ones_f, sq], writes=[pb])
            P.op("act", lambda e, pb=pb, o=o, w_=w_: e.activation(out=rs[:, o:o + w_], in_=pb[:, 0:w_], func=AF.Ln,
                                                                 scale=1.0 / 64, bias=1e-6),
                 reads=[pb], writes=[rs] if first else [], accs=[] if first else [rs])
            first = False
            o += w_
        P.op("act", lambda e: e.activation(out=rs[:, 0:width], in_=rs[:, 0:width], func=AF.Exp, scale=-0.5),
             reads=[rs], writes=[rs])
        P.op("dve", lambda e: e.scalar_tensor_tensor(out=dst_ap, in0=src_ap, scalar=prm[:, gcol:gcol + 1], in1=rs[:, 0:width],
                                                    op0=ALU.mult, op1=ALU.mult), reads=[src_t, prm, rs],
             writes=[dst_t] if first_dst else [], accs=[] if first_dst else [dst_t])

    for pi in range(npc):
        s = pi % 2
        t0 = pi * TP
        KX, VX, QX, GX, O = kx[s], vx[s], qx[s], gx[s], ob[s]
        lo = 128 if pi == 0 else 0
        P.dma("sp", KX.c, lambda e, KX=KX, t0=t0, lo=lo: e.dma_start(out=KX[:, lo:W], in_=krows[:, t0 - 128 + lo:t0 + TP]),
              writes=[KX])
        P.dma("sp", VX.c, lambda e, VX=VX, t0=t0, lo=lo: e.dma_start(out=VX[:, lo:W], in_=vrows[:, t0 - 128 + lo:t0 + TP]),
              writes=[VX])
        for h in range(NH):
            P.dma("sp", QX.c, lambda e, QX=QX, t0=t0, h=h: e.dma_start(out=QX[:, h, :], in_=qrows[h][:, t0:t0 + TP]),
                  writes=[QX] if h == 0 else [], accs=[] if h == 0 else [QX])
            P.dma("sp", GX.c, lambda e, GX=GX, t0=t0, h=h: e.dma_start(out=GX[:, h, :], in_=grows[h][:, t0:t0 + TP]),
                  writes=[GX] if h == 0 else [], accs=[] if h == 0 else [GX])
        norm(KX[:, lo:W], KX, W - lo, 1, kn[:, lo:W], kn)
        norm(QX[:, :, :].rearrange("p h t -> p (h t)"), QX, NH * TP, 3, qn[:, :, :].rearrange("p h t -> p (h t)"), qn)
        b0 = lo // 128
        pt = pn[npn % 2]
        npn += 1
        for b in range(b0, nbk + 1):
            P.op("pe", lambda e, VX=VX, b=b, pt=pt: e.transpose(out=psc[0][0][:, b * 64:(b + 1) * 64], in_=VX[:, b * 128:(b + 1) * 128],
                                                               identity=K.identf[0:64, 0:64]),
                 reads=[VX, K.identf], writes=[psc[0][0]] if b == b0 else [], accs=[] if b == b0 else [psc[0][0]],
                 signal=(b == nbk))
        P.op("act", lambda e, b0=b0: e.activation(out=vb[:, b0:nbk + 1, :],
                                                 in_=psc[0][0][:, b0 * 64:(nbk + 1) * 64].rearrange("p (b d) -> p b d", d=64),
                                                 func=AF.Copy), reads=[psc[0][0]], writes=[vb])
        for n in range(nbk):
            has_prev = not (pi == 0 and n == 0)
            par = nsc % 2
            nsc += 1
            qs = qn[:, :, n * 128:(n + 1) * 128]
            srcs = [(0, kn[:, (n + 1) * 128:(n + 2) * 128])]
            if has_prev:
                srcs.append((1, kn[:, n * 128:(n + 1) * 128]))
            for j, kap in srcs:
                sc = psc[par][j]
                Eb = E[par][j]
                P.op("pe", lambda e, sc=sc, kap=kap, qs=qs: e.matmul(sc[:, :], lhsT=kap, rhs=qs, start=True, stop=True),
                     reads=[kn, qn], writes=[sc])
                P.op("act", lambda e, sc=sc, Eb=Eb: e.activation(out=Eb[:, :, :],
                                                                in_=sc[:, :].rearrange("p (h q) -> p h q", q=128), func=AF.Exp),
                     reads=[sc], writes=[Eb])
                P.op("dve" if j == 0 else "pool", lambda e, Eb=Eb, j=j: e.tensor_tensor(out=Eb[:, :, :], in0=Eb[:, :, :],
                                                                                       in1=mask4[:, j, :, :], op=ALU.mult),
                     reads=[Eb, mask4], writes=[Eb])
            for (pacc, lhs_cur, lhs_prev) in ((pnum, vb[:, n + 1, :], vb[:, n, :]),
                                              (pden, K.ones_b[:, 0:64], K.ones_b[:, 0:64])):
                P.op("pe", lambda e, pacc=pacc, lhs_cur=lhs_cur, par=par, has_prev=has_prev: e.matmul(
                    pacc[:, :], lhsT=lhs_cur, rhs=E[par][0][:, :, :], start=True, stop=not has_prev),
                    reads=[vb, K.ones_b, E[par][0]], writes=[pacc], signal=False)
                if has_prev:
                    P.op("pe", lambda e, pacc=pacc, lhs_prev=lhs_prev, par=par: e.matmul(
                        pacc[:, :], lhsT=lhs_prev, rhs=E[par][1][:, :, :], start=False, stop=True),
                        reads=[vb, K.ones_b, E[par][1]], accs=[pacc], signal=False)
            P.signal_last("pe")
            P.op("dve", lambda e: e.tensor_tensor(out=dn[:, :, :], in0=pden[:, :].rearrange("p (h q) -> p h q", q=128),
                                                 in1=sinkrow[:, :, :], op=ALU.add), reads=[pden, sinkrow], writes=[dn])
            P.op("act", lambda e: e.activation(out=dn[:, :, :], in_=dn[:, :, :], func=AF.Ln), reads=[dn], writes=[dn])
            P.op("act", lambda e: e.activation(out=dn[:, :, :], in_=dn[:, :, :], func=AF.Exp, scale=-1.0), reads=[dn], writes=[dn])
            P.op("dve", lambda e, n=n: e.tensor_tensor(out=yy[:, :, n * 128:(n + 1) * 128],
                                                      in0=pnum[:, :].rearrange("p (h q) -> p h q", q=128), in1=dn[:, :, :],
                                                      op=ALU.mult), reads=[pnum, dn],
                 writes=[yy] if n == 0 else [], accs=[] if n == 0 else [yy])
        P.op("act", lambda e, GX=GX: e.activation(out=GX[:, :, :], in_=GX[:, :, :], func=AF.Silu), reads=[GX], writes=[GX])
        P.op("dve", lambda e, GX=GX, O=O: e.tensor_tensor(out=O[:, :, :], in0=yy[:, :, :], in1=GX[:, :, :], op=ALU.mult),
             reads=[yy, GX], writes=[O])
        for h in range(NH):
            P.dma("sp", O.c, lambda e, O=O, t0=t0, h=h: e.dma_start(out=orows[h][:, t0:t0 + TP], in_=O[:, h, :]), reads=[O])


import ml_dtypes

NCORES = 8
SEQ = 16384
NT = SEQ // NCORES
NCH_SEQ = 69


def _consts_np():
    s_ = np.arange(128)[:, None]
    t_ = np.arange(128)[None, :]
    reset = np.ones((64, 1024), np.float32)
    reset[:, ::128] = 0
    return dict(
        c_if=np.eye(128, dtype=np.float32),
        c_ib=np.eye(128).astype(ml_dtypes.bfloat16),
        c_mask=np.concatenate([(s_ <= t_), (s_ > t_)], axis=1).astype(ml_dtypes.bfloat16),
        c_ui=np.concatenate([(t_ > s_), (t_ >= s_)], axis=1).astype(np.float32),
        c_sl=(s_ > t_).astype(np.float32),
        c_reset=reset,
    )


def _din(nc, name, shape, dt=F32):
    return nc.dram_tensor(name, list(shape), dt, kind="ExternalInput").ap()


def _dout(nc, name, shape, dt=F32):
    return nc.dram_tensor(name, list(shape), dt, kind="ExternalOutput").ap()


def _mk_consts(nc, P, rw=False):
    K = Consts(P, _din(nc, "c_if", [128, 128]), _din(nc, "c_ib", [128, 128], BF16), _din(nc, "c_mask", [128, 256], BF16))
    RK = None
    if rw:
        RK = RwConsts(P, _din(nc, "c_ui", [128, 256]), _din(nc, "c_sl", [128, 128]), _din(nc, "c_reset", [64, 1024]))
    return K, RK


def build_tok(with_out, with_proj):
    nc = bass.Bass("TRN2", target_bir_lowering=False)
    P = Prog(nc)
    K, _ = _mk_consts(nc, P)
    x = _din(nc, "x", [NT, 2048])
    xcur = x
    if with_out:
        oT = _din(nc, "oT", [2048, NT], BF16)
        wo = _din(nc, "wo", [2048, 2048])
        xout = _dout(nc, "xout", [NT, 2048])
        out_stage(P, K, NT, x, lambda k: oT[128 * k:128 * k + 128, :], wo, xout)
        P.end_stage()
        xcur = xout
    if with_proj:
        g = _din(nc, "g", [128, 16])
        w = _din(nc, "w", [2048, 5440])
        pT = _dout(nc, "pT", [NCH_SEQ * 64, NT])
        pmem = nc.dram_tensor("pmem", [1024, NT], F32).ap()
        omem = _dout(nc, "omem", [512, NT], BF16)

        def dst64(i):
            if i < NCH_SEQ:
                return pT[64 * i:64 * i + 64, :]
            j = i - NCH_SEQ
            return pmem[64 * j:64 * j + 64, :]
        proj_stage(P, K, NT, xcur, g, w, 5440, dst64)
        P.end_stage()
        mem_stage(P, K, NT, lambda h: pmem[128 * h:128 * h + 128, :], lambda h: pmem[512 + 128 * h:512 + 128 * h + 128, :],
                  lambda h: omem[128 * h:128 * h + 128, :], _din(nc, "mem", [256, 2048]), _din(nc, "memg", [128, 16]),
                  _din(nc, "wkv", [2048, 1024]), _din(nc, "mqg", [128, 1]), _din(nc, "mkg", [128, 1]))
        P.end_stage()
    P.close()
    return nc


def build_head():
    nc = bass.Bass("TRN2", target_bir_lowering=False)
    P = Prog(nc)
    K, RK = _mk_consts(nc, P, rw=True)
    pin = _din(nc, "pin", [11 * 64, SEQ])
    sm = _din(nc, "sm", [64, 32])
    lw = _din(nc, "lw", [2, 64, 64])
    lup = _din(nc, "lup", [64, 64])
    oR = _dout(nc, "oR", [192, SEQ], BF16)

    def rows(i, lo=0, hi=64):
        return pin[64 * i + lo:64 * i + hi, :]
    lru_stage(P, 64, SEQ, rows(0), rows(1), oR[0:64, :], sm[:, 0:4], sm[:, 4:5], lw[0:1], sm[:, 5:6], lw[1:2], sm[:, 6:7],
              sm[:, 7:8])
    P.end_stage()
    rwkv_stage(P, K, RK, SEQ, rows(2), rows(3), rows(4), rows(5, 0, 32), rows(5, 32, 64), rows(6), oR[64:128, :],
               sm[:, 8:24], lup[0:32, :], lup[32:64, :])
    P.end_stage()
    swa_stage(P, K, SEQ, rows(7), rows(8), rows(9), rows(10), oR[128:192, :], sm[:, 24:25], sm[:, 25:26], sm[:, 26:27])
    P.end_stage()
    P.close()
    return nc


def _g16(v):
    return np.ascontiguousarray(np.asarray(v, np.float32).reshape(16, 128).T)


def _head_small(inp, l, h):
    hs = slice(64 * h, 64 * h + 64)
    sm = np.zeros((64, 32), np.float32)
    sm[:, 0:4] = inp["conv_w"][l][:, hs].T
    sm[:, 4] = inp["conv_b"][l][hs]
    sm[:, 5] = inp["lru_ba"][l][hs]
    sm[:, 6] = inp["lru_bx"][l][hs]
    sm[:, 7] = inp["lru_lambda"][l][hs]
    mu = inp["rw_mu"][l]
    sm[:, 8] = mu[0:512][hs]
    sm[:, 9] = mu[512:1024][hs]
    sm[:, 10] = mu[1024:1536][hs]
    sm[:, 11] = mu[1536:1600]
    sm[:, 12] = inp["rw_w0"][l][hs]
    sm[:, 13] = inp["rw_a0"][l][hs]
    sm[:, 14] = inp["rw_k_k"][l][hs]
    sm[:, 15] = inp["rw_k_a"][l][hs]
    sm[:, 16] = inp["rw_r_k"][l][h]
    sm[:, 17] = inp["rw_gn_g"][l][hs]
    sm[:, 18] = inp["rw_gn_b"][l][hs]
    sm[:, 24] = inp["swa_q_g"][l]
    sm[:, 25] = inp["swa_k_g"][l]
    sm[:, 26] = inp["swa_sinks"][l][h]
    lw = np.stack([inp["lru_wa"][l][h], inp["lru_wx"][l][h]]).astype(np.float32)
    lup = np.concatenate([inp["rw_w_up"][l][:, hs], inp["rw_a_up"][l][:, hs]], axis=0).astype(np.float32)
    return sm, lw, np.ascontiguousarray(lup)


def kernel(**inputs):
    inp = {k: np.asarray(v) for k, v in inputs.items()}
    cst = _consts_np()
    cst_tok = {k: cst[k] for k in ("c_if", "c_ib", "c_mask")}
    cores = list(range(NCORES))
    x = np.ascontiguousarray(inp["x"][0], dtype=np.float32)
    xs = [np.ascontiguousarray(x[c * NT:(c + 1) * NT]) for c in cores]
    nc_head = None
    oT = None
    for l in range(2):
        with_out = l > 0
        nc_tok = build_tok(with_out, True)
        maps = []
        for c in cores:
            m = dict(cst_tok)
            m.update(x=xs[c], g=_g16(inp["norm_g"][l]), w=np.ascontiguousarray(inp["w_in"][l], dtype=np.float32),
                     mem=np.ascontiguousarray(inp["mem"][0], dtype=np.float32), memg=_g16(inp["mem_norm_g"][l]),
                     wkv=np.ascontiguousarray(inp["w_mem_kv"][l], dtype=np.float32),
                     mqg=inp["mem_q_g"][l].reshape(128, 1).astype(np.float32),
                     mkg=inp["mem_k_g"][l].reshape(128, 1).astype(np.float32))
            if with_out:
                m.update(oT=oT[c], wo=np.ascontiguousarray(inp["w_out"][l - 1], dtype=np.float32))
            maps.append(m)
        res = run_bass_kernel_spmd(nc_tok, maps, core_ids=cores).results
        if with_out:
            xs = [res[c]["xout"] for c in cores]
        pT_all = np.concatenate([res[c]["pT"].reshape(NCH_SEQ, 64, NT) for c in cores], axis=2)
        omem = [res[c]["omem"] for c in cores]
        del res
        if nc_head is None:
            nc_head = build_head()
        maps = []
        for h in cores:
            idx = [h, 8 + h, 16 + h, 24 + h, 32 + h, 40, 41 + h, 49 + h, 57 + h // 4, 59 + h // 4, 61 + h]
            sm, lw, lup = _head_small(inp, l, h)
            m = dict(cst)
            m.update(pin=np.ascontiguousarray(pT_all[idx].reshape(11 * 64, SEQ)), sm=sm, lw=lw, lup=lup)
            maps.append(m)
        del pT_all
        res = run_bass_kernel_spmd(nc_head, maps, core_ids=cores).results
        oR = np.stack([res[h]["oR"] for h in cores])
        del res, maps
        oT = []
        for c in cores:
            o = np.empty((2048, NT), dtype=oR.dtype)
            for grp in range(3):
                o[grp * 512:(grp + 1) * 512] = oR[:, grp * 64:(grp + 1) * 64, c * NT:(c + 1) * NT].reshape(512, NT)
            o[1536:2048] = omem[c]
            oT.append(o)
    nc_fin = build_tok(True, False)
    maps = []
    for c in cores:
        m = dict(cst_tok)
        m.update(x=xs[c], oT=oT[c], wo=np.ascontiguousarray(inp["w_out"][1], dtype=np.float32))
        maps.append(m)
    res = run_bass_kernel_spmd(nc_fin, maps, core_ids=cores).results
    out = np.concatenate([res[c]["xout"] for c in cores], axis=0)
    return out.reshape(1, SEQ, 2048).astype(np.float32)
```
